# Optimizing a Trainium2 kernel written in Bass

```python
import math
import jax, jax.numpy as jnp
from jax import lax
import numpy as np

D_MODEL = 1024
BATCH = 8
SEQ = 2048
DEPTH = 1
DEC_BATCH = 128
DEC_SEQ = 4
PAST_LEN = 16384
PAGE_SIZE = 128

D_MIX = 2 * D_MODEL
S5_WIDTH = D_MIX // 4
S5_GROUP = 16
S5_GROUPS = S5_WIDTH // S5_GROUP
S5_STATE = 64
SSD_WIDTH = D_MIX - S5_WIDTH
SSD_HEAD_DIM = 64
SSD_HEADS = SSD_WIDTH // SSD_HEAD_DIM
SSD_GROUPS = 4
SSD_STATE = 128
SSD_CONV = 4
SSD_CHUNK = 128
SSD_XBC = SSD_WIDTH + 2 * SSD_GROUPS * SSD_STATE
IN_WIDTH = S5_WIDTH + SSD_WIDTH + SSD_XBC + SSD_HEADS
D_FF = ((8 * D_MODEL // 3 + 255) // 256) * 256
EPS = 1e-6

kernel_name = "hymba_s5_ssd_macaron_step"


def rmsnorm(x, w):
    xf = x.astype(jnp.float32)
    xf = xf * lax.rsqrt(jnp.mean(xf * xf, axis=-1, keepdims=True) + EPS)
    return (xf * w.astype(jnp.float32)).astype(x.dtype)


def swiglu(h, w_gate, w_up, w_down):
    return (jax.nn.silu(h @ w_gate) * (h @ w_up)) @ w_down


def _cplx_combine(e1, e2):
    a1r, a1i, b1r, b1i = e1
    a2r, a2i, b2r, b2i = e2
    return (a2r * a1r - a2i * a1i, a2r * a1i + a2i * a1r,
            a2r * b1r - a2i * b1i + b2r, a2r * b1i + a2i * b1r + b2i)


def s5_scan(u, s_re0, s_im0, lam_re, lam_im, log_step, b_re, b_im, c_re, c_im, d_skip):
    f32 = jnp.float32
    lam_re, lam_im = lam_re.astype(f32), lam_im.astype(f32)
    step = jnp.exp(log_step.astype(f32))[:, None]
    mag = jnp.exp(lam_re * step)
    ang = lam_im * step
    abar_re, abar_im = mag * jnp.cos(ang), mag * jnp.sin(ang)
    den = lam_re * lam_re + lam_im * lam_im
    nre, nim = abar_re - 1.0, abar_im
    coef_re = (nre * lam_re + nim * lam_im) / den
    coef_im = (nim * lam_re - nre * lam_im) / den
    b_re, b_im = b_re.astype(f32), b_im.astype(f32)
    bbar_re = coef_re[..., None] * b_re - coef_im[..., None] * b_im
    bbar_im = coef_re[..., None] * b_im + coef_im[..., None] * b_re
    bu_re = jnp.einsum('blgh,gph->lbgp', u, bbar_re)
    bu_im = jnp.einsum('blgh,gph->lbgp', u, bbar_im)
    s_re0, s_im0 = s_re0.astype(f32), s_im0.astype(f32)
    bu_re = bu_re.at[0].add(abar_re * s_re0 - abar_im * s_im0)
    bu_im = bu_im.at[0].add(abar_re * s_im0 + abar_im * s_re0)
    l = u.shape[1]
    a_re = jnp.broadcast_to(abar_re[None, None], (l, 1) + abar_re.shape)
    a_im = jnp.broadcast_to(abar_im[None, None], (l, 1) + abar_im.shape)
    _, _, x_re, x_im = lax.associative_scan(_cplx_combine, (a_re, a_im, bu_re, bu_im), axis=0)
    y = (jnp.einsum('lbgp,ghp->blgh', x_re, c_re.astype(f32))
         - jnp.einsum('lbgp,ghp->blgh', x_im, c_im.astype(f32))
         + d_skip.astype(f32) * u)
    return y, x_re[-1], x_im[-1]


def ssd_chunked(x, dt, a, bmat, cmat, h0):
    b, l, nh, p = x.shape
    g, n = bmat.shape[2], bmat.shape[3]
    r = nh // g
    cs = math.gcd(l, SSD_CHUNK)
    nc = l // cs
    xd = (x * dt[..., None]).reshape(b, nc, cs, g, r, p)
    la = (dt * a).reshape(b, nc, cs, g, r)
    bm = bmat.reshape(b, nc, cs, g, n)
    cm = cmat.reshape(b, nc, cs, g, n)
    acum = jnp.cumsum(la, axis=2)
    diff = acum[:, :, :, None] - acum[:, :, None, :]
    mask = jnp.tril(jnp.ones((cs, cs), dtype=bool))[None, None, :, :, None, None]
    decay = jnp.exp(jnp.where(mask, diff, -jnp.inf))
    cb = jnp.einsum('bclgn,bcsgn->bclsg', cm, bm)
    y_diag = jnp.einsum('bclsg,bclsgr,bcsgrp->bclgrp', cb, decay, xd)
    dstate = jnp.exp(acum[:, :, -1:] - acum)
    states = jnp.einsum('bclgn,bclgr,bclgrp->bcgrpn', bm, dstate, xd)
    chunk_decay = jnp.exp(acum[:, :, -1])

    def step(hc, inp):
        st, dec = inp
        return hc * dec[..., None, None] + st, hc

    h_init = h0.astype(jnp.float32).reshape(b, g, r, p, n)
    h_fin, h_enter = lax.scan(step, h_init, (jnp.moveaxis(states, 1, 0), jnp.moveaxis(chunk_decay, 1, 0)))
    h_enter = jnp.moveaxis(h_enter, 0, 1)
    y_off = jnp.einsum('bclgn,bcgrpn,bclgr->bclgrp', cm, h_enter, jnp.exp(acum))
    y = (y_diag + y_off).reshape(b, l, nh, p)
    return y, h_fin.reshape(b, nh, p, n)


def mixer(h, s5_re0, s5_im0, ssd0, conv0, w_in, s5_lambda_re, s5_lambda_im, s5_log_step,
          s5_b_re, s5_b_im, s5_c_re, s5_c_im, s5_d, s5_w_glu, s5_b_glu,
          ssd_conv_w, ssd_conv_b, ssd_dt_bias, ssd_a_log, ssd_d, ssd_norm, w_out):
    f32 = jnp.float32
    b, l, _ = h.shape
    proj = (h @ w_in).astype(f32)
    u, z, xbc, dt_raw = jnp.split(
        proj, [S5_WIDTH, S5_WIDTH + SSD_WIDTH, S5_WIDTH + SSD_WIDTH + SSD_XBC], axis=-1)
    y5, s5_re1, s5_im1 = s5_scan(u.reshape(b, l, S5_GROUPS, S5_GROUP), s5_re0, s5_im0,
                                 s5_lambda_re, s5_lambda_im, s5_log_step,
                                 s5_b_re, s5_b_im, s5_c_re, s5_c_im, s5_d)
    v = jax.nn.gelu(y5.reshape(b, l, S5_WIDTH))
    o5 = v * jax.nn.sigmoid(v @ s5_w_glu.astype(f32) + s5_b_glu.astype(f32))
    xbc_full = jnp.concatenate([conv0.astype(f32), xbc], axis=1)
    conv_w = ssd_conv_w.astype(f32)
    conv = ssd_conv_b.astype(f32) + sum(conv_w[k] * xbc_full[:, k:k + l] for k in range(SSD_CONV))
    conv1 = xbc_full[:, l:]
    xbc_c = jax.nn.silu(conv)
    xs, bm, cm = jnp.split(xbc_c, [SSD_WIDTH, SSD_WIDTH + SSD_GROUPS * SSD_STATE], axis=-1)
    dt = jax.nn.softplus(dt_raw + ssd_dt_bias.astype(f32))
    a = -jnp.exp(ssd_a_log.astype(f32))
    xh = xs.reshape(b, l, SSD_HEADS, SSD_HEAD_DIM)
    ys, ssd1 = ssd_chunked(xh, dt, a, bm.reshape(b, l, SSD_GROUPS, SSD_STATE),
                           cm.reshape(b, l, SSD_GROUPS, SSD_STATE), ssd0)
    ys = (ys + ssd_d.astype(f32)[:, None] * xh).reshape(b, l, SSD_WIDTH) * jax.nn.silu(z)
    yg = ys.reshape(b, l, SSD_GROUPS, SSD_WIDTH // SSD_GROUPS)
    yg = yg * lax.rsqrt(jnp.mean(yg * yg, axis=-1, keepdims=True) + EPS)
    o_ssd = yg.reshape(b, l, SSD_WIDTH) * ssd_norm.astype(f32)
    out = jnp.concatenate([o5, o_ssd], axis=-1).astype(h.dtype) @ w_out
    return out, s5_re1, s5_im1, ssd1, conv1


def setup_inputs(seed: int = 0) -> dict:
    key = jax.random.key(seed)
    ks = jax.random.split(key, 40)
    f32 = jnp.float32
    nrm = lambda k, shape, s: (jax.random.normal(k, shape, f32) * s)
    D, L = D_MODEL, DEPTH
    G, P, H = S5_GROUPS, S5_STATE, S5_GROUP
    dt0 = jnp.exp(jax.random.uniform(ks[27], (L, SSD_HEADS), f32, math.log(1e-3), math.log(1e-1)))
    return {
        "x_prompt": nrm(ks[0], (BATCH, SEQ, D), 1.0),
        "x_sample": nrm(ks[1], (DEC_BATCH, DEC_SEQ, D), 1.0),
        "state_s5_re": nrm(ks[2], (L, DEC_BATCH, G, P), 0.5),
        "state_s5_im": nrm(ks[3], (L, DEC_BATCH, G, P), 0.5),
        "state_ssd": nrm(ks[4], (L, DEC_BATCH, SSD_HEADS, SSD_HEAD_DIM, SSD_STATE), 0.1),
        "state_conv": nrm(ks[5], (L, DEC_BATCH, SSD_CONV - 1, SSD_XBC), 1.0),
        "ffn1_norm": 1.0 + nrm(ks[6], (L, D), 0.01),
        "ffn1_w_gate": nrm(ks[7], (L, D, D_FF), D ** -0.5),
        "ffn1_w_up": nrm(ks[8], (L, D, D_FF), D ** -0.5),
        "ffn1_w_down": nrm(ks[9], (L, D_FF, D), D_FF ** -0.5),
        "mix_norm": 1.0 + nrm(ks[10], (L, D), 0.01),
        "w_in": nrm(ks[11], (L, D, IN_WIDTH), D ** -0.5),
        "s5_lambda_re": -0.5 + nrm(ks[12], (L, G, P), 0.01),
        "s5_lambda_im": jnp.pi * jnp.arange(P, dtype=f32) + nrm(ks[13], (L, G, P), 0.01),
        "s5_log_step": jax.random.uniform(ks[14], (L, G), f32, math.log(1e-3), math.log(1e-1)),
        "s5_b_re": nrm(ks[15], (L, G, P, H), (2 * H) ** -0.5),
        "s5_b_im": nrm(ks[16], (L, G, P, H), (2 * H) ** -0.5),
        "s5_c_re": nrm(ks[17], (L, G, H, P), (2 * P) ** -0.5),
        "s5_c_im": nrm(ks[18], (L, G, H, P), (2 * P) ** -0.5),
        "s5_d": nrm(ks[19], (L, G, H), 1.0),
        "s5_w_glu": nrm(ks[20], (L, S5_WIDTH, S5_WIDTH), S5_WIDTH ** -0.5),
        "s5_b_glu": nrm(ks[21], (L, S5_WIDTH), 0.01),
        "ssd_conv_w": nrm(ks[22], (L, SSD_CONV, SSD_XBC), SSD_CONV ** -0.5),
        "ssd_conv_b": nrm(ks[23], (L, SSD_XBC), 0.01),
        "ssd_dt_bias": dt0 + jnp.log(-jnp.expm1(-dt0)),
        "ssd_a_log": jnp.log(jax.random.uniform(ks[24], (L, SSD_HEADS), f32, 1.0, 16.0)),
        "ssd_d": 1.0 + nrm(ks[25], (L, SSD_HEADS), 0.1),
        "ssd_norm": 1.0 + nrm(ks[26], (L, SSD_WIDTH), 0.01),
        "w_out": nrm(ks[28], (L, D_MIX, D), D_MIX ** -0.5),
        "ffn2_norm": 1.0 + nrm(ks[29], (L, D), 0.01),
        "ffn2_w_gate": nrm(ks[30], (L, D, D_FF), D ** -0.5),
        "ffn2_w_up": nrm(ks[31], (L, D, D_FF), D ** -0.5),
        "ffn2_w_down": nrm(ks[32], (L, D_FF, D), D_FF ** -0.5),
        "final_norm": 1.0 + nrm(ks[33], (D,), 0.01),
    }


def reference(x_prompt, x_sample, state_s5_re, state_s5_im, state_ssd, state_conv,
              ffn1_norm, ffn1_w_gate, ffn1_w_up, ffn1_w_down, mix_norm, w_in,
              s5_lambda_re, s5_lambda_im, s5_log_step, s5_b_re, s5_b_im, s5_c_re, s5_c_im,
              s5_d, s5_w_glu, s5_b_glu, ssd_conv_w, ssd_conv_b, ssd_dt_bias, ssd_a_log,
              ssd_d, ssd_norm, w_out, ffn2_norm, ffn2_w_gate, ffn2_w_up, ffn2_w_down, final_norm):
    f32 = jnp.float32
    xp, xs = x_prompt, x_sample
    p_states = (jnp.zeros((BATCH, S5_GROUPS, S5_STATE), f32),
                jnp.zeros((BATCH, S5_GROUPS, S5_STATE), f32),
                jnp.zeros((BATCH, SSD_HEADS, SSD_HEAD_DIM, SSD_STATE), f32),
                jnp.zeros((BATCH, SSD_CONV - 1, SSD_XBC), f32))
    new_p = ([], [], [], [])
    new_s = ([], [], [], [])
    for i in range(DEPTH):
        mix_params = (w_in[i], s5_lambda_re[i], s5_lambda_im[i], s5_log_step[i],
                      s5_b_re[i], s5_b_im[i], s5_c_re[i], s5_c_im[i], s5_d[i],
                      s5_w_glu[i], s5_b_glu[i], ssd_conv_w[i], ssd_conv_b[i],
                      ssd_dt_bias[i], ssd_a_log[i], ssd_d[i], ssd_norm[i], w_out[i])
        s_states = (state_s5_re[i], state_s5_im[i], state_ssd[i], state_conv[i])
        outs = []
        for x, st, acc in ((xp, p_states, new_p), (xs, s_states, new_s)):
            x = x + 0.5 * swiglu(rmsnorm(x, ffn1_norm[i]), ffn1_w_gate[i], ffn1_w_up[i], ffn1_w_down[i])
            m, s5r, s5i, ssd1, conv1 = mixer(rmsnorm(x, mix_norm[i]), *st, *mix_params)
            x = x + m.astype(x.dtype)
            x = x + 0.5 * swiglu(rmsnorm(x, ffn2_norm[i]), ffn2_w_gate[i], ffn2_w_up[i], ffn2_w_down[i])
            for lst, val in zip(acc, (s5r, s5i, ssd1, conv1)):
                lst.append(val)
            outs.append(x)
        xp, xs = outs
    y_prompt = rmsnorm(xp, final_norm)
    y_sample = rmsnorm(xs, final_norm)
    return (y_prompt, y_sample,
            jnp.stack(new_p[0]), jnp.stack(new_p[1]), jnp.stack(new_p[2]), jnp.stack(new_p[3]),
            jnp.stack(new_s[0]), jnp.stack(new_s[1]), jnp.stack(new_s[2]), jnp.stack(new_s[3]))
```

```python
import os
import numpy as np
import concourse.bass as bass
import concourse.mybir as mybir
from concourse.bass_utils import run_bass_kernel_spmd

F32 = mybir.dt.float32
BF16 = mybir.dt.bfloat16
AF = mybir.ActivationFunctionType
ALU = mybir.AluOpType

NCORES = 8
D = 1024
DFF = 2816
TP = 2048
TS = 64
NT = TP + TS
EPS = 1e-6
NDMA = 24
NSW = 64


class Sched:
    def __init__(self, nc):
        self.nc = nc
        self.e = {'pe': nc.tensor, 'act': nc.scalar, 'dve': nc.vector, 'pool': nc.gpsimd, 'sp': nc.sync}
        self.sem = {k: nc.alloc_semaphore('s_' + k) for k in self.e}
        self.cnt = {k: 0 for k in self.e}
        self.waited = {}
        self.lastw = {}
        self.readers = {}
        self.dsem = [nc.alloc_semaphore('d%d' % i) for i in range(NDMA + NSW)]
        self.duse = [0] * (NDMA + NSW)
        self.dnext = 0
        self.swnext = NDMA
        self.nwait = 0
        self.nins = 0
        self._rec = None

    def _semobj(self, sk):
        return self.sem[sk] if isinstance(sk, str) else self.dsem[sk[1]]

    def _deps(self, r, w):
        deps = {}

        def add(tok):
            sk, v = tok
            if deps.get(sk, 0) < v:
                deps[sk] = v
        for k in r:
            if k in self.lastw:
                add(self.lastw[k])
        for k in w:
            if k in self.lastw:
                add(self.lastw[k])
            for sk, v in self.readers.get(k, {}).items():
                add((sk, v))
        return deps

    def _wait(self, eng, deps):
        for sk, v in deps.items():
            if sk == eng:
                if eng == 'pe' or os.environ.get('KNOSAME') == '1':
                    continue
                assert v <= self.cnt[eng], "same-engine dep on non-inc instruction"
            if self.waited.get((eng, sk), 0) >= v:
                continue
            self.e[eng].wait_ge(self._semobj(sk), v)
            self.waited[(eng, sk)] = v
            self.nwait += 1

    def _record(self, tok, r, w):
        sk, v = tok
        for k in r:
            d = self.readers.setdefault(k, {})
            if d.get(sk, 0) < v:
                d[sk] = v
        for k in w:
            self.lastw[k] = tok
            self.readers[k] = {}

    @staticmethod
    def _split(r, w):
        def isps(k):
            return (isinstance(k, tuple) and k[0] == 'ps') or (isinstance(k, str) and k.startswith('ps'))
        r2 = [k for k in r if not isps(k)]
        w2 = list(w) + [k for k in r if isps(k)]
        return r2, w2

    def record(self, f):
        self._rec = []
        f()
        out, self._rec = self._rec, None
        return out

    def replay(self, items):
        for it in items:
            if it[0] == 'op':
                self.op(*it[1:])
            else:
                self.dma(*it[1:-1], **it[-1])

    @staticmethod
    def merge(a, b):
        out = []
        ia = ib = 0
        na, nb = len(a), len(b)
        while ia < na or ib < nb:
            if ib >= nb or (ia < na and ia * nb <= ib * na):
                out.append(a[ia])
                ia += 1
            else:
                out.append(b[ib])
                ib += 1
        return out

    def op(self, eng, fn, r=(), w=(), inc=True):
        if self._rec is not None:
            self._rec.append(('op', eng, fn, tuple(r), tuple(w), inc))
            return
        r, w = self._split(r, w)
        self._wait(eng, self._deps(r, w))
        ins = fn(self.e[eng])
        self.nins += 1
        if inc:
            self.cnt[eng] += 1
            ins.then_inc(self.sem[eng], 1)
            tok = (eng, self.cnt[eng])
        else:
            tok = (eng, self.cnt[eng] + 1)
        self._record(tok, r, w)

    def dma(self, q, out, in_, r=(), w=(), **kw):
        if self._rec is not None:
            self._rec.append(('dma', q, out, in_, tuple(r), tuple(w), kw))
            return
        if q == 'pool':
            i = self.swnext
            self.swnext += 1
            assert i < NDMA + NSW, "out of SW-DMA semaphores"
        else:
            i = self.dnext
            self.dnext = (self.dnext + 1) % NDMA
        deps = self._deps(r, w)
        if self.duse[i] > 0:
            sk = ('d', i)
            deps[sk] = max(deps.get(sk, 0), self.duse[i] * 16)
        self._wait(q, deps)
        ins = self.e[q].dma_start(out=out, in_=in_, **kw)
        self.duse[i] += 1
        ins.then_inc(self.dsem[i], 16)
        self.nins += 1
        self._record((('d', i), self.duse[i] * 16), r, w)

    def barrier(self):
        toks = {k: self.cnt[k] for k in ('pe', 'act', 'dve', 'pool') if self.cnt[k] > 0}
        dt = {('d', i): self.duse[i] * 16 for i in range(NDMA + NSW) if self.duse[i] > 0}
        for eng in ('pe', 'act', 'dve', 'pool', 'sp'):
            deps = {k: v for k, v in toks.items() if k != eng}
            deps.update(dt)
            self._wait(eng, deps)

    def finish(self):
        for i in range(NDMA + NSW):
            if self.duse[i] > 0:
                self.e['sp'].wait_ge(self.dsem[i], self.duse[i] * 16)
        for k in ('pe', 'act', 'dve', 'pool'):
            if self.cnt[k] > 0:
                self.e['sp'].wait_ge(self.sem[k], self.cnt[k])


def token_tiles():
    tiles = [(i * 512, 512) for i in range(TP // 512)]
    tiles.append((TP, TS))
    return tiles


def build(stage=1):
    nc = bass.Bass("TRN2", target_bir_lowering=False)
    S = Sched(nc)

    def din(name, shape, dt=F32):
        return nc.dram_tensor(name, list(shape), dt, kind="ExternalInput").ap()

    def dout(name, shape, dt=F32):
        return nc.dram_tensor(name, list(shape), dt, kind="ExternalOutput").ap()

    def sb(name, shape, dt):
        return nc.alloc_sbuf_tensor(name, list(shape), dt).ap()

    xT = din("xT", [D, NT])
    normw_d = din("normw", [128, 32])
    wg_d = [din("ffn%d_wg" % i, [D, DFF]) for i in (1, 2)]
    wu_d = [din("ffn%d_wu" % i, [D, DFF]) for i in (1, 2)]
    wd_d = [din("ffn%d_wd" % i, [DFF, D]) for i in (1, 2)]
    yT = dout("yT", [D, NT])
    cmat_d = din("cmat", [128, 2048])
    hp_d = din("hp", [128, 72])
    nbc_d = din("nbc", [128, 1536])
    wssd_d = din("w_ssd", [4, D, 1030])
    woutssd_d = din("w_out_ssd", [4, 384, D])
    cwb_d = din("cwb", [128, 4, 5, 5])
    conv0_d = din("conv0", [128, 4, 5, 48])
    ssd0_d = din("ssd0", [16, 4, 128, 384])
    convp_d = dout("convp", [128, 4, 5, 3])
    convs_d = dout("convs", [128, 4, 5, 48])
    ssdp_d = dout("ssdp", [4, 128, 384])
    ssds_d = dout("ssds", [16, 4, 128, 384])
    winu_d = din("w_in_u", [D, 512])
    wouts5_d = din("w_out_s5", [512, D])
    wglu_d = din("w_glu", [512, 512])
    s5v_d = din("s5v", [128, 8])
    lam_d = din("lam", [128, 3, 272])
    bT_d = din("bT", [128, 2, 4, 64])
    emask_d = din("emask", [128, 8])
    cpad_d = din("cpad", [2, 128, 16, 128])
    s5s0_d = din("s5s0", [128, 2, 16, 16])
    s5p_d = dout("s5p", [128, 2, 16])
    s5s_d = dout("s5s", [128, 2, 16, 16])

    x = sb("x", [128, 8, NT], F32)
    h = sb("h", [128, 8, NT], BF16)
    normw = sb("normw_sb", [128, 32], F32)
    ones_bf = sb("ones_bf", [128, 128], BF16)
    epsc = sb("epsc", [128, 1], F32)
    ps = [nc.alloc_psum_tensor("ps%d" % i, [128, 512], F32).ap() for i in range(8)]

    ARENA = 111040
    arena = sb("arena", [128, ARENA // 2], BF16)

    class Carver:
        def __init__(self):
            self.off = 0

        def get(self, shape, dt):
            n = 1
            for s_ in shape[1:]:
                n *= s_
            nb = n * (2 if dt == BF16 else 4)
            nb = (nb + 31) // 32 * 32
            assert self.off + nb <= ARENA, ("arena overflow", self.off + nb)
            ap = arena[:shape[0], self.off // 2:(self.off + nb) // 2]
            self.off += nb
            if dt != BF16:
                ap = ap.bitcast(dt)
            ap = ap[:, :n]
            if len(shape) == 3:
                ap = ap.rearrange("p (a b) -> p a b", a=shape[1])
            elif len(shape) == 4:
                ap = ap.rearrange("p (a b c) -> p a b c", a=shape[1], b=shape[2])
            return ap

    GMAX = 6
    cv = Carver()
    wg_s = [cv.get([128, 8, GMAX * 128], BF16) for i in range(2)]
    wu_s = [cv.get([128, 8, GMAX * 128], BF16) for i in range(2)]
    wd_s = [cv.get([128, GMAX, D], BF16) for i in range(2)]
    sq = cv.get([128, 8, 512], BF16)
    rstd = [cv.get([128, 512], F32) for i in range(2)]
    sg = [cv.get([128, 512], F32) for i in range(2)]
    actb = [cv.get([128, GMAX, 512], BF16) for i in range(2)]
    ystage = [cv.get([128, 512], F32) for i in range(2)]

    S.op('pool', lambda e: e.memset(ones_bf, 1.0), w=['ones_bf'])
    S.op('pool', lambda e: e.memset(epsc, EPS), w=['epsc'])
    S.dma('sp', normw, normw_d, w=['normw'])
    xT_v = xT.rearrange("(c p) t -> p c t", p=128)
    for c in range(8):
        S.dma('sp', x[:, c, :], xT_v[:, c, :], w=[('x', c, tt) for tt in range(5)])

    TT = token_tiles()

    def rmsnorm(widx, out_fn):
        for tt, (t0, n) in enumerate(TT):
            for c in range(8):
                S.op('act', lambda e, c=c: e.activation(out=sq[:, c, :n], in_=x[:, c, t0:t0 + n], func=AF.Square),
                     r=[('x', c, tt)], w=[('sq', c)])
            for c in range(8):
                S.op('pe', lambda e, c=c: e.matmul(ps[0][:, :n], ones_bf, sq[:, c, :n], start=(c == 0), stop=(c == 7)),
                     r=[('sq', c), 'ones_bf'], w=[('ps', 0)], inc=(c == 7))
            rb = rstd[tt % 2]
            S.op('act', lambda e: e.activation(out=rb[:, :n], in_=ps[0][:, :n], func=AF.Sqrt, bias=epsc[:, 0:1], scale=1.0 / D),
                 r=[('ps', 0), 'epsc'], w=[('rstd', tt % 2)])
            S.op('dve', lambda e: e.reciprocal(out=rb[:, :n], in_=rb[:, :n]), r=[('rstd', tt % 2)], w=[('rstd', tt % 2)])
            for c in range(8):
                out_fn(tt, c, t0, n, rb)

    def norm_to_h(widx):
        def f(tt, c, t0, n, rb):
            S.op('dve', lambda e: e.scalar_tensor_tensor(out=h[:, c, t0:t0 + n], in0=x[:, c, t0:t0 + n],
                                                         scalar=normw[:, widx * 8 + c:widx * 8 + c + 1], in1=rb[:, :n],
                                                         op0=ALU.mult, op1=ALU.mult),
                 r=[('x', c, tt), ('rstd', tt % 2), 'normw'], w=[('h', c, tt)])
        rmsnorm(widx, f)

    def ffn(fi):
        Wg = wg_d[fi].rearrange("(kc p) f -> p kc f", p=128)
        Wu = wu_d[fi].rearrange("(kc p) f -> p kc f", p=128)
        Wd = wd_d[fi].rearrange("(fc p) d -> p fc d", p=128)
        groups = [(0, 6), (6, 6), (12, 6), (18, 4)]

        def load(gi):
            c0, G = groups[gi]
            b = gi % 2
            for kc in range(0, 8, 4):
                S.dma('pool', wg_s[b][:, kc:kc + 4, :G * 128], Wg[:, kc:kc + 4, c0 * 128:(c0 + G) * 128], w=[('wg', b, kc)])
                S.dma('pool', wu_s[b][:, kc:kc + 4, :G * 128], Wu[:, kc:kc + 4, c0 * 128:(c0 + G) * 128], w=[('wu', b, kc)])
            S.dma('pool', wd_s[b][:, :G, :], Wd[:, c0:c0 + G, :], w=[('wd', b)])
        load(0)
        pcount = 0
        ocount = 0
        for gi, (c0, G) in enumerate(groups):
            b = gi % 2
            if gi + 1 < len(groups):
                load(gi + 1)
            for tt, (t0, n) in enumerate(TT):
                ab = (gi * len(TT) + tt) % 2
                for fc in range(G):
                    pgi = pcount % 2
                    pui = 2 + pcount % 2
                    pcount += 1
                    for kc in range(8):
                        S.op('pe', lambda e, kc=kc: e.matmul(ps[pgi][:, :n], wg_s[b][:, kc, fc * 128:(fc + 1) * 128], h[:, kc, t0:t0 + n],
                                                             start=(kc == 0), stop=(kc == 7)),
                             r=[('wg', b, kc // 4 * 4), ('h', kc, tt)], w=[('ps', pgi)], inc=(kc == 7))
                    for kc in range(8):
                        S.op('pe', lambda e, kc=kc: e.matmul(ps[pui][:, :n], wu_s[b][:, kc, fc * 128:(fc + 1) * 128], h[:, kc, t0:t0 + n],
                                                             start=(kc == 0), stop=(kc == 7)),
                             r=[('wu', b, kc // 4 * 4), ('h', kc, tt)], w=[('ps', pui)], inc=(kc == 7))
                    sgb = sg[pcount % 2]
                    S.op('act', lambda e: e.activation(out=sgb[:, :n], in_=ps[pgi][:, :n], func=AF.Silu),
                         r=[('ps', pgi)], w=[('sg', pcount % 2)])
                    S.op('dve', lambda e: e.tensor_tensor(out=actb[ab][:, fc, :n], in0=ps[pui][:, :n], in1=sgb[:, :n], op=ALU.mult),
                         r=[('ps', pui), ('sg', pcount % 2)], w=[('actb', ab, fc)])
                for dc in range(8):
                    poi = 4 + ocount % 4
                    ocount += 1
                    for fc in range(G):
                        S.op('pe', lambda e, fc=fc: e.matmul(ps[poi][:, :n], wd_s[b][:, fc, dc * 128:(dc + 1) * 128], actb[ab][:, fc, :n],
                                                             start=(fc == 0), stop=(fc == G - 1)),
                             r=[('wd', b), ('actb', ab, fc)], w=[('ps', poi)], inc=(fc == G - 1))
                    S.op('dve', lambda e: e.scalar_tensor_tensor(out=x[:, dc, t0:t0 + n], in0=ps[poi][:, :n], scalar=0.5,
                                                                 in1=x[:, dc, t0:t0 + n], op0=ALU.mult, op1=ALU.add),
                         r=[('ps', poi), ('x', dc, tt)], w=[('x', dc, tt)])

    def TTo(eng, out, a, b, op, r, w):
        S.op(eng, lambda e: e.tensor_tensor(out=out, in0=a, in1=b, op=op), r=r, w=w)

    def TSo(eng, out, a, s1, op0, r, w, s2=None, op1=None):
        if op1 is None:
            S.op(eng, lambda e: e.tensor_scalar(out=out, in0=a, scalar1=s1, scalar2=None, op0=op0), r=r, w=w)
        else:
            S.op(eng, lambda e: e.tensor_scalar(out=out, in0=a, scalar1=s1, scalar2=s2, op0=op0, op1=op1), r=r, w=w)

    def STTo(eng, out, in0, sc, in1, op0, op1, r, w):
        S.op(eng, lambda e: e.scalar_tensor_tensor(out=out, in0=in0, scalar=sc, in1=in1, op0=op0, op1=op1), r=r, w=w)

    def ACTo(out, in_, func, r, w, **kw):
        S.op('act', lambda e: e.activation(out=out, in_=in_, func=func, **kw), r=r, w=w)

    def MMo(out, lhsT, rhs, r, w, start=True, stop=True, inc=True):
        S.op('pe', lambda e: e.matmul(out, lhsT, rhs, start=start, stop=stop), r=r, w=w, inc=inc)

    def TRo(out, in_, ident, r, w, inc=True):
        S.op('pe', lambda e: e.transpose(out, in_, ident), r=r, w=w, inc=inc)

    def CPo(eng, out, in_, r, w):
        if eng == 'act':
            S.op(eng, lambda e: e.activation(out=out, in_=in_, func=AF.Copy), r=r, w=w)
        else:
            S.op(eng, lambda e: e.tensor_copy(out=out, in_=in_), r=r, w=w)

    big_rot = [0]

    def bigbank():
        b = big_rot[0] % 2
        big_rot[0] += 1
        return b

    def ssd_phase():
        S.barrier()
        cv = Carver()
        cm = cv.get([128, 2048], F32)
        S.dma('sp', cm, cmat_d, w=['cm'])
        TI = cm[:, 0:128]
        SU = cm[:, 128:256]
        TIb = cm[:, 256:384]
        SUb = cm[:, 384:512]
        identf = cm[:, 512:640]
        Emat = cm[:, 640:656]
        ident_bf = cv.get([128, 128], BF16)
        CPo('dve', ident_bf, identf, ['cm'], ['ident_bf'])
        ones_f = cv.get([128, 128], F32)
        S.op('pool', lambda e: e.memset(ones_f, 1.0), w=['ones_f'])
        onec = cv.get([128, 1], F32)
        S.op('pool', lambda e: e.memset(onec, 1.0), w=['onec'])
        hp = cv.get([128, 72], F32)
        S.dma('sp', hp, hp_d, w=['hp'])
        abc = cv.get([128, 24], F32)
        ACTo(abc, hp[:, 24:48], AF.Exp, ['hp'], ['abc'])
        TSo('dve', abc, abc, -1.0, ALU.mult, ['abc'], ['abc'])
        cwb = cv.get([128, 4, 5, 5], F32)
        S.dma('sp', cwb, cwb_d, w=['cwb'])

        wssd = cv.get([128, 8, 1030], BF16)
        wout = cv.get([128, 3, 1024], BF16)
        diag = cv.get([128, 5, 4, 128], BF16)
        nbc = cv.get([128, 384], F32)
        xraw = [cv.get([128, 5, 515], BF16) for _ in range(2)]
        xraw_s = cv.get([128, 5, 16, 7], BF16)
        conv0s = cv.get([128, 5, 48], F32)
        convo = cv.get([128, 5, 3], F32)
        convos = cv.get([128, 5, 16, 3], F32)
        fm = cv.get([128, 5, 512], BF16)
        ofm = cv.get([128, 3, 512], BF16)
        dt6 = cv.get([128, 6], F32)
        la = cv.get([128, 6], F32)
        rhsla = cv.get([128, 768], F32)
        rhscd = cv.get([128, 96], F32)
        decT = cv.get([128, 768], BF16)
        MT2 = [cv.get([128, 768], BF16) for _ in range(2)]
        CBm = cv.get([128, 128], BF16)
        ea2 = [cv.get([128, 12], F32) for _ in range(2)]
        cdx2 = [cv.get([128, 96], F32) for _ in range(2)]
        xd2 = [cv.get([128, 6, 64], BF16) for _ in range(2)]
        xdd2 = [cv.get([128, 6, 64], BF16) for _ in range(2)]
        Btm2 = [cv.get([128, 128], BF16) for _ in range(2)]
        Btmm = [cv.get([128, 128], BF16) for _ in range(4)]
        skipx2 = [cv.get([128, 6, 64], F32) for _ in range(2)]
        yv = cv.get([128, 6, 64], F32)
        sz4 = [cv.get([128, 384], F32) for _ in range(4)]
        t6s = [cv.get([128, 6], F32) for _ in range(4)]
        junk = cv.get([128, 384], BF16)
        ss = cv.get([128, 1], F32)
        otm = cv.get([128, 384], BF16)
        hT = cv.get([128, 6, 64], F32)
        htmp = cv.get([128, 6, 64], F32)
        hTb = cv.get([128, 384], BF16)
        hs = [cv.get([128, 6, 64], F32) for _ in range(4)]
        hsb = [cv.get([128, 384], BF16) for _ in range(4)]
        hso = [cv.get([128, 6, 64], F32) for _ in range(4)]
        Cmask = cv.get([128, 16, 64], BF16)
        print("ssd arena used", cv.off)

        def bc6(ap, T):
            return ap[:T, :, None].to_broadcast([T, 6, 64])

        def ssd_pre(g, T, tok0, ci):
            g6 = slice(g * 6, g * 6 + 6)
            bk = 2 if ci % 2 == 0 else 6
            kb = 'ps%d' % bk
            for kc in range(8):
                MMo(ps[bk][:T, 0:390], h[:, kc, tok0:tok0 + T], wssd[:, kc, 640:1030], r=[('h', kc), 'wssd'], w=[kb],
                    start=(kc == 0), stop=(kc == 7), inc=(kc == 7))
            TTo('dve', t6s[ci][:T], ps[bk][:T, 384:390], hp[:T, g6], ALU.add, [kb, 'hp'], [('t6', ci)])
            ACTo(sz4[ci][:T, :], ps[bk][:T, 0:384], AF.Silu, [kb], [('sz4', ci)])

        def ssd_late(g, T, tok0, col0, sample, last, p, tI, sU, MT, xd, xdd, Btm, skipx, sz, ea, cdx,
                     kMT, kxd, kxdd, kBtm, kskipx, ksz, kea, kcdx, yv2):
            for hh in range(6):
                MMo(ps[7][:T, hh * 64:(hh + 1) * 64], MT[:T, hh * T:(hh + 1) * T], xd[:T, hh, :], r=[kMT, kxd], w=['ps7'], inc=(hh == 5))
            if not sample:
                MMo(ps[3][:T, 0:384], fm[:, 4, col0:col0 + T], hTb, r=['fmC', 'hTb'], w=['ps3'])
                MMo(ps[4][:, 0:384], Btm[:T, :], xdd[:T].rearrange("p a b -> p (a b)"), r=[kBtm, kxdd], w=['ps4'])
            else:
                TTo('pool', Cmask, fm[:, 4, None, col0:col0 + T].to_broadcast([128, 16, T]),
                    cm[:, 1024:2048].rearrange("p (a b) -> p a b", a=16), ALU.mult, ['fmC', 'cm'], ['Cmask'])
                for b in range(16):
                    rb_ = b % 4
                    S.dma('sp', hs[rb_].rearrange("p a b -> p (a b)"), ssd0_d[b, g], w=[('hs', rb_)])
                    CPo('act', hsb[rb_], hs[rb_].rearrange("p a b -> p (a b)"), [('hs', rb_)], [('hsb', rb_)])
                    MMo(ps[3][:T, 0:384], Cmask[:, b, :], hsb[rb_], r=['Cmask', ('hsb', rb_)], w=['ps3'], start=(b == 0), stop=(b == 15),
                        inc=(b == 15))
                    TSo('dve', Btmm[rb_][:T, :], Btm[:T, :], Emat[:T, b:b + 1], ALU.mult, [kBtm, 'cm'], [('Btmm', rb_)])
                    MMo(ps[4][:, 0:384], Btmm[rb_][:T, :], xdd[:T].rearrange("p a b -> p (a b)"), r=[('Btmm', rb_), kxdd], w=['ps4'])
                    TTo('dve', hso[rb_], hs[rb_], cdx[:, b * 6:(b + 1) * 6][:, :, None].to_broadcast([128, 6, 64]), ALU.mult,
                        [('hs', rb_), kcdx], [('hso', rb_)])
                    TTo('dve', hso[rb_].rearrange("p a b -> p (a b)"), hso[rb_].rearrange("p a b -> p (a b)"), ps[4][:, 0:384], ALU.add,
                        [('hso', rb_), 'ps4'], [('hso', rb_)])
                    S.dma('sp', ssds_d[b, g], hso[rb_].rearrange("p a b -> p (a b)"), r=[('hso', rb_)], w=[('ssds', b, g)])
            TTo('dve', yv[:T], ps[3][:T, 0:384].rearrange("p (a b) -> p a b", a=6), bc6(ea[:, 0:6], T), ALU.mult, ['ps3', kea], ['yv'])
            TTo('dve', yv2, yv2, ps[7][:T, 0:384], ALU.add, ['yv', 'ps7'], ['yv'])
            TTo('pool', yv[:T], yv[:T], skipx[:T], ALU.add, ['yv', kskipx], ['yv'])
            TTo('pool', yv2, yv2, sz[:T, :], ALU.mult, ['yv', ksz], ['yv'])
            ACTo(junk[:T, :], yv2, AF.Square, ['yv'], ['junk', 'ss'], accum_out=ss[:T, 0:1])
            ACTo(ss[:T, :], ss[:T, :], AF.Ln, ['ss', 'epsc'], ['ss'], bias=epsc[:T, 0:1], scale=1.0 / 384)
            ACTo(ss[:T, :], ss[:T, :], AF.Exp, ['ss'], ['ss'], scale=-0.5)
            STTo('dve', otm[:T, :], yv2, ss[:T, 0:1], nbc[:T, :], ALU.mult, ALU.mult, ['yv', 'ss', 'nbc'], ['otm'])
            if not sample:
                TTo('dve', htmp, hT, cdx[:, 0:6][:, :, None].to_broadcast([128, 6, 64]), ALU.mult, ['hT', kcdx], ['htmp'])
                TTo('dve', hT.rearrange("p a b -> p (a b)"), htmp.rearrange("p a b -> p (a b)"), ps[4][:, 0:384], ALU.add,
                    ['htmp', 'ps4'], ['hT'])
                if last:
                    S.dma('sp', ssdp_d[g], hT.rearrange("p a b -> p (a b)"), r=['hT'], w=[('ssdp', g)])
                else:
                    CPo('act', hTb, hT.rearrange("p a b -> p (a b)"), ['hT'], ['hTb'])
            po = ps[7].bitcast(BF16)[:, 0:512]
            for j in range(3):
                TRo(po[:, j * 128:j * 128 + T], otm[:T, j * 128:(j + 1) * 128], ident_bf[:T, :T], r=['otm', 'ident_bf'], w=['ps7'], inc=(j == 2))
            CPo('act', ofm[:, :, col0:col0 + T], po[:, 0:384].rearrange("p (a b) -> p a b", a=3)[:, :, :T], ['ps7'], ['ofm'])


        def ssd_chunk(g, T, tok0, col0, sample, last, p, part, ci=0):
            g6 = slice(g * 6, g * 6 + 6)
            tI = TIb if sample else TI
            sU = SUb if sample else SU
            MT, xd, xdd, Btm, skipx, sz, ea, cdx = MT2[p], xd2[p], xdd2[p], Btm2[p], skipx2[p], sz4[ci], ea2[p], cdx2[p]
            kMT, kxd, kxdd, kBtm, kskipx, ksz, kea, kcdx = [(nm, p) for nm in ('MT', 'xd', 'xdd', 'Btm', 'skipx', 'sz', 'ea', 'cdx')]
            ksz = ('sz4', ci)
            yv2 = yv[:T].rearrange("p a b -> p (a b)")
            if part == 'late':
                return ssd_late(g, T, tok0, col0, sample, last, p, tI, sU, MT, xd, xdd, Btm, skipx, sz, ea, cdx,
                                kMT, kxd, kxdd, kBtm, kskipx, ksz, kea, kcdx, yv2)
            t6 = t6s[ci]
            kt6 = ('t6', ci)
            ACTo(t6[:T], t6[:T], AF.Exp, [kt6], [kt6])
            ACTo(dt6[:T], t6[:T], AF.Ln, [kt6, 'onec'], ['dt6'], bias=onec[:T, 0:1], scale=1.0)
            TTo('dve', la[:T], dt6[:T], abc[:T, g6], ALU.mult, ['dt6', 'abc'], ['la'])
            rl3 = rhsla[:T, :6 * T].rearrange("p (a b) -> p a b", a=6)
            TTo('pool', rl3, la[:T, :, None].to_broadcast([T, 6, T]), tI[:T, None, :T].to_broadcast([T, 6, T]), ALU.mult,
                ['la', 'cm'], ['rhsla'])
            ncd = 6
            if sample:
                ncd = 96
                TTo('pool', rhscd[:T, :96].rearrange("p (a b) -> p a b", a=16), la[:T, None, :].to_broadcast([T, 16, 6]),
                    Emat[:T, :, None].to_broadcast([T, 16, 6]), ALU.mult, ['la', 'cm'], ['rhscd'])
            MMo(ps[0][:T, :3 * T], sU[:T, :T], rhsla[:T, 0:3 * T], r=['cm', 'rhsla'], w=[('ps', 0)])
            MMo(ps[1][:T, :3 * T], sU[:T, :T], rhsla[:T, 3 * T:6 * T], r=['cm', 'rhsla'], w=[('ps', 1)])
            MMo(ps[5][:T, 0:6], tI[:T, :T], la[:T, :], r=['cm', 'la'], w=['ps5'])
            MMo(ps[5][:T, 6:12], sU[:T, :T], la[:T, :], r=['cm', 'la'], w=['ps5'])
            if sample:
                MMo(ps[5][:, 16:16 + 96], ones_f[:T, :], rhscd[:T, :96], r=['ones_f', 'rhscd'], w=['ps5'])
            else:
                MMo(ps[5][:, 16:22], ones_f[:T, :], la[:T, :], r=['ones_f', 'la'], w=['ps5'])
            ACTo(decT[:T, 0:3 * T], ps[0][:T, :3 * T], AF.Exp, [('ps', 0)], ['decTa'])
            ACTo(decT[:T, 3 * T:6 * T], ps[1][:T, :3 * T], AF.Exp, [('ps', 1)], ['decTb'])
            ACTo(ea[:T, :], ps[5][:T, 0:12], AF.Exp, ['ps5'], [kea])
            ACTo(cdx[:, :ncd], ps[5][:, 16:16 + ncd], AF.Exp, ['ps5'], [kcdx])
            MMo(ps[5][:T, 128:128 + T], fm[:, 3, col0:col0 + T], fm[:, 4, col0:col0 + T], r=['fmB', 'fmC'], w=['ps5'])
            TTo('dve', CBm[:T, :T], ps[5][:T, 128:128 + T], tI[:T, :T], ALU.mult, ['ps5', 'cm'], ['CBm'])
            TTo('dve', MT[:T, :6 * T].rearrange("p (a b) -> p a b", a=6), decT[:T, :6 * T].rearrange("p (a b) -> p a b", a=6),
                CBm[:T, None, :T].to_broadcast([T, 6, T]), ALU.mult, ['decTa', 'decTb', 'CBm'], [kMT])
            pt = ps[6].bitcast(BF16)
            for j in range(4):
                TRo(pt[:T, j * 128:(j + 1) * 128], fm[:, j, col0:col0 + T], ident_bf, r=[('fmx', j) if j < 3 else 'fmB', 'ident_bf'],
                    w=['ps6'], inc=(j == 3))
            xs3 = pt[:T, 0:384].rearrange("p (a b) -> p a b", a=6)
            TTo('dve', xd[:T], xs3, bc6(dt6, T), ALU.mult, ['ps6', 'dt6'], [kxd])
            TTo('dve', xdd[:T], xd[:T], bc6(ea[:, 6:12], T), ALU.mult, [kxd, kea], [kxdd])
            CPo('act', Btm[:T, :], pt[:T, 384:512], ['ps6'], [kBtm])
            TTo('dve', skipx[:T], xs3, bc6(hp[:, 48 + g * 6:54 + g * 6], T), ALU.mult, ['ps6', 'hp'], [kskipx])
            return

        STOP = 0
        KT = os.environ.get('KTILES')
        KG = int(os.environ.get('KG', '4'))
        KCUT = int(os.environ.get('KCUT', '99'))
        for g in range(KG):
            S.dma('pool', wssd[:, 0:4, :], wssd_d[g].rearrange("(kc p) f -> p kc f", p=128)[:, 0:4, :], w=['wssd'])
            S.dma('pool', wssd[:, 4:8, :], wssd_d[g].rearrange("(kc p) f -> p kc f", p=128)[:, 4:8, :], w=['wssd'])
            S.dma('pool', wout, woutssd_d[g].rearrange("(j p) d -> p j d", p=128), w=['wout'])
            S.dma('sp', nbc, nbc_d[:, g * 384:(g + 1) * 384], w=['nbc'])
            S.dma('sp', conv0s, conv0_d[:, g], w=['conv0s'])
            for cc in range(5):
                for k in range(4):
                    TSo('dve', diag[:, cc, k, :], ident_bf, cwb[:, g, cc, k:k + 1], ALU.mult, ['ident_bf', 'cwb'], ['diag'])
            S.op('pool', lambda e: e.memset(hT.rearrange("p a b -> p (a b)"), 0.0), w=['hT'])
            S.op('pool', lambda e: e.memset(hTb, 0.0), w=['hTb'])
            S.op('pool', lambda e: e.memset(xraw[1][:, :, 512:515], 0.0), w=[('xraw', 1)])
            CPo('dve', xraw_s[:, :, :, 0:3], conv0s.rearrange("p c (b k) -> p c b k", b=16), ['conv0s'], ['xraw_s'])
            for tt, (t0, n) in enumerate(TT):
                if KT is not None and str(tt) not in KT.split(','):
                    continue
                sample = (tt == 4)
                xr = xraw[tt % 2]
                for cc in range(5):
                    bk = bigbank()
                    for kc in range(8):
                        MMo(ps[bk][:, :n], wssd[:, kc, cc * 128:(cc + 1) * 128], h[:, kc, t0:t0 + n], r=['wssd', ('h', kc)], w=[('ps', bk)],
                            start=(kc == 0), stop=(kc == 7), inc=(kc == 7))
                    if not sample:
                        CPo('act', xr[:, cc, 3:3 + n], ps[bk][:, :n], [('ps', bk)], [('xraw', tt % 2)])
                        CPo('dve', xr[:, cc, 0:3], xraw[(tt + 1) % 2][:, cc, 512:515], [('xraw', (tt + 1) % 2)], [('xraw', tt % 2)])
                        if tt == 3:
                            CPo('dve', convo[:, cc, :], ps[bk][:, 509:512], [('ps', bk)], ['convo'])
                    else:
                        p3 = ps[bk][:, :64].rearrange("p (b l) -> p b l", b=16)
                        CPo('act', xraw_s[:, cc, :, 3:7], p3, [('ps', bk)], ['xraw_s'])
                        CPo('dve', convos[:, cc, :, :], p3[:, :, 1:4], [('ps', bk)], ['convos'])
                if tt == 3:
                    S.dma('sp', convp_d[:, g], convo, r=['convo'], w=[('convp', g)])
                if sample:
                    S.dma('sp', convs_d[:, g], convos.rearrange("p c b k -> p c (b k)"), r=['convos'], w=[('convs', g)])
                for cc in range(5):
                    bk = bigbank()
                    for k in range(4):
                        if not sample:
                            rhs = xr[:, cc, k:k + n]
                            rk = ('xraw', tt % 2)
                        else:
                            rhs = xraw_s[:, cc, :, k:k + 4]
                            rk = 'xraw_s'
                        MMo(ps[bk][:, :n], diag[:, cc, k, :], rhs, r=['diag', rk], w=[('ps', bk)], start=(k == 0), stop=(k == 3), inc=(k == 3))
                    wk = ('fmx', cc) if cc < 3 else ('fmB' if cc == 3 else 'fmC')
                    ACTo(fm[:, cc, :n], ps[bk][:, :n], AF.Silu, [('ps', bk), 'cwb'], [wk], bias=cwb[:, g, cc, 4:5], scale=1.0)
                if not sample:
                    def ch(ci, part):
                        return S.record(lambda: ssd_chunk(g, 128, t0 + ci * 128, ci * 128, False, (tt == 3 and ci == 3), ci % 2, part, ci))
                    for ci in range(4):
                        ssd_pre(g, 128, t0 + ci * 128, ci)
                    E = [ch(ci, 'early') for ci in range(4)]
                    Lt = [ch(ci, 'late') for ci in range(4)]
                    S.replay(E[0])
                    for ci in range(4):
                        S.replay(S.merge(E[ci + 1] if ci + 1 < 4 else [], Lt[ci]))
                else:
                    ssd_pre(g, 64, t0, 0)
                    ssd_chunk(g, 64, t0, 0, True, False, 0, 'early')
                    ssd_chunk(g, 64, t0, 0, True, False, 0, 'late')
                for dc in range(8):
                    bk = bigbank()
                    for j in range(3):
                        MMo(ps[bk][:, :n], wout[:, j, dc * 128:(dc + 1) * 128], ofm[:, j, :n], r=['wout', 'ofm'], w=[('ps', bk)],
                            start=(j == 0), stop=(j == 2), inc=(j == 2))
                    TTo('dve', x[:, dc, t0:t0 + n], x[:, dc, t0:t0 + n], ps[bk][:, :n], ALU.add, [('ps', bk), ('x', dc, tt)], [('x', dc, tt)])


    def s5_phase():
        S.barrier()
        cv = Carver()
        I32 = mybir.dt.int32
        PI = 3.14159265358979
        winu = cv.get([128, 8, 512], BF16)
        wouts5 = cv.get([128, 4, 1024], BF16)
        wglu = cv.get([128, 4, 512], BF16)
        lhsB = [cv.get([128, 4, 8, 64], BF16) for _ in range(2)]
        lhsC = [cv.get([128, 16, 128], BF16) for _ in range(2)]
        u_all = cv.get([128, 4, NT], BF16)
        S.dma('pool', winu, winu_d.rearrange("(kc p) f -> p kc f", p=128), w=['winu'])
        S.dma('pool', wouts5, wouts5_d.rearrange("(kc p) f -> p kc f", p=128), w=['wouts5'])
        S.dma('pool', wglu, wglu_d.rearrange("(kc p) f -> p kc f", p=128), w=['wglu'])
        for c in range(2):
            S.dma('pool', lhsC[c], cpad_d[c], w=[('lhsC', c)])
        TSo('dve', lhsC[1], lhsC[1], -1.0, ALU.mult, [('lhsC', 1)], [('lhsC', 1)])
        for tt, (t0, n) in enumerate(TT):
            for kt in range(4):
                bk = bigbank()
                for kc in range(8):
                    MMo(ps[bk][:, :n], winu[:, kc, kt * 128:(kt + 1) * 128], h[:, kc, t0:t0 + n], r=['winu'], w=[('ps', bk)],
                        start=(kc == 0), stop=(kc == 7), inc=(kc == 7))
                CPo('act', u_all[:, kt, t0:t0 + n], ps[bk][:, :n], [('ps', bk)], [('u', tt)])
        S.barrier()
        W = 272
        s5v = cv.get([128, 8], F32)
        S.dma('sp', s5v, s5v_d, w=['s5v'])
        emk = cv.get([128, 8], F32)
        S.dma('sp', emk, emask_d, w=['emk'])
        iot = cv.get([128, 256], F32)
        S.dma('sp', iot, cmat_d[:, 768:1024], w=['iot'])
        st0 = cv.get([128, 2, 16, 16], F32)
        S.dma('sp', st0, s5s0_d, w=['st0'])
        mag = cv.get([128, W], F32)
        carry = cv.get([128, 2, 16], F32)
        carry_s = cv.get([128, 2, 16, 16], F32)
        off_pre = cv.off
        lam = cv.get([128, 3, W], F32)
        S.dma('sp', lam, lam_d, w=['lam'])
        bT = cv.get([128, 2, 4, 64], F32)
        S.dma('sp', bT, bT_d, w=['bT'])
        pre = [cv.get([128, W], F32) for _ in range(11)]
        stp, xr, ang, sn, cs, ar, ai, t1, t2, cr, ci = pre
        ki = cv.get([128, 512], I32)
        kf = cv.get([128, 512], F32)
        rr = cv.get([128, 512], F32)

        def sin_of(out, a, n, shift, key):
            S.op('dve', lambda e: e.tensor_scalar(out=ki[:, :n], in0=a, scalar1=shift, scalar2=1.0 / (2 * PI), op0=ALU.add, op1=ALU.mult),
                 r=[key], w=['ki'])
            CPo('dve', kf[:, :n], ki[:, :n], ['ki'], ['kf'])
            STTo('dve', rr[:, :n], kf[:, :n], -2 * PI, a, ALU.mult, ALU.add, ['kf', key], ['rr'])
            S.op('dve', lambda e: e.tensor_scalar(out=rr[:, :n], in0=rr[:, :n], scalar1=shift, scalar2=-3.1415925, op0=ALU.add, op1=ALU.max),
                 r=['rr'], w=['rr'])
            TSo('dve', rr[:, :n], rr[:, :n], 3.1415925, ALU.min, ['rr'], ['rr'])
            ACTo(out, rr[:, :n], AF.Sin, ['rr'], [key + '_o'])

        ACTo(stp, lam[:, 2, :], AF.Exp, ['lam'], ['pre'])
        TTo('dve', xr, lam[:, 0, :], stp, ALU.mult, ['lam', 'pre'], ['pre'])
        TTo('dve', ang, lam[:, 1, :], stp, ALU.mult, ['lam', 'pre'], ['ang'])
        ACTo(mag, xr, AF.Exp, ['pre'], ['pre'])
        sin_of(sn, ang, W, 0.0, 'ang')
        sin_of(cs, ang, W, PI / 2, 'ang')
        P_ = ['pre', 'ang_o', 'lam']
        TTo('dve', ar, mag, cs, ALU.mult, P_, ['pre'])
        TTo('dve', ai, mag, sn, ALU.mult, P_, ['pre'])
        TTo('dve', t1, lam[:, 0, :], lam[:, 0, :], ALU.mult, P_, ['pre'])
        TTo('dve', t2, lam[:, 1, :], lam[:, 1, :], ALU.mult, P_, ['pre'])
        TTo('dve', t1, t1, t2, ALU.add, P_, ['pre'])
        S.op('dve', lambda e: e.reciprocal(out=t1, in_=t1), r=P_, w=['pre'])
        TSo('dve', t2, ar, -1.0, ALU.add, P_, ['pre'])
        TTo('dve', cr, t2, lam[:, 0, :], ALU.mult, P_, ['pre'])
        TTo('dve', ci, ai, lam[:, 1, :], ALU.mult, P_, ['pre'])
        TTo('dve', cr, cr, ci, ALU.add, P_, ['pre'])
        TTo('dve', cr, cr, t1, ALU.mult, P_, ['pre'])
        TTo('dve', ci, ai, lam[:, 0, :], ALU.mult, P_, ['pre'])
        TTo('dve', t2, t2, lam[:, 1, :], ALU.mult, P_, ['pre'])
        TTo('dve', ci, ci, t2, ALU.subtract, P_, ['pre'])
        TTo('dve', ci, ci, t1, ALU.mult, P_, ['pre'])
        bb = [cv.get([128, 4, 64], F32) for _ in range(2)]
        tb = cv.get([128, 4, 64], F32)
        cr3 = cr[:, 0:256].rearrange("p (a b) -> p a b", a=4)
        ci3 = ci[:, 0:256].rearrange("p (a b) -> p a b", a=4)
        TTo('dve', bb[0], cr3, bT[:, 0], ALU.mult, P_ + ['bT'], ['bb'])
        TTo('dve', tb, ci3, bT[:, 1], ALU.mult, P_ + ['bT'], ['tb'])
        TTo('dve', bb[0], bb[0], tb, ALU.subtract, ['bb', 'tb'], ['bb'])
        TTo('dve', bb[1], cr3, bT[:, 1], ALU.mult, P_ + ['bT'], ['bb'])
        TTo('dve', tb, ci3, bT[:, 0], ALU.mult, P_ + ['bT'], ['tb'])
        TTo('dve', bb[1], bb[1], tb, ALU.add, ['bb', 'tb'], ['bb'])
        for c in range(2):
            for kt in range(4):
                TTo('dve', lhsB[c][:, kt], bb[c][:, kt, None, :].to_broadcast([128, 8, 64]), emk[:, :, None].to_broadcast([128, 8, 64]),
                    ALU.mult, ['bb', 'emk'], [('lhsB', c)])
        tabf = h.rearrange("p a b -> p (a b)").bitcast(F32)
        Ec = tabf[:, 0:4096].rearrange("p (a b) -> p a b", a=16)
        Es = tabf[:, 4096:8192].rearrange("p (a b) -> p a b", a=16)
        angt = cv.get([128, 2, 256], F32)
        for i0 in range(0, 16, 2):
            TTo('dve', angt, ang[:, 256 + i0:258 + i0][:, :, None].to_broadcast([128, 2, 256]), iot[:, None, :].to_broadcast([128, 2, 256]),
                ALU.mult, ['ang', 'iot'], ['angt'])
            af = angt.rearrange("p a b -> p (a b)")
            sin_of(Es[:, i0:i0 + 2, :].rearrange("p a b -> p (a b)"), af, 512, 0.0, 'angt')
            sin_of(Ec[:, i0:i0 + 2, :].rearrange("p a b -> p (a b)"), af, 512, PI / 2, 'angt')
        TAB = ['angt_o']
        S.barrier()
        cv.off = off_pre
        S.op('pool', lambda e: e.memset(carry.rearrange("p a b -> p (a b)"), 0.0), w=['carry'])
        tmA = [cv.get([128, 2, 256], F32) for _ in range(4)]
        wreg = winu.rearrange("p a b -> p (a b)").bitcast(F32)
        tmB = [wreg[:, k * 512:(k + 1) * 512].rearrange("p (a b) -> p a b", a=2) for k in range(4)]
        tm2 = [tmA, tmB]
        wv2 = [[cv.get([128, 2, 256], F32) for _ in range(2)] for _ in range(2)]
        Wv2 = [[cv.get([128, 2, 256], F32) for _ in range(2)] for _ in range(2)]
        sv2 = [[cv.get([128, 2, 256], F32) for _ in range(2)] for _ in range(2)]
        rt_single = cv.get([128, 256], F32)
        rt2 = [rt_single, rt_single]
        hist2 = [cv.get([128, 2, 512], BF16) for _ in range(2)]
        y5 = cv.get([128, 512], F32)
        g1 = cv.get([128, 512], F32)
        v_bf = cv.get([128, 4, 512], BF16)
        gs = g1
        o5 = cv.get([128, 4, 512], BF16)
        print("s5 arena used", cv.off)
        magq = mag[:, 256:272]

        def bct(tab, i, nseg, L):
            return tab[:, i, None, 0:L].to_broadcast([128, nseg, L])

        for tt, (t0, n) in enumerate(TT):
            sample = (tt == 4)
            nseg, L = (16, 4) if sample else (2, 256)

            def v3(ap):
                return ap.rearrange("p a b -> p (a b)")[:, :nseg * L].rearrange("p (a b) -> p a b", a=nseg)
            for kt in range(4):
                cb = 4 if kt % 2 == 0 else 7
                kcb = 'ps%d' % cb
                for jp in (0, 2):
                    ctxs = []
                    for j in (jp, jp + 1):
                        i = 4 * kt + j
                        par = i % 2
                        pre_b, pim_b = (2, 3) if par == 0 else (5, 6)
                        ctxs.append(dict(i=i, j=j, par=par, tm=tm2[par], wv=wv2[par], Wv=Wv2[par], sv=sv2[par], hist=hist2[par],
                                         pre_b=pre_b, pim_b=pim_b, kre='ps%d' % pre_b, kim='ps%d' % pim_b))
                    for cx in ctxs:
                        i, j = cx['i'], cx['j']
                        lB = [lhsB[c].rearrange("p a b c -> p a (b c)")[:, kt, j * 128:(j + 1) * 128] for c in range(2)]
                        MMo(ps[cx['pre_b']][:, :n], lB[0], u_all[:, kt, t0:t0 + n], r=[('lhsB', 0), ('u', tt)], w=[cx['kre']])
                        MMo(ps[cx['pim_b']][:, :n], lB[1], u_all[:, kt, t0:t0 + n], r=[('lhsB', 1), ('u', tt)], w=[cx['kim']])
                    for cx in ctxs:
                        i, par, tm = cx['i'], cx['par'], cx['tm']
                        Pre = ps[cx['pre_b']][:, :n].rearrange("p (a b) -> p a b", a=nseg)
                        Pim = ps[cx['pim_b']][:, :n].rearrange("p (a b) -> p a b", a=nseg)
                        ec, es = bct(Ec, i, nseg, L), bct(Es, i, nseg, L)
                        TTo('dve', v3(tm[0]), Pre, ec, ALU.mult, [cx['kre']] + TAB, [('tm', par, 0)])
                        TTo('dve', v3(tm[1]), Pim, es, ALU.mult, [cx['kim']] + TAB, [('tm', par, 1)])
                        TTo('dve', v3(tm[2]), Pim, ec, ALU.mult, [cx['kim']] + TAB, [('tm', par, 2)])
                        TTo('dve', v3(tm[3]), Pre, es, ALU.mult, [cx['kre']] + TAB, [('tm', par, 3)])
                    for cx in ctxs:
                        par, tm, wv = cx['par'], cx['tm'], cx['wv']
                        TTo('pool', v3(wv[0]), v3(tm[0]), v3(tm[1]), ALU.add, [('tm', par, 0), ('tm', par, 1)], [('wv', par, 0)])
                        TTo('pool', v3(wv[1]), v3(tm[2]), v3(tm[3]), ALU.subtract, [('tm', par, 2), ('tm', par, 3)], [('wv', par, 1)])
                    groups = [list(range(16))] if sample else [[0], [1]]
                    for grp in groups:
                        g0, g1_ = grp[0], grp[-1] + 1
                        for cx in ctxs:
                            i, par, wv, Wv, sv = cx['i'], cx['par'], cx['wv'], cx['Wv'], cx['sv']
                            rbc = magq[:, i:i + 1].to_broadcast([128, L])
                            for sg_ in grp:
                                for c in range(2):
                                    if sample:
                                        init = st0[:, c, i, sg_:sg_ + 1]
                                        ik = 'st0'
                                    elif sg_ == 0:
                                        init = carry[:, c, i:i + 1]
                                        ik = 'carry'
                                    else:
                                        init = sv[c][:, 0, 255:256]
                                        ik = ('sv', par, c)
                                    S.op('dve', lambda e: e.tensor_tensor_scan(out=v3(Wv[c])[:, sg_, :], data0=rbc, data1=v3(wv[c])[:, sg_, :],
                                                                               initial=init, op0=ALU.mult, op1=ALU.add),
                                         r=['pre', ('wv', par, c), ik], w=[('Wv', par, c)])
                        for cx in ctxs:
                            i, par, tm, Wv = cx['i'], cx['par'], cx['tm'], cx['Wv']
                            ecg = Ec[:, i, None, 0:L].to_broadcast([128, g1_ - g0, L])
                            esg = Es[:, i, None, 0:L].to_broadcast([128, g1_ - g0, L])
                            TTo('dve', v3(tm[0])[:, g0:g1_], v3(Wv[0])[:, g0:g1_], ecg, ALU.mult, [('Wv', par, 0)] + TAB, [('tm', par, 0)])
                            TTo('dve', v3(tm[1])[:, g0:g1_], v3(Wv[1])[:, g0:g1_], esg, ALU.mult, [('Wv', par, 1)] + TAB, [('tm', par, 1)])
                        for cx in ctxs:
                            i, par, tm, Wv, sv = cx['i'], cx['par'], cx['tm'], cx['Wv'], cx['sv']
                            ecg = Ec[:, i, None, 0:L].to_broadcast([128, g1_ - g0, L])
                            esg = Es[:, i, None, 0:L].to_broadcast([128, g1_ - g0, L])
                            TTo('pool', v3(tm[2])[:, g0:g1_], v3(Wv[1])[:, g0:g1_], ecg, ALU.mult, [('Wv', par, 1)] + TAB, [('tm', par, 2)])
                            TTo('pool', v3(tm[3])[:, g0:g1_], v3(Wv[0])[:, g0:g1_], esg, ALU.mult, [('Wv', par, 0)] + TAB, [('tm', par, 3)])
                            TTo('pool', v3(sv[0])[:, g0:g1_], v3(tm[0])[:, g0:g1_], v3(tm[1])[:, g0:g1_], ALU.subtract,
                                [('tm', par, 0), ('tm', par, 1)], [('sv', par, 0)])
                            TTo('pool', v3(sv[1])[:, g0:g1_], v3(tm[2])[:, g0:g1_], v3(tm[3])[:, g0:g1_], ALU.add,
                                [('tm', par, 2), ('tm', par, 3)], [('sv', par, 1)])
                    for cx in ctxs:
                        i, j, par, sv, hist = cx['i'], cx['j'], cx['par'], cx['sv'], cx['hist']
                        for c in range(2):
                            svf = sv[c].rearrange("p a b -> p (a b)")
                            CPo('act', hist[:, c, :n], svf[:, :n], [('sv', par, c)], [('hist', par, c)])
                            if sample:
                                CPo('act', carry_s[:, c, i, :], v3(sv[c])[:, :, 3], [('sv', par, c)], ['carry_s'])
                            else:
                                CPo('act', carry[:, c, i:i + 1], svf[:, 511:512], [('sv', par, c)], ['carry'])
                            MMo(ps[cb][:, :n], lhsC[c][:, i, :], hist[:, c, :n], r=[('lhsC', c), ('hist', par, c)], w=[kcb],
                                start=(j == 0 and c == 0), stop=(j == 3 and c == 1), inc=True)
                STTo('dve', y5[:, :n], u_all[:, kt, t0:t0 + n], s5v[:, kt:kt + 1], ps[cb][:, :n], ALU.mult, ALU.add, [('u', tt), 's5v', kcb], ['y5'])
                TTo('pool', g1[:, :n], y5[:, :n], y5[:, :n], ALU.mult, ['y5'], ['g1'])
                S.op('dve', lambda e: e.tensor_scalar(out=g1[:, :n], in0=g1[:, :n], scalar1=0.044715, scalar2=1.0, op0=ALU.mult, op1=ALU.add),
                     r=['g1'], w=['g1'])
                TTo('pool', g1[:, :n], g1[:, :n], y5[:, :n], ALU.mult, ['g1', 'y5'], ['g1'])
                ACTo(g1[:, :n], g1[:, :n], AF.Sigmoid, ['g1'], ['g1'], scale=1.5957691216057308)
                TTo('dve', v_bf[:, kt, :n], g1[:, :n], y5[:, :n], ALU.mult, ['g1', 'y5'], [('v', kt)])
            for mo in range(4):
                bk = bigbank()
                for kt in range(4):
                    MMo(ps[bk][:, :n], wglu[:, kt, mo * 128:(mo + 1) * 128], v_bf[:, kt, :n], r=['wglu', ('v', kt)], w=[('ps', bk)],
                        start=(kt == 0), stop=(kt == 3), inc=(kt == 3))
                ACTo(gs[:, :n], ps[bk][:, :n], AF.Sigmoid, [('ps', bk), 's5v'], ['g1'], bias=s5v[:, 4 + mo:5 + mo], scale=1.0)
                TTo('dve', o5[:, mo, :n], v_bf[:, mo, :n], gs[:, :n], ALU.mult, [('v', mo), 'g1'], [('o5', mo)])
            for dc in range(8):
                bk = bigbank()
                for mo in range(4):
                    MMo(ps[bk][:, :n], wouts5[:, mo, dc * 128:(dc + 1) * 128], o5[:, mo, :n], r=['wouts5', ('o5', mo)], w=[('ps', bk)],
                        start=(mo == 0), stop=(mo == 3), inc=(mo == 3))
                TTo('dve', x[:, dc, t0:t0 + n], x[:, dc, t0:t0 + n], ps[bk][:, :n], ALU.add, [('ps', bk), ('x', dc, tt)], [('x', dc, tt)])
        S.dma('sp', s5p_d, carry, r=['carry'], w=['s5p'])
        S.dma('sp', s5s_d, carry_s, r=['carry_s'], w=['s5s'])


    PH = os.environ.get('KPH', 'n1,f1,n2,ssd,s5,n3,f2').split(',')
    if 'n1' in PH:
        norm_to_h(0)
    if 'f1' in PH:
        ffn(0)
    if 'n2' in PH:
        norm_to_h(1)
    if 'ssd' in PH:
        ssd_phase()
    if 's5' in PH:
        s5_phase()
    S.barrier()
    if 'n3' in PH:
        norm_to_h(2)
    if 'f2' in PH:
        ffn(1)

    yT_v = yT.rearrange("(c p) t -> p c t", p=128)
    ycount = [0]

    def final_out(tt, c, t0, n, rb):
        yb = ycount[0] % 2
        ycount[0] += 1
        S.op('dve', lambda e: e.scalar_tensor_tensor(out=ystage[yb][:, :n], in0=x[:, c, t0:t0 + n],
                                                     scalar=normw[:, 24 + c:24 + c + 1], in1=rb[:, :n],
                                                     op0=ALU.mult, op1=ALU.mult),
             r=[('x', c, tt), ('rstd', tt % 2), 'normw'], w=[('ystage', yb)])
        S.dma('sp', yT_v[:, c, t0:t0 + n], ystage[yb][:, :n], r=[('ystage', yb)], w=[('yT', c, tt)])
    rmsnorm(3, final_out)
    S.finish()
    print("instructions", S.nins, "waits", S.nwait)
    return nc


_NC_CACHE = {}


def _consts():
    f32 = np.float32
    cm = np.zeros((128, 2048), f32)
    k = np.arange(128)
    cm[:, 0:128] = (k[:, None] <= k[None, :])
    cm[:, 128:256] = (k[:, None] > k[None, :])
    same = (k[:, None] // 4 == k[None, :] // 4) & (k[:, None] < 64) & (k[None, :] < 64)
    cm[:, 256:384] = (k[:, None] <= k[None, :]) & same
    cm[:, 384:512] = (k[:, None] > k[None, :]) & same
    cm[:, 512:640] = np.eye(128)
    cm[:64, 640:656] = (np.arange(64)[:, None] // 4 == np.arange(16)[None, :])
    cm[:, 768:1024] = np.arange(1, 257)[None, :]
    et = (np.arange(64)[None, :] // 4 == np.arange(16)[:, None]).astype(f32)
    cm[:, 1024:2048] = et.reshape(1, 1024)
    return cm


def _chan(g, cc):
    if cc < 3:
        return 384 * g + 128 * cc
    if cc == 3:
        return 1536 + 128 * g
    return 2048 + 128 * g


def _s5_host(inp):
    f32 = np.float32
    A = lambda k: np.asarray(inp[k], f32)
    w_in = A("w_in")[0]
    w_out = A("w_out")[0]
    s5v = np.empty((128, 8), f32)
    s5v[:, 0:4] = A("s5_d")[0].reshape(4, 128).T
    s5v[:, 4:8] = A("s5_b_glu")[0].reshape(4, 128).T
    lre, lim, lst = A("s5_lambda_re")[0], A("s5_lambda_im")[0], A("s5_log_step")[0]
    lam = np.empty((128, 3, 272), f32)
    for arr_i, arr in enumerate((lre, lim)):
        rep = arr.reshape(4, 8, 64)
        rep = np.repeat(rep.transpose(1, 0, 2)[:, None], 16, axis=1)
        lam[:, arr_i, 0:256] = rep.reshape(128, 256)
        qq = arr.reshape(16, 2, 64).transpose(1, 2, 0).reshape(128, 16)
        lam[:, arr_i, 256:272] = qq
    rep = np.repeat(lst.reshape(4, 8).T[:, None, :, None], 16, axis=1)
    lam[:, 2, 0:256] = np.broadcast_to(rep, (8, 16, 4, 64)).reshape(128, 256)
    qq = np.broadcast_to(lst.reshape(16, 2).T[:, None, :], (2, 64, 16)).reshape(128, 16)
    lam[:, 2, 256:272] = qq
    bT = np.empty((128, 2, 4, 64), f32)
    for ci, k in enumerate(("s5_b_re", "s5_b_im")):
        b = A(k)[0].reshape(4, 8, 64, 16)
        bT[:, ci] = b.transpose(1, 3, 0, 2).reshape(128, 4, 64)
    emask = (np.arange(128)[:, None] // 16 == np.arange(8)[None, :]).astype(f32)
    cpad = np.zeros((2, 128, 16, 128), f32)
    for ci, k in enumerate(("s5_c_re", "s5_c_im")):
        cc = A(k)[0]
        for g in range(32):
            i, q0, gl = g // 2, (g % 2) * 64, g % 8
            cpad[ci, q0:q0 + 64, i, gl * 16:(gl + 1) * 16] = cc[g].T
    return {"w_in_u": np.ascontiguousarray(w_in[:, 0:512]), "w_out_s5": np.ascontiguousarray(w_out[0:512]),
            "w_glu": np.ascontiguousarray(A("s5_w_glu")[0]), "s5v": s5v, "lam": lam, "bT": bT, "emask": emask, "cpad": cpad}


def _s5_core(inp, c):
    f32 = np.float32
    out = np.empty((128, 2, 16, 16), f32)
    for ci, k in enumerate(("state_s5_re", "state_s5_im")):
        s = np.asarray(inp[k], f32)[0, 16 * c:16 * c + 16]
        out[:, ci] = s.reshape(16, 16, 2, 64).transpose(2, 3, 1, 0).reshape(128, 16, 16)
    return {"s5s0": out}


def kernel(**inp):
    f32 = np.float32
    A = lambda k: np.asarray(inp[k], f32)
    xp = A("x_prompt")
    xs = A("x_sample")
    normw = np.stack([A(k).reshape(D) for k in ("ffn1_norm", "mix_norm", "ffn2_norm", "final_norm")])
    normw_l = np.ascontiguousarray(normw.reshape(4, 8, 128).transpose(2, 0, 1).reshape(128, 32))
    w_in = A("w_in")[0]
    w_out = A("w_out")[0]
    w_ssd = np.empty((4, D, 1030), f32)
    w_out_ssd = np.empty((4, 384, D), f32)
    for g in range(4):
        w_ssd[g, :, 0:384] = w_in[:, 2048 + 384 * g:2048 + 384 * (g + 1)]
        w_ssd[g, :, 384:512] = w_in[:, 3584 + 128 * g:3584 + 128 * (g + 1)]
        w_ssd[g, :, 512:640] = w_in[:, 4096 + 128 * g:4096 + 128 * (g + 1)]
        w_ssd[g, :, 640:1024] = w_in[:, 512 + 384 * g:512 + 384 * (g + 1)]
        w_ssd[g, :, 1024:1030] = w_in[:, 4608 + 6 * g:4608 + 6 * (g + 1)]
        w_out_ssd[g] = w_out[512 + 384 * g:512 + 384 * (g + 1)]
    conv_w = A("ssd_conv_w")[0]
    conv_b = A("ssd_conv_b")[0]
    cwb = np.empty((128, 4, 5, 5), f32)
    for g in range(4):
        for cc in range(5):
            c0 = _chan(g, cc)
            cwb[:, g, cc, 0:4] = conv_w[:, c0:c0 + 128].T
            cwb[:, g, cc, 4] = conv_b[c0:c0 + 128]
    hp = np.concatenate([A("ssd_dt_bias")[0], A("ssd_a_log")[0], A("ssd_d")[0]])[None, :].repeat(128, 0)
    nbc = A("ssd_norm")[0][None, :].repeat(128, 0)
    sconv = A("state_conv")[0]
    sssd = A("state_ssd")[0]
    shared = {
        "normw": normw_l,
        "ffn1_wg": np.ascontiguousarray(A("ffn1_w_gate")[0]), "ffn1_wu": np.ascontiguousarray(A("ffn1_w_up")[0]),
        "ffn1_wd": np.ascontiguousarray(A("ffn1_w_down")[0]),
        "ffn2_wg": np.ascontiguousarray(A("ffn2_w_gate")[0]), "ffn2_wu": np.ascontiguousarray(A("ffn2_w_up")[0]),
        "ffn2_wd": np.ascontiguousarray(A("ffn2_w_down")[0]),
        "cmat": _consts(), "hp": np.ascontiguousarray(hp), "nbc": np.ascontiguousarray(nbc),
        "w_ssd": w_ssd, "w_out_ssd": w_out_ssd, "cwb": cwb,
    }
    shared.update(_s5_host(inp))
    in_maps = []
    for c in range(NCORES):
        bs = slice(16 * c, 16 * c + 16)
        xc = np.concatenate([xp[c], xs[bs].reshape(TS, D)], axis=0)
        m = dict(shared)
        m["xT"] = np.ascontiguousarray(xc.T)
        cv0 = np.empty((128, 4, 5, 16, 3), f32)
        for g in range(4):
            for cc in range(5):
                c0 = _chan(g, cc)
                cv0[:, g, cc] = sconv[bs, :, c0:c0 + 128].transpose(2, 0, 1)
        m["conv0"] = cv0.reshape(128, 4, 5, 48)
        st = sssd[bs].reshape(16, 4, 6, 64, 128).transpose(0, 1, 4, 2, 3).reshape(16, 4, 128, 384)
        m["ssd0"] = np.ascontiguousarray(st)
        m.update(_s5_core(inp, c))
        in_maps.append(m)
    if "nc" not in _NC_CACHE:
        _NC_CACHE["nc"] = build()
    nc = _NC_CACHE["nc"]
    res = run_bass_kernel_spmd(nc, in_maps, core_ids=list(range(NCORES)))
    R = res.results
    yp = np.empty((8, TP, D), f32)
    ys = np.empty((128, 4, D), f32)
    ssd_p = np.empty((1, 8, 24, 64, 128), f32)
    ssd_s = np.empty((1, 128, 24, 64, 128), f32)
    conv_p = np.empty((1, 8, 3, 2560), f32)
    conv_s = np.empty((1, 128, 3, 2560), f32)
    s5p = np.empty((2, 1, 8, 32, 64), f32)
    s5s = np.empty((2, 1, 128, 32, 64), f32)
    for c in range(NCORES):
        bs = slice(16 * c, 16 * c + 16)
        y = R[c]["yT"].T
        yp[c] = y[:TP]
        ys[bs] = y[TP:].reshape(16, 4, D)
        ssd_p[0, c] = R[c]["ssdp"].reshape(4, 128, 6, 64).transpose(0, 2, 3, 1).reshape(24, 64, 128)
        ssd_s[0, bs] = R[c]["ssds"].reshape(16, 4, 128, 6, 64).transpose(0, 1, 3, 4, 2).reshape(16, 24, 64, 128)
        cp = R[c]["convp"]
        cs = R[c]["convs"].reshape(128, 4, 5, 16, 3)
        for g in range(4):
            for cc in range(5):
                c0 = _chan(g, cc)
                conv_p[0, c, :, c0:c0 + 128] = cp[:, g, cc, :].T
                conv_s[0, bs, :, c0:c0 + 128] = cs[:, g, cc].transpose(1, 2, 0)
        a = R[c]["s5p"]
        s5p[:, 0, c] = a.reshape(2, 64, 2, 16).transpose(2, 3, 0, 1).reshape(2, 32, 64)
        b_ = R[c]["s5s"]
        s5s[:, 0, bs] = b_.reshape(2, 64, 2, 16, 16).transpose(2, 4, 3, 0, 1).reshape(2, 16, 32, 64)
    return (yp, ys, s5p[0], s5p[1], ssd_p, conv_p, s5s[0], s5s[1], ssd_s, conv_s)
```

```python
import os
import numpy as np
import concourse.bass as bass
import concourse.mybir as mybir
from concourse.bass_utils import run_bass_kernel_spmd

F32 = mybir.dt.float32
BF16 = mybir.dt.bfloat16
AF = mybir.ActivationFunctionType
ALU = mybir.AluOpType

NCORES = 8
D = 1024
DFF = 2816
TP = 2048
TS = 64
NT = TP + TS
EPS = 1e-6
NDMA = 24
NSW = 64


class Sched:
    def __init__(self, nc):
        self.nc = nc
        self.e = {'pe': nc.tensor, 'act': nc.scalar, 'dve': nc.vector, 'pool': nc.gpsimd, 'sp': nc.sync}
        self.sem = {k: nc.alloc_semaphore('s_' + k) for k in self.e}
        self.cnt = {k: 0 for k in self.e}
        self.waited = {}
        self.lastw = {}
        self.readers = {}
        self.dsem = [nc.alloc_semaphore('d%d' % i) for i in range(NDMA + NSW)]
        self.duse = [0] * (NDMA + NSW)
        self.dnext = 0
        self.swnext = NDMA
        self.nwait = 0
        self.nins = 0
        self._rec = None

    def _semobj(self, sk):
        return self.sem[sk] if isinstance(sk, str) else self.dsem[sk[1]]

    def _deps(self, r, w):
        deps = {}

        def add(tok):
            sk, v = tok
            if deps.get(sk, 0) < v:
                deps[sk] = v
        for k in r:
            if k in self.lastw:
                add(self.lastw[k])
        for k in w:
            if k in self.lastw:
                add(self.lastw[k])
            for sk, v in self.readers.get(k, {}).items():
                add((sk, v))
        return deps

    def _wait(self, eng, deps):
        for sk, v in deps.items():
            if sk == eng:
                if eng == 'pe' or os.environ.get('KNOSAME') == '1':
                    continue
                assert v <= self.cnt[eng], "same-engine dep on non-inc instruction"
            if self.waited.get((eng, sk), 0) >= v:
                continue
            self.e[eng].wait_ge(self._semobj(sk), v)
            self.waited[(eng, sk)] = v
            self.nwait += 1

    def _record(self, tok, r, w):
        sk, v = tok
        for k in r:
            d = self.readers.setdefault(k, {})
            if d.get(sk, 0) < v:
                d[sk] = v
        for k in w:
            self.lastw[k] = tok
            self.readers[k] = {}

    @staticmethod
    def _split(r, w):
        def isps(k):
            return (isinstance(k, tuple) and k[0] == 'ps') or (isinstance(k, str) and k.startswith('ps'))
        r2 = [k for k in r if not isps(k)]
        w2 = list(w) + [k for k in r if isps(k)]
        return r2, w2

    def record(self, f):
        self._rec = []
        f()
        out, self._rec = self._rec, None
        return out

    def replay(self, items):
        for it in items:
            if it[0] == 'op':
                self.op(*it[1:])
            else:
                self.dma(*it[1:-1], **it[-1])

    @staticmethod
    def merge(a, b):
        out = []
        ia = ib = 0
        na, nb = len(a), len(b)
        while ia < na or ib < nb:
            if ib >= nb or (ia < na and ia * nb <= ib * na):
                out.append(a[ia])
                ia += 1
            else:
                out.append(b[ib])
                ib += 1
        return out

    def op(self, eng, fn, r=(), w=(), inc=True):
        if self._rec is not None:
            self._rec.append(('op', eng, fn, tuple(r), tuple(w), inc))
            return
        r, w = self._split(r, w)
        self._wait(eng, self._deps(r, w))
        ins = fn(self.e[eng])
        self.nins += 1
        if inc:
            self.cnt[eng] += 1
            ins.then_inc(self.sem[eng], 1)
            tok = (eng, self.cnt[eng])
        else:
            tok = (eng, self.cnt[eng] + 1)
        self._record(tok, r, w)

    def dma(self, q, out, in_, r=(), w=(), **kw):
        if self._rec is not None:
            self._rec.append(('dma', q, out, in_, tuple(r), tuple(w), kw))
            return
        if q == 'pool':
            i = self.swnext
            self.swnext += 1
            assert i < NDMA + NSW, "out of SW-DMA semaphores"
        else:
            i = self.dnext
            self.dnext = (self.dnext + 1) % NDMA
        deps = self._deps(r, w)
        if self.duse[i] > 0:
            sk = ('d', i)
            deps[sk] = max(deps.get(sk, 0), self.duse[i] * 16)
        self._wait(q, deps)
        ins = self.e[q].dma_start(out=out, in_=in_, **kw)
        self.duse[i] += 1
        ins.then_inc(self.dsem[i], 16)
        self.nins += 1
        self._record((('d', i), self.duse[i] * 16), r, w)

    def barrier(self):
        toks = {k: self.cnt[k] for k in ('pe', 'act', 'dve', 'pool') if self.cnt[k] > 0}
        dt = {('d', i): self.duse[i] * 16 for i in range(NDMA + NSW) if self.duse[i] > 0}
        for eng in ('pe', 'act', 'dve', 'pool', 'sp'):
            deps = {k: v for k, v in toks.items() if k != eng}
            deps.update(dt)
            self._wait(eng, deps)

    def finish(self):
        for i in range(NDMA + NSW):
            if self.duse[i] > 0:
                self.e['sp'].wait_ge(self.dsem[i], self.duse[i] * 16)
        for k in ('pe', 'act', 'dve', 'pool'):
            if self.cnt[k] > 0:
                self.e['sp'].wait_ge(self.sem[k], self.cnt[k])


def token_tiles():
    tiles = [(i * 512, 512) for i in range(TP // 512)]
    tiles.append((TP, TS))
    return tiles


def build(stage=1):
    nc = bass.Bass("TRN2", target_bir_lowering=False)
    S = Sched(nc)

    def din(name, shape, dt=F32):
        return nc.dram_tensor(name, list(shape), dt, kind="ExternalInput").ap()

    def dout(name, shape, dt=F32):
        return nc.dram_tensor(name, list(shape), dt, kind="ExternalOutput").ap()

    def sb(name, shape, dt):
        return nc.alloc_sbuf_tensor(name, list(shape), dt).ap()

    xT = din("xT", [D, NT])
    normw_d = din("normw", [128, 32])
    wg_d = [din("ffn%d_wg" % i, [D, DFF]) for i in (1, 2)]
    wu_d = [din("ffn%d_wu" % i, [D, DFF]) for i in (1, 2)]
    wd_d = [din("ffn%d_wd" % i, [DFF, D]) for i in (1, 2)]
    yT = dout("yT", [D, NT])
    cmat_d = din("cmat", [128, 2048])
    hp_d = din("hp", [128, 72])
    nbc_d = din("nbc", [128, 1536])
    wssd_d = din("w_ssd", [4, D, 1030])
    woutssd_d = din("w_out_ssd", [4, 384, D])
    cwb_d = din("cwb", [128, 4, 5, 5])
    conv0_d = din("conv0", [128, 4, 5, 48])
    ssd0_d = din("ssd0", [16, 4, 128, 384])
    convp_d = dout("convp", [128, 4, 5, 3])
    convs_d = dout("convs", [128, 4, 5, 48])
    ssdp_d = dout("ssdp", [4, 128, 384])
    ssds_d = dout("ssds", [16, 4, 128, 384])
    winu_d = din("w_in_u", [D, 512])
    wouts5_d = din("w_out_s5", [512, D])
    wglu_d = din("w_glu", [512, 512])
    s5v_d = din("s5v", [128, 8])
    lam_d = din("lam", [128, 3, 272])
    bT_d = din("bT", [128, 2, 4, 64])
    emask_d = din("emask", [128, 8])
    cpad_d = din("cpad", [2, 128, 16, 128])
    s5s0_d = din("s5s0", [128, 2, 16, 16])
    s5p_d = dout("s5p", [128, 2, 16])
    s5s_d = dout("s5s", [128, 2, 16, 16])

    x = sb("x", [128, 8, NT], F32)
    h = sb("h", [128, 8, NT], BF16)
    normw = sb("normw_sb", [128, 32], F32)
    ones_bf = sb("ones_bf", [128, 128], BF16)
    epsc = sb("epsc", [128, 1], F32)
    ps = [nc.alloc_psum_tensor("ps%d" % i, [128, 512], F32).ap() for i in range(8)]

    ARENA = 111040
    arena = sb("arena", [128, ARENA // 2], BF16)

    class Carver:
        def __init__(self):
            self.off = 0

        def get(self, shape, dt):
            n = 1
            for s_ in shape[1:]:
                n *= s_
            nb = n * (2 if dt == BF16 else 4)
            nb = (nb + 31) // 32 * 32
            assert self.off + nb <= ARENA, ("arena overflow", self.off + nb)
            ap = arena[:shape[0], self.off // 2:(self.off + nb) // 2]
            self.off += nb
            if dt != BF16:
                ap = ap.bitcast(dt)
            ap = ap[:, :n]
            if len(shape) == 3:
                ap = ap.rearrange("p (a b) -> p a b", a=shape[1])
            elif len(shape) == 4:
                ap = ap.rearrange("p (a b c) -> p a b c", a=shape[1], b=shape[2])
            return ap

    GMAX = 6
    cv = Carver()
    wg_s = [cv.get([128, 8, GMAX * 128], BF16) for i in range(2)]
    wu_s = [cv.get([128, 8, GMAX * 128], BF16) for i in range(2)]
    wd_s = [cv.get([128, GMAX, D], BF16) for i in range(2)]
    sq = cv.get([128, 8, 512], BF16)
    rstd = [cv.get([128, 512], F32) for i in range(2)]
    sg = [cv.get([128, 512], F32) for i in range(2)]
    actb = [cv.get([128, GMAX, 512], BF16) for i in range(2)]
    ystage = [cv.get([128, 512], F32) for i in range(2)]

    S.op('pool', lambda e: e.memset(ones_bf, 1.0), w=['ones_bf'])
    S.op('pool', lambda e: e.memset(epsc, EPS), w=['epsc'])
    S.dma('sp', normw, normw_d, w=['normw'])
    xT_v = xT.rearrange("(c p) t -> p c t", p=128)
    for c in range(8):
        S.dma('sp', x[:, c, :], xT_v[:, c, :], w=[('x', c, tt) for tt in range(5)])

    TT = token_tiles()

    def rmsnorm(widx, out_fn, tiles=None):
        for tt, (t0, n) in enumerate(TT):
            if tiles is not None and tt not in tiles:
                continue
            for c in range(8):
                S.op('act', lambda e, c=c: e.activation(out=sq[:, c, :n], in_=x[:, c, t0:t0 + n], func=AF.Square),
                     r=[('x', c, tt)], w=[('sq', c)])
            for c in range(8):
                S.op('pe', lambda e, c=c: e.matmul(ps[0][:, :n], ones_bf, sq[:, c, :n], start=(c == 0), stop=(c == 7)),
                     r=[('sq', c), 'ones_bf'], w=[('ps', 0)], inc=(c == 7))
            rb = rstd[tt % 2]
            S.op('act', lambda e: e.activation(out=rb[:, :n], in_=ps[0][:, :n], func=AF.Sqrt, bias=epsc[:, 0:1], scale=1.0 / D),
                 r=[('ps', 0), 'epsc'], w=[('rstd', tt % 2)])
            S.op('dve', lambda e: e.reciprocal(out=rb[:, :n], in_=rb[:, :n]), r=[('rstd', tt % 2)], w=[('rstd', tt % 2)])
            for c in range(8):
                out_fn(tt, c, t0, n, rb)

    def norm_to_h(widx, tiles=None):
        def f(tt, c, t0, n, rb):
            S.op('dve', lambda e: e.scalar_tensor_tensor(out=h[:, c, t0:t0 + n], in0=x[:, c, t0:t0 + n],
                                                         scalar=normw[:, widx * 8 + c:widx * 8 + c + 1], in1=rb[:, :n],
                                                         op0=ALU.mult, op1=ALU.mult),
                 r=[('x', c, tt), ('rstd', tt % 2), 'normw'], w=[('h', c, tt)])
        rmsnorm(widx, f, tiles)

    def ffn(fi, after_tile=None):
        Wg = wg_d[fi].rearrange("(kc p) f -> p kc f", p=128)
        Wu = wu_d[fi].rearrange("(kc p) f -> p kc f", p=128)
        Wd = wd_d[fi].rearrange("(fc p) d -> p fc d", p=128)
        groups = [(0, 6), (6, 6), (12, 6), (18, 4)]

        def load(gi):
            c0, G = groups[gi]
            b = gi % 2
            for kc in range(0, 8, 4):
                S.dma('pool', wg_s[b][:, kc:kc + 4, :G * 128], Wg[:, kc:kc + 4, c0 * 128:(c0 + G) * 128], w=[('wg', b, kc)])
                S.dma('pool', wu_s[b][:, kc:kc + 4, :G * 128], Wu[:, kc:kc + 4, c0 * 128:(c0 + G) * 128], w=[('wu', b, kc)])
            S.dma('pool', wd_s[b][:, :G, :], Wd[:, c0:c0 + G, :], w=[('wd', b)])
        load(0)
        pcount = 0
        ocount = 0
        for gi, (c0, G) in enumerate(groups):
            b = gi % 2
            if gi + 1 < len(groups):
                load(gi + 1)
            for tt, (t0, n) in enumerate(TT):
                ab = (gi * len(TT) + tt) % 2
                for fc in range(G):
                    pgi = pcount % 2
                    pui = 2 + pcount % 2
                    pcount += 1
                    for kc in range(8):
                        S.op('pe', lambda e, kc=kc: e.matmul(ps[pgi][:, :n], wg_s[b][:, kc, fc * 128:(fc + 1) * 128], h[:, kc, t0:t0 + n],
                                                             start=(kc == 0), stop=(kc == 7)),
                             r=[('wg', b, kc // 4 * 4), ('h', kc, tt)], w=[('ps', pgi)], inc=(kc == 7))
                    for kc in range(8):
                        S.op('pe', lambda e, kc=kc: e.matmul(ps[pui][:, :n], wu_s[b][:, kc, fc * 128:(fc + 1) * 128], h[:, kc, t0:t0 + n],
                                                             start=(kc == 0), stop=(kc == 7)),
                             r=[('wu', b, kc // 4 * 4), ('h', kc, tt)], w=[('ps', pui)], inc=(kc == 7))
                    sgb = sg[pcount % 2]
                    S.op('act', lambda e: e.activation(out=sgb[:, :n], in_=ps[pgi][:, :n], func=AF.Silu),
                         r=[('ps', pgi)], w=[('sg', pcount % 2)])
                    S.op('dve', lambda e: e.tensor_tensor(out=actb[ab][:, fc, :n], in0=ps[pui][:, :n], in1=sgb[:, :n], op=ALU.mult),
                         r=[('ps', pui), ('sg', pcount % 2)], w=[('actb', ab, fc)])
                for dc in range(8):
                    poi = 4 + ocount % 4
                    ocount += 1
                    for fc in range(G):
                        S.op('pe', lambda e, fc=fc: e.matmul(ps[poi][:, :n], wd_s[b][:, fc, dc * 128:(dc + 1) * 128], actb[ab][:, fc, :n],
                                                             start=(fc == 0), stop=(fc == G - 1)),
                             r=[('wd', b), ('actb', ab, fc)], w=[('ps', poi)], inc=(fc == G - 1))
                    S.op('dve', lambda e: e.scalar_tensor_tensor(out=x[:, dc, t0:t0 + n], in0=ps[poi][:, :n], scalar=0.5,
                                                                 in1=x[:, dc, t0:t0 + n], op0=ALU.mult, op1=ALU.add),
                         r=[('ps', poi), ('x', dc, tt)], w=[('x', dc, tt)])
                if after_tile is not None and gi == len(groups) - 1:
                    after_tile(tt)

    def TTo(eng, out, a, b, op, r, w):
        S.op(eng, lambda e: e.tensor_tensor(out=out, in0=a, in1=b, op=op), r=r, w=w)

    def TSo(eng, out, a, s1, op0, r, w, s2=None, op1=None):
        if op1 is None:
            S.op(eng, lambda e: e.tensor_scalar(out=out, in0=a, scalar1=s1, scalar2=None, op0=op0), r=r, w=w)
        else:
            S.op(eng, lambda e: e.tensor_scalar(out=out, in0=a, scalar1=s1, scalar2=s2, op0=op0, op1=op1), r=r, w=w)

    def STTo(eng, out, in0, sc, in1, op0, op1, r, w):
        S.op(eng, lambda e: e.scalar_tensor_tensor(out=out, in0=in0, scalar=sc, in1=in1, op0=op0, op1=op1), r=r, w=w)

    def ACTo(out, in_, func, r, w, **kw):
        S.op('act', lambda e: e.activation(out=out, in_=in_, func=func, **kw), r=r, w=w)

    def MMo(out, lhsT, rhs, r, w, start=True, stop=True, inc=True):
        S.op('pe', lambda e: e.matmul(out, lhsT, rhs, start=start, stop=stop), r=r, w=w, inc=inc)

    def TRo(out, in_, ident, r, w, inc=True):
        S.op('pe', lambda e: e.transpose(out, in_, ident), r=r, w=w, inc=inc)

    def CPo(eng, out, in_, r, w):
        if eng == 'act':
            S.op(eng, lambda e: e.activation(out=out, in_=in_, func=AF.Copy), r=r, w=w)
        else:
            S.op(eng, lambda e: e.tensor_copy(out=out, in_=in_), r=r, w=w)

    big_rot = [0]

    def bigbank():
        b = big_rot[0] % 2
        big_rot[0] += 1
        return b

    def ssd_phase():
        S.barrier()
        cv = Carver()
        cm = cv.get([128, 2048], F32)
        S.dma('sp', cm, cmat_d, w=['cm'])
        TI = cm[:, 0:128]
        SU = cm[:, 128:256]
        TIb = cm[:, 256:384]
        SUb = cm[:, 384:512]
        identf = cm[:, 512:640]
        Emat = cm[:, 640:656]
        ident_bf = cv.get([128, 128], BF16)
        CPo('dve', ident_bf, identf, ['cm'], ['ident_bf'])
        ones_f = cv.get([128, 128], F32)
        S.op('pool', lambda e: e.memset(ones_f, 1.0), w=['ones_f'])
        onec = cv.get([128, 1], F32)
        S.op('pool', lambda e: e.memset(onec, 1.0), w=['onec'])
        hp = cv.get([128, 72], F32)
        S.dma('sp', hp, hp_d, w=['hp'])
        abc = cv.get([128, 24], F32)
        ACTo(abc, hp[:, 24:48], AF.Exp, ['hp'], ['abc'])
        TSo('dve', abc, abc, -1.0, ALU.mult, ['abc'], ['abc'])
        cwb = cv.get([128, 4, 5, 5], F32)
        S.dma('sp', cwb, cwb_d, w=['cwb'])

        wssd = cv.get([128, 8, 1030], BF16)
        wout = cv.get([128, 3, 1024], BF16)
        diag = cv.get([128, 5, 4, 128], BF16)
        nbc = cv.get([128, 384], F32)
        xraw = [cv.get([128, 5, 515], BF16) for _ in range(2)]
        xraw_s = cv.get([128, 5, 16, 7], BF16)
        conv0s = cv.get([128, 5, 48], F32)
        convo = cv.get([128, 5, 3], F32)
        convos = cv.get([128, 5, 16, 3], F32)
        fm = cv.get([128, 5, 512], BF16)
        ofm = cv.get([128, 3, 512], BF16)
        dt6 = cv.get([128, 6], F32)
        la = cv.get([128, 6], F32)
        rhsla = cv.get([128, 768], F32)
        rhscd = cv.get([128, 96], F32)
        decT = cv.get([128, 768], BF16)
        MT2 = [cv.get([128, 768], BF16) for _ in range(2)]
        CBm = cv.get([128, 128], BF16)
        ea2 = [cv.get([128, 12], F32) for _ in range(2)]
        cdx2 = [cv.get([128, 96], F32) for _ in range(2)]
        xd2 = [cv.get([128, 6, 64], BF16) for _ in range(2)]
        xdd2 = [cv.get([128, 6, 64], BF16) for _ in range(2)]
        Btm2 = [cv.get([128, 128], BF16) for _ in range(2)]
        Btmm = [cv.get([128, 128], BF16) for _ in range(4)]
        skipx2 = [cv.get([128, 6, 64], F32) for _ in range(2)]
        yv = cv.get([128, 6, 64], F32)
        sz4 = [cv.get([128, 384], F32) for _ in range(4)]
        t6s = [cv.get([128, 6], F32) for _ in range(4)]
        junk = cv.get([128, 384], BF16)
        ss = cv.get([128, 1], F32)
        otm = cv.get([128, 384], BF16)
        hT = cv.get([128, 6, 64], F32)
        htmp = cv.get([128, 6, 64], F32)
        hTb = cv.get([128, 384], BF16)
        hs = [cv.get([128, 6, 64], F32) for _ in range(4)]
        hsb = [cv.get([128, 384], BF16) for _ in range(4)]
        hso = [cv.get([128, 6, 64], F32) for _ in range(4)]
        Cmask = cv.get([128, 16, 64], BF16)
        print("ssd arena used", cv.off)

        def bc6(ap, T):
            return ap[:T, :, None].to_broadcast([T, 6, 64])

        def ssd_pre(g, T, tok0, ci):
            g6 = slice(g * 6, g * 6 + 6)
            bk = 2 if ci % 2 == 0 else 6
            kb = 'ps%d' % bk
            for kc in range(8):
                MMo(ps[bk][:T, 0:390], h[:, kc, tok0:tok0 + T], wssd[:, kc, 640:1030], r=[('h', kc), 'wssd'], w=[kb],
                    start=(kc == 0), stop=(kc == 7), inc=(kc == 7))
            TTo('dve', t6s[ci][:T], ps[bk][:T, 384:390], hp[:T, g6], ALU.add, [kb, 'hp'], [('t6', ci)])
            ACTo(sz4[ci][:T, :], ps[bk][:T, 0:384], AF.Silu, [kb], [('sz4', ci)])

        def ssd_late(g, T, tok0, col0, sample, last, p, tI, sU, MT, xd, xdd, Btm, skipx, sz, ea, cdx,
                     kMT, kxd, kxdd, kBtm, kskipx, ksz, kea, kcdx, yv2):
            for hh in range(6):
                MMo(ps[7][:T, hh * 64:(hh + 1) * 64], MT[:T, hh * T:(hh + 1) * T], xd[:T, hh, :], r=[kMT, kxd], w=['ps7'], inc=(hh == 5))
            if not sample:
                MMo(ps[3][:T, 0:384], fm[:, 4, col0:col0 + T], hTb, r=['fmC', 'hTb'], w=['ps3'])
                MMo(ps[4][:, 0:384], Btm[:T, :], xdd[:T].rearrange("p a b -> p (a b)"), r=[kBtm, kxdd], w=['ps4'])
            else:
                TTo('pool', Cmask, fm[:, 4, None, col0:col0 + T].to_broadcast([128, 16, T]),
                    cm[:, 1024:2048].rearrange("p (a b) -> p a b", a=16), ALU.mult, ['fmC', 'cm'], ['Cmask'])
                for b in range(16):
                    rb_ = b % 4
                    S.dma('sp', hs[rb_].rearrange("p a b -> p (a b)"), ssd0_d[b, g], w=[('hs', rb_)])
                    CPo('act', hsb[rb_], hs[rb_].rearrange("p a b -> p (a b)"), [('hs', rb_)], [('hsb', rb_)])
                    MMo(ps[3][:T, 0:384], Cmask[:, b, :], hsb[rb_], r=['Cmask', ('hsb', rb_)], w=['ps3'], start=(b == 0), stop=(b == 15),
                        inc=(b == 15))
                    TSo('dve', Btmm[rb_][:T, :], Btm[:T, :], Emat[:T, b:b + 1], ALU.mult, [kBtm, 'cm'], [('Btmm', rb_)])
                    MMo(ps[4][:, 0:384], Btmm[rb_][:T, :], xdd[:T].rearrange("p a b -> p (a b)"), r=[('Btmm', rb_), kxdd], w=['ps4'])
                    TTo('dve', hso[rb_], hs[rb_], cdx[:, b * 6:(b + 1) * 6][:, :, None].to_broadcast([128, 6, 64]), ALU.mult,
                        [('hs', rb_), kcdx], [('hso', rb_)])
                    TTo('dve', hso[rb_].rearrange("p a b -> p (a b)"), hso[rb_].rearrange("p a b -> p (a b)"), ps[4][:, 0:384], ALU.add,
                        [('hso', rb_), 'ps4'], [('hso', rb_)])
                    S.dma('sp', ssds_d[b, g], hso[rb_].rearrange("p a b -> p (a b)"), r=[('hso', rb_)], w=[('ssds', b, g)])
            TTo('dve', yv[:T], ps[3][:T, 0:384].rearrange("p (a b) -> p a b", a=6), bc6(ea[:, 0:6], T), ALU.mult, ['ps3', kea], ['yv'])
            TTo('dve', yv2, yv2, ps[7][:T, 0:384], ALU.add, ['yv', 'ps7'], ['yv'])
            TTo('pool', yv[:T], yv[:T], skipx[:T], ALU.add, ['yv', kskipx], ['yv'])
            TTo('pool', yv2, yv2, sz[:T, :], ALU.mult, ['yv', ksz], ['yv'])
            ACTo(junk[:T, :], yv2, AF.Square, ['yv'], ['junk', 'ss'], accum_out=ss[:T, 0:1])
            ACTo(ss[:T, :], ss[:T, :], AF.Ln, ['ss', 'epsc'], ['ss'], bias=epsc[:T, 0:1], scale=1.0 / 384)
            ACTo(ss[:T, :], ss[:T, :], AF.Exp, ['ss'], ['ss'], scale=-0.5)
            STTo('dve', otm[:T, :], yv2, ss[:T, 0:1], nbc[:T, :], ALU.mult, ALU.mult, ['yv', 'ss', 'nbc'], ['otm'])
            if not sample:
                TTo('dve', htmp, hT, cdx[:, 0:6][:, :, None].to_broadcast([128, 6, 64]), ALU.mult, ['hT', kcdx], ['htmp'])
                TTo('dve', hT.rearrange("p a b -> p (a b)"), htmp.rearrange("p a b -> p (a b)"), ps[4][:, 0:384], ALU.add,
                    ['htmp', 'ps4'], ['hT'])
                if last:
                    S.dma('sp', ssdp_d[g], hT.rearrange("p a b -> p (a b)"), r=['hT'], w=[('ssdp', g)])
                else:
                    CPo('act', hTb, hT.rearrange("p a b -> p (a b)"), ['hT'], ['hTb'])
            po = ps[7].bitcast(BF16)[:, 0:512]
            for j in range(3):
                TRo(po[:, j * 128:j * 128 + T], otm[:T, j * 128:(j + 1) * 128], ident_bf[:T, :T], r=['otm', 'ident_bf'], w=['ps7'], inc=(j == 2))
            CPo('act', ofm[:, :, col0:col0 + T], po[:, 0:384].rearrange("p (a b) -> p a b", a=3)[:, :, :T], ['ps7'], ['ofm'])


        def ssd_chunk(g, T, tok0, col0, sample, last, p, part, ci=0):
            g6 = slice(g * 6, g * 6 + 6)
            tI = TIb if sample else TI
            sU = SUb if sample else SU
            MT, xd, xdd, Btm, skipx, sz, ea, cdx = MT2[p], xd2[p], xdd2[p], Btm2[p], skipx2[p], sz4[ci], ea2[p], cdx2[p]
            kMT, kxd, kxdd, kBtm, kskipx, ksz, kea, kcdx = [(nm, p) for nm in ('MT', 'xd', 'xdd', 'Btm', 'skipx', 'sz', 'ea', 'cdx')]
            ksz = ('sz4', ci)
            yv2 = yv[:T].rearrange("p a b -> p (a b)")
            if part == 'late':
                return ssd_late(g, T, tok0, col0, sample, last, p, tI, sU, MT, xd, xdd, Btm, skipx, sz, ea, cdx,
                                kMT, kxd, kxdd, kBtm, kskipx, ksz, kea, kcdx, yv2)
            t6 = t6s[ci]
            kt6 = ('t6', ci)
            ACTo(t6[:T], t6[:T], AF.Exp, [kt6], [kt6])
            ACTo(dt6[:T], t6[:T], AF.Ln, [kt6, 'onec'], ['dt6'], bias=onec[:T, 0:1], scale=1.0)
            TTo('dve', la[:T], dt6[:T], abc[:T, g6], ALU.mult, ['dt6', 'abc'], ['la'])
            rl3 = rhsla[:T, :6 * T].rearrange("p (a b) -> p a b", a=6)
            TTo('pool', rl3, la[:T, :, None].to_broadcast([T, 6, T]), tI[:T, None, :T].to_broadcast([T, 6, T]), ALU.mult,
                ['la', 'cm'], ['rhsla'])
            ncd = 6
            if sample:
                ncd = 96
                TTo('pool', rhscd[:T, :96].rearrange("p (a b) -> p a b", a=16), la[:T, None, :].to_broadcast([T, 16, 6]),
                    Emat[:T, :, None].to_broadcast([T, 16, 6]), ALU.mult, ['la', 'cm'], ['rhscd'])
            MMo(ps[0][:T, :3 * T], sU[:T, :T], rhsla[:T, 0:3 * T], r=['cm', 'rhsla'], w=[('ps', 0)])
            MMo(ps[1][:T, :3 * T], sU[:T, :T], rhsla[:T, 3 * T:6 * T], r=['cm', 'rhsla'], w=[('ps', 1)])
            MMo(ps[5][:T, 0:6], tI[:T, :T], la[:T, :], r=['cm', 'la'], w=['ps5'])
            MMo(ps[5][:T, 6:12], sU[:T, :T], la[:T, :], r=['cm', 'la'], w=['ps5'])
            if sample:
                MMo(ps[5][:, 16:16 + 96], ones_f[:T, :], rhscd[:T, :96], r=['ones_f', 'rhscd'], w=['ps5'])
            else:
                MMo(ps[5][:, 16:22], ones_f[:T, :], la[:T, :], r=['ones_f', 'la'], w=['ps5'])
            ACTo(decT[:T, 0:3 * T], ps[0][:T, :3 * T], AF.Exp, [('ps', 0)], ['decTa'])
            ACTo(decT[:T, 3 * T:6 * T], ps[1][:T, :3 * T], AF.Exp, [('ps', 1)], ['decTb'])
            ACTo(ea[:T, :], ps[5][:T, 0:12], AF.Exp, ['ps5'], [kea])
            ACTo(cdx[:, :ncd], ps[5][:, 16:16 + ncd], AF.Exp, ['ps5'], [kcdx])
            MMo(ps[5][:T, 128:128 + T], fm[:, 3, col0:col0 + T], fm[:, 4, col0:col0 + T], r=['fmB', 'fmC'], w=['ps5'])
            TTo('dve', CBm[:T, :T], ps[5][:T, 128:128 + T], tI[:T, :T], ALU.mult, ['ps5', 'cm'], ['CBm'])
            TTo('dve', MT[:T, :6 * T].rearrange("p (a b) -> p a b", a=6), decT[:T, :6 * T].rearrange("p (a b) -> p a b", a=6),
                CBm[:T, None, :T].to_broadcast([T, 6, T]), ALU.mult, ['decTa', 'decTb', 'CBm'], [kMT])
            pt = ps[6].bitcast(BF16)
            for j in range(4):
                TRo(pt[:T, j * 128:(j + 1) * 128], fm[:, j, col0:col0 + T], ident_bf, r=[('fmx', j) if j < 3 else 'fmB', 'ident_bf'],
                    w=['ps6'], inc=(j == 3))
            xs3 = pt[:T, 0:384].rearrange("p (a b) -> p a b", a=6)
            TTo('dve', xd[:T], xs3, bc6(dt6, T), ALU.mult, ['ps6', 'dt6'], [kxd])
            TTo('dve', xdd[:T], xd[:T], bc6(ea[:, 6:12], T), ALU.mult, [kxd, kea], [kxdd])
            CPo('act', Btm[:T, :], pt[:T, 384:512], ['ps6'], [kBtm])
            TTo('dve', skipx[:T], xs3, bc6(hp[:, 48 + g * 6:54 + g * 6], T), ALU.mult, ['ps6', 'hp'], [kskipx])
            return

        STOP = 0
        KT = os.environ.get('KTILES')
        KG = int(os.environ.get('KG', '4'))
        KCUT = int(os.environ.get('KCUT', '99'))
        for g in range(KG):
            S.dma('pool', wssd[:, 0:4, :], wssd_d[g].rearrange("(kc p) f -> p kc f", p=128)[:, 0:4, :], w=['wssd'])
            S.dma('pool', wssd[:, 4:8, :], wssd_d[g].rearrange("(kc p) f -> p kc f", p=128)[:, 4:8, :], w=['wssd'])
            S.dma('pool', wout, woutssd_d[g].rearrange("(j p) d -> p j d", p=128), w=['wout'])
            S.dma('sp', nbc, nbc_d[:, g * 384:(g + 1) * 384], w=['nbc'])
            S.dma('sp', conv0s, conv0_d[:, g], w=['conv0s'])
            for cc in range(5):
                for k in range(4):
                    TSo('dve', diag[:, cc, k, :], ident_bf, cwb[:, g, cc, k:k + 1], ALU.mult, ['ident_bf', 'cwb'], ['diag'])
            S.op('pool', lambda e: e.memset(hT.rearrange("p a b -> p (a b)"), 0.0), w=['hT'])
            S.op('pool', lambda e: e.memset(hTb, 0.0), w=['hTb'])
            S.op('pool', lambda e: e.memset(xraw[1][:, :, 512:515], 0.0), w=[('xraw', 1)])
            CPo('dve', xraw_s[:, :, :, 0:3], conv0s.rearrange("p c (b k) -> p c b k", b=16), ['conv0s'], ['xraw_s'])
            for tt, (t0, n) in enumerate(TT):
                if KT is not None and str(tt) not in KT.split(','):
                    continue
                sample = (tt == 4)
                xr = xraw[tt % 2]
                for cc in range(5):
                    bk = bigbank()
                    for kc in range(8):
                        MMo(ps[bk][:, :n], wssd[:, kc, cc * 128:(cc + 1) * 128], h[:, kc, t0:t0 + n], r=['wssd', ('h', kc)], w=[('ps', bk)],
                            start=(kc == 0), stop=(kc == 7), inc=(kc == 7))
                    if not sample:
                        CPo('act', xr[:, cc, 3:3 + n], ps[bk][:, :n], [('ps', bk)], [('xraw', tt % 2)])
                        CPo('dve', xr[:, cc, 0:3], xraw[(tt + 1) % 2][:, cc, 512:515], [('xraw', (tt + 1) % 2)], [('xraw', tt % 2)])
                        if tt == 3:
                            CPo('dve', convo[:, cc, :], ps[bk][:, 509:512], [('ps', bk)], ['convo'])
                    else:
                        p3 = ps[bk][:, :64].rearrange("p (b l) -> p b l", b=16)
                        CPo('act', xraw_s[:, cc, :, 3:7], p3, [('ps', bk)], ['xraw_s'])
                        CPo('dve', convos[:, cc, :, :], p3[:, :, 1:4], [('ps', bk)], ['convos'])
                if tt == 3:
                    S.dma('sp', convp_d[:, g], convo, r=['convo'], w=[('convp', g)])
                if sample:
                    S.dma('sp', convs_d[:, g], convos.rearrange("p c b k -> p c (b k)"), r=['convos'], w=[('convs', g)])
                for cc in range(5):
                    bk = bigbank()
                    for k in range(4):
                        if not sample:
                            rhs = xr[:, cc, k:k + n]
                            rk = ('xraw', tt % 2)
                        else:
                            rhs = xraw_s[:, cc, :, k:k + 4]
                            rk = 'xraw_s'
                        MMo(ps[bk][:, :n], diag[:, cc, k, :], rhs, r=['diag', rk], w=[('ps', bk)], start=(k == 0), stop=(k == 3), inc=(k == 3))
                    wk = ('fmx', cc) if cc < 3 else ('fmB' if cc == 3 else 'fmC')
                    ACTo(fm[:, cc, :n], ps[bk][:, :n], AF.Silu, [('ps', bk), 'cwb'], [wk], bias=cwb[:, g, cc, 4:5], scale=1.0)
                if not sample:
                    def ch(ci, part):
                        return S.record(lambda: ssd_chunk(g, 128, t0 + ci * 128, ci * 128, False, (tt == 3 and ci == 3), ci % 2, part, ci))
                    for ci in range(4):
                        ssd_pre(g, 128, t0 + ci * 128, ci)
                    E = [ch(ci, 'early') for ci in range(4)]
                    Lt = [ch(ci, 'late') for ci in range(4)]
                    S.replay(E[0])
                    for ci in range(4):
                        S.replay(S.merge(E[ci + 1] if ci + 1 < 4 else [], Lt[ci]))
                else:
                    ssd_pre(g, 64, t0, 0)
                    ssd_chunk(g, 64, t0, 0, True, False, 0, 'early')
                    ssd_chunk(g, 64, t0, 0, True, False, 0, 'late')
                for dc in range(8):
                    bk = bigbank()
                    for j in range(3):
                        MMo(ps[bk][:, :n], wout[:, j, dc * 128:(dc + 1) * 128], ofm[:, j, :n], r=['wout', 'ofm'], w=[('ps', bk)],
                            start=(j == 0), stop=(j == 2), inc=(j == 2))
                    TTo('dve', x[:, dc, t0:t0 + n], x[:, dc, t0:t0 + n], ps[bk][:, :n], ALU.add, [('ps', bk), ('x', dc, tt)], [('x', dc, tt)])


    def s5_phase():
        S.barrier()
        cv = Carver()
        I32 = mybir.dt.int32
        PI = 3.14159265358979
        winu = cv.get([128, 8, 512], BF16)
        wouts5 = cv.get([128, 4, 1024], BF16)
        wglu = cv.get([128, 4, 512], BF16)
        lhsB = [cv.get([128, 4, 8, 64], BF16) for _ in range(2)]
        lhsC = [cv.get([128, 16, 128], BF16) for _ in range(2)]
        u_all = cv.get([128, 4, NT], BF16)
        S.dma('pool', winu, winu_d.rearrange("(kc p) f -> p kc f", p=128), w=['winu'])
        S.dma('pool', wouts5, wouts5_d.rearrange("(kc p) f -> p kc f", p=128), w=['wouts5'])
        S.dma('pool', wglu, wglu_d.rearrange("(kc p) f -> p kc f", p=128), w=['wglu'])
        for c in range(2):
            S.dma('pool', lhsC[c], cpad_d[c], w=[('lhsC', c)])
        TSo('dve', lhsC[1], lhsC[1], -1.0, ALU.mult, [('lhsC', 1)], [('lhsC', 1)])
        for tt, (t0, n) in enumerate(TT):
            for kt in range(4):
                bk = bigbank()
                for kc in range(8):
                    MMo(ps[bk][:, :n], winu[:, kc, kt * 128:(kt + 1) * 128], h[:, kc, t0:t0 + n], r=['winu'], w=[('ps', bk)],
                        start=(kc == 0), stop=(kc == 7), inc=(kc == 7))
                CPo('act', u_all[:, kt, t0:t0 + n], ps[bk][:, :n], [('ps', bk)], [('u', tt)])
        S.barrier()
        W = 272
        s5v = cv.get([128, 8], F32)
        S.dma('sp', s5v, s5v_d, w=['s5v'])
        emk = cv.get([128, 8], F32)
        S.dma('sp', emk, emask_d, w=['emk'])
        iot = cv.get([128, 256], F32)
        S.dma('sp', iot, cmat_d[:, 768:1024], w=['iot'])
        st0 = cv.get([128, 2, 16, 16], F32)
        S.dma('sp', st0, s5s0_d, w=['st0'])
        mag = cv.get([128, W], F32)
        carry = cv.get([128, 2, 16], F32)
        carry_s = cv.get([128, 2, 16, 16], F32)
        off_pre = cv.off
        lam = cv.get([128, 3, W], F32)
        S.dma('sp', lam, lam_d, w=['lam'])
        bT = cv.get([128, 2, 4, 64], F32)
        S.dma('sp', bT, bT_d, w=['bT'])
        pre = [cv.get([128, W], F32) for _ in range(11)]
        stp, xr, ang, sn, cs, ar, ai, t1, t2, cr, ci = pre
        ki = cv.get([128, 512], I32)
        kf = cv.get([128, 512], F32)
        rr = cv.get([128, 512], F32)

        def sin_of(out, a, n, shift, key):
            S.op('dve', lambda e: e.tensor_scalar(out=ki[:, :n], in0=a, scalar1=shift, scalar2=1.0 / (2 * PI), op0=ALU.add, op1=ALU.mult),
                 r=[key], w=['ki'])
            CPo('dve', kf[:, :n], ki[:, :n], ['ki'], ['kf'])
            STTo('dve', rr[:, :n], kf[:, :n], -2 * PI, a, ALU.mult, ALU.add, ['kf', key], ['rr'])
            S.op('dve', lambda e: e.tensor_scalar(out=rr[:, :n], in0=rr[:, :n], scalar1=shift, scalar2=-3.1415925, op0=ALU.add, op1=ALU.max),
                 r=['rr'], w=['rr'])
            TSo('dve', rr[:, :n], rr[:, :n], 3.1415925, ALU.min, ['rr'], ['rr'])
            ACTo(out, rr[:, :n], AF.Sin, ['rr'], [key + '_o'])

        ACTo(stp, lam[:, 2, :], AF.Exp, ['lam'], ['pre'])
        TTo('dve', xr, lam[:, 0, :], stp, ALU.mult, ['lam', 'pre'], ['pre'])
        TTo('dve', ang, lam[:, 1, :], stp, ALU.mult, ['lam', 'pre'], ['ang'])
        ACTo(mag, xr, AF.Exp, ['pre'], ['pre'])
        sin_of(sn, ang, W, 0.0, 'ang')
        sin_of(cs, ang, W, PI / 2, 'ang')
        P_ = ['pre', 'ang_o', 'lam']
        TTo('dve', ar, mag, cs, ALU.mult, P_, ['pre'])
        TTo('dve', ai, mag, sn, ALU.mult, P_, ['pre'])
        TTo('dve', t1, lam[:, 0, :], lam[:, 0, :], ALU.mult, P_, ['pre'])
        TTo('dve', t2, lam[:, 1, :], lam[:, 1, :], ALU.mult, P_, ['pre'])
        TTo('dve', t1, t1, t2, ALU.add, P_, ['pre'])
        S.op('dve', lambda e: e.reciprocal(out=t1, in_=t1), r=P_, w=['pre'])
        TSo('dve', t2, ar, -1.0, ALU.add, P_, ['pre'])
        TTo('dve', cr, t2, lam[:, 0, :], ALU.mult, P_, ['pre'])
        TTo('dve', ci, ai, lam[:, 1, :], ALU.mult, P_, ['pre'])
        TTo('dve', cr, cr, ci, ALU.add, P_, ['pre'])
        TTo('dve', cr, cr, t1, ALU.mult, P_, ['pre'])
        TTo('dve', ci, ai, lam[:, 0, :], ALU.mult, P_, ['pre'])
        TTo('dve', t2, t2, lam[:, 1, :], ALU.mult, P_, ['pre'])
        TTo('dve', ci, ci, t2, ALU.subtract, P_, ['pre'])
        TTo('dve', ci, ci, t1, ALU.mult, P_, ['pre'])
        bb = [cv.get([128, 4, 64], F32) for _ in range(2)]
        tb = cv.get([128, 4, 64], F32)
        cr3 = cr[:, 0:256].rearrange("p (a b) -> p a b", a=4)
        ci3 = ci[:, 0:256].rearrange("p (a b) -> p a b", a=4)
        TTo('dve', bb[0], cr3, bT[:, 0], ALU.mult, P_ + ['bT'], ['bb'])
        TTo('dve', tb, ci3, bT[:, 1], ALU.mult, P_ + ['bT'], ['tb'])
        TTo('dve', bb[0], bb[0], tb, ALU.subtract, ['bb', 'tb'], ['bb'])
        TTo('dve', bb[1], cr3, bT[:, 1], ALU.mult, P_ + ['bT'], ['bb'])
        TTo('dve', tb, ci3, bT[:, 0], ALU.mult, P_ + ['bT'], ['tb'])
        TTo('dve', bb[1], bb[1], tb, ALU.add, ['bb', 'tb'], ['bb'])
        for c in range(2):
            for kt in range(4):
                TTo('dve', lhsB[c][:, kt], bb[c][:, kt, None, :].to_broadcast([128, 8, 64]), emk[:, :, None].to_broadcast([128, 8, 64]),
                    ALU.mult, ['bb', 'emk'], [('lhsB', c)])
        tabf = h.rearrange("p a b -> p (a b)").bitcast(F32)
        Ec = tabf[:, 0:4096].rearrange("p (a b) -> p a b", a=16)
        Es = tabf[:, 4096:8192].rearrange("p (a b) -> p a b", a=16)
        angt = cv.get([128, 2, 256], F32)
        for i0 in range(0, 16, 2):
            TTo('dve', angt, ang[:, 256 + i0:258 + i0][:, :, None].to_broadcast([128, 2, 256]), iot[:, None, :].to_broadcast([128, 2, 256]),
                ALU.mult, ['ang', 'iot'], ['angt'])
            af = angt.rearrange("p a b -> p (a b)")
            sin_of(Es[:, i0:i0 + 2, :].rearrange("p a b -> p (a b)"), af, 512, 0.0, 'angt')
            sin_of(Ec[:, i0:i0 + 2, :].rearrange("p a b -> p (a b)"), af, 512, PI / 2, 'angt')
        TAB = ['angt_o']
        S.barrier()
        cv.off = off_pre
        S.op('pool', lambda e: e.memset(carry.rearrange("p a b -> p (a b)"), 0.0), w=['carry'])
        tmA = [cv.get([128, 2, 256], F32) for _ in range(4)]
        wreg = winu.rearrange("p a b -> p (a b)").bitcast(F32)
        tmB = [wreg[:, k * 512:(k + 1) * 512].rearrange("p (a b) -> p a b", a=2) for k in range(4)]
        tm2 = [tmA, tmB]
        wv2 = [[cv.get([128, 2, 256], F32) for _ in range(2)] for _ in range(2)]
        Wv2 = [[cv.get([128, 2, 256], F32) for _ in range(2)] for _ in range(2)]
        sv2 = [[cv.get([128, 2, 256], F32) for _ in range(2)] for _ in range(2)]
        rt_single = cv.get([128, 256], F32)
        rt2 = [rt_single, rt_single]
        hist2 = [cv.get([128, 2, 512], BF16) for _ in range(2)]
        y5 = cv.get([128, 512], F32)
        g1 = cv.get([128, 512], F32)
        v_bf = cv.get([128, 4, 512], BF16)
        gs = g1
        o5 = cv.get([128, 4, 512], BF16)
        print("s5 arena used", cv.off)
        magq = mag[:, 256:272]

        def bct(tab, i, nseg, L):
            return tab[:, i, None, 0:L].to_broadcast([128, nseg, L])

        for tt, (t0, n) in enumerate(TT):
            sample = (tt == 4)
            nseg, L = (16, 4) if sample else (2, 256)

            def v3(ap):
                return ap.rearrange("p a b -> p (a b)")[:, :nseg * L].rearrange("p (a b) -> p a b", a=nseg)
            for kt in range(4):
                cb = 4 if kt % 2 == 0 else 7
                kcb = 'ps%d' % cb
                for jp in (0, 2):
                    ctxs = []
                    for j in (jp, jp + 1):
                        i = 4 * kt + j
                        par = i % 2
                        pre_b, pim_b = (2, 3) if par == 0 else (5, 6)
                        ctxs.append(dict(i=i, j=j, par=par, tm=tm2[par], wv=wv2[par], Wv=Wv2[par], sv=sv2[par], hist=hist2[par],
                                         pre_b=pre_b, pim_b=pim_b, kre='ps%d' % pre_b, kim='ps%d' % pim_b))
                    for cx in ctxs:
                        i, j = cx['i'], cx['j']
                        lB = [lhsB[c].rearrange("p a b c -> p a (b c)")[:, kt, j * 128:(j + 1) * 128] for c in range(2)]
                        MMo(ps[cx['pre_b']][:, :n], lB[0], u_all[:, kt, t0:t0 + n], r=[('lhsB', 0), ('u', tt)], w=[cx['kre']])
                        MMo(ps[cx['pim_b']][:, :n], lB[1], u_all[:, kt, t0:t0 + n], r=[('lhsB', 1), ('u', tt)], w=[cx['kim']])
                    for cx in ctxs:
                        i, par, tm = cx['i'], cx['par'], cx['tm']
                        Pre = ps[cx['pre_b']][:, :n].rearrange("p (a b) -> p a b", a=nseg)
                        Pim = ps[cx['pim_b']][:, :n].rearrange("p (a b) -> p a b", a=nseg)
                        ec, es = bct(Ec, i, nseg, L), bct(Es, i, nseg, L)
                        TTo('dve', v3(tm[0]), Pre, ec, ALU.mult, [cx['kre']] + TAB, [('tm', par, 0)])
                        TTo('dve', v3(tm[1]), Pim, es, ALU.mult, [cx['kim']] + TAB, [('tm', par, 1)])
                        TTo('dve', v3(tm[2]), Pim, ec, ALU.mult, [cx['kim']] + TAB, [('tm', par, 2)])
                        TTo('dve', v3(tm[3]), Pre, es, ALU.mult, [cx['kre']] + TAB, [('tm', par, 3)])
                    for cx in ctxs:
                        par, tm, wv = cx['par'], cx['tm'], cx['wv']
                        TTo('pool', v3(wv[0]), v3(tm[0]), v3(tm[1]), ALU.add, [('tm', par, 0), ('tm', par, 1)], [('wv', par, 0)])
                        TTo('pool', v3(wv[1]), v3(tm[2]), v3(tm[3]), ALU.subtract, [('tm', par, 2), ('tm', par, 3)], [('wv', par, 1)])
                    groups = [list(range(16))] if sample else [[0], [1]]
                    for grp in groups:
                        g0, g1_ = grp[0], grp[-1] + 1
                        for cx in ctxs:
                            i, par, wv, Wv, sv = cx['i'], cx['par'], cx['wv'], cx['Wv'], cx['sv']
                            rbc = magq[:, i:i + 1].to_broadcast([128, L])
                            for sg_ in grp:
                                for c in range(2):
                                    if sample:
                                        init = st0[:, c, i, sg_:sg_ + 1]
                                        ik = 'st0'
                                    elif sg_ == 0:
                                        init = carry[:, c, i:i + 1]
                                        ik = 'carry'
                                    else:
                                        init = sv[c][:, 0, 255:256]
                                        ik = ('sv', par, c)
                                    S.op('dve', lambda e: e.tensor_tensor_scan(out=v3(Wv[c])[:, sg_, :], data0=rbc, data1=v3(wv[c])[:, sg_, :],
                                                                               initial=init, op0=ALU.mult, op1=ALU.add),
                                         r=['pre', ('wv', par, c), ik], w=[('Wv', par, c)])
                        for cx in ctxs:
                            i, par, tm, Wv = cx['i'], cx['par'], cx['tm'], cx['Wv']
                            ecg = Ec[:, i, None, 0:L].to_broadcast([128, g1_ - g0, L])
                            esg = Es[:, i, None, 0:L].to_broadcast([128, g1_ - g0, L])
                            TTo('dve', v3(tm[0])[:, g0:g1_], v3(Wv[0])[:, g0:g1_], ecg, ALU.mult, [('Wv', par, 0)] + TAB, [('tm', par, 0)])
                            TTo('dve', v3(tm[1])[:, g0:g1_], v3(Wv[1])[:, g0:g1_], esg, ALU.mult, [('Wv', par, 1)] + TAB, [('tm', par, 1)])
                        for cx in ctxs:
                            i, par, tm, Wv, sv = cx['i'], cx['par'], cx['tm'], cx['Wv'], cx['sv']
                            ecg = Ec[:, i, None, 0:L].to_broadcast([128, g1_ - g0, L])
                            esg = Es[:, i, None, 0:L].to_broadcast([128, g1_ - g0, L])
                            TTo('pool', v3(tm[2])[:, g0:g1_], v3(Wv[1])[:, g0:g1_], ecg, ALU.mult, [('Wv', par, 1)] + TAB, [('tm', par, 2)])
                            TTo('pool', v3(tm[3])[:, g0:g1_], v3(Wv[0])[:, g0:g1_], esg, ALU.mult, [('Wv', par, 0)] + TAB, [('tm', par, 3)])
                            TTo('pool', v3(sv[0])[:, g0:g1_], v3(tm[0])[:, g0:g1_], v3(tm[1])[:, g0:g1_], ALU.subtract,
                                [('tm', par, 0), ('tm', par, 1)], [('sv', par, 0)])
                            TTo('pool', v3(sv[1])[:, g0:g1_], v3(tm[2])[:, g0:g1_], v3(tm[3])[:, g0:g1_], ALU.add,
                                [('tm', par, 2), ('tm', par, 3)], [('sv', par, 1)])
                    for cx in ctxs:
                        i, j, par, sv, hist = cx['i'], cx['j'], cx['par'], cx['sv'], cx['hist']
                        for c in range(2):
                            svf = sv[c].rearrange("p a b -> p (a b)")
                            CPo('act', hist[:, c, :n], svf[:, :n], [('sv', par, c)], [('hist', par, c)])
                            if sample:
                                CPo('act', carry_s[:, c, i, :], v3(sv[c])[:, :, 3], [('sv', par, c)], ['carry_s'])
                            else:
                                CPo('act', carry[:, c, i:i + 1], svf[:, 511:512], [('sv', par, c)], ['carry'])
                            MMo(ps[cb][:, :n], lhsC[c][:, i, :], hist[:, c, :n], r=[('lhsC', c), ('hist', par, c)], w=[kcb],
                                start=(j == 0 and c == 0), stop=(j == 3 and c == 1), inc=True)
                STTo('dve', y5[:, :n], u_all[:, kt, t0:t0 + n], s5v[:, kt:kt + 1], ps[cb][:, :n], ALU.mult, ALU.add, [('u', tt), 's5v', kcb], ['y5'])
                TTo('pool', g1[:, :n], y5[:, :n], y5[:, :n], ALU.mult, ['y5'], ['g1'])
                S.op('dve', lambda e: e.tensor_scalar(out=g1[:, :n], in0=g1[:, :n], scalar1=0.044715, scalar2=1.0, op0=ALU.mult, op1=ALU.add),
                     r=['g1'], w=['g1'])
                TTo('pool', g1[:, :n], g1[:, :n], y5[:, :n], ALU.mult, ['g1', 'y5'], ['g1'])
                ACTo(g1[:, :n], g1[:, :n], AF.Sigmoid, ['g1'], ['g1'], scale=1.5957691216057308)
                TTo('dve', v_bf[:, kt, :n], g1[:, :n], y5[:, :n], ALU.mult, ['g1', 'y5'], [('v', kt)])
            for mo in range(4):
                bk = bigbank()
                for kt in range(4):
                    MMo(ps[bk][:, :n], wglu[:, kt, mo * 128:(mo + 1) * 128], v_bf[:, kt, :n], r=['wglu', ('v', kt)], w=[('ps', bk)],
                        start=(kt == 0), stop=(kt == 3), inc=(kt == 3))
                ACTo(gs[:, :n], ps[bk][:, :n], AF.Sigmoid, [('ps', bk), 's5v'], ['g1'], bias=s5v[:, 4 + mo:5 + mo], scale=1.0)
                TTo('dve', o5[:, mo, :n], v_bf[:, mo, :n], gs[:, :n], ALU.mult, [('v', mo), 'g1'], [('o5', mo)])
            for dc in range(8):
                bk = bigbank()
                for mo in range(4):
                    MMo(ps[bk][:, :n], wouts5[:, mo, dc * 128:(dc + 1) * 128], o5[:, mo, :n], r=['wouts5', ('o5', mo)], w=[('ps', bk)],
                        start=(mo == 0), stop=(mo == 3), inc=(mo == 3))
                TTo('dve', x[:, dc, t0:t0 + n], x[:, dc, t0:t0 + n], ps[bk][:, :n], ALU.add, [('ps', bk), ('x', dc, tt)], [('x', dc, tt)])
        S.dma('sp', s5p_d, carry, r=['carry'], w=['s5p'])
        S.dma('sp', s5s_d, carry_s, r=['carry_s'], w=['s5s'])


    PH = os.environ.get('KPH', 'n1,f1,n2,ssd,s5,n3,f2').split(',')
    if 'n1' in PH:
        norm_to_h(0)
    if 'f1' in PH:
        ffn(0, after_tile=(lambda tt: norm_to_h(1, tiles=[tt])) if 'n2' in PH else None)
    elif 'n2' in PH:
        norm_to_h(1)
    if 'ssd' in PH:
        ssd_phase()
    if 's5' in PH:
        s5_phase()
    S.barrier()
    if 'n3' in PH:
        norm_to_h(2)
    yT_v = yT.rearrange("(c p) t -> p c t", p=128)
    ycount = [0]

    def final_out(tt, c, t0, n, rb):
        yb = ycount[0] % 2
        ycount[0] += 1
        S.op('dve', lambda e: e.scalar_tensor_tensor(out=ystage[yb][:, :n], in0=x[:, c, t0:t0 + n],
                                                     scalar=normw[:, 24 + c:24 + c + 1], in1=rb[:, :n],
                                                     op0=ALU.mult, op1=ALU.mult),
             r=[('x', c, tt), ('rstd', tt % 2), 'normw'], w=[('ystage', yb)])
        S.dma('sp', yT_v[:, c, t0:t0 + n], ystage[yb][:, :n], r=[('ystage', yb)], w=[('yT', c, tt)])
    if 'f2' in PH:
        ffn(1, after_tile=lambda tt: rmsnorm(3, final_out, tiles=[tt]))
    else:
        rmsnorm(3, final_out)
    S.finish()
    print("instructions", S.nins, "waits", S.nwait)
    return nc


_NC_CACHE = {}


def _consts():
    f32 = np.float32
    cm = np.zeros((128, 2048), f32)
    k = np.arange(128)
    cm[:, 0:128] = (k[:, None] <= k[None, :])
    cm[:, 128:256] = (k[:, None] > k[None, :])
    same = (k[:, None] // 4 == k[None, :] // 4) & (k[:, None] < 64) & (k[None, :] < 64)
    cm[:, 256:384] = (k[:, None] <= k[None, :]) & same
    cm[:, 384:512] = (k[:, None] > k[None, :]) & same
    cm[:, 512:640] = np.eye(128)
    cm[:64, 640:656] = (np.arange(64)[:, None] // 4 == np.arange(16)[None, :])
    cm[:, 768:1024] = np.arange(1, 257)[None, :]
    et = (np.arange(64)[None, :] // 4 == np.arange(16)[:, None]).astype(f32)
    cm[:, 1024:2048] = et.reshape(1, 1024)
    return cm


def _chan(g, cc):
    if cc < 3:
        return 384 * g + 128 * cc
    if cc == 3:
        return 1536 + 128 * g
    return 2048 + 128 * g


def _s5_host(inp):
    f32 = np.float32
    A = lambda k: np.asarray(inp[k], f32)
    w_in = A("w_in")[0]
    w_out = A("w_out")[0]
    s5v = np.empty((128, 8), f32)
    s5v[:, 0:4] = A("s5_d")[0].reshape(4, 128).T
    s5v[:, 4:8] = A("s5_b_glu")[0].reshape(4, 128).T
    lre, lim, lst = A("s5_lambda_re")[0], A("s5_lambda_im")[0], A("s5_log_step")[0]
    lam = np.empty((128, 3, 272), f32)
    for arr_i, arr in enumerate((lre, lim)):
        rep = arr.reshape(4, 8, 64)
        rep = np.repeat(rep.transpose(1, 0, 2)[:, None], 16, axis=1)
        lam[:, arr_i, 0:256] = rep.reshape(128, 256)
        qq = arr.reshape(16, 2, 64).transpose(1, 2, 0).reshape(128, 16)
        lam[:, arr_i, 256:272] = qq
    rep = np.repeat(lst.reshape(4, 8).T[:, None, :, None], 16, axis=1)
    lam[:, 2, 0:256] = np.broadcast_to(rep, (8, 16, 4, 64)).reshape(128, 256)
    qq = np.broadcast_to(lst.reshape(16, 2).T[:, None, :], (2, 64, 16)).reshape(128, 16)
    lam[:, 2, 256:272] = qq
    bT = np.empty((128, 2, 4, 64), f32)
    for ci, k in enumerate(("s5_b_re", "s5_b_im")):
        b = A(k)[0].reshape(4, 8, 64, 16)
        bT[:, ci] = b.transpose(1, 3, 0, 2).reshape(128, 4, 64)
    emask = (np.arange(128)[:, None] // 16 == np.arange(8)[None, :]).astype(f32)
    cpad = np.zeros((2, 128, 16, 128), f32)
    for ci, k in enumerate(("s5_c_re", "s5_c_im")):
        cc = A(k)[0]
        for g in range(32):
            i, q0, gl = g // 2, (g % 2) * 64, g % 8
            cpad[ci, q0:q0 + 64, i, gl * 16:(gl + 1) * 16] = cc[g].T
    return {"w_in_u": np.ascontiguousarray(w_in[:, 0:512]), "w_out_s5": np.ascontiguousarray(w_out[0:512]),
            "w_glu": np.ascontiguousarray(A("s5_w_glu")[0]), "s5v": s5v, "lam": lam, "bT": bT, "emask": emask, "cpad": cpad}


def _s5_core(inp, c):
    f32 = np.float32
    out = np.empty((128, 2, 16, 16), f32)
    for ci, k in enumerate(("state_s5_re", "state_s5_im")):
        s = np.asarray(inp[k], f32)[0, 16 * c:16 * c + 16]
        out[:, ci] = s.reshape(16, 16, 2, 64).transpose(2, 3, 1, 0).reshape(128, 16, 16)
    return {"s5s0": out}


def kernel(**inp):
    f32 = np.float32
    A = lambda k: np.asarray(inp[k], f32)
    xp = A("x_prompt")
    xs = A("x_sample")
    normw = np.stack([A(k).reshape(D) for k in ("ffn1_norm", "mix_norm", "ffn2_norm", "final_norm")])
    normw_l = np.ascontiguousarray(normw.reshape(4, 8, 128).transpose(2, 0, 1).reshape(128, 32))
    w_in = A("w_in")[0]
    w_out = A("w_out")[0]
    w_ssd = np.empty((4, D, 1030), f32)
    w_out_ssd = np.empty((4, 384, D), f32)
    for g in range(4):
        w_ssd[g, :, 0:384] = w_in[:, 2048 + 384 * g:2048 + 384 * (g + 1)]
        w_ssd[g, :, 384:512] = w_in[:, 3584 + 128 * g:3584 + 128 * (g + 1)]
        w_ssd[g, :, 512:640] = w_in[:, 4096 + 128 * g:4096 + 128 * (g + 1)]
        w_ssd[g, :, 640:1024] = w_in[:, 512 + 384 * g:512 + 384 * (g + 1)]
        w_ssd[g, :, 1024:1030] = w_in[:, 4608 + 6 * g:4608 + 6 * (g + 1)]
        w_out_ssd[g] = w_out[512 + 384 * g:512 + 384 * (g + 1)]
    conv_w = A("ssd_conv_w")[0]
    conv_b = A("ssd_conv_b")[0]
    cwb = np.empty((128, 4, 5, 5), f32)
    for g in range(4):
        for cc in range(5):
            c0 = _chan(g, cc)
            cwb[:, g, cc, 0:4] = conv_w[:, c0:c0 + 128].T
            cwb[:, g, cc, 4] = conv_b[c0:c0 + 128]
    hp = np.concatenate([A("ssd_dt_bias")[0], A("ssd_a_log")[0], A("ssd_d")[0]])[None, :].repeat(128, 0)
    nbc = A("ssd_norm")[0][None, :].repeat(128, 0)
    sconv = A("state_conv")[0]
    sssd = A("state_ssd")[0]
    shared = {
        "normw": normw_l,
        "ffn1_wg": np.ascontiguousarray(A("ffn1_w_gate")[0]), "ffn1_wu": np.ascontiguousarray(A("ffn1_w_up")[0]),
        "ffn1_wd": np.ascontiguousarray(A("ffn1_w_down")[0]),
        "ffn2_wg": np.ascontiguousarray(A("ffn2_w_gate")[0]), "ffn2_wu": np.ascontiguousarray(A("ffn2_w_up")[0]),
        "ffn2_wd": np.ascontiguousarray(A("ffn2_w_down")[0]),
        "cmat": _consts(), "hp": np.ascontiguousarray(hp), "nbc": np.ascontiguousarray(nbc),
        "w_ssd": w_ssd, "w_out_ssd": w_out_ssd, "cwb": cwb,
    }
    shared.update(_s5_host(inp))
    in_maps = []
    for c in range(NCORES):
        bs = slice(16 * c, 16 * c + 16)
        xc = np.concatenate([xp[c], xs[bs].reshape(TS, D)], axis=0)
        m = dict(shared)
        m["xT"] = np.ascontiguousarray(xc.T)
        cv0 = np.empty((128, 4, 5, 16, 3), f32)
        for g in range(4):
            for cc in range(5):
                c0 = _chan(g, cc)
                cv0[:, g, cc] = sconv[bs, :, c0:c0 + 128].transpose(2, 0, 1)
        m["conv0"] = cv0.reshape(128, 4, 5, 48)
        st = sssd[bs].reshape(16, 4, 6, 64, 128).transpose(0, 1, 4, 2, 3).reshape(16, 4, 128, 384)
        m["ssd0"] = np.ascontiguousarray(st)
        m.update(_s5_core(inp, c))
        in_maps.append(m)
    if "nc" not in _NC_CACHE:
        _NC_CACHE["nc"] = build()
    nc = _NC_CACHE["nc"]
    res = run_bass_kernel_spmd(nc, in_maps, core_ids=list(range(NCORES)))
    R = res.results
    yp = np.empty((8, TP, D), f32)
    ys = np.empty((128, 4, D), f32)
    ssd_p = np.empty((1, 8, 24, 64, 128), f32)
    ssd_s = np.empty((1, 128, 24, 64, 128), f32)
    conv_p = np.empty((1, 8, 3, 2560), f32)
    conv_s = np.empty((1, 128, 3, 2560), f32)
    s5p = np.empty((2, 1, 8, 32, 64), f32)
    s5s = np.empty((2, 1, 128, 32, 64), f32)
    for c in range(NCORES):
        bs = slice(16 * c, 16 * c + 16)
        y = R[c]["yT"].T
        yp[c] = y[:TP]
        ys[bs] = y[TP:].reshape(16, 4, D)
        ssd_p[0, c] = R[c]["ssdp"].reshape(4, 128, 6, 64).transpose(0, 2, 3, 1).reshape(24, 64, 128)
        ssd_s[0, bs] = R[c]["ssds"].reshape(16, 4, 128, 6, 64).transpose(0, 1, 3, 4, 2).reshape(16, 24, 64, 128)
        cp = R[c]["convp"]
        cs = R[c]["convs"].reshape(128, 4, 5, 16, 3)
        for g in range(4):
            for cc in range(5):
                c0 = _chan(g, cc)
                conv_p[0, c, :, c0:c0 + 128] = cp[:, g, cc, :].T
                conv_s[0, bs, :, c0:c0 + 128] = cs[:, g, cc].transpose(1, 2, 0)
        a = R[c]["s5p"]
        s5p[:, 0, c] = a.reshape(2, 64, 2, 16).transpose(2, 3, 0, 1).reshape(2, 32, 64)
        b_ = R[c]["s5s"]
        s5s[:, 0, bs] = b_.reshape(2, 64, 2, 16, 16).transpose(2, 4, 3, 0, 1).reshape(2, 16, 32, 64)
    return (yp, ys, s5p[0], s5p[1], ssd_p, conv_p, s5s[0], s5s[1], ssd_s, conv_s)
```

```python
import os
import numpy as np
import concourse.bass as bass
import concourse.mybir as mybir
from concourse.bass_utils import run_bass_kernel_spmd

F32 = mybir.dt.float32
BF16 = mybir.dt.bfloat16
AF = mybir.ActivationFunctionType
ALU = mybir.AluOpType

NCORES = 8
D = 1024
DFF = 2816
TP = 2048
TS = 64
NT = TP + TS
EPS = 1e-6
NDMA = 24
NSW = 64


class Sched:
    def __init__(self, nc):
        self.nc = nc
        self.e = {'pe': nc.tensor, 'act': nc.scalar, 'dve': nc.vector, 'pool': nc.gpsimd, 'sp': nc.sync}
        self.sem = {k: nc.alloc_semaphore('s_' + k) for k in self.e}
        self.cnt = {k: 0 for k in self.e}
        self.waited = {}
        self.lastw = {}
        self.readers = {}
        self.dsem = [nc.alloc_semaphore('d%d' % i) for i in range(NDMA + NSW)]
        self.duse = [0] * (NDMA + NSW)
        self.dnext = 0
        self.swnext = NDMA
        self.nwait = 0
        self.nins = 0
        self._rec = None

    def _semobj(self, sk):
        return self.sem[sk] if isinstance(sk, str) else self.dsem[sk[1]]

    def _deps(self, r, w):
        deps = {}

        def add(tok):
            sk, v = tok
            if deps.get(sk, 0) < v:
                deps[sk] = v
        for k in r:
            if k in self.lastw:
                add(self.lastw[k])
        for k in w:
            if k in self.lastw:
                add(self.lastw[k])
            for sk, v in self.readers.get(k, {}).items():
                add((sk, v))
        return deps

    def _wait(self, eng, deps):
        for sk, v in deps.items():
            if sk == eng:
                if eng == 'pe' or os.environ.get('KNOSAME') == '1':
                    continue
                assert v <= self.cnt[eng], "same-engine dep on non-inc instruction"
            if self.waited.get((eng, sk), 0) >= v:
                continue
            self.e[eng].wait_ge(self._semobj(sk), v)
            self.waited[(eng, sk)] = v
            self.nwait += 1

    def _record(self, tok, r, w):
        sk, v = tok
        for k in r:
            d = self.readers.setdefault(k, {})
            if d.get(sk, 0) < v:
                d[sk] = v
        for k in w:
            self.lastw[k] = tok
            self.readers[k] = {}

    @staticmethod
    def _split(r, w):
        def isps(k):
            return (isinstance(k, tuple) and k[0] == 'ps') or (isinstance(k, str) and k.startswith('ps'))
        r2 = [k for k in r if not isps(k)]
        w2 = list(w) + [k for k in r if isps(k)]
        return r2, w2

    def record(self, f):
        self._rec = []
        f()
        out, self._rec = self._rec, None
        return out

    def replay(self, items):
        for it in items:
            if it[0] == 'op':
                self.op(*it[1:])
            else:
                self.dma(*it[1:-1], **it[-1])

    @staticmethod
    def merge(a, b):
        out = []
        ia = ib = 0
        na, nb = len(a), len(b)
        while ia < na or ib < nb:
            if ib >= nb or (ia < na and ia * nb <= ib * na):
                out.append(a[ia])
                ia += 1
            else:
                out.append(b[ib])
                ib += 1
        return out

    def op(self, eng, fn, r=(), w=(), inc=True):
        if self._rec is not None:
            self._rec.append(('op', eng, fn, tuple(r), tuple(w), inc))
            return
        r, w = self._split(r, w)
        self._wait(eng, self._deps(r, w))
        ins = fn(self.e[eng])
        self.nins += 1
        if inc:
            self.cnt[eng] += 1
            ins.then_inc(self.sem[eng], 1)
            tok = (eng, self.cnt[eng])
        else:
            tok = (eng, self.cnt[eng] + 1)
        self._record(tok, r, w)

    def dma(self, q, out, in_, r=(), w=(), **kw):
        if self._rec is not None:
            self._rec.append(('dma', q, out, in_, tuple(r), tuple(w), kw))
            return
        if q == 'pool':
            i = self.swnext
            self.swnext += 1
            assert i < NDMA + NSW, "out of SW-DMA semaphores"
        else:
            i = self.dnext
            self.dnext = (self.dnext + 1) % NDMA
        deps = self._deps(r, w)
        if self.duse[i] > 0:
            sk = ('d', i)
            deps[sk] = max(deps.get(sk, 0), self.duse[i] * 16)
        self._wait(q, deps)
        ins = self.e[q].dma_start(out=out, in_=in_, **kw)
        self.duse[i] += 1
        ins.then_inc(self.dsem[i], 16)
        self.nins += 1
        self._record((('d', i), self.duse[i] * 16), r, w)

    def barrier(self):
        toks = {k: self.cnt[k] for k in ('pe', 'act', 'dve', 'pool') if self.cnt[k] > 0}
        dt = {('d', i): self.duse[i] * 16 for i in range(NDMA + NSW) if self.duse[i] > 0}
        for eng in ('pe', 'act', 'dve', 'pool', 'sp'):
            deps = {k: v for k, v in toks.items() if k != eng}
            deps.update(dt)
            self._wait(eng, deps)

    def finish(self):
        for i in range(NDMA + NSW):
            if self.duse[i] > 0:
                self.e['sp'].wait_ge(self.dsem[i], self.duse[i] * 16)
        for k in ('pe', 'act', 'dve', 'pool'):
            if self.cnt[k] > 0:
                self.e['sp'].wait_ge(self.sem[k], self.cnt[k])


def token_tiles():
    tiles = [(i * 512, 512) for i in range(TP // 512)]
    tiles.append((TP, TS))
    return tiles


def build(stage=1):
    nc = bass.Bass("TRN2", target_bir_lowering=False)
    S = Sched(nc)

    def din(name, shape, dt=F32):
        return nc.dram_tensor(name, list(shape), dt, kind="ExternalInput").ap()

    def dout(name, shape, dt=F32):
        return nc.dram_tensor(name, list(shape), dt, kind="ExternalOutput").ap()

    def sb(name, shape, dt):
        return nc.alloc_sbuf_tensor(name, list(shape), dt).ap()

    xT = din("xT", [D, NT])
    normw_d = din("normw", [128, 32])
    wg_d = [din("ffn%d_wg" % i, [D, DFF]) for i in (1, 2)]
    wu_d = [din("ffn%d_wu" % i, [D, DFF]) for i in (1, 2)]
    wd_d = [din("ffn%d_wd" % i, [DFF, D]) for i in (1, 2)]
    yT = dout("yT", [D, NT])
    cmat_d = din("cmat", [128, 2048])
    hp_d = din("hp", [128, 72])
    nbc_d = din("nbc", [128, 1536])
    wssd_d = din("w_ssd", [4, D, 1030])
    woutssd_d = din("w_out_ssd", [4, 384, D])
    cwb_d = din("cwb", [128, 4, 5, 5])
    conv0_d = din("conv0", [128, 4, 5, 48])
    ssd0_d = din("ssd0", [16, 4, 128, 384])
    convp_d = dout("convp", [128, 4, 5, 3])
    convs_d = dout("convs", [128, 4, 5, 48])
    ssdp_d = dout("ssdp", [4, 128, 384])
    ssds_d = dout("ssds", [16, 4, 128, 384])
    winu_d = din("w_in_u", [D, 512])
    wouts5_d = din("w_out_s5", [512, D])
    wglu_d = din("w_glu", [512, 512])
    s5v_d = din("s5v", [128, 8])
    lam_d = din("lam", [128, 3, 272])
    bT_d = din("bT", [128, 2, 4, 64])
    emask_d = din("emask", [128, 8])
    cpad_d = din("cpad", [2, 128, 16, 128])
    s5s0_d = din("s5s0", [128, 2, 16, 16])
    s5p_d = dout("s5p", [128, 2, 16])
    s5s_d = dout("s5s", [128, 2, 16, 16])

    x = sb("x", [128, 8, NT], F32)
    h = sb("h", [128, 8, NT], BF16)
    normw = sb("normw_sb", [128, 32], F32)
    ones_bf = sb("ones_bf", [128, 128], BF16)
    epsc = sb("epsc", [128, 1], F32)
    ps = [nc.alloc_psum_tensor("ps%d" % i, [128, 512], F32).ap() for i in range(8)]

    ARENA = 111040
    arena = sb("arena", [128, ARENA // 2], BF16)

    class Carver:
        def __init__(self):
            self.off = 0

        def get(self, shape, dt):
            n = 1
            for s_ in shape[1:]:
                n *= s_
            nb = n * (2 if dt == BF16 else 4)
            nb = (nb + 31) // 32 * 32
            assert self.off + nb <= ARENA, ("arena overflow", self.off + nb)
            ap = arena[:shape[0], self.off // 2:(self.off + nb) // 2]
            self.off += nb
            if dt != BF16:
                ap = ap.bitcast(dt)
            ap = ap[:, :n]
            if len(shape) == 3:
                ap = ap.rearrange("p (a b) -> p a b", a=shape[1])
            elif len(shape) == 4:
                ap = ap.rearrange("p (a b c) -> p a b c", a=shape[1], b=shape[2])
            return ap

    GMAX = 6
    cv = Carver()
    wg_s = [cv.get([128, 8, GMAX * 128], BF16) for i in range(2)]
    wu_s = [cv.get([128, 8, GMAX * 128], BF16) for i in range(2)]
    wd_s = [cv.get([128, GMAX, D], BF16) for i in range(2)]
    sq = cv.get([128, 8, 512], BF16)
    rstd = [cv.get([128, 512], F32) for i in range(2)]
    sg = [cv.get([128, 512], F32) for i in range(2)]
    actb = [cv.get([128, GMAX, 512], BF16) for i in range(2)]
    ystage = [cv.get([128, 512], F32) for i in range(2)]

    S.op('pool', lambda e: e.memset(ones_bf, 1.0), w=['ones_bf'])
    S.op('pool', lambda e: e.memset(epsc, EPS), w=['epsc'])
    S.dma('sp', normw, normw_d, w=['normw'])
    xT_v = xT.rearrange("(c p) t -> p c t", p=128)
    for c in range(8):
        S.dma('sp', x[:, c, :], xT_v[:, c, :], w=[('x', c, tt) for tt in range(5)])

    TT = token_tiles()

    def rmsnorm(widx, out_fn, tiles=None):
        for tt, (t0, n) in enumerate(TT):
            if tiles is not None and tt not in tiles:
                continue
            for c in range(8):
                S.op('act', lambda e, c=c: e.activation(out=sq[:, c, :n], in_=x[:, c, t0:t0 + n], func=AF.Square),
                     r=[('x', c, tt)], w=[('sq', c)])
            for c in range(8):
                S.op('pe', lambda e, c=c: e.matmul(ps[0][:, :n], ones_bf, sq[:, c, :n], start=(c == 0), stop=(c == 7)),
                     r=[('sq', c), 'ones_bf'], w=[('ps', 0)], inc=(c == 7))
            rb = rstd[tt % 2]
            S.op('act', lambda e: e.activation(out=rb[:, :n], in_=ps[0][:, :n], func=AF.Sqrt, bias=epsc[:, 0:1], scale=1.0 / D),
                 r=[('ps', 0), 'epsc'], w=[('rstd', tt % 2)])
            S.op('dve', lambda e: e.reciprocal(out=rb[:, :n], in_=rb[:, :n]), r=[('rstd', tt % 2)], w=[('rstd', tt % 2)])
            for c in range(8):
                out_fn(tt, c, t0, n, rb)

    def norm_to_h(widx, tiles=None):
        def f(tt, c, t0, n, rb):
            S.op('dve', lambda e: e.scalar_tensor_tensor(out=h[:, c, t0:t0 + n], in0=x[:, c, t0:t0 + n],
                                                         scalar=normw[:, widx * 8 + c:widx * 8 + c + 1], in1=rb[:, :n],
                                                         op0=ALU.mult, op1=ALU.mult),
                 r=[('x', c, tt), ('rstd', tt % 2), 'normw'], w=[('h', c, tt)])
        rmsnorm(widx, f, tiles)

    def ffn(fi, after_tile=None):
        Wg = wg_d[fi].rearrange("(kc p) f -> p kc f", p=128)
        Wu = wu_d[fi].rearrange("(kc p) f -> p kc f", p=128)
        Wd = wd_d[fi].rearrange("(fc p) d -> p fc d", p=128)
        groups = [(0, 6), (6, 6), (12, 6), (18, 4)]

        def load(gi):
            c0, G = groups[gi]
            b = gi % 2
            for kc in range(0, 8, 4):
                S.dma('pool', wg_s[b][:, kc:kc + 4, :G * 128], Wg[:, kc:kc + 4, c0 * 128:(c0 + G) * 128], w=[('wg', b, kc)])
                S.dma('pool', wu_s[b][:, kc:kc + 4, :G * 128], Wu[:, kc:kc + 4, c0 * 128:(c0 + G) * 128], w=[('wu', b, kc)])
            S.dma('pool', wd_s[b][:, :G, :], Wd[:, c0:c0 + G, :], w=[('wd', b)])
        load(0)
        pcount = 0
        ocount = 0
        for gi, (c0, G) in enumerate(groups):
            b = gi % 2
            if gi + 1 < len(groups):
                load(gi + 1)
            for tt, (t0, n) in enumerate(TT):
                ab = (gi * len(TT) + tt) % 2
                for fc in range(G):
                    pgi = pcount % 2
                    pui = 2 + pcount % 2
                    pcount += 1
                    for kc in range(8):
                        S.op('pe', lambda e, kc=kc: e.matmul(ps[pgi][:, :n], wg_s[b][:, kc, fc * 128:(fc + 1) * 128], h[:, kc, t0:t0 + n],
                                                             start=(kc == 0), stop=(kc == 7)),
                             r=[('wg', b, kc // 4 * 4), ('h', kc, tt)], w=[('ps', pgi)], inc=(kc == 7))
                    for kc in range(8):
                        S.op('pe', lambda e, kc=kc: e.matmul(ps[pui][:, :n], wu_s[b][:, kc, fc * 128:(fc + 1) * 128], h[:, kc, t0:t0 + n],
                                                             start=(kc == 0), stop=(kc == 7)),
                             r=[('wu', b, kc // 4 * 4), ('h', kc, tt)], w=[('ps', pui)], inc=(kc == 7))
                    sgb = sg[pcount % 2]
                    S.op('act', lambda e: e.activation(out=sgb[:, :n], in_=ps[pgi][:, :n], func=AF.Silu),
                         r=[('ps', pgi)], w=[('sg', pcount % 2)])
                    S.op('dve', lambda e: e.tensor_tensor(out=actb[ab][:, fc, :n], in0=ps[pui][:, :n], in1=sgb[:, :n], op=ALU.mult),
                         r=[('ps', pui), ('sg', pcount % 2)], w=[('actb', ab, fc)])
                for dc in range(8):
                    poi = 4 + ocount % 4
                    ocount += 1
                    for fc in range(G):
                        S.op('pe', lambda e, fc=fc: e.matmul(ps[poi][:, :n], wd_s[b][:, fc, dc * 128:(dc + 1) * 128], actb[ab][:, fc, :n],
                                                             start=(fc == 0), stop=(fc == G - 1)),
                             r=[('wd', b), ('actb', ab, fc)], w=[('ps', poi)], inc=(fc == G - 1))
                    S.op('dve', lambda e: e.scalar_tensor_tensor(out=x[:, dc, t0:t0 + n], in0=ps[poi][:, :n], scalar=0.5,
                                                                 in1=x[:, dc, t0:t0 + n], op0=ALU.mult, op1=ALU.add),
                         r=[('ps', poi), ('x', dc, tt)], w=[('x', dc, tt)])
                if after_tile is not None and gi == len(groups) - 1:
                    after_tile(tt)

    def TTo(eng, out, a, b, op, r, w):
        S.op(eng, lambda e: e.tensor_tensor(out=out, in0=a, in1=b, op=op), r=r, w=w)

    def TSo(eng, out, a, s1, op0, r, w, s2=None, op1=None):
        if op1 is None:
            S.op(eng, lambda e: e.tensor_scalar(out=out, in0=a, scalar1=s1, scalar2=None, op0=op0), r=r, w=w)
        else:
            S.op(eng, lambda e: e.tensor_scalar(out=out, in0=a, scalar1=s1, scalar2=s2, op0=op0, op1=op1), r=r, w=w)

    def STTo(eng, out, in0, sc, in1, op0, op1, r, w):
        S.op(eng, lambda e: e.scalar_tensor_tensor(out=out, in0=in0, scalar=sc, in1=in1, op0=op0, op1=op1), r=r, w=w)

    def ACTo(out, in_, func, r, w, **kw):
        S.op('act', lambda e: e.activation(out=out, in_=in_, func=func, **kw), r=r, w=w)

    def MMo(out, lhsT, rhs, r, w, start=True, stop=True, inc=True):
        S.op('pe', lambda e: e.matmul(out, lhsT, rhs, start=start, stop=stop), r=r, w=w, inc=inc)

    def TRo(out, in_, ident, r, w, inc=True):
        S.op('pe', lambda e: e.transpose(out, in_, ident), r=r, w=w, inc=inc)

    def CPo(eng, out, in_, r, w):
        if eng == 'act':
            S.op(eng, lambda e: e.activation(out=out, in_=in_, func=AF.Copy), r=r, w=w)
        else:
            S.op(eng, lambda e: e.tensor_copy(out=out, in_=in_), r=r, w=w)

    big_rot = [0]

    def bigbank():
        b = big_rot[0] % 2
        big_rot[0] += 1
        return b

    def ssd_phase():
        S.barrier()
        cv = Carver()
        cm = cv.get([128, 2048], F32)
        S.dma('sp', cm, cmat_d, w=['cm'])
        TI = cm[:, 0:128]
        SU = cm[:, 128:256]
        TIb = cm[:, 256:384]
        SUb = cm[:, 384:512]
        identf = cm[:, 512:640]
        Emat = cm[:, 640:656]
        ident_bf = cv.get([128, 128], BF16)
        CPo('dve', ident_bf, identf, ['cm'], ['ident_bf'])
        ones_f = cv.get([128, 128], F32)
        S.op('pool', lambda e: e.memset(ones_f, 1.0), w=['ones_f'])
        onec = cv.get([128, 1], F32)
        S.op('pool', lambda e: e.memset(onec, 1.0), w=['onec'])
        hp = cv.get([128, 72], F32)
        S.dma('sp', hp, hp_d, w=['hp'])
        abc = cv.get([128, 24], F32)
        ACTo(abc, hp[:, 24:48], AF.Exp, ['hp'], ['abc'])
        TSo('dve', abc, abc, -1.0, ALU.mult, ['abc'], ['abc'])
        cwb = cv.get([128, 4, 5, 5], F32)
        S.dma('sp', cwb, cwb_d, w=['cwb'])

        wssd = cv.get([128, 8, 1030], BF16)
        wout = cv.get([128, 3, 1024], BF16)
        diag = cv.get([128, 5, 4, 128], BF16)
        nbc = cv.get([128, 384], F32)
        xraw = [cv.get([128, 5, 515], BF16) for _ in range(2)]
        xraw_s = cv.get([128, 5, 16, 7], BF16)
        conv0s = cv.get([128, 5, 48], F32)
        convo = cv.get([128, 5, 3], F32)
        convos = cv.get([128, 5, 16, 3], F32)
        fm = cv.get([128, 5, 512], BF16)
        ofm = cv.get([128, 3, 512], BF16)
        dt6 = cv.get([128, 6], F32)
        la = cv.get([128, 6], F32)
        rhsla = cv.get([128, 768], F32)
        rhscd = cv.get([128, 96], F32)
        decT = cv.get([128, 768], BF16)
        MT2 = [cv.get([128, 768], BF16) for _ in range(2)]
        CBm = cv.get([128, 128], BF16)
        ea2 = [cv.get([128, 12], F32) for _ in range(2)]
        cdx2 = [cv.get([128, 96], F32) for _ in range(2)]
        xd2 = [cv.get([128, 6, 64], BF16) for _ in range(2)]
        xdd2 = [cv.get([128, 6, 64], BF16) for _ in range(2)]
        Btm2 = [cv.get([128, 128], BF16) for _ in range(2)]
        Btmm = [cv.get([128, 128], BF16) for _ in range(4)]
        skipx2 = [cv.get([128, 6, 64], F32) for _ in range(2)]
        yv = cv.get([128, 6, 64], F32)
        sz4 = [cv.get([128, 384], F32) for _ in range(4)]
        t6s = [cv.get([128, 6], F32) for _ in range(4)]
        junk = cv.get([128, 384], BF16)
        ss = cv.get([128, 1], F32)
        otm = cv.get([128, 384], BF16)
        hT = cv.get([128, 6, 64], F32)
        htmp = cv.get([128, 6, 64], F32)
        hTb = cv.get([128, 384], BF16)
        hs = [cv.get([128, 6, 64], F32) for _ in range(4)]
        hsb = [cv.get([128, 384], BF16) for _ in range(4)]
        hso = [cv.get([128, 6, 64], F32) for _ in range(4)]
        Cmask = cv.get([128, 16, 64], BF16)
        print("ssd arena used", cv.off)

        def bc6(ap, T):
            return ap[:T, :, None].to_broadcast([T, 6, 64])

        def ssd_pre(g, T, tok0, ci):
            g6 = slice(g * 6, g * 6 + 6)
            bk = 2 if ci % 2 == 0 else 6
            kb = 'ps%d' % bk
            for kc in range(8):
                MMo(ps[bk][:T, 0:390], h[:, kc, tok0:tok0 + T], wssd[:, kc, 640:1030], r=[('h', kc), 'wssd'], w=[kb],
                    start=(kc == 0), stop=(kc == 7), inc=(kc == 7))
            TTo('dve', t6s[ci][:T], ps[bk][:T, 384:390], hp[:T, g6], ALU.add, [kb, 'hp'], [('t6', ci)])
            ACTo(sz4[ci][:T, :], ps[bk][:T, 0:384], AF.Silu, [kb], [('sz4', ci)])

        def ssd_late(g, T, tok0, col0, sample, last, p, tI, sU, MT, xd, xdd, Btm, skipx, sz, ea, cdx,
                     kMT, kxd, kxdd, kBtm, kskipx, ksz, kea, kcdx, yv2):
            for hh in range(6):
                MMo(ps[7][:T, hh * 64:(hh + 1) * 64], MT[:T, hh * T:(hh + 1) * T], xd[:T, hh, :], r=[kMT, kxd], w=['ps7'], inc=(hh == 5))
            if not sample:
                MMo(ps[3][:T, 0:384], fm[:, 4, col0:col0 + T], hTb, r=['fmC', 'hTb'], w=['ps3'])
                MMo(ps[4][:, 0:384], Btm[:T, :], xdd[:T].rearrange("p a b -> p (a b)"), r=[kBtm, kxdd], w=['ps4'])
            else:
                TTo('pool', Cmask, fm[:, 4, None, col0:col0 + T].to_broadcast([128, 16, T]),
                    cm[:, 1024:2048].rearrange("p (a b) -> p a b", a=16), ALU.mult, ['fmC', 'cm'], ['Cmask'])
                dh_banks = [(4, 'ps4'), (0, ('ps', 0)), (1, ('ps', 1)), (2, 'ps2')]
                for b4 in range(0, 16, 4):
                    bs_ = list(range(b4, b4 + 4))
                    for b in bs_:
                        rb_ = b % 4
                        S.dma('sp', hs[rb_].rearrange("p a b -> p (a b)"), ssd0_d[b, g], w=[('hs', rb_)])
                    for b in bs_:
                        rb_ = b % 4
                        CPo('act', hsb[rb_], hs[rb_].rearrange("p a b -> p (a b)"), [('hs', rb_)], [('hsb', rb_)])
                        TSo('dve', Btmm[rb_][:T, :], Btm[:T, :], Emat[:T, b:b + 1], ALU.mult, [kBtm, 'cm'], [('Btmm', rb_)])
                    for b in bs_:
                        rb_ = b % 4
                        MMo(ps[3][:T, 0:384], Cmask[:, b, :], hsb[rb_], r=['Cmask', ('hsb', rb_)], w=['ps3'], start=(b == 0), stop=(b == 15),
                            inc=(b == 15))
                    for b in bs_:
                        rb_ = b % 4
                        bk_, kb_ = dh_banks[rb_]
                        MMo(ps[bk_][:, 0:384], Btmm[rb_][:T, :], xdd[:T].rearrange("p a b -> p (a b)"), r=[('Btmm', rb_), kxdd], w=[kb_])
                    for b in bs_:
                        rb_ = b % 4
                        bk_, kb_ = dh_banks[rb_]
                        TTo('dve', hso[rb_], hs[rb_], cdx[:, b * 6:(b + 1) * 6][:, :, None].to_broadcast([128, 6, 64]), ALU.mult,
                            [('hs', rb_), kcdx], [('hso', rb_)])
                        TTo('dve', hso[rb_].rearrange("p a b -> p (a b)"), hso[rb_].rearrange("p a b -> p (a b)"), ps[bk_][:, 0:384], ALU.add,
                            [('hso', rb_), kb_], [('hso', rb_)])
                        S.dma('sp', ssds_d[b, g], hso[rb_].rearrange("p a b -> p (a b)"), r=[('hso', rb_)], w=[('ssds', b, g)])
            TTo('dve', yv[:T], ps[3][:T, 0:384].rearrange("p (a b) -> p a b", a=6), bc6(ea[:, 0:6], T), ALU.mult, ['ps3', kea], ['yv'])
            TTo('dve', yv2, yv2, ps[7][:T, 0:384], ALU.add, ['yv', 'ps7'], ['yv'])
            TTo('pool', yv[:T], yv[:T], skipx[:T], ALU.add, ['yv', kskipx], ['yv'])
            TTo('pool', yv2, yv2, sz[:T, :], ALU.mult, ['yv', ksz], ['yv'])
            ACTo(junk[:T, :], yv2, AF.Square, ['yv'], ['junk', 'ss'], accum_out=ss[:T, 0:1])
            ACTo(ss[:T, :], ss[:T, :], AF.Ln, ['ss', 'epsc'], ['ss'], bias=epsc[:T, 0:1], scale=1.0 / 384)
            ACTo(ss[:T, :], ss[:T, :], AF.Exp, ['ss'], ['ss'], scale=-0.5)
            STTo('dve', otm[:T, :], yv2, ss[:T, 0:1], nbc[:T, :], ALU.mult, ALU.mult, ['yv', 'ss', 'nbc'], ['otm'])
            if not sample:
                TTo('dve', htmp, hT, cdx[:, 0:6][:, :, None].to_broadcast([128, 6, 64]), ALU.mult, ['hT', kcdx], ['htmp'])
                TTo('dve', hT.rearrange("p a b -> p (a b)"), htmp.rearrange("p a b -> p (a b)"), ps[4][:, 0:384], ALU.add,
                    ['htmp', 'ps4'], ['hT'])
                if last:
                    S.dma('sp', ssdp_d[g], hT.rearrange("p a b -> p (a b)"), r=['hT'], w=[('ssdp', g)])
                else:
                    CPo('act', hTb, hT.rearrange("p a b -> p (a b)"), ['hT'], ['hTb'])
            po = ps[7].bitcast(BF16)[:, 0:512]
            for j in range(3):
                TRo(po[:, j * 128:j * 128 + T], otm[:T, j * 128:(j + 1) * 128], ident_bf[:T, :T], r=['otm', 'ident_bf'], w=['ps7'], inc=(j == 2))
            CPo('act', ofm[:, :, col0:col0 + T], po[:, 0:384].rearrange("p (a b) -> p a b", a=3)[:, :, :T], ['ps7'], ['ofm'])


        def ssd_chunk(g, T, tok0, col0, sample, last, p, part, ci=0):
            g6 = slice(g * 6, g * 6 + 6)
            tI = TIb if sample else TI
            sU = SUb if sample else SU
            MT, xd, xdd, Btm, skipx, sz, ea, cdx = MT2[p], xd2[p], xdd2[p], Btm2[p], skipx2[p], sz4[ci], ea2[p], cdx2[p]
            kMT, kxd, kxdd, kBtm, kskipx, ksz, kea, kcdx = [(nm, p) for nm in ('MT', 'xd', 'xdd', 'Btm', 'skipx', 'sz', 'ea', 'cdx')]
            ksz = ('sz4', ci)
            yv2 = yv[:T].rearrange("p a b -> p (a b)")
            if part == 'late':
                return ssd_late(g, T, tok0, col0, sample, last, p, tI, sU, MT, xd, xdd, Btm, skipx, sz, ea, cdx,
                                kMT, kxd, kxdd, kBtm, kskipx, ksz, kea, kcdx, yv2)
            t6 = t6s[ci]
            kt6 = ('t6', ci)
            ACTo(t6[:T], t6[:T], AF.Exp, [kt6], [kt6])
            ACTo(dt6[:T], t6[:T], AF.Ln, [kt6, 'onec'], ['dt6'], bias=onec[:T, 0:1], scale=1.0)
            TTo('dve', la[:T], dt6[:T], abc[:T, g6], ALU.mult, ['dt6', 'abc'], ['la'])
            rl3 = rhsla[:T, :6 * T].rearrange("p (a b) -> p a b", a=6)
            TTo('pool', rl3, la[:T, :, None].to_broadcast([T, 6, T]), tI[:T, None, :T].to_broadcast([T, 6, T]), ALU.mult,
                ['la', 'cm'], ['rhsla'])
            ncd = 6
            if sample:
                ncd = 96
                TTo('pool', rhscd[:T, :96].rearrange("p (a b) -> p a b", a=16), la[:T, None, :].to_broadcast([T, 16, 6]),
                    Emat[:T, :, None].to_broadcast([T, 16, 6]), ALU.mult, ['la', 'cm'], ['rhscd'])
            MMo(ps[0][:T, :3 * T], sU[:T, :T], rhsla[:T, 0:3 * T], r=['cm', 'rhsla'], w=[('ps', 0)])
            MMo(ps[1][:T, :3 * T], sU[:T, :T], rhsla[:T, 3 * T:6 * T], r=['cm', 'rhsla'], w=[('ps', 1)])
            MMo(ps[5][:T, 0:6], tI[:T, :T], la[:T, :], r=['cm', 'la'], w=['ps5'])
            MMo(ps[5][:T, 6:12], sU[:T, :T], la[:T, :], r=['cm', 'la'], w=['ps5'])
            if sample:
                MMo(ps[5][:, 16:16 + 96], ones_f[:T, :], rhscd[:T, :96], r=['ones_f', 'rhscd'], w=['ps5'])
            else:
                MMo(ps[5][:, 16:22], ones_f[:T, :], la[:T, :], r=['ones_f', 'la'], w=['ps5'])
            ACTo(decT[:T, 0:3 * T], ps[0][:T, :3 * T], AF.Exp, [('ps', 0)], ['decTa'])
            ACTo(decT[:T, 3 * T:6 * T], ps[1][:T, :3 * T], AF.Exp, [('ps', 1)], ['decTb'])
            ACTo(ea[:T, :], ps[5][:T, 0:12], AF.Exp, ['ps5'], [kea])
            ACTo(cdx[:, :ncd], ps[5][:, 16:16 + ncd], AF.Exp, ['ps5'], [kcdx])
            MMo(ps[5][:T, 128:128 + T], fm[:, 3, col0:col0 + T], fm[:, 4, col0:col0 + T], r=['fmB', 'fmC'], w=['ps5'])
            TTo('dve', CBm[:T, :T], ps[5][:T, 128:128 + T], tI[:T, :T], ALU.mult, ['ps5', 'cm'], ['CBm'])
            TTo('dve', MT[:T, :6 * T].rearrange("p (a b) -> p a b", a=6), decT[:T, :6 * T].rearrange("p (a b) -> p a b", a=6),
                CBm[:T, None, :T].to_broadcast([T, 6, T]), ALU.mult, ['decTa', 'decTb', 'CBm'], [kMT])
            pt = ps[6].bitcast(BF16)
            for j in range(4):
                TRo(pt[:T, j * 128:(j + 1) * 128], fm[:, j, col0:col0 + T], ident_bf, r=[('fmx', j) if j < 3 else 'fmB', 'ident_bf'],
                    w=['ps6'], inc=(j == 3))
            xs3 = pt[:T, 0:384].rearrange("p (a b) -> p a b", a=6)
            TTo('dve', xd[:T], xs3, bc6(dt6, T), ALU.mult, ['ps6', 'dt6'], [kxd])
            TTo('dve', xdd[:T], xd[:T], bc6(ea[:, 6:12], T), ALU.mult, [kxd, kea], [kxdd])
            CPo('act', Btm[:T, :], pt[:T, 384:512], ['ps6'], [kBtm])
            TTo('dve', skipx[:T], xs3, bc6(hp[:, 48 + g * 6:54 + g * 6], T), ALU.mult, ['ps6', 'hp'], [kskipx])
            return

        STOP = 0
        KT = os.environ.get('KTILES')
        KG = int(os.environ.get('KG', '4'))
        KCUT = int(os.environ.get('KCUT', '99'))
        for g in range(KG):
            S.dma('pool', wssd[:, 0:4, :], wssd_d[g].rearrange("(kc p) f -> p kc f", p=128)[:, 0:4, :], w=['wssd'])
            S.dma('pool', wssd[:, 4:8, :], wssd_d[g].rearrange("(kc p) f -> p kc f", p=128)[:, 4:8, :], w=['wssd'])
            S.dma('pool', wout, woutssd_d[g].rearrange("(j p) d -> p j d", p=128), w=['wout'])
            S.dma('sp', nbc, nbc_d[:, g * 384:(g + 1) * 384], w=['nbc'])
            S.dma('sp', conv0s, conv0_d[:, g], w=['conv0s'])
            for cc in range(5):
                for k in range(4):
                    TSo('dve', diag[:, cc, k, :], ident_bf, cwb[:, g, cc, k:k + 1], ALU.mult, ['ident_bf', 'cwb'], ['diag'])
            S.op('pool', lambda e: e.memset(hT.rearrange("p a b -> p (a b)"), 0.0), w=['hT'])
            S.op('pool', lambda e: e.memset(hTb, 0.0), w=['hTb'])
            S.op('pool', lambda e: e.memset(xraw[1][:, :, 512:515], 0.0), w=[('xraw', 1)])
            CPo('dve', xraw_s[:, :, :, 0:3], conv0s.rearrange("p c (b k) -> p c b k", b=16), ['conv0s'], ['xraw_s'])
            for tt, (t0, n) in enumerate(TT):
                if KT is not None and str(tt) not in KT.split(','):
                    continue
                sample = (tt == 4)
                xr = xraw[tt % 2]
                for cc in range(5):
                    bk = bigbank()
                    for kc in range(8):
                        MMo(ps[bk][:, :n], wssd[:, kc, cc * 128:(cc + 1) * 128], h[:, kc, t0:t0 + n], r=['wssd', ('h', kc)], w=[('ps', bk)],
                            start=(kc == 0), stop=(kc == 7), inc=(kc == 7))
                    if not sample:
                        CPo('act', xr[:, cc, 3:3 + n], ps[bk][:, :n], [('ps', bk)], [('xraw', tt % 2)])
                        CPo('dve', xr[:, cc, 0:3], xraw[(tt + 1) % 2][:, cc, 512:515], [('xraw', (tt + 1) % 2)], [('xraw', tt % 2)])
                        if tt == 3:
                            CPo('dve', convo[:, cc, :], ps[bk][:, 509:512], [('ps', bk)], ['convo'])
                    else:
                        p3 = ps[bk][:, :64].rearrange("p (b l) -> p b l", b=16)
                        CPo('act', xraw_s[:, cc, :, 3:7], p3, [('ps', bk)], ['xraw_s'])
                        CPo('dve', convos[:, cc, :, :], p3[:, :, 1:4], [('ps', bk)], ['convos'])
                if tt == 3:
                    S.dma('sp', convp_d[:, g], convo, r=['convo'], w=[('convp', g)])
                if sample:
                    S.dma('sp', convs_d[:, g], convos.rearrange("p c b k -> p c (b k)"), r=['convos'], w=[('convs', g)])
                for cc in range(5):
                    bk = bigbank()
                    for k in range(4):
                        if not sample:
                            rhs = xr[:, cc, k:k + n]
                            rk = ('xraw', tt % 2)
                        else:
                            rhs = xraw_s[:, cc, :, k:k + 4]
                            rk = 'xraw_s'
                        MMo(ps[bk][:, :n], diag[:, cc, k, :], rhs, r=['diag', rk], w=[('ps', bk)], start=(k == 0), stop=(k == 3), inc=(k == 3))
                    wk = ('fmx', cc) if cc < 3 else ('fmB' if cc == 3 else 'fmC')
                    ACTo(fm[:, cc, :n], ps[bk][:, :n], AF.Silu, [('ps', bk), 'cwb'], [wk], bias=cwb[:, g, cc, 4:5], scale=1.0)
                if not sample:
                    def ch(ci, part):
                        return S.record(lambda: ssd_chunk(g, 128, t0 + ci * 128, ci * 128, False, (tt == 3 and ci == 3), ci % 2, part, ci))
                    for ci in range(4):
                        ssd_pre(g, 128, t0 + ci * 128, ci)
                    E = [ch(ci, 'early') for ci in range(4)]
                    Lt = [ch(ci, 'late') for ci in range(4)]
                    S.replay(E[0])
                    for ci in range(4):
                        S.replay(S.merge(E[ci + 1] if ci + 1 < 4 else [], Lt[ci]))
                else:
                    ssd_pre(g, 64, t0, 0)
                    ssd_chunk(g, 64, t0, 0, True, False, 0, 'early')
                    ssd_chunk(g, 64, t0, 0, True, False, 0, 'late')
                for dc in range(8):
                    bk = bigbank()
                    for j in range(3):
                        MMo(ps[bk][:, :n], wout[:, j, dc * 128:(dc + 1) * 128], ofm[:, j, :n], r=['wout', 'ofm'], w=[('ps', bk)],
                            start=(j == 0), stop=(j == 2), inc=(j == 2))
                    TTo('dve', x[:, dc, t0:t0 + n], x[:, dc, t0:t0 + n], ps[bk][:, :n], ALU.add, [('ps', bk), ('x', dc, tt)], [('x', dc, tt)])


    def s5_phase():
        S.barrier()
        cv = Carver()
        I32 = mybir.dt.int32
        PI = 3.14159265358979
        winu = cv.get([128, 8, 512], BF16)
        wouts5 = cv.get([128, 4, 1024], BF16)
        wglu = cv.get([128, 4, 512], BF16)
        lhsB = [cv.get([128, 4, 8, 64], BF16) for _ in range(2)]
        lhsC = [cv.get([128, 16, 128], BF16) for _ in range(2)]
        u_all = cv.get([128, 4, NT], BF16)
        S.dma('pool', winu, winu_d.rearrange("(kc p) f -> p kc f", p=128), w=['winu'])
        S.dma('pool', wouts5, wouts5_d.rearrange("(kc p) f -> p kc f", p=128), w=['wouts5'])
        S.dma('pool', wglu, wglu_d.rearrange("(kc p) f -> p kc f", p=128), w=['wglu'])
        for c in range(2):
            S.dma('pool', lhsC[c], cpad_d[c], w=[('lhsC', c)])
        TSo('dve', lhsC[1], lhsC[1], -1.0, ALU.mult, [('lhsC', 1)], [('lhsC', 1)])
        for tt, (t0, n) in enumerate(TT):
            for kt in range(4):
                bk = bigbank()
                for kc in range(8):
                    MMo(ps[bk][:, :n], winu[:, kc, kt * 128:(kt + 1) * 128], h[:, kc, t0:t0 + n], r=['winu'], w=[('ps', bk)],
                        start=(kc == 0), stop=(kc == 7), inc=(kc == 7))
                CPo('act', u_all[:, kt, t0:t0 + n], ps[bk][:, :n], [('ps', bk)], [('u', tt)])
        S.barrier()
        W = 272
        s5v = cv.get([128, 8], F32)
        S.dma('sp', s5v, s5v_d, w=['s5v'])
        emk = cv.get([128, 8], F32)
        S.dma('sp', emk, emask_d, w=['emk'])
        iot = cv.get([128, 256], F32)
        S.dma('sp', iot, cmat_d[:, 768:1024], w=['iot'])
        st0 = cv.get([128, 2, 16, 16], F32)
        S.dma('sp', st0, s5s0_d, w=['st0'])
        mag = cv.get([128, W], F32)
        carry = cv.get([128, 2, 16], F32)
        carry_s = cv.get([128, 2, 16, 16], F32)
        off_pre = cv.off
        lam = cv.get([128, 3, W], F32)
        S.dma('sp', lam, lam_d, w=['lam'])
        bT = cv.get([128, 2, 4, 64], F32)
        S.dma('sp', bT, bT_d, w=['bT'])
        pre = [cv.get([128, W], F32) for _ in range(11)]
        stp, xr, ang, sn, cs, ar, ai, t1, t2, cr, ci = pre
        ki = cv.get([128, 512], I32)
        kf = cv.get([128, 512], F32)
        rr = cv.get([128, 512], F32)

        def sin_of(out, a, n, shift, key):
            S.op('dve', lambda e: e.tensor_scalar(out=ki[:, :n], in0=a, scalar1=shift, scalar2=1.0 / (2 * PI), op0=ALU.add, op1=ALU.mult),
                 r=[key], w=['ki'])
            CPo('dve', kf[:, :n], ki[:, :n], ['ki'], ['kf'])
            STTo('dve', rr[:, :n], kf[:, :n], -2 * PI, a, ALU.mult, ALU.add, ['kf', key], ['rr'])
            S.op('dve', lambda e: e.tensor_scalar(out=rr[:, :n], in0=rr[:, :n], scalar1=shift, scalar2=-3.1415925, op0=ALU.add, op1=ALU.max),
                 r=['rr'], w=['rr'])
            TSo('dve', rr[:, :n], rr[:, :n], 3.1415925, ALU.min, ['rr'], ['rr'])
            ACTo(out, rr[:, :n], AF.Sin, ['rr'], [key + '_o'])

        ACTo(stp, lam[:, 2, :], AF.Exp, ['lam'], ['pre'])
        TTo('dve', xr, lam[:, 0, :], stp, ALU.mult, ['lam', 'pre'], ['pre'])
        TTo('dve', ang, lam[:, 1, :], stp, ALU.mult, ['lam', 'pre'], ['ang'])
        ACTo(mag, xr, AF.Exp, ['pre'], ['pre'])
        sin_of(sn, ang, W, 0.0, 'ang')
        sin_of(cs, ang, W, PI / 2, 'ang')
        P_ = ['pre', 'ang_o', 'lam']
        TTo('dve', ar, mag, cs, ALU.mult, P_, ['pre'])
        TTo('dve', ai, mag, sn, ALU.mult, P_, ['pre'])
        TTo('dve', t1, lam[:, 0, :], lam[:, 0, :], ALU.mult, P_, ['pre'])
        TTo('dve', t2, lam[:, 1, :], lam[:, 1, :], ALU.mult, P_, ['pre'])
        TTo('dve', t1, t1, t2, ALU.add, P_, ['pre'])
        S.op('dve', lambda e: e.reciprocal(out=t1, in_=t1), r=P_, w=['pre'])
        TSo('dve', t2, ar, -1.0, ALU.add, P_, ['pre'])
        TTo('dve', cr, t2, lam[:, 0, :], ALU.mult, P_, ['pre'])
        TTo('dve', ci, ai, lam[:, 1, :], ALU.mult, P_, ['pre'])
        TTo('dve', cr, cr, ci, ALU.add, P_, ['pre'])
        TTo('dve', cr, cr, t1, ALU.mult, P_, ['pre'])
        TTo('dve', ci, ai, lam[:, 0, :], ALU.mult, P_, ['pre'])
        TTo('dve', t2, t2, lam[:, 1, :], ALU.mult, P_, ['pre'])
        TTo('dve', ci, ci, t2, ALU.subtract, P_, ['pre'])
        TTo('dve', ci, ci, t1, ALU.mult, P_, ['pre'])
        bb = [cv.get([128, 4, 64], F32) for _ in range(2)]
        tb = cv.get([128, 4, 64], F32)
        cr3 = cr[:, 0:256].rearrange("p (a b) -> p a b", a=4)
        ci3 = ci[:, 0:256].rearrange("p (a b) -> p a b", a=4)
        TTo('dve', bb[0], cr3, bT[:, 0], ALU.mult, P_ + ['bT'], ['bb'])
        TTo('dve', tb, ci3, bT[:, 1], ALU.mult, P_ + ['bT'], ['tb'])
        TTo('dve', bb[0], bb[0], tb, ALU.subtract, ['bb', 'tb'], ['bb'])
        TTo('dve', bb[1], cr3, bT[:, 1], ALU.mult, P_ + ['bT'], ['bb'])
        TTo('dve', tb, ci3, bT[:, 0], ALU.mult, P_ + ['bT'], ['tb'])
        TTo('dve', bb[1], bb[1], tb, ALU.add, ['bb', 'tb'], ['bb'])
        for c in range(2):
            for kt in range(4):
                TTo('dve', lhsB[c][:, kt], bb[c][:, kt, None, :].to_broadcast([128, 8, 64]), emk[:, :, None].to_broadcast([128, 8, 64]),
                    ALU.mult, ['bb', 'emk'], [('lhsB', c)])
        tabf = h.rearrange("p a b -> p (a b)").bitcast(F32)
        Ec = tabf[:, 0:4096].rearrange("p (a b) -> p a b", a=16)
        Es = tabf[:, 4096:8192].rearrange("p (a b) -> p a b", a=16)
        angt = cv.get([128, 2, 256], F32)
        for i0 in range(0, 16, 2):
            TTo('dve', angt, ang[:, 256 + i0:258 + i0][:, :, None].to_broadcast([128, 2, 256]), iot[:, None, :].to_broadcast([128, 2, 256]),
                ALU.mult, ['ang', 'iot'], ['angt'])
            af = angt.rearrange("p a b -> p (a b)")
            sin_of(Es[:, i0:i0 + 2, :].rearrange("p a b -> p (a b)"), af, 512, 0.0, 'angt')
            sin_of(Ec[:, i0:i0 + 2, :].rearrange("p a b -> p (a b)"), af, 512, PI / 2, 'angt')
        TAB = ['angt_o']
        S.barrier()
        cv.off = off_pre
        S.op('pool', lambda e: e.memset(carry.rearrange("p a b -> p (a b)"), 0.0), w=['carry'])
        tmA = [cv.get([128, 2, 256], F32) for _ in range(4)]
        wreg = winu.rearrange("p a b -> p (a b)").bitcast(F32)
        tmB = [wreg[:, k * 512:(k + 1) * 512].rearrange("p (a b) -> p a b", a=2) for k in range(4)]
        tm2 = [tmA, tmB]
        wv2 = [[cv.get([128, 2, 256], F32) for _ in range(2)] for _ in range(2)]
        Wv2 = [[cv.get([128, 2, 256], F32) for _ in range(2)] for _ in range(2)]
        sv2 = [[cv.get([128, 2, 256], F32) for _ in range(2)] for _ in range(2)]
        rt_single = cv.get([128, 256], F32)
        rt2 = [rt_single, rt_single]
        hist2 = [cv.get([128, 2, 512], BF16) for _ in range(2)]
        y5 = cv.get([128, 512], F32)
        g1 = cv.get([128, 512], F32)
        v_bf = cv.get([128, 4, 512], BF16)
        gs = g1
        o5 = cv.get([128, 4, 512], BF16)
        print("s5 arena used", cv.off)
        magq = mag[:, 256:272]

        def bct(tab, i, nseg, L):
            return tab[:, i, None, 0:L].to_broadcast([128, nseg, L])

        for tt, (t0, n) in enumerate(TT):
            sample = (tt == 4)
            nseg, L = (16, 4) if sample else (2, 256)

            def v3(ap):
                return ap.rearrange("p a b -> p (a b)")[:, :nseg * L].rearrange("p (a b) -> p a b", a=nseg)
            for kt in range(4):
                cb = 4 if kt % 2 == 0 else 7
                kcb = 'ps%d' % cb
                for jp in (0, 2):
                    ctxs = []
                    for j in (jp, jp + 1):
                        i = 4 * kt + j
                        par = i % 2
                        pre_b, pim_b = (2, 3) if par == 0 else (5, 6)
                        ctxs.append(dict(i=i, j=j, par=par, tm=tm2[par], wv=wv2[par], Wv=Wv2[par], sv=sv2[par], hist=hist2[par],
                                         pre_b=pre_b, pim_b=pim_b, kre='ps%d' % pre_b, kim='ps%d' % pim_b))
                    for cx in ctxs:
                        i, j = cx['i'], cx['j']
                        lB = [lhsB[c].rearrange("p a b c -> p a (b c)")[:, kt, j * 128:(j + 1) * 128] for c in range(2)]
                        MMo(ps[cx['pre_b']][:, :n], lB[0], u_all[:, kt, t0:t0 + n], r=[('lhsB', 0), ('u', tt)], w=[cx['kre']])
                        MMo(ps[cx['pim_b']][:, :n], lB[1], u_all[:, kt, t0:t0 + n], r=[('lhsB', 1), ('u', tt)], w=[cx['kim']])
                    for cx in ctxs:
                        i, par, tm = cx['i'], cx['par'], cx['tm']
                        Pre = ps[cx['pre_b']][:, :n].rearrange("p (a b) -> p a b", a=nseg)
                        Pim = ps[cx['pim_b']][:, :n].rearrange("p (a b) -> p a b", a=nseg)
                        ec, es = bct(Ec, i, nseg, L), bct(Es, i, nseg, L)
                        TTo('dve', v3(tm[0]), Pre, ec, ALU.mult, [cx['kre']] + TAB, [('tm', par, 0)])
                        TTo('dve', v3(tm[1]), Pim, es, ALU.mult, [cx['kim']] + TAB, [('tm', par, 1)])
                        TTo('dve', v3(tm[2]), Pim, ec, ALU.mult, [cx['kim']] + TAB, [('tm', par, 2)])
                        TTo('dve', v3(tm[3]), Pre, es, ALU.mult, [cx['kre']] + TAB, [('tm', par, 3)])
                    for cx in ctxs:
                        par, tm, wv = cx['par'], cx['tm'], cx['wv']
                        TTo('pool', v3(wv[0]), v3(tm[0]), v3(tm[1]), ALU.add, [('tm', par, 0), ('tm', par, 1)], [('wv', par, 0)])
                        TTo('pool', v3(wv[1]), v3(tm[2]), v3(tm[3]), ALU.subtract, [('tm', par, 2), ('tm', par, 3)], [('wv', par, 1)])
                    groups = [list(range(16))] if sample else [[0], [1]]
                    for grp in groups:
                        g0, g1_ = grp[0], grp[-1] + 1
                        for cx in ctxs:
                            i, par, wv, Wv, sv = cx['i'], cx['par'], cx['wv'], cx['Wv'], cx['sv']
                            rbc = magq[:, i:i + 1].to_broadcast([128, L])
                            for sg_ in grp:
                                for c in range(2):
                                    if sample:
                                        init = st0[:, c, i, sg_:sg_ + 1]
                                        ik = 'st0'
                                    elif sg_ == 0:
                                        init = carry[:, c, i:i + 1]
                                        ik = 'carry'
                                    else:
                                        init = sv[c][:, 0, 255:256]
                                        ik = ('sv', par, c)
                                    S.op('dve', lambda e: e.tensor_tensor_scan(out=v3(Wv[c])[:, sg_, :], data0=rbc, data1=v3(wv[c])[:, sg_, :],
                                                                               initial=init, op0=ALU.mult, op1=ALU.add),
                                         r=['pre', ('wv', par, c), ik], w=[('Wv', par, c)])
                        for cx in ctxs:
                            i, par, tm, Wv = cx['i'], cx['par'], cx['tm'], cx['Wv']
                            ecg = Ec[:, i, None, 0:L].to_broadcast([128, g1_ - g0, L])
                            esg = Es[:, i, None, 0:L].to_broadcast([128, g1_ - g0, L])
                            TTo('dve', v3(tm[0])[:, g0:g1_], v3(Wv[0])[:, g0:g1_], ecg, ALU.mult, [('Wv', par, 0)] + TAB, [('tm', par, 0)])
                            TTo('dve', v3(tm[1])[:, g0:g1_], v3(Wv[1])[:, g0:g1_], esg, ALU.mult, [('Wv', par, 1)] + TAB, [('tm', par, 1)])
                        for cx in ctxs:
                            i, par, tm, Wv, sv = cx['i'], cx['par'], cx['tm'], cx['Wv'], cx['sv']
                            ecg = Ec[:, i, None, 0:L].to_broadcast([128, g1_ - g0, L])
                            esg = Es[:, i, None, 0:L].to_broadcast([128, g1_ - g0, L])
                            TTo('pool', v3(tm[2])[:, g0:g1_], v3(Wv[1])[:, g0:g1_], ecg, ALU.mult, [('Wv', par, 1)] + TAB, [('tm', par, 2)])
                            TTo('pool', v3(tm[3])[:, g0:g1_], v3(Wv[0])[:, g0:g1_], esg, ALU.mult, [('Wv', par, 0)] + TAB, [('tm', par, 3)])
                            TTo('pool', v3(sv[0])[:, g0:g1_], v3(tm[0])[:, g0:g1_], v3(tm[1])[:, g0:g1_], ALU.subtract,
                                [('tm', par, 0), ('tm', par, 1)], [('sv', par, 0)])
                            TTo('pool', v3(sv[1])[:, g0:g1_], v3(tm[2])[:, g0:g1_], v3(tm[3])[:, g0:g1_], ALU.add,
                                [('tm', par, 2), ('tm', par, 3)], [('sv', par, 1)])
                    for cx in ctxs:
                        i, j, par, sv, hist = cx['i'], cx['j'], cx['par'], cx['sv'], cx['hist']
                        for c in range(2):
                            svf = sv[c].rearrange("p a b -> p (a b)")
                            CPo('act', hist[:, c, :n], svf[:, :n], [('sv', par, c)], [('hist', par, c)])
                            if sample:
                                CPo('act', carry_s[:, c, i, :], v3(sv[c])[:, :, 3], [('sv', par, c)], ['carry_s'])
                            else:
                                CPo('act', carry[:, c, i:i + 1], svf[:, 511:512], [('sv', par, c)], ['carry'])
                            MMo(ps[cb][:, :n], lhsC[c][:, i, :], hist[:, c, :n], r=[('lhsC', c), ('hist', par, c)], w=[kcb],
                                start=(j == 0 and c == 0), stop=(j == 3 and c == 1), inc=True)
                STTo('dve', y5[:, :n], u_all[:, kt, t0:t0 + n], s5v[:, kt:kt + 1], ps[cb][:, :n], ALU.mult, ALU.add, [('u', tt), 's5v', kcb], ['y5'])
                TTo('pool', g1[:, :n], y5[:, :n], y5[:, :n], ALU.mult, ['y5'], ['g1'])
                S.op('dve', lambda e: e.tensor_scalar(out=g1[:, :n], in0=g1[:, :n], scalar1=0.044715, scalar2=1.0, op0=ALU.mult, op1=ALU.add),
                     r=['g1'], w=['g1'])
                TTo('pool', g1[:, :n], g1[:, :n], y5[:, :n], ALU.mult, ['g1', 'y5'], ['g1'])
                ACTo(g1[:, :n], g1[:, :n], AF.Sigmoid, ['g1'], ['g1'], scale=1.5957691216057308)
                TTo('dve', v_bf[:, kt, :n], g1[:, :n], y5[:, :n], ALU.mult, ['g1', 'y5'], [('v', kt)])
            for mo in range(4):
                bk = bigbank()
                for kt in range(4):
                    MMo(ps[bk][:, :n], wglu[:, kt, mo * 128:(mo + 1) * 128], v_bf[:, kt, :n], r=['wglu', ('v', kt)], w=[('ps', bk)],
                        start=(kt == 0), stop=(kt == 3), inc=(kt == 3))
                ACTo(gs[:, :n], ps[bk][:, :n], AF.Sigmoid, [('ps', bk), 's5v'], ['g1'], bias=s5v[:, 4 + mo:5 + mo], scale=1.0)
                TTo('dve', o5[:, mo, :n], v_bf[:, mo, :n], gs[:, :n], ALU.mult, [('v', mo), 'g1'], [('o5', mo)])
            for dc in range(8):
                bk = bigbank()
                for mo in range(4):
                    MMo(ps[bk][:, :n], wouts5[:, mo, dc * 128:(dc + 1) * 128], o5[:, mo, :n], r=['wouts5', ('o5', mo)], w=[('ps', bk)],
                        start=(mo == 0), stop=(mo == 3), inc=(mo == 3))
                TTo('dve', x[:, dc, t0:t0 + n], x[:, dc, t0:t0 + n], ps[bk][:, :n], ALU.add, [('ps', bk), ('x', dc, tt)], [('x', dc, tt)])
        S.dma('sp', s5p_d, carry, r=['carry'], w=['s5p'])
        S.dma('sp', s5s_d, carry_s, r=['carry_s'], w=['s5s'])


    PH = os.environ.get('KPH', 'n1,f1,n2,ssd,s5,n3,f2').split(',')
    if 'n1' in PH:
        norm_to_h(0)
    if 'f1' in PH:
        ffn(0, after_tile=(lambda tt: norm_to_h(1, tiles=[tt])) if 'n2' in PH else None)
    elif 'n2' in PH:
        norm_to_h(1)
    if 'ssd' in PH:
        ssd_phase()
    if 's5' in PH:
        s5_phase()
    S.barrier()
    if 'n3' in PH:
        norm_to_h(2)
    yT_v = yT.rearrange("(c p) t -> p c t", p=128)
    ycount = [0]

    def final_out(tt, c, t0, n, rb):
        yb = ycount[0] % 2
        ycount[0] += 1
        S.op('dve', lambda e: e.scalar_tensor_tensor(out=ystage[yb][:, :n], in0=x[:, c, t0:t0 + n],
                                                     scalar=normw[:, 24 + c:24 + c + 1], in1=rb[:, :n],
                                                     op0=ALU.mult, op1=ALU.mult),
             r=[('x', c, tt), ('rstd', tt % 2), 'normw'], w=[('ystage', yb)])
        S.dma('sp', yT_v[:, c, t0:t0 + n], ystage[yb][:, :n], r=[('ystage', yb)], w=[('yT', c, tt)])
    if 'f2' in PH:
        ffn(1, after_tile=lambda tt: rmsnorm(3, final_out, tiles=[tt]))
    else:
        rmsnorm(3, final_out)
    S.finish()
    print("instructions", S.nins, "waits", S.nwait)
    return nc


_NC_CACHE = {}


def _consts():
    f32 = np.float32
    cm = np.zeros((128, 2048), f32)
    k = np.arange(128)
    cm[:, 0:128] = (k[:, None] <= k[None, :])
    cm[:, 128:256] = (k[:, None] > k[None, :])
    same = (k[:, None] // 4 == k[None, :] // 4) & (k[:, None] < 64) & (k[None, :] < 64)
    cm[:, 256:384] = (k[:, None] <= k[None, :]) & same
    cm[:, 384:512] = (k[:, None] > k[None, :]) & same
    cm[:, 512:640] = np.eye(128)
    cm[:64, 640:656] = (np.arange(64)[:, None] // 4 == np.arange(16)[None, :])
    cm[:, 768:1024] = np.arange(1, 257)[None, :]
    et = (np.arange(64)[None, :] // 4 == np.arange(16)[:, None]).astype(f32)
    cm[:, 1024:2048] = et.reshape(1, 1024)
    return cm


def _chan(g, cc):
    if cc < 3:
        return 384 * g + 128 * cc
    if cc == 3:
        return 1536 + 128 * g
    return 2048 + 128 * g


def _s5_host(inp):
    f32 = np.float32
    A = lambda k: np.asarray(inp[k], f32)
    w_in = A("w_in")[0]
    w_out = A("w_out")[0]
    s5v = np.empty((128, 8), f32)
    s5v[:, 0:4] = A("s5_d")[0].reshape(4, 128).T
    s5v[:, 4:8] = A("s5_b_glu")[0].reshape(4, 128).T
    lre, lim, lst = A("s5_lambda_re")[0], A("s5_lambda_im")[0], A("s5_log_step")[0]
    lam = np.empty((128, 3, 272), f32)
    for arr_i, arr in enumerate((lre, lim)):
        rep = arr.reshape(4, 8, 64)
        rep = np.repeat(rep.transpose(1, 0, 2)[:, None], 16, axis=1)
        lam[:, arr_i, 0:256] = rep.reshape(128, 256)
        qq = arr.reshape(16, 2, 64).transpose(1, 2, 0).reshape(128, 16)
        lam[:, arr_i, 256:272] = qq
    rep = np.repeat(lst.reshape(4, 8).T[:, None, :, None], 16, axis=1)
    lam[:, 2, 0:256] = np.broadcast_to(rep, (8, 16, 4, 64)).reshape(128, 256)
    qq = np.broadcast_to(lst.reshape(16, 2).T[:, None, :], (2, 64, 16)).reshape(128, 16)
    lam[:, 2, 256:272] = qq
    bT = np.empty((128, 2, 4, 64), f32)
    for ci, k in enumerate(("s5_b_re", "s5_b_im")):
        b = A(k)[0].reshape(4, 8, 64, 16)
        bT[:, ci] = b.transpose(1, 3, 0, 2).reshape(128, 4, 64)
    emask = (np.arange(128)[:, None] // 16 == np.arange(8)[None, :]).astype(f32)
    cpad = np.zeros((2, 128, 16, 128), f32)
    for ci, k in enumerate(("s5_c_re", "s5_c_im")):
        cc = A(k)[0]
        for g in range(32):
            i, q0, gl = g // 2, (g % 2) * 64, g % 8
            cpad[ci, q0:q0 + 64, i, gl * 16:(gl + 1) * 16] = cc[g].T
    return {"w_in_u": np.ascontiguousarray(w_in[:, 0:512]), "w_out_s5": np.ascontiguousarray(w_out[0:512]),
            "w_glu": np.ascontiguousarray(A("s5_w_glu")[0]), "s5v": s5v, "lam": lam, "bT": bT, "emask": emask, "cpad": cpad}


def _s5_core(inp, c):
    f32 = np.float32
    out = np.empty((128, 2, 16, 16), f32)
    for ci, k in enumerate(("state_s5_re", "state_s5_im")):
        s = np.asarray(inp[k], f32)[0, 16 * c:16 * c + 16]
        out[:, ci] = s.reshape(16, 16, 2, 64).transpose(2, 3, 1, 0).reshape(128, 16, 16)
    return {"s5s0": out}


def kernel(**inp):
    f32 = np.float32
    A = lambda k: np.asarray(inp[k], f32)
    xp = A("x_prompt")
    xs = A("x_sample")
    normw = np.stack([A(k).reshape(D) for k in ("ffn1_norm", "mix_norm", "ffn2_norm", "final_norm")])
    normw_l = np.ascontiguousarray(normw.reshape(4, 8, 128).transpose(2, 0, 1).reshape(128, 32))
    w_in = A("w_in")[0]
    w_out = A("w_out")[0]
    w_ssd = np.empty((4, D, 1030), f32)
    w_out_ssd = np.empty((4, 384, D), f32)
    for g in range(4):
        w_ssd[g, :, 0:384] = w_in[:, 2048 + 384 * g:2048 + 384 * (g + 1)]
        w_ssd[g, :, 384:512] = w_in[:, 3584 + 128 * g:3584 + 128 * (g + 1)]
        w_ssd[g, :, 512:640] = w_in[:, 4096 + 128 * g:4096 + 128 * (g + 1)]
        w_ssd[g, :, 640:1024] = w_in[:, 512 + 384 * g:512 + 384 * (g + 1)]
        w_ssd[g, :, 1024:1030] = w_in[:, 4608 + 6 * g:4608 + 6 * (g + 1)]
        w_out_ssd[g] = w_out[512 + 384 * g:512 + 384 * (g + 1)]
    conv_w = A("ssd_conv_w")[0]
    conv_b = A("ssd_conv_b")[0]
    cwb = np.empty((128, 4, 5, 5), f32)
    for g in range(4):
        for cc in range(5):
            c0 = _chan(g, cc)
            cwb[:, g, cc, 0:4] = conv_w[:, c0:c0 + 128].T
            cwb[:, g, cc, 4] = conv_b[c0:c0 + 128]
    hp = np.concatenate([A("ssd_dt_bias")[0], A("ssd_a_log")[0], A("ssd_d")[0]])[None, :].repeat(128, 0)
    nbc = A("ssd_norm")[0][None, :].repeat(128, 0)
    sconv = A("state_conv")[0]
    sssd = A("state_ssd")[0]
    shared = {
        "normw": normw_l,
        "ffn1_wg": np.ascontiguousarray(A("ffn1_w_gate")[0]), "ffn1_wu": np.ascontiguousarray(A("ffn1_w_up")[0]),
        "ffn1_wd": np.ascontiguousarray(A("ffn1_w_down")[0]),
        "ffn2_wg": np.ascontiguousarray(A("ffn2_w_gate")[0]), "ffn2_wu": np.ascontiguousarray(A("ffn2_w_up")[0]),
        "ffn2_wd": np.ascontiguousarray(A("ffn2_w_down")[0]),
        "cmat": _consts(), "hp": np.ascontiguousarray(hp), "nbc": np.ascontiguousarray(nbc),
        "w_ssd": w_ssd, "w_out_ssd": w_out_ssd, "cwb": cwb,
    }
    shared.update(_s5_host(inp))
    in_maps = []
    for c in range(NCORES):
        bs = slice(16 * c, 16 * c + 16)
        xc = np.concatenate([xp[c], xs[bs].reshape(TS, D)], axis=0)
        m = dict(shared)
        m["xT"] = np.ascontiguousarray(xc.T)
        cv0 = np.empty((128, 4, 5, 16, 3), f32)
        for g in range(4):
            for cc in range(5):
                c0 = _chan(g, cc)
                cv0[:, g, cc] = sconv[bs, :, c0:c0 + 128].transpose(2, 0, 1)
        m["conv0"] = cv0.reshape(128, 4, 5, 48)
        st = sssd[bs].reshape(16, 4, 6, 64, 128).transpose(0, 1, 4, 2, 3).reshape(16, 4, 128, 384)
        m["ssd0"] = np.ascontiguousarray(st)
        m.update(_s5_core(inp, c))
        in_maps.append(m)
    if "nc" not in _NC_CACHE:
        _NC_CACHE["nc"] = build()
    nc = _NC_CACHE["nc"]
    res = run_bass_kernel_spmd(nc, in_maps, core_ids=list(range(NCORES)))
    R = res.results
    yp = np.empty((8, TP, D), f32)
    ys = np.empty((128, 4, D), f32)
    ssd_p = np.empty((1, 8, 24, 64, 128), f32)
    ssd_s = np.empty((1, 128, 24, 64, 128), f32)
    conv_p = np.empty((1, 8, 3, 2560), f32)
    conv_s = np.empty((1, 128, 3, 2560), f32)
    s5p = np.empty((2, 1, 8, 32, 64), f32)
    s5s = np.empty((2, 1, 128, 32, 64), f32)
    for c in range(NCORES):
        bs = slice(16 * c, 16 * c + 16)
        y = R[c]["yT"].T
        yp[c] = y[:TP]
        ys[bs] = y[TP:].reshape(16, 4, D)
        ssd_p[0, c] = R[c]["ssdp"].reshape(4, 128, 6, 64).transpose(0, 2, 3, 1).reshape(24, 64, 128)
        ssd_s[0, bs] = R[c]["ssds"].reshape(16, 4, 128, 6, 64).transpose(0, 1, 3, 4, 2).reshape(16, 24, 64, 128)
        cp = R[c]["convp"]
        cs = R[c]["convs"].reshape(128, 4, 5, 16, 3)
        for g in range(4):
            for cc in range(5):
                c0 = _chan(g, cc)
                conv_p[0, c, :, c0:c0 + 128] = cp[:, g, cc, :].T
                conv_s[0, bs, :, c0:c0 + 128] = cs[:, g, cc].transpose(1, 2, 0)
        a = R[c]["s5p"]
        s5p[:, 0, c] = a.reshape(2, 64, 2, 16).transpose(2, 3, 0, 1).reshape(2, 32, 64)
        b_ = R[c]["s5s"]
        s5s[:, 0, bs] = b_.reshape(2, 64, 2, 16, 16).transpose(2, 4, 3, 0, 1).reshape(2, 16, 32, 64)
    return (yp, ys, s5p[0], s5p[1], ssd_p, conv_p, s5s[0], s5s[1], ssd_s, conv_s)
```

```python
import os
import numpy as np
import concourse.bass as bass
import concourse.mybir as mybir
from concourse.bass_utils import run_bass_kernel_spmd

F32 = mybir.dt.float32
BF16 = mybir.dt.bfloat16
AF = mybir.ActivationFunctionType
ALU = mybir.AluOpType

NCORES = 8
D = 1024
DFF = 2816
TP = 2048
TS = 64
NT = TP + TS
EPS = 1e-6
NDMA = 24
NSW = 64


class Sched:
    def __init__(self, nc):
        self.nc = nc
        self.e = {'pe': nc.tensor, 'act': nc.scalar, 'dve': nc.vector, 'pool': nc.gpsimd, 'sp': nc.sync}
        self.sem = {k: nc.alloc_semaphore('s_' + k) for k in self.e}
        self.cnt = {k: 0 for k in self.e}
        self.waited = {}
        self.lastw = {}
        self.readers = {}
        self.dsem = [nc.alloc_semaphore('d%d' % i) for i in range(NDMA + NSW)]
        self.duse = [0] * (NDMA + NSW)
        self.dnext = 0
        self.swnext = NDMA
        self.nwait = 0
        self.nins = 0
        self._rec = None

    def _semobj(self, sk):
        return self.sem[sk] if isinstance(sk, str) else self.dsem[sk[1]]

    def _deps(self, r, w):
        deps = {}

        def add(tok):
            sk, v = tok
            if deps.get(sk, 0) < v:
                deps[sk] = v
        for k in r:
            if k in self.lastw:
                add(self.lastw[k])
        for k in w:
            if k in self.lastw:
                add(self.lastw[k])
            for sk, v in self.readers.get(k, {}).items():
                add((sk, v))
        return deps

    def _wait(self, eng, deps, attach=False):
        need = []
        for sk, v in deps.items():
            if sk == eng:
                if eng == 'pe':
                    continue
                assert v <= self.cnt[eng], "same-engine dep on non-inc instruction"
            if self.waited.get((eng, sk), 0) >= v:
                continue
            need.append((sk, v))
            self.waited[(eng, sk)] = v
        pend = None
        if attach and need and eng != 'pe':
            pend = need.pop()
        for sk, v in need:
            self.e[eng].wait_ge(self._semobj(sk), v)
            self.nwait += 1
        return pend

    def _record(self, tok, r, w):
        sk, v = tok
        for k in r:
            d = self.readers.setdefault(k, {})
            if d.get(sk, 0) < v:
                d[sk] = v
        for k in w:
            self.lastw[k] = tok
            self.readers[k] = {}

    @staticmethod
    def _split(r, w):
        def isps(k):
            return (isinstance(k, tuple) and k[0] == 'ps') or (isinstance(k, str) and k.startswith('ps'))
        r2 = [k for k in r if not isps(k)]
        w2 = list(w) + [k for k in r if isps(k)]
        return r2, w2

    def record(self, f):
        self._rec = []
        f()
        out, self._rec = self._rec, None
        return out

    def replay(self, items):
        for it in items:
            if it[0] == 'op':
                self.op(*it[1:])
            else:
                self.dma(*it[1:-1], **it[-1])

    @staticmethod
    def merge(a, b):
        out = []
        ia = ib = 0
        na, nb = len(a), len(b)
        while ia < na or ib < nb:
            if ib >= nb or (ia < na and ia * nb <= ib * na):
                out.append(a[ia])
                ia += 1
            else:
                out.append(b[ib])
                ib += 1
        return out

    def op(self, eng, fn, r=(), w=(), inc=True):
        if self._rec is not None:
            self._rec.append(('op', eng, fn, tuple(r), tuple(w), inc))
            return
        r, w = self._split(r, w)
        pend = self._wait(eng, self._deps(r, w), attach=True)
        ins = fn(self.e[eng])
        if pend is not None:
            ins._wait_ge(self._semobj(pend[0]), pend[1])
        self.nins += 1
        if inc:
            self.cnt[eng] += 1
            ins.then_inc(self.sem[eng], 1)
            tok = (eng, self.cnt[eng])
        else:
            tok = (eng, self.cnt[eng] + 1)
        self._record(tok, r, w)

    def dma(self, q, out, in_, r=(), w=(), **kw):
        if self._rec is not None:
            self._rec.append(('dma', q, out, in_, tuple(r), tuple(w), kw))
            return
        if q == 'pool':
            i = self.swnext
            self.swnext += 1
            assert i < NDMA + NSW, "out of SW-DMA semaphores"
        else:
            i = self.dnext
            self.dnext = (self.dnext + 1) % NDMA
        deps = self._deps(r, w)
        if self.duse[i] > 0:
            sk = ('d', i)
            deps[sk] = max(deps.get(sk, 0), self.duse[i] * 16)
        pend = self._wait(q, deps, attach=(q == 'sp'))
        ins = self.e[q].dma_start(out=out, in_=in_, **kw)
        if pend is not None:
            ins._wait_ge(self._semobj(pend[0]), pend[1])
        self.duse[i] += 1
        ins.then_inc(self.dsem[i], 16)
        self.nins += 1
        self._record((('d', i), self.duse[i] * 16), r, w)

    def barrier(self):
        toks = {k: self.cnt[k] for k in ('pe', 'act', 'dve', 'pool') if self.cnt[k] > 0}
        dt = {('d', i): self.duse[i] * 16 for i in range(NDMA + NSW) if self.duse[i] > 0}
        for eng in ('pe', 'act', 'dve', 'pool', 'sp'):
            deps = {k: v for k, v in toks.items() if k != eng}
            deps.update(dt)
            self._wait(eng, deps)

    def finish(self):
        for i in range(NDMA + NSW):
            if self.duse[i] > 0:
                self.e['sp'].wait_ge(self.dsem[i], self.duse[i] * 16)
        for k in ('pe', 'act', 'dve', 'pool'):
            if self.cnt[k] > 0:
                self.e['sp'].wait_ge(self.sem[k], self.cnt[k])


def token_tiles():
    tiles = [(i * 512, 512) for i in range(TP // 512)]
    tiles.append((TP, TS))
    return tiles


def build(stage=1):
    nc = bass.Bass("TRN2", target_bir_lowering=False)
    S = Sched(nc)

    def din(name, shape, dt=F32):
        return nc.dram_tensor(name, list(shape), dt, kind="ExternalInput").ap()

    def dout(name, shape, dt=F32):
        return nc.dram_tensor(name, list(shape), dt, kind="ExternalOutput").ap()

    def sb(name, shape, dt):
        return nc.alloc_sbuf_tensor(name, list(shape), dt).ap()

    xT = din("xT", [D, NT])
    normw_d = din("normw", [128, 32])
    wg_d = [din("ffn%d_wg" % i, [D, DFF]) for i in (1, 2)]
    wu_d = [din("ffn%d_wu" % i, [D, DFF]) for i in (1, 2)]
    wd_d = [din("ffn%d_wd" % i, [DFF, D]) for i in (1, 2)]
    yT = dout("yT", [D, NT])
    cmat_d = din("cmat", [128, 2048])
    hp_d = din("hp", [128, 72])
    nbc_d = din("nbc", [128, 1536])
    wssd_d = din("w_ssd", [4, D, 1030])
    woutssd_d = din("w_out_ssd", [4, 384, D])
    cwb_d = din("cwb", [128, 4, 5, 5])
    conv0_d = din("conv0", [128, 4, 5, 48])
    ssd0_d = din("ssd0", [16, 4, 128, 384])
    convp_d = dout("convp", [128, 4, 5, 3])
    convs_d = dout("convs", [128, 4, 5, 48])
    ssdp_d = dout("ssdp", [4, 128, 384])
    ssds_d = dout("ssds", [16, 4, 128, 384])
    winu_d = din("w_in_u", [D, 512])
    wouts5_d = din("w_out_s5", [512, D])
    wglu_d = din("w_glu", [512, 512])
    s5v_d = din("s5v", [128, 72])
    lam_d = din("lam", [128, 3, 272])
    bT_d = din("bT", [128, 2, 4, 64])
    emask_d = din("emask", [128, 8])
    cpad_d = din("cpad", [2, 128, 16, 128])
    s5s0_d = din("s5s0", [128, 2, 16, 16])
    s5p_d = dout("s5p", [128, 2, 16])
    s5s_d = dout("s5s", [128, 2, 16, 16])

    x = sb("x", [128, 8, NT], F32)
    h = sb("h", [128, 8, NT], BF16)
    normw = sb("normw_sb", [128, 32], F32)
    ones_bf = sb("ones_bf", [128, 128], BF16)
    epsc = sb("epsc", [128, 1], F32)
    ps = [nc.alloc_psum_tensor("ps%d" % i, [128, 512], F32).ap() for i in range(8)]

    ARENA = 111040
    arena = sb("arena", [128, ARENA // 2], BF16)

    class Carver:
        def __init__(self):
            self.off = 0

        def get(self, shape, dt):
            n = 1
            for s_ in shape[1:]:
                n *= s_
            nb = n * (2 if dt == BF16 else 4)
            nb = (nb + 31) // 32 * 32
            assert self.off + nb <= ARENA, ("arena overflow", self.off + nb)
            ap = arena[:shape[0], self.off // 2:(self.off + nb) // 2]
            self.off += nb
            if dt != BF16:
                ap = ap.bitcast(dt)
            ap = ap[:, :n]
            if len(shape) == 3:
                ap = ap.rearrange("p (a b) -> p a b", a=shape[1])
            elif len(shape) == 4:
                ap = ap.rearrange("p (a b c) -> p a b c", a=shape[1], b=shape[2])
            return ap

    GMAX = 6
    cv = Carver()
    wg_s = [cv.get([128, 8, GMAX * 128], BF16) for i in range(2)]
    wu_s = [cv.get([128, 8, GMAX * 128], BF16) for i in range(2)]
    wd_s = [cv.get([128, GMAX, D], BF16) for i in range(2)]
    sq = cv.get([128, 8, 512], BF16)
    rstd = [cv.get([128, 512], F32) for i in range(2)]
    sg = [cv.get([128, 512], F32) for i in range(2)]
    actb = [cv.get([128, GMAX, 512], BF16) for i in range(2)]
    ystage = [cv.get([128, 512], F32) for i in range(2)]

    S.op('pool', lambda e: e.memset(ones_bf, 1.0), w=['ones_bf'])
    S.op('pool', lambda e: e.memset(epsc, EPS), w=['epsc'])
    S.dma('sp', normw, normw_d, w=['normw'])
    xT_v = xT.rearrange("(c p) t -> p c t", p=128)
    for c in range(8):
        S.dma('sp', x[:, c, :], xT_v[:, c, :], w=[('x', c, tt) for tt in range(5)])

    TT = token_tiles()

    def rmsnorm(widx, out_fn, tiles=None):
        for tt, (t0, n) in enumerate(TT):
            if tiles is not None and tt not in tiles:
                continue
            for c in range(8):
                S.op('act', lambda e, c=c: e.activation(out=sq[:, c, :n], in_=x[:, c, t0:t0 + n], func=AF.Square),
                     r=[('x', c, tt)], w=[('sq', c)])
            for c in range(8):
                S.op('pe', lambda e, c=c: e.matmul(ps[0][:, :n], ones_bf, sq[:, c, :n], start=(c == 0), stop=(c == 7)),
                     r=[('sq', c), 'ones_bf'], w=[('ps', 0)], inc=(c == 7))
            rb = rstd[tt % 2]
            S.op('act', lambda e: e.activation(out=rb[:, :n], in_=ps[0][:, :n], func=AF.Sqrt, bias=epsc[:, 0:1], scale=1.0 / D),
                 r=[('ps', 0), 'epsc'], w=[('rstd', tt % 2)])
            S.op('dve', lambda e: e.reciprocal(out=rb[:, :n], in_=rb[:, :n]), r=[('rstd', tt % 2)], w=[('rstd', tt % 2)])
            for c in range(8):
                out_fn(tt, c, t0, n, rb)

    def norm_to_h(widx, tiles=None):
        def f(tt, c, t0, n, rb):
            S.op('dve', lambda e: e.scalar_tensor_tensor(out=h[:, c, t0:t0 + n], in0=x[:, c, t0:t0 + n],
                                                         scalar=normw[:, widx * 8 + c:widx * 8 + c + 1], in1=rb[:, :n],
                                                         op0=ALU.mult, op1=ALU.mult),
                 r=[('x', c, tt), ('rstd', tt % 2), 'normw'], w=[('h', c, tt)])
        rmsnorm(widx, f, tiles)

    def ffn(fi, after_tile=None):
        Wg = wg_d[fi].rearrange("(kc p) f -> p kc f", p=128)
        Wu = wu_d[fi].rearrange("(kc p) f -> p kc f", p=128)
        Wd = wd_d[fi].rearrange("(fc p) d -> p fc d", p=128)
        groups = [(0, 6), (6, 6), (12, 6), (18, 4)]

        def load(gi):
            c0, G = groups[gi]
            b = gi % 2
            for kc in range(0, 8, 4):
                S.dma('pool', wg_s[b][:, kc:kc + 4, :G * 128], Wg[:, kc:kc + 4, c0 * 128:(c0 + G) * 128], w=[('wg', b, kc)])
                S.dma('pool', wu_s[b][:, kc:kc + 4, :G * 128], Wu[:, kc:kc + 4, c0 * 128:(c0 + G) * 128], w=[('wu', b, kc)])
            S.dma('pool', wd_s[b][:, :G, :], Wd[:, c0:c0 + G, :], w=[('wd', b)])
        load(0)
        pcount = 0
        ocount = 0
        for gi, (c0, G) in enumerate(groups):
            b = gi % 2
            if gi + 1 < len(groups):
                load(gi + 1)
            for tt, (t0, n) in enumerate(TT):
                ab = (gi * len(TT) + tt) % 2
                for fc in range(G):
                    pgi = pcount % 2
                    pui = 2 + pcount % 2
                    pcount += 1
                    for kc in range(8):
                        S.op('pe', lambda e, kc=kc: e.matmul(ps[pgi][:, :n], wg_s[b][:, kc, fc * 128:(fc + 1) * 128], h[:, kc, t0:t0 + n],
                                                             start=(kc == 0), stop=(kc == 7)),
                             r=[('wg', b, kc // 4 * 4), ('h', kc, tt)], w=[('ps', pgi)], inc=(kc == 7))
                    for kc in range(8):
                        S.op('pe', lambda e, kc=kc: e.matmul(ps[pui][:, :n], wu_s[b][:, kc, fc * 128:(fc + 1) * 128], h[:, kc, t0:t0 + n],
                                                             start=(kc == 0), stop=(kc == 7)),
                             r=[('wu', b, kc // 4 * 4), ('h', kc, tt)], w=[('ps', pui)], inc=(kc == 7))
                    sgb = sg[pcount % 2]
                    S.op('act', lambda e: e.activation(out=sgb[:, :n], in_=ps[pgi][:, :n], func=AF.Silu),
                         r=[('ps', pgi)], w=[('sg', pcount % 2)])
                    S.op('dve', lambda e: e.tensor_tensor(out=actb[ab][:, fc, :n], in0=ps[pui][:, :n], in1=sgb[:, :n], op=ALU.mult),
                         r=[('ps', pui), ('sg', pcount % 2)], w=[('actb', ab, fc)])
                for dc in range(8):
                    poi = 4 + ocount % 4
                    ocount += 1
                    for fc in range(G):
                        S.op('pe', lambda e, fc=fc: e.matmul(ps[poi][:, :n], wd_s[b][:, fc, dc * 128:(dc + 1) * 128], actb[ab][:, fc, :n],
                                                             start=(fc == 0), stop=(fc == G - 1)),
                             r=[('wd', b), ('actb', ab, fc)], w=[('ps', poi)], inc=(fc == G - 1))
                    S.op('dve', lambda e: e.scalar_tensor_tensor(out=x[:, dc, t0:t0 + n], in0=ps[poi][:, :n], scalar=0.5,
                                                                 in1=x[:, dc, t0:t0 + n], op0=ALU.mult, op1=ALU.add),
                         r=[('ps', poi), ('x', dc, tt)], w=[('x', dc, tt)])
                if after_tile is not None and gi == len(groups) - 1:
                    after_tile(tt)

    def TTo(eng, out, a, b, op, r, w):
        S.op(eng, lambda e: e.tensor_tensor(out=out, in0=a, in1=b, op=op), r=r, w=w)

    def TSo(eng, out, a, s1, op0, r, w, s2=None, op1=None):
        if op1 is None:
            S.op(eng, lambda e: e.tensor_scalar(out=out, in0=a, scalar1=s1, scalar2=None, op0=op0), r=r, w=w)
        else:
            S.op(eng, lambda e: e.tensor_scalar(out=out, in0=a, scalar1=s1, scalar2=s2, op0=op0, op1=op1), r=r, w=w)

    def STTo(eng, out, in0, sc, in1, op0, op1, r, w):
        S.op(eng, lambda e: e.scalar_tensor_tensor(out=out, in0=in0, scalar=sc, in1=in1, op0=op0, op1=op1), r=r, w=w)

    def ACTo(out, in_, func, r, w, **kw):
        S.op('act', lambda e: e.activation(out=out, in_=in_, func=func, **kw), r=r, w=w)

    def MMo(out, lhsT, rhs, r, w, start=True, stop=True, inc=True):
        S.op('pe', lambda e: e.matmul(out, lhsT, rhs, start=start, stop=stop), r=r, w=w, inc=inc)

    def TRo(out, in_, ident, r, w, inc=True):
        S.op('pe', lambda e: e.transpose(out, in_, ident), r=r, w=w, inc=inc)

    def CPo(eng, out, in_, r, w):
        if eng == 'act':
            S.op(eng, lambda e: e.activation(out=out, in_=in_, func=AF.Copy), r=r, w=w)
        else:
            S.op(eng, lambda e: e.tensor_copy(out=out, in_=in_), r=r, w=w)

    big_rot = [0]

    def bigbank():
        b = big_rot[0] % 2
        big_rot[0] += 1
        return b

    def ssd_phase():
        S.barrier()
        cv = Carver()
        cm = cv.get([128, 2048], F32)
        S.dma('sp', cm, cmat_d, w=['cm'])
        TI = cm[:, 0:128]
        SU = cm[:, 128:256]
        TIb = cm[:, 256:384]
        SUb = cm[:, 384:512]
        identf = cm[:, 512:640]
        Emat = cm[:, 640:656]
        ident_bf = cv.get([128, 128], BF16)
        CPo('dve', ident_bf, identf, ['cm'], ['ident_bf'])
        ones_f = cv.get([128, 128], F32)
        S.op('pool', lambda e: e.memset(ones_f, 1.0), w=['ones_f'])
        onec = cv.get([128, 1], F32)
        S.op('pool', lambda e: e.memset(onec, 1.0), w=['onec'])
        hp = cv.get([128, 72], F32)
        S.dma('sp', hp, hp_d, w=['hp'])
        abc = cv.get([128, 24], F32)
        ACTo(abc, hp[:, 24:48], AF.Exp, ['hp'], ['abc'])
        TSo('dve', abc, abc, -1.0, ALU.mult, ['abc'], ['abc'])
        cwb = cv.get([128, 4, 5, 5], F32)
        S.dma('sp', cwb, cwb_d, w=['cwb'])

        wssd = cv.get([128, 8, 1030], BF16)
        wout = cv.get([128, 3, 1024], BF16)
        diag = cv.get([128, 5, 4, 128], BF16)
        nbc = cv.get([128, 384], F32)
        xraw = [cv.get([128, 5, 515], BF16) for _ in range(2)]
        xraw_s = cv.get([128, 5, 16, 7], BF16)
        conv0s = cv.get([128, 5, 48], F32)
        convo = cv.get([128, 5, 3], F32)
        convos = cv.get([128, 5, 16, 3], F32)
        fm = cv.get([128, 5, 512], BF16)
        ofm = cv.get([128, 3, 512], BF16)
        dt6 = cv.get([128, 6], F32)
        la = cv.get([128, 6], F32)
        rhsla = cv.get([128, 768], F32)
        rhscd = cv.get([128, 96], F32)
        decT = cv.get([128, 768], BF16)
        MT2 = [cv.get([128, 768], BF16) for _ in range(2)]
        CBm = cv.get([128, 128], BF16)
        ea2 = [cv.get([128, 12], F32) for _ in range(2)]
        cdx2 = [cv.get([128, 96], F32) for _ in range(2)]
        xd2 = [cv.get([128, 6, 64], BF16) for _ in range(2)]
        xdd2 = [cv.get([128, 6, 64], BF16) for _ in range(2)]
        Btm2 = [cv.get([128, 128], BF16) for _ in range(2)]
        Btmm = [cv.get([128, 128], BF16) for _ in range(4)]
        skipx2 = [cv.get([128, 6, 64], F32) for _ in range(2)]
        yv = cv.get([128, 6, 64], F32)
        sz4 = [cv.get([128, 384], F32) for _ in range(4)]
        t6s = [cv.get([128, 6], F32) for _ in range(4)]
        junk = cv.get([128, 384], BF16)
        ss = cv.get([128, 1], F32)
        otm = cv.get([128, 384], BF16)
        hT = cv.get([128, 6, 64], F32)
        htmp = cv.get([128, 6, 64], F32)
        hTb = cv.get([128, 384], BF16)
        hs = [cv.get([128, 6, 64], F32) for _ in range(4)]
        hsb = [cv.get([128, 384], BF16) for _ in range(4)]
        hso = [cv.get([128, 6, 64], F32) for _ in range(4)]
        Cmask = cv.get([128, 16, 64], BF16)
        print("ssd arena used", cv.off)

        def bc6(ap, T):
            return ap[:T, :, None].to_broadcast([T, 6, 64])

        def ssd_pre(g, T, tok0, ci):
            g6 = slice(g * 6, g * 6 + 6)
            bk = 2 if ci % 2 == 0 else 6
            kb = 'ps%d' % bk
            for kc in range(8):
                MMo(ps[bk][:T, 0:390], h[:, kc, tok0:tok0 + T], wssd[:, kc, 640:1030], r=[('h', kc), 'wssd'], w=[kb],
                    start=(kc == 0), stop=(kc == 7), inc=(kc == 7))
            TTo('dve', t6s[ci][:T], ps[bk][:T, 384:390], hp[:T, g6], ALU.add, [kb, 'hp'], [('t6', ci)])
            ACTo(sz4[ci][:T, :], ps[bk][:T, 0:384], AF.Silu, [kb], [('sz4', ci)])

        def ssd_late(g, T, tok0, col0, sample, last, p, tI, sU, MT, xd, xdd, Btm, skipx, sz, ea, cdx,
                     kMT, kxd, kxdd, kBtm, kskipx, ksz, kea, kcdx, yv2):
            for hh in range(6):
                MMo(ps[7][:T, hh * 64:(hh + 1) * 64], MT[:T, hh * T:(hh + 1) * T], xd[:T, hh, :], r=[kMT, kxd], w=['ps7'], inc=(hh == 5))
            if not sample:
                MMo(ps[3][:T, 0:384], fm[:, 4, col0:col0 + T], hTb, r=['fmC', 'hTb'], w=['ps3'])
                MMo(ps[4][:, 0:384], Btm[:T, :], xdd[:T].rearrange("p a b -> p (a b)"), r=[kBtm, kxdd], w=['ps4'])
            else:
                TTo('pool', Cmask, fm[:, 4, None, col0:col0 + T].to_broadcast([128, 16, T]),
                    cm[:, 1024:2048].rearrange("p (a b) -> p a b", a=16), ALU.mult, ['fmC', 'cm'], ['Cmask'])
                dh_banks = [(4, 'ps4'), (0, ('ps', 0)), (1, ('ps', 1)), (2, 'ps2')]
                for b4 in range(0, 16, 4):
                    bs_ = list(range(b4, b4 + 4))
                    for b in bs_:
                        rb_ = b % 4
                        S.dma('sp', hs[rb_].rearrange("p a b -> p (a b)"), ssd0_d[b, g], w=[('hs', rb_)])
                    for b in bs_:
                        rb_ = b % 4
                        CPo('act', hsb[rb_], hs[rb_].rearrange("p a b -> p (a b)"), [('hs', rb_)], [('hsb', rb_)])
                        TSo('dve', Btmm[rb_][:T, :], Btm[:T, :], Emat[:T, b:b + 1], ALU.mult, [kBtm, 'cm'], [('Btmm', rb_)])
                    for b in bs_:
                        rb_ = b % 4
                        MMo(ps[3][:T, 0:384], Cmask[:, b, :], hsb[rb_], r=['Cmask', ('hsb', rb_)], w=['ps3'], start=(b == 0), stop=(b == 15),
                            inc=(b == 15))
                    for b in bs_:
                        rb_ = b % 4
                        bk_, kb_ = dh_banks[rb_]
                        MMo(ps[bk_][:, 0:384], Btmm[rb_][:T, :], xdd[:T].rearrange("p a b -> p (a b)"), r=[('Btmm', rb_), kxdd], w=[kb_])
                    for b in bs_:
                        rb_ = b % 4
                        bk_, kb_ = dh_banks[rb_]
                        TTo('dve', hso[rb_], hs[rb_], cdx[:, b * 6:(b + 1) * 6][:, :, None].to_broadcast([128, 6, 64]), ALU.mult,
                            [('hs', rb_), kcdx], [('hso', rb_)])
                        TTo('dve', hso[rb_].rearrange("p a b -> p (a b)"), hso[rb_].rearrange("p a b -> p (a b)"), ps[bk_][:, 0:384], ALU.add,
                            [('hso', rb_), kb_], [('hso', rb_)])
                        S.dma('sp', ssds_d[b, g], hso[rb_].rearrange("p a b -> p (a b)"), r=[('hso', rb_)], w=[('ssds', b, g)])
            TTo('dve', yv[:T], ps[3][:T, 0:384].rearrange("p (a b) -> p a b", a=6), bc6(ea[:, 0:6], T), ALU.mult, ['ps3', kea], ['yv'])
            TTo('dve', yv2, yv2, ps[7][:T, 0:384], ALU.add, ['yv', 'ps7'], ['yv'])
            TTo('pool', yv[:T], yv[:T], skipx[:T], ALU.add, ['yv', kskipx], ['yv'])
            TTo('pool', yv2, yv2, sz[:T, :], ALU.mult, ['yv', ksz], ['yv'])
            ACTo(junk[:T, :], yv2, AF.Square, ['yv'], ['junk', 'ss'], accum_out=ss[:T, 0:1])
            ACTo(ss[:T, :], ss[:T, :], AF.Ln, ['ss', 'epsc'], ['ss'], bias=epsc[:T, 0:1], scale=1.0 / 384)
            ACTo(ss[:T, :], ss[:T, :], AF.Exp, ['ss'], ['ss'], scale=-0.5)
            STTo('dve', otm[:T, :], yv2, ss[:T, 0:1], nbc[:T, :], ALU.mult, ALU.mult, ['yv', 'ss', 'nbc'], ['otm'])
            if not sample:
                TTo('dve', htmp, hT, cdx[:, 0:6][:, :, None].to_broadcast([128, 6, 64]), ALU.mult, ['hT', kcdx], ['htmp'])
                TTo('dve', hT.rearrange("p a b -> p (a b)"), htmp.rearrange("p a b -> p (a b)"), ps[4][:, 0:384], ALU.add,
                    ['htmp', 'ps4'], ['hT'])
                if last:
                    S.dma('sp', ssdp_d[g], hT.rearrange("p a b -> p (a b)"), r=['hT'], w=[('ssdp', g)])
                else:
                    CPo('act', hTb, hT.rearrange("p a b -> p (a b)"), ['hT'], ['hTb'])
            po = ps[7].bitcast(BF16)[:, 0:512]
            for j in range(3):
                TRo(po[:, j * 128:j * 128 + T], otm[:T, j * 128:(j + 1) * 128], ident_bf[:T, :T], r=['otm', 'ident_bf'], w=['ps7'], inc=(j == 2))
            CPo('act', ofm[:, :, col0:col0 + T], po[:, 0:384].rearrange("p (a b) -> p a b", a=3)[:, :, :T], ['ps7'], ['ofm'])


        def ssd_chunk(g, T, tok0, col0, sample, last, p, part, ci=0):
            g6 = slice(g * 6, g * 6 + 6)
            tI = TIb if sample else TI
            sU = SUb if sample else SU
            MT, xd, xdd, Btm, skipx, sz, ea, cdx = MT2[p], xd2[p], xdd2[p], Btm2[p], skipx2[p], sz4[ci], ea2[p], cdx2[p]
            kMT, kxd, kxdd, kBtm, kskipx, ksz, kea, kcdx = [(nm, p) for nm in ('MT', 'xd', 'xdd', 'Btm', 'skipx', 'sz', 'ea', 'cdx')]
            ksz = ('sz4', ci)
            yv2 = yv[:T].rearrange("p a b -> p (a b)")
            if part == 'late':
                return ssd_late(g, T, tok0, col0, sample, last, p, tI, sU, MT, xd, xdd, Btm, skipx, sz, ea, cdx,
                                kMT, kxd, kxdd, kBtm, kskipx, ksz, kea, kcdx, yv2)
            t6 = t6s[ci]
            kt6 = ('t6', ci)
            ACTo(t6[:T], t6[:T], AF.Exp, [kt6], [kt6])
            ACTo(dt6[:T], t6[:T], AF.Ln, [kt6, 'onec'], ['dt6'], bias=onec[:T, 0:1], scale=1.0)
            TTo('dve', la[:T], dt6[:T], abc[:T, g6], ALU.mult, ['dt6', 'abc'], ['la'])
            rl3 = rhsla[:T, :6 * T].rearrange("p (a b) -> p a b", a=6)
            TTo('pool', rl3, la[:T, :, None].to_broadcast([T, 6, T]), tI[:T, None, :T].to_broadcast([T, 6, T]), ALU.mult,
                ['la', 'cm'], ['rhsla'])
            ncd = 6
            if sample:
                ncd = 96
                TTo('pool', rhscd[:T, :96].rearrange("p (a b) -> p a b", a=16), la[:T, None, :].to_broadcast([T, 16, 6]),
                    Emat[:T, :, None].to_broadcast([T, 16, 6]), ALU.mult, ['la', 'cm'], ['rhscd'])
            MMo(ps[0][:T, :3 * T], sU[:T, :T], rhsla[:T, 0:3 * T], r=['cm', 'rhsla'], w=[('ps', 0)])
            MMo(ps[1][:T, :3 * T], sU[:T, :T], rhsla[:T, 3 * T:6 * T], r=['cm', 'rhsla'], w=[('ps', 1)])
            MMo(ps[5][:T, 0:6], tI[:T, :T], la[:T, :], r=['cm', 'la'], w=['ps5'])
            MMo(ps[5][:T, 6:12], sU[:T, :T], la[:T, :], r=['cm', 'la'], w=['ps5'])
            if sample:
                MMo(ps[5][:, 16:16 + 96], ones_f[:T, :], rhscd[:T, :96], r=['ones_f', 'rhscd'], w=['ps5'])
            else:
                MMo(ps[5][:, 16:22], ones_f[:T, :], la[:T, :], r=['ones_f', 'la'], w=['ps5'])
            ACTo(decT[:T, 0:3 * T], ps[0][:T, :3 * T], AF.Exp, [('ps', 0)], ['decTa'])
            ACTo(decT[:T, 3 * T:6 * T], ps[1][:T, :3 * T], AF.Exp, [('ps', 1)], ['decTb'])
            ACTo(ea[:T, :], ps[5][:T, 0:12], AF.Exp, ['ps5'], [kea])
            ACTo(cdx[:, :ncd], ps[5][:, 16:16 + ncd], AF.Exp, ['ps5'], [kcdx])
            MMo(ps[5][:T, 128:128 + T], fm[:, 3, col0:col0 + T], fm[:, 4, col0:col0 + T], r=['fmB', 'fmC'], w=['ps5'])
            TTo('dve', CBm[:T, :T], ps[5][:T, 128:128 + T], tI[:T, :T], ALU.mult, ['ps5', 'cm'], ['CBm'])
            TTo('dve', MT[:T, :6 * T].rearrange("p (a b) -> p a b", a=6), decT[:T, :6 * T].rearrange("p (a b) -> p a b", a=6),
                CBm[:T, None, :T].to_broadcast([T, 6, T]), ALU.mult, ['decTa', 'decTb', 'CBm'], [kMT])
            pt = ps[6].bitcast(BF16)
            for j in range(4):
                TRo(pt[:T, j * 128:(j + 1) * 128], fm[:, j, col0:col0 + T], ident_bf, r=[('fmx', j) if j < 3 else 'fmB', 'ident_bf'],
                    w=['ps6'], inc=(j == 3))
            xs3 = pt[:T, 0:384].rearrange("p (a b) -> p a b", a=6)
            TTo('dve', xd[:T], xs3, bc6(dt6, T), ALU.mult, ['ps6', 'dt6'], [kxd])
            TTo('dve', xdd[:T], xd[:T], bc6(ea[:, 6:12], T), ALU.mult, [kxd, kea], [kxdd])
            CPo('act', Btm[:T, :], pt[:T, 384:512], ['ps6'], [kBtm])
            TTo('dve', skipx[:T], xs3, bc6(hp[:, 48 + g * 6:54 + g * 6], T), ALU.mult, ['ps6', 'hp'], [kskipx])
            return

        STOP = 0
        KT = os.environ.get('KTILES')
        KG = int(os.environ.get('KG', '4'))
        KCUT = int(os.environ.get('KCUT', '99'))
        for g in range(KG):
            S.dma('pool', wssd[:, 0:4, :], wssd_d[g].rearrange("(kc p) f -> p kc f", p=128)[:, 0:4, :], w=['wssd'])
            S.dma('pool', wssd[:, 4:8, :], wssd_d[g].rearrange("(kc p) f -> p kc f", p=128)[:, 4:8, :], w=['wssd'])
            S.dma('pool', wout, woutssd_d[g].rearrange("(j p) d -> p j d", p=128), w=['wout'])
            S.dma('sp', nbc, nbc_d[:, g * 384:(g + 1) * 384], w=['nbc'])
            S.dma('sp', conv0s, conv0_d[:, g], w=['conv0s'])
            for cc in range(5):
                for k in range(4):
                    TSo('dve', diag[:, cc, k, :], ident_bf, cwb[:, g, cc, k:k + 1], ALU.mult, ['ident_bf', 'cwb'], ['diag'])
            S.op('pool', lambda e: e.memset(hT.rearrange("p a b -> p (a b)"), 0.0), w=['hT'])
            S.op('pool', lambda e: e.memset(hTb, 0.0), w=['hTb'])
            S.op('pool', lambda e: e.memset(xraw[1][:, :, 512:515], 0.0), w=[('xraw', 1)])
            CPo('dve', xraw_s[:, :, :, 0:3], conv0s.rearrange("p c (b k) -> p c b k", b=16), ['conv0s'], ['xraw_s'])
            for tt, (t0, n) in enumerate(TT):
                if KT is not None and str(tt) not in KT.split(','):
                    continue
                sample = (tt == 4)
                xr = xraw[tt % 2]
                for cc in range(5):
                    bk = bigbank()
                    for kc in range(8):
                        MMo(ps[bk][:, :n], wssd[:, kc, cc * 128:(cc + 1) * 128], h[:, kc, t0:t0 + n], r=['wssd', ('h', kc)], w=[('ps', bk)],
                            start=(kc == 0), stop=(kc == 7), inc=(kc == 7))
                    if not sample:
                        CPo('act', xr[:, cc, 3:3 + n], ps[bk][:, :n], [('ps', bk)], [('xraw', tt % 2)])
                        CPo('dve', xr[:, cc, 0:3], xraw[(tt + 1) % 2][:, cc, 512:515], [('xraw', (tt + 1) % 2)], [('xraw', tt % 2)])
                        if tt == 3:
                            CPo('dve', convo[:, cc, :], ps[bk][:, 509:512], [('ps', bk)], ['convo'])
                    else:
                        p3 = ps[bk][:, :64].rearrange("p (b l) -> p b l", b=16)
                        CPo('act', xraw_s[:, cc, :, 3:7], p3, [('ps', bk)], ['xraw_s'])
                        CPo('dve', convos[:, cc, :, :], p3[:, :, 1:4], [('ps', bk)], ['convos'])
                if tt == 3:
                    S.dma('sp', convp_d[:, g], convo, r=['convo'], w=[('convp', g)])
                if sample:
                    S.dma('sp', convs_d[:, g], convos.rearrange("p c b k -> p c (b k)"), r=['convos'], w=[('convs', g)])
                for cc in range(5):
                    bk = bigbank()
                    for k in range(4):
                        if not sample:
                            rhs = xr[:, cc, k:k + n]
                            rk = ('xraw', tt % 2)
                        else:
                            rhs = xraw_s[:, cc, :, k:k + 4]
                            rk = 'xraw_s'
                        MMo(ps[bk][:, :n], diag[:, cc, k, :], rhs, r=['diag', rk], w=[('ps', bk)], start=(k == 0), stop=(k == 3), inc=(k == 3))
                    wk = ('fmx', cc) if cc < 3 else ('fmB' if cc == 3 else 'fmC')
                    ACTo(fm[:, cc, :n], ps[bk][:, :n], AF.Silu, [('ps', bk), 'cwb'], [wk], bias=cwb[:, g, cc, 4:5], scale=1.0)
                if not sample:
                    def ch(ci, part):
                        return S.record(lambda: ssd_chunk(g, 128, t0 + ci * 128, ci * 128, False, (tt == 3 and ci == 3), ci % 2, part, ci))
                    for ci in range(4):
                        ssd_pre(g, 128, t0 + ci * 128, ci)
                    E = [ch(ci, 'early') for ci in range(4)]
                    Lt = [ch(ci, 'late') for ci in range(4)]
                    S.replay(E[0])
                    for ci in range(4):
                        S.replay(S.merge(E[ci + 1] if ci + 1 < 4 else [], Lt[ci]))
                else:
                    ssd_pre(g, 64, t0, 0)
                    ssd_chunk(g, 64, t0, 0, True, False, 0, 'early')
                    ssd_chunk(g, 64, t0, 0, True, False, 0, 'late')
                for dc in range(8):
                    bk = bigbank()
                    for j in range(3):
                        MMo(ps[bk][:, :n], wout[:, j, dc * 128:(dc + 1) * 128], ofm[:, j, :n], r=['wout', 'ofm'], w=[('ps', bk)],
                            start=(j == 0), stop=(j == 2), inc=(j == 2))
                    TTo('dve', x[:, dc, t0:t0 + n], x[:, dc, t0:t0 + n], ps[bk][:, :n], ALU.add, [('ps', bk), ('x', dc, tt)], [('x', dc, tt)])


    def s5_phase():
        S.barrier()
        cv = Carver()
        I32 = mybir.dt.int32
        PI = 3.14159265358979
        winu = cv.get([128, 8, 512], BF16)
        wouts5 = cv.get([128, 4, 1024], BF16)
        wglu = cv.get([128, 4, 512], BF16)
        lhsB = [cv.get([128, 4, 8, 64], BF16) for _ in range(2)]
        lhsC = [cv.get([128, 16, 128], BF16) for _ in range(2)]
        u_all = cv.get([128, 4, NT], BF16)
        S.dma('pool', winu, winu_d.rearrange("(kc p) f -> p kc f", p=128), w=['winu'])
        S.dma('pool', wouts5, wouts5_d.rearrange("(kc p) f -> p kc f", p=128), w=['wouts5'])
        S.dma('pool', wglu, wglu_d.rearrange("(kc p) f -> p kc f", p=128), w=['wglu'])
        for c in range(2):
            S.dma('pool', lhsC[c], cpad_d[c], w=[('lhsC', c)])
        TSo('dve', lhsC[1], lhsC[1], -1.0, ALU.mult, [('lhsC', 1)], [('lhsC', 1)])
        for tt, (t0, n) in enumerate(TT):
            for kt in range(4):
                bk = bigbank()
                for kc in range(8):
                    MMo(ps[bk][:, :n], winu[:, kc, kt * 128:(kt + 1) * 128], h[:, kc, t0:t0 + n], r=['winu'], w=[('ps', bk)],
                        start=(kc == 0), stop=(kc == 7), inc=(kc == 7))
                CPo('act', u_all[:, kt, t0:t0 + n], ps[bk][:, :n], [('ps', bk)], [('u', tt)])
        S.barrier()
        W = 272
        s5v = cv.get([128, 72], F32)
        S.dma('sp', s5v, s5v_d, w=['s5v'])
        emk = cv.get([128, 8], F32)
        S.dma('sp', emk, emask_d, w=['emk'])
        iot = cv.get([128, 256], F32)
        S.dma('sp', iot, cmat_d[:, 768:1024], w=['iot'])
        st0 = cv.get([128, 2, 16, 16], F32)
        S.dma('sp', st0, s5s0_d, w=['st0'])
        mag = cv.get([128, W], F32)
        carry = cv.get([128, 2, 16], F32)
        carry_s = cv.get([128, 2, 16, 16], F32)
        off_pre = cv.off
        lam = cv.get([128, 3, W], F32)
        S.dma('sp', lam, lam_d, w=['lam'])
        bT = cv.get([128, 2, 4, 64], F32)
        S.dma('sp', bT, bT_d, w=['bT'])
        pre = [cv.get([128, W], F32) for _ in range(11)]
        stp, xr, ang, sn, cs, ar, ai, t1, t2, cr, ci = pre
        ki = cv.get([128, 512], I32)
        kf = cv.get([128, 512], F32)
        rr = cv.get([128, 512], F32)

        def sin_of(out, a, n, shift, key):
            S.op('dve', lambda e: e.tensor_scalar(out=ki[:, :n], in0=a, scalar1=shift, scalar2=1.0 / (2 * PI), op0=ALU.add, op1=ALU.mult),
                 r=[key], w=['ki'])
            CPo('dve', kf[:, :n], ki[:, :n], ['ki'], ['kf'])
            STTo('dve', rr[:, :n], kf[:, :n], -2 * PI, a, ALU.mult, ALU.add, ['kf', key], ['rr'])
            S.op('dve', lambda e: e.tensor_scalar(out=rr[:, :n], in0=rr[:, :n], scalar1=shift, scalar2=-3.1415925, op0=ALU.add, op1=ALU.max),
                 r=['rr'], w=['rr'])
            TSo('dve', rr[:, :n], rr[:, :n], 3.1415925, ALU.min, ['rr'], ['rr'])
            ACTo(out, rr[:, :n], AF.Sin, ['rr'], [key + '_o'])

        ACTo(stp, lam[:, 2, :], AF.Exp, ['lam'], ['pre'])
        TTo('dve', xr, lam[:, 0, :], stp, ALU.mult, ['lam', 'pre'], ['pre'])
        TTo('dve', ang, lam[:, 1, :], stp, ALU.mult, ['lam', 'pre'], ['ang'])
        ACTo(mag, xr, AF.Exp, ['pre'], ['pre'])
        sin_of(sn, ang, W, 0.0, 'ang')
        sin_of(cs, ang, W, PI / 2, 'ang')
        P_ = ['pre', 'ang_o', 'lam']
        TTo('dve', ar, mag, cs, ALU.mult, P_, ['pre'])
        TTo('dve', ai, mag, sn, ALU.mult, P_, ['pre'])
        TTo('dve', t1, lam[:, 0, :], lam[:, 0, :], ALU.mult, P_, ['pre'])
        TTo('dve', t2, lam[:, 1, :], lam[:, 1, :], ALU.mult, P_, ['pre'])
        TTo('dve', t1, t1, t2, ALU.add, P_, ['pre'])
        S.op('dve', lambda e: e.reciprocal(out=t1, in_=t1), r=P_, w=['pre'])
        TSo('dve', t2, ar, -1.0, ALU.add, P_, ['pre'])
        TTo('dve', cr, t2, lam[:, 0, :], ALU.mult, P_, ['pre'])
        TTo('dve', ci, ai, lam[:, 1, :], ALU.mult, P_, ['pre'])
        TTo('dve', cr, cr, ci, ALU.add, P_, ['pre'])
        TTo('dve', cr, cr, t1, ALU.mult, P_, ['pre'])
        TTo('dve', ci, ai, lam[:, 0, :], ALU.mult, P_, ['pre'])
        TTo('dve', t2, t2, lam[:, 1, :], ALU.mult, P_, ['pre'])
        TTo('dve', ci, ci, t2, ALU.subtract, P_, ['pre'])
        TTo('dve', ci, ci, t1, ALU.mult, P_, ['pre'])
        bb = [cv.get([128, 4, 64], F32) for _ in range(2)]
        tb = cv.get([128, 4, 64], F32)
        cr3 = cr[:, 0:256].rearrange("p (a b) -> p a b", a=4)
        ci3 = ci[:, 0:256].rearrange("p (a b) -> p a b", a=4)
        TTo('dve', bb[0], cr3, bT[:, 0], ALU.mult, P_ + ['bT'], ['bb'])
        TTo('dve', tb, ci3, bT[:, 1], ALU.mult, P_ + ['bT'], ['tb'])
        TTo('dve', bb[0], bb[0], tb, ALU.subtract, ['bb', 'tb'], ['bb'])
        TTo('dve', bb[1], cr3, bT[:, 1], ALU.mult, P_ + ['bT'], ['bb'])
        TTo('dve', tb, ci3, bT[:, 0], ALU.mult, P_ + ['bT'], ['tb'])
        TTo('dve', bb[1], bb[1], tb, ALU.add, ['bb', 'tb'], ['bb'])
        for c in range(2):
            for kt in range(4):
                TTo('dve', lhsB[c][:, kt], bb[c][:, kt, None, :].to_broadcast([128, 8, 64]), emk[:, :, None].to_broadcast([128, 8, 64]),
                    ALU.mult, ['bb', 'emk'], [('lhsB', c)])
        tabf = h.rearrange("p a b -> p (a b)").bitcast(F32)
        Ec = tabf[:, 0:4096].rearrange("p (a b) -> p a b", a=16)
        Es = tabf[:, 4096:8192].rearrange("p (a b) -> p a b", a=16)
        angt = cv.get([128, 2, 256], F32)
        for i0 in range(0, 16, 2):
            TTo('dve', angt, ang[:, 256 + i0:258 + i0][:, :, None].to_broadcast([128, 2, 256]), iot[:, None, :].to_broadcast([128, 2, 256]),
                ALU.mult, ['ang', 'iot'], ['angt'])
            af = angt.rearrange("p a b -> p (a b)")
            sin_of(Es[:, i0:i0 + 2, :].rearrange("p a b -> p (a b)"), af, 512, 0.0, 'angt')
            sin_of(Ec[:, i0:i0 + 2, :].rearrange("p a b -> p (a b)"), af, 512, PI / 2, 'angt')
        TAB = ['angt_o']
        S.barrier()
        cv.off = off_pre
        S.op('pool', lambda e: e.memset(carry.rearrange("p a b -> p (a b)"), 0.0), w=['carry'])
        tmA = [cv.get([128, 2, 256], F32) for _ in range(4)]
        wreg = winu.rearrange("p a b -> p (a b)").bitcast(F32)
        tmB = [wreg[:, k * 512:(k + 1) * 512].rearrange("p (a b) -> p a b", a=2) for k in range(4)]
        tm2 = [tmA, tmB]
        wv2 = [[cv.get([128, 2, 256], F32) for _ in range(2)] for _ in range(2)]
        Wv2 = [[cv.get([128, 2, 256], F32) for _ in range(2)] for _ in range(2)]
        sv2 = [[cv.get([128, 2, 256], F32) for _ in range(2)] for _ in range(2)]
        rt_single = cv.get([128, 256], F32)
        rt2 = [rt_single, rt_single]
        hist2 = [cv.get([128, 2, 512], BF16) for _ in range(2)]
        y5 = cv.get([128, 512], F32)
        g1 = cv.get([128, 512], F32)
        v_bf = cv.get([128, 4, 512], BF16)
        gs = g1
        o5 = cv.get([128, 4, 512], BF16)
        print("s5 arena used", cv.off)
        magq = mag[:, 256:272]

        def bct(tab, i, nseg, L):
            return tab[:, i, None, 0:L].to_broadcast([128, nseg, L])

        for tt, (t0, n) in enumerate(TT):
            sample = (tt == 4)
            nseg, L = (16, 4) if sample else (2, 256)

            def v3(ap):
                return ap.rearrange("p a b -> p (a b)")[:, :nseg * L].rearrange("p (a b) -> p a b", a=nseg)
            for kt in range(4):
                cb = 4 if kt % 2 == 0 else 7
                kcb = 'ps%d' % cb
                for jp in (0, 2):
                    ctxs = []
                    for j in (jp, jp + 1):
                        i = 4 * kt + j
                        par = i % 2
                        pre_b, pim_b = (2, 3) if par == 0 else (5, 6)
                        ctxs.append(dict(i=i, j=j, par=par, tm=tm2[par], wv=wv2[par], Wv=Wv2[par], sv=sv2[par], hist=hist2[par],
                                         pre_b=pre_b, pim_b=pim_b, kre='ps%d' % pre_b, kim='ps%d' % pim_b))
                    for cx in ctxs:
                        i, j = cx['i'], cx['j']
                        lB = [lhsB[c].rearrange("p a b c -> p a (b c)")[:, kt, j * 128:(j + 1) * 128] for c in range(2)]
                        MMo(ps[cx['pre_b']][:, :n], lB[0], u_all[:, kt, t0:t0 + n], r=[('lhsB', 0), ('u', tt)], w=[cx['kre']])
                        MMo(ps[cx['pim_b']][:, :n], lB[1], u_all[:, kt, t0:t0 + n], r=[('lhsB', 1), ('u', tt)], w=[cx['kim']])
                    for cx in ctxs:
                        i, par, tm = cx['i'], cx['par'], cx['tm']
                        Pre = ps[cx['pre_b']][:, :n].rearrange("p (a b) -> p a b", a=nseg)
                        Pim = ps[cx['pim_b']][:, :n].rearrange("p (a b) -> p a b", a=nseg)
                        ec, es = bct(Ec, i, nseg, L), bct(Es, i, nseg, L)
                        TTo('dve', v3(tm[0]), Pre, ec, ALU.mult, [cx['kre']] + TAB, [('tm', par, 0)])
                        TTo('dve', v3(tm[1]), Pim, es, ALU.mult, [cx['kim']] + TAB, [('tm', par, 1)])
                        TTo('dve', v3(tm[2]), Pim, ec, ALU.mult, [cx['kim']] + TAB, [('tm', par, 2)])
                        TTo('dve', v3(tm[3]), Pre, es, ALU.mult, [cx['kre']] + TAB, [('tm', par, 3)])
                    for cx in ctxs:
                        par, tm, wv = cx['par'], cx['tm'], cx['wv']
                        TTo('pool', v3(wv[0]), v3(tm[0]), v3(tm[1]), ALU.add, [('tm', par, 0), ('tm', par, 1)], [('wv', par, 0)])
                        TTo('pool', v3(wv[1]), v3(tm[2]), v3(tm[3]), ALU.subtract, [('tm', par, 2), ('tm', par, 3)], [('wv', par, 1)])
                    groups = [list(range(16))] if sample else [[0], [1]]
                    for grp in groups:
                        g0, g1_ = grp[0], grp[-1] + 1
                        for cx in ctxs:
                            i, par, wv, Wv, sv = cx['i'], cx['par'], cx['wv'], cx['Wv'], cx['sv']
                            rbc = magq[:, i:i + 1].to_broadcast([128, L])
                            if sample:
                                rt64 = tm2[par][0].rearrange("p a b -> p (a b)")[:, 64:128]
                                TSo('dve', rt64, s5v[:, 8:72], magq[:, i:i + 1], ALU.mult, ['s5v', 'pre', ('tm', par, 0)], [('rt64', par)])
                                for c in range(2):
                                    w0 = v3(wv[c])[:, :, 0]
                                    STTo('dve', w0, st0[:, c, i, :], magq[:, i:i + 1], w0, ALU.mult, ALU.add, ['st0', 'pre', ('wv', par, c)], [('wv', par, c)])
                                    wf = wv[c].rearrange("p a b -> p (a b)")[:, 0:64]
                                    Wf = Wv[c].rearrange("p a b -> p (a b)")[:, 0:64]
                                    S.op('dve', lambda e: e.tensor_tensor_scan(out=Wf, data0=rt64, data1=wf, initial=0.0, op0=ALU.mult, op1=ALU.add),
                                         r=[('rt64', par), ('wv', par, c)], w=[('Wv', par, c)])
                                continue
                            for sg_ in grp:
                                for c in range(2):
                                    if sample:
                                        init = st0[:, c, i, sg_:sg_ + 1]
                                        ik = 'st0'
                                    elif sg_ == 0:
                                        init = carry[:, c, i:i + 1]
                                        ik = 'carry'
                                    else:
                                        init = sv[c][:, 0, 255:256]
                                        ik = ('sv', par, c)
                                    S.op('dve', lambda e: e.tensor_tensor_scan(out=v3(Wv[c])[:, sg_, :], data0=rbc, data1=v3(wv[c])[:, sg_, :],
                                                                               initial=init, op0=ALU.mult, op1=ALU.add),
                                         r=['pre', ('wv', par, c), ik], w=[('Wv', par, c)])
                        for cx in ctxs:
                            i, par, tm, Wv = cx['i'], cx['par'], cx['tm'], cx['Wv']
                            ecg = Ec[:, i, None, 0:L].to_broadcast([128, g1_ - g0, L])
                            esg = Es[:, i, None, 0:L].to_broadcast([128, g1_ - g0, L])
                            TTo('dve', v3(tm[0])[:, g0:g1_], v3(Wv[0])[:, g0:g1_], ecg, ALU.mult, [('Wv', par, 0)] + TAB, [('tm', par, 0)])
                            TTo('dve', v3(tm[1])[:, g0:g1_], v3(Wv[1])[:, g0:g1_], esg, ALU.mult, [('Wv', par, 1)] + TAB, [('tm', par, 1)])
                        for cx in ctxs:
                            i, par, tm, Wv, sv = cx['i'], cx['par'], cx['tm'], cx['Wv'], cx['sv']
                            ecg = Ec[:, i, None, 0:L].to_broadcast([128, g1_ - g0, L])
                            esg = Es[:, i, None, 0:L].to_broadcast([128, g1_ - g0, L])
                            TTo('pool', v3(tm[2])[:, g0:g1_], v3(Wv[1])[:, g0:g1_], ecg, ALU.mult, [('Wv', par, 1)] + TAB, [('tm', par, 2)])
                            TTo('pool', v3(tm[3])[:, g0:g1_], v3(Wv[0])[:, g0:g1_], esg, ALU.mult, [('Wv', par, 0)] + TAB, [('tm', par, 3)])
                            TTo('pool', v3(sv[0])[:, g0:g1_], v3(tm[0])[:, g0:g1_], v3(tm[1])[:, g0:g1_], ALU.subtract,
                                [('tm', par, 0), ('tm', par, 1)], [('sv', par, 0)])
                            TTo('pool', v3(sv[1])[:, g0:g1_], v3(tm[2])[:, g0:g1_], v3(tm[3])[:, g0:g1_], ALU.add,
                                [('tm', par, 2), ('tm', par, 3)], [('sv', par, 1)])
                    for cx in ctxs:
                        i, j, par, sv, hist = cx['i'], cx['j'], cx['par'], cx['sv'], cx['hist']
                        for c in range(2):
                            svf = sv[c].rearrange("p a b -> p (a b)")
                            CPo('act', hist[:, c, :n], svf[:, :n], [('sv', par, c)], [('hist', par, c)])
                            if sample:
                                CPo('act', carry_s[:, c, i, :], v3(sv[c])[:, :, 3], [('sv', par, c)], ['carry_s'])
                            else:
                                CPo('act', carry[:, c, i:i + 1], svf[:, 511:512], [('sv', par, c)], ['carry'])
                            MMo(ps[cb][:, :n], lhsC[c][:, i, :], hist[:, c, :n], r=[('lhsC', c), ('hist', par, c)], w=[kcb],
                                start=(j == 0 and c == 0), stop=(j == 3 and c == 1), inc=True)
                STTo('dve', y5[:, :n], u_all[:, kt, t0:t0 + n], s5v[:, kt:kt + 1], ps[cb][:, :n], ALU.mult, ALU.add, [('u', tt), 's5v', kcb], ['y5'])
                TTo('pool', g1[:, :n], y5[:, :n], y5[:, :n], ALU.mult, ['y5'], ['g1'])
                S.op('dve', lambda e: e.tensor_scalar(out=g1[:, :n], in0=g1[:, :n], scalar1=0.044715, scalar2=1.0, op0=ALU.mult, op1=ALU.add),
                     r=['g1'], w=['g1'])
                TTo('pool', g1[:, :n], g1[:, :n], y5[:, :n], ALU.mult, ['g1', 'y5'], ['g1'])
                ACTo(g1[:, :n], g1[:, :n], AF.Sigmoid, ['g1'], ['g1'], scale=1.5957691216057308)
                TTo('dve', v_bf[:, kt, :n], g1[:, :n], y5[:, :n], ALU.mult, ['g1', 'y5'], [('v', kt)])
            for mo in range(4):
                bk = bigbank()
                for kt in range(4):
                    MMo(ps[bk][:, :n], wglu[:, kt, mo * 128:(mo + 1) * 128], v_bf[:, kt, :n], r=['wglu', ('v', kt)], w=[('ps', bk)],
                        start=(kt == 0), stop=(kt == 3), inc=(kt == 3))
                ACTo(gs[:, :n], ps[bk][:, :n], AF.Sigmoid, [('ps', bk), 's5v'], ['g1'], bias=s5v[:, 4 + mo:5 + mo], scale=1.0)
                TTo('dve', o5[:, mo, :n], v_bf[:, mo, :n], gs[:, :n], ALU.mult, [('v', mo), 'g1'], [('o5', mo)])
            for dc in range(8):
                bk = bigbank()
                for mo in range(4):
                    MMo(ps[bk][:, :n], wouts5[:, mo, dc * 128:(dc + 1) * 128], o5[:, mo, :n], r=['wouts5', ('o5', mo)], w=[('ps', bk)],
                        start=(mo == 0), stop=(mo == 3), inc=(mo == 3))
                TTo('dve', x[:, dc, t0:t0 + n], x[:, dc, t0:t0 + n], ps[bk][:, :n], ALU.add, [('ps', bk), ('x', dc, tt)], [('x', dc, tt)])
        S.dma('sp', s5p_d, carry, r=['carry'], w=['s5p'])
        S.dma('sp', s5s_d, carry_s, r=['carry_s'], w=['s5s'])


    PH = os.environ.get('KPH', 'n1,f1,n2,ssd,s5,n3,f2').split(',')
    if 'n1' in PH:
        norm_to_h(0)
    if 'f1' in PH:
        ffn(0, after_tile=(lambda tt: norm_to_h(1, tiles=[tt])) if 'n2' in PH else None)
    elif 'n2' in PH:
        norm_to_h(1)
    if 'ssd' in PH:
        ssd_phase()
    if 's5' in PH:
        s5_phase()
    S.barrier()
    if 'n3' in PH:
        norm_to_h(2)
    yT_v = yT.rearrange("(c p) t -> p c t", p=128)
    ycount = [0]

    def final_out(tt, c, t0, n, rb):
        yb = ycount[0] % 2
        ycount[0] += 1
        S.op('dve', lambda e: e.scalar_tensor_tensor(out=ystage[yb][:, :n], in0=x[:, c, t0:t0 + n],
                                                     scalar=normw[:, 24 + c:24 + c + 1], in1=rb[:, :n],
                                                     op0=ALU.mult, op1=ALU.mult),
             r=[('x', c, tt), ('rstd', tt % 2), 'normw'], w=[('ystage', yb)])
        S.dma('sp', yT_v[:, c, t0:t0 + n], ystage[yb][:, :n], r=[('ystage', yb)], w=[('yT', c, tt)])
    if 'f2' in PH:
        ffn(1, after_tile=lambda tt: rmsnorm(3, final_out, tiles=[tt]))
    else:
        rmsnorm(3, final_out)
    S.finish()
    print("instructions", S.nins, "waits", S.nwait)
    return nc


_NC_CACHE = {}


def _consts():
    f32 = np.float32
    cm = np.zeros((128, 2048), f32)
    k = np.arange(128)
    cm[:, 0:128] = (k[:, None] <= k[None, :])
    cm[:, 128:256] = (k[:, None] > k[None, :])
    same = (k[:, None] // 4 == k[None, :] // 4) & (k[:, None] < 64) & (k[None, :] < 64)
    cm[:, 256:384] = (k[:, None] <= k[None, :]) & same
    cm[:, 384:512] = (k[:, None] > k[None, :]) & same
    cm[:, 512:640] = np.eye(128)
    cm[:64, 640:656] = (np.arange(64)[:, None] // 4 == np.arange(16)[None, :])
    cm[:, 768:1024] = np.arange(1, 257)[None, :]
    et = (np.arange(64)[None, :] // 4 == np.arange(16)[:, None]).astype(f32)
    cm[:, 1024:2048] = et.reshape(1, 1024)
    return cm


def _chan(g, cc):
    if cc < 3:
        return 384 * g + 128 * cc
    if cc == 3:
        return 1536 + 128 * g
    return 2048 + 128 * g


def _s5_host(inp):
    f32 = np.float32
    A = lambda k: np.asarray(inp[k], f32)
    w_in = A("w_in")[0]
    w_out = A("w_out")[0]
    s5v = np.empty((128, 72), f32)
    s5v[:, 8:72] = (np.arange(64) % 4 != 0).astype(f32)[None, :]
    s5v[:, 0:4] = A("s5_d")[0].reshape(4, 128).T
    s5v[:, 4:8] = A("s5_b_glu")[0].reshape(4, 128).T
    lre, lim, lst = A("s5_lambda_re")[0], A("s5_lambda_im")[0], A("s5_log_step")[0]
    lam = np.empty((128, 3, 272), f32)
    for arr_i, arr in enumerate((lre, lim)):
        rep = arr.reshape(4, 8, 64)
        rep = np.repeat(rep.transpose(1, 0, 2)[:, None], 16, axis=1)
        lam[:, arr_i, 0:256] = rep.reshape(128, 256)
        qq = arr.reshape(16, 2, 64).transpose(1, 2, 0).reshape(128, 16)
        lam[:, arr_i, 256:272] = qq
    rep = np.repeat(lst.reshape(4, 8).T[:, None, :, None], 16, axis=1)
    lam[:, 2, 0:256] = np.broadcast_to(rep, (8, 16, 4, 64)).reshape(128, 256)
    qq = np.broadcast_to(lst.reshape(16, 2).T[:, None, :], (2, 64, 16)).reshape(128, 16)
    lam[:, 2, 256:272] = qq
    bT = np.empty((128, 2, 4, 64), f32)
    for ci, k in enumerate(("s5_b_re", "s5_b_im")):
        b = A(k)[0].reshape(4, 8, 64, 16)
        bT[:, ci] = b.transpose(1, 3, 0, 2).reshape(128, 4, 64)
    emask = (np.arange(128)[:, None] // 16 == np.arange(8)[None, :]).astype(f32)
    cpad = np.zeros((2, 128, 16, 128), f32)
    for ci, k in enumerate(("s5_c_re", "s5_c_im")):
        cc = A(k)[0]
        for g in range(32):
            i, q0, gl = g // 2, (g % 2) * 64, g % 8
            cpad[ci, q0:q0 + 64, i, gl * 16:(gl + 1) * 16] = cc[g].T
    return {"w_in_u": np.ascontiguousarray(w_in[:, 0:512]), "w_out_s5": np.ascontiguousarray(w_out[0:512]),
            "w_glu": np.ascontiguousarray(A("s5_w_glu")[0]), "s5v": s5v, "lam": lam, "bT": bT, "emask": emask, "cpad": cpad}


def _s5_core(inp, c):
    f32 = np.float32
    out = np.empty((128, 2, 16, 16), f32)
    for ci, k in enumerate(("state_s5_re", "state_s5_im")):
        s = np.asarray(inp[k], f32)[0, 16 * c:16 * c + 16]
        out[:, ci] = s.reshape(16, 16, 2, 64).transpose(2, 3, 1, 0).reshape(128, 16, 16)
    return {"s5s0": out}


def kernel(**inp):
    f32 = np.float32
    A = lambda k: np.asarray(inp[k], f32)
    xp = A("x_prompt")
    xs = A("x_sample")
    normw = np.stack([A(k).reshape(D) for k in ("ffn1_norm", "mix_norm", "ffn2_norm", "final_norm")])
    normw_l = np.ascontiguousarray(normw.reshape(4, 8, 128).transpose(2, 0, 1).reshape(128, 32))
    w_in = A("w_in")[0]
    w_out = A("w_out")[0]
    w_ssd = np.empty((4, D, 1030), f32)
    w_out_ssd = np.empty((4, 384, D), f32)
    for g in range(4):
        w_ssd[g, :, 0:384] = w_in[:, 2048 + 384 * g:2048 + 384 * (g + 1)]
        w_ssd[g, :, 384:512] = w_in[:, 3584 + 128 * g:3584 + 128 * (g + 1)]
        w_ssd[g, :, 512:640] = w_in[:, 4096 + 128 * g:4096 + 128 * (g + 1)]
        w_ssd[g, :, 640:1024] = w_in[:, 512 + 384 * g:512 + 384 * (g + 1)]
        w_ssd[g, :, 1024:1030] = w_in[:, 4608 + 6 * g:4608 + 6 * (g + 1)]
        w_out_ssd[g] = w_out[512 + 384 * g:512 + 384 * (g + 1)]
    conv_w = A("ssd_conv_w")[0]
    conv_b = A("ssd_conv_b")[0]
    cwb = np.empty((128, 4, 5, 5), f32)
    for g in range(4):
        for cc in range(5):
            c0 = _chan(g, cc)
            cwb[:, g, cc, 0:4] = conv_w[:, c0:c0 + 128].T
            cwb[:, g, cc, 4] = conv_b[c0:c0 + 128]
    hp = np.concatenate([A("ssd_dt_bias")[0], A("ssd_a_log")[0], A("ssd_d")[0]])[None, :].repeat(128, 0)
    nbc = A("ssd_norm")[0][None, :].repeat(128, 0)
    sconv = A("state_conv")[0]
    sssd = A("state_ssd")[0]
    shared = {
        "normw": normw_l,
        "ffn1_wg": np.ascontiguousarray(A("ffn1_w_gate")[0]), "ffn1_wu": np.ascontiguousarray(A("ffn1_w_up")[0]),
        "ffn1_wd": np.ascontiguousarray(A("ffn1_w_down")[0]),
        "ffn2_wg": np.ascontiguousarray(A("ffn2_w_gate")[0]), "ffn2_wu": np.ascontiguousarray(A("ffn2_w_up")[0]),
        "ffn2_wd": np.ascontiguousarray(A("ffn2_w_down")[0]),
        "cmat": _consts(), "hp": np.ascontiguousarray(hp), "nbc": np.ascontiguousarray(nbc),
        "w_ssd": w_ssd, "w_out_ssd": w_out_ssd, "cwb": cwb,
    }
    shared.update(_s5_host(inp))
    in_maps = []
    for c in range(NCORES):
        bs = slice(16 * c, 16 * c + 16)
        xc = np.concatenate([xp[c], xs[bs].reshape(TS, D)], axis=0)
        m = dict(shared)
        m["xT"] = np.ascontiguousarray(xc.T)
        cv0 = np.empty((128, 4, 5, 16, 3), f32)
        for g in range(4):
            for cc in range(5):
                c0 = _chan(g, cc)
                cv0[:, g, cc] = sconv[bs, :, c0:c0 + 128].transpose(2, 0, 1)
        m["conv0"] = cv0.reshape(128, 4, 5, 48)
        st = sssd[bs].reshape(16, 4, 6, 64, 128).transpose(0, 1, 4, 2, 3).reshape(16, 4, 128, 384)
        m["ssd0"] = np.ascontiguousarray(st)
        m.update(_s5_core(inp, c))
        in_maps.append(m)
    if "nc" not in _NC_CACHE:
        _NC_CACHE["nc"] = build()
    nc = _NC_CACHE["nc"]
    res = run_bass_kernel_spmd(nc, in_maps, core_ids=list(range(NCORES)))
    R = res.results
    yp = np.empty((8, TP, D), f32)
    ys = np.empty((128, 4, D), f32)
    ssd_p = np.empty((1, 8, 24, 64, 128), f32)
    ssd_s = np.empty((1, 128, 24, 64, 128), f32)
    conv_p = np.empty((1, 8, 3, 2560), f32)
    conv_s = np.empty((1, 128, 3, 2560), f32)
    s5p = np.empty((2, 1, 8, 32, 64), f32)
    s5s = np.empty((2, 1, 128, 32, 64), f32)
    for c in range(NCORES):
        bs = slice(16 * c, 16 * c + 16)
        y = R[c]["yT"].T
        yp[c] = y[:TP]
        ys[bs] = y[TP:].reshape(16, 4, D)
        ssd_p[0, c] = R[c]["ssdp"].reshape(4, 128, 6, 64).transpose(0, 2, 3, 1).reshape(24, 64, 128)
        ssd_s[0, bs] = R[c]["ssds"].reshape(16, 4, 128, 6, 64).transpose(0, 1, 3, 4, 2).reshape(16, 24, 64, 128)
        cp = R[c]["convp"]
        cs = R[c]["convs"].reshape(128, 4, 5, 16, 3)
        for g in range(4):
            for cc in range(5):
                c0 = _chan(g, cc)
                conv_p[0, c, :, c0:c0 + 128] = cp[:, g, cc, :].T
                conv_s[0, bs, :, c0:c0 + 128] = cs[:, g, cc].transpose(1, 2, 0)
        a = R[c]["s5p"]
        s5p[:, 0, c] = a.reshape(2, 64, 2, 16).transpose(2, 3, 0, 1).reshape(2, 32, 64)
        b_ = R[c]["s5s"]
        s5s[:, 0, bs] = b_.reshape(2, 64, 2, 16, 16).transpose(2, 4, 3, 0, 1).reshape(2, 16, 32, 64)
    return (yp, ys, s5p[0], s5p[1], ssd_p, conv_p, s5s[0], s5s[1], ssd_s, conv_s)
```

```python
import os
import numpy as np
import concourse.bass as bass
import concourse.mybir as mybir
from concourse.bass_utils import run_bass_kernel_spmd

F32 = mybir.dt.float32
BF16 = mybir.dt.bfloat16
AF = mybir.ActivationFunctionType
ALU = mybir.AluOpType

NCORES = 8
D = 1024
DFF = 2816
TP = 2048
TS = 64
NT = TP + TS
EPS = 1e-6
NDMA = 24
NSW = 64


class Sched:
    def __init__(self, nc):
        self.nc = nc
        self.e = {'pe': nc.tensor, 'act': nc.scalar, 'dve': nc.vector, 'pool': nc.gpsimd, 'sp': nc.sync}
        self.sem = {k: nc.alloc_semaphore('s_' + k) for k in self.e}
        self.cnt = {k: 0 for k in self.e}
        self.waited = {}
        self.lastw = {}
        self.readers = {}
        self.dsem = [nc.alloc_semaphore('d%d' % i) for i in range(NDMA + NSW)]
        self.duse = [0] * (NDMA + NSW)
        self.dnext = 0
        self.swnext = NDMA
        self.nwait = 0
        self.nins = 0
        self._rec = None

    def _semobj(self, sk):
        return self.sem[sk] if isinstance(sk, str) else self.dsem[sk[1]]

    def _deps(self, r, w):
        deps = {}

        def add(tok):
            sk, v = tok
            if deps.get(sk, 0) < v:
                deps[sk] = v
        for k in r:
            if k in self.lastw:
                add(self.lastw[k])
        for k in w:
            if k in self.lastw:
                add(self.lastw[k])
            for sk, v in self.readers.get(k, {}).items():
                add((sk, v))
        return deps

    def _wait(self, eng, deps, attach=False):
        need = []
        for sk, v in deps.items():
            if sk == eng:
                if eng == 'pe':
                    continue
                assert v <= self.cnt[eng], "same-engine dep on non-inc instruction"
            if self.waited.get((eng, sk), 0) >= v:
                continue
            need.append((sk, v))
            self.waited[(eng, sk)] = v
        pend = None
        if attach and need and eng != 'pe':
            pend = need.pop()
        for sk, v in need:
            self.e[eng].wait_ge(self._semobj(sk), v)
            self.nwait += 1
        return pend

    def _record(self, tok, r, w):
        sk, v = tok
        for k in r:
            d = self.readers.setdefault(k, {})
            if d.get(sk, 0) < v:
                d[sk] = v
        for k in w:
            self.lastw[k] = tok
            self.readers[k] = {}

    @staticmethod
    def _split(r, w):
        def isps(k):
            return (isinstance(k, tuple) and k[0] == 'ps') or (isinstance(k, str) and k.startswith('ps'))
        r2 = [k for k in r if not isps(k)]
        w2 = list(w) + [k for k in r if isps(k)]
        return r2, w2

    def record(self, f):
        self._rec = []
        f()
        out, self._rec = self._rec, None
        return out

    def replay(self, items):
        for it in items:
            if it[0] == 'op':
                self.op(*it[1:])
            else:
                self.dma(*it[1:-1], **it[-1])

    @staticmethod
    def merge(a, b):
        out = []
        ia = ib = 0
        na, nb = len(a), len(b)
        while ia < na or ib < nb:
            if ib >= nb or (ia < na and ia * nb <= ib * na):
                out.append(a[ia])
                ia += 1
            else:
                out.append(b[ib])
                ib += 1
        return out

    def op(self, eng, fn, r=(), w=(), inc=True):
        if self._rec is not None:
            self._rec.append(('op', eng, fn, tuple(r), tuple(w), inc))
            return
        r, w = self._split(r, w)
        pend = self._wait(eng, self._deps(r, w), attach=True)
        ins = fn(self.e[eng])
        if pend is not None:
            ins._wait_ge(self._semobj(pend[0]), pend[1])
        self.nins += 1
        if inc:
            self.cnt[eng] += 1
            ins.then_inc(self.sem[eng], 1)
            tok = (eng, self.cnt[eng])
        else:
            tok = (eng, self.cnt[eng] + 1)
        self._record(tok, r, w)

    def dma(self, q, out, in_, r=(), w=(), **kw):
        if self._rec is not None:
            self._rec.append(('dma', q, out, in_, tuple(r), tuple(w), kw))
            return
        if q == 'pool':
            i = self.swnext
            self.swnext += 1
            assert i < NDMA + NSW, "out of SW-DMA semaphores"
        else:
            i = self.dnext
            self.dnext = (self.dnext + 1) % NDMA
        deps = self._deps(r, w)
        if self.duse[i] > 0:
            sk = ('d', i)
            deps[sk] = max(deps.get(sk, 0), self.duse[i] * 16)
        pend = self._wait(q, deps, attach=(q == 'sp'))
        ins = self.e[q].dma_start(out=out, in_=in_, **kw)
        if pend is not None:
            ins._wait_ge(self._semobj(pend[0]), pend[1])
        self.duse[i] += 1
        ins.then_inc(self.dsem[i], 16)
        self.nins += 1
        self._record((('d', i), self.duse[i] * 16), r, w)

    def barrier(self):
        toks = {k: self.cnt[k] for k in ('pe', 'act', 'dve', 'pool') if self.cnt[k] > 0}
        dt = {('d', i): self.duse[i] * 16 for i in range(NDMA + NSW) if self.duse[i] > 0}
        for eng in ('pe', 'act', 'dve', 'pool', 'sp'):
            deps = {k: v for k, v in toks.items() if k != eng}
            deps.update(dt)
            self._wait(eng, deps)

    def finish(self):
        for i in range(NDMA + NSW):
            if self.duse[i] > 0:
                self.e['sp'].wait_ge(self.dsem[i], self.duse[i] * 16)
        for k in ('pe', 'act', 'dve', 'pool'):
            if self.cnt[k] > 0:
                self.e['sp'].wait_ge(self.sem[k], self.cnt[k])


def token_tiles():
    tiles = [(i * 512, 512) for i in range(TP // 512)]
    tiles.append((TP, TS))
    return tiles


def build(stage=1):
    nc = bass.Bass("TRN2", target_bir_lowering=False)
    S = Sched(nc)

    def din(name, shape, dt=F32):
        return nc.dram_tensor(name, list(shape), dt, kind="ExternalInput").ap()

    def dout(name, shape, dt=F32):
        return nc.dram_tensor(name, list(shape), dt, kind="ExternalOutput").ap()

    def sb(name, shape, dt):
        return nc.alloc_sbuf_tensor(name, list(shape), dt).ap()

    xT = din("xT", [D, NT])
    normw_d = din("normw", [128, 32])
    wg_d = [din("ffn%d_wg" % i, [D, DFF]) for i in (1, 2)]
    wu_d = [din("ffn%d_wu" % i, [D, DFF]) for i in (1, 2)]
    wd_d = [din("ffn%d_wd" % i, [DFF, D]) for i in (1, 2)]
    yT = dout("yT", [D, NT])
    cmat_d = din("cmat", [128, 2048])
    hp_d = din("hp", [128, 72])
    nbc_d = din("nbc", [128, 1536])
    wssd_d = din("w_ssd", [4, D, 1030])
    woutssd_d = din("w_out_ssd", [4, 384, D])
    cwb_d = din("cwb", [128, 4, 5, 5])
    conv0_d = din("conv0", [128, 4, 5, 48])
    ssd0_d = din("ssd0", [16, 4, 128, 384])
    convp_d = dout("convp", [128, 4, 5, 3])
    convs_d = dout("convs", [128, 4, 5, 48])
    ssdp_d = dout("ssdp", [4, 128, 384])
    ssds_d = dout("ssds", [16, 4, 128, 384])
    winu_d = din("w_in_u", [D, 512])
    wouts5_d = din("w_out_s5", [512, D])
    wglu_d = din("w_glu", [512, 512])
    s5v_d = din("s5v", [128, 72])
    lam_d = din("lam", [128, 3, 272])
    bT_d = din("bT", [128, 2, 4, 64])
    emask_d = din("emask", [128, 8])
    cpad_d = din("cpad", [2, 128, 16, 128])
    s5s0_d = din("s5s0", [128, 2, 16, 16])
    s5p_d = dout("s5p", [128, 2, 16])
    s5s_d = dout("s5s", [128, 2, 16, 16])

    x = sb("x", [128, 8, NT], F32)
    h = sb("h", [128, 8, NT], BF16)
    normw = sb("normw_sb", [128, 32], F32)
    ones_bf = sb("ones_bf", [128, 128], BF16)
    epsc = sb("epsc", [128, 1], F32)
    ps = [nc.alloc_psum_tensor("ps%d" % i, [128, 512], F32).ap() for i in range(8)]

    ARENA = 111040
    arena = sb("arena", [128, ARENA // 2], BF16)

    class Carver:
        def __init__(self):
            self.off = 0

        def get(self, shape, dt):
            n = 1
            for s_ in shape[1:]:
                n *= s_
            nb = n * (2 if dt == BF16 else 4)
            nb = (nb + 31) // 32 * 32
            assert self.off + nb <= ARENA, ("arena overflow", self.off + nb)
            ap = arena[:shape[0], self.off // 2:(self.off + nb) // 2]
            self.off += nb
            if dt != BF16:
                ap = ap.bitcast(dt)
            ap = ap[:, :n]
            if len(shape) == 3:
                ap = ap.rearrange("p (a b) -> p a b", a=shape[1])
            elif len(shape) == 4:
                ap = ap.rearrange("p (a b c) -> p a b c", a=shape[1], b=shape[2])
            return ap

    GMAX = 6
    cv = Carver()
    wg_s = [cv.get([128, 8, GMAX * 128], BF16) for i in range(2)]
    wu_s = [cv.get([128, 8, GMAX * 128], BF16) for i in range(2)]
    wd_s = [cv.get([128, GMAX, D], BF16) for i in range(2)]
    sq = cv.get([128, 8, 512], BF16)
    rstd = [cv.get([128, 512], F32) for i in range(2)]
    sg = [cv.get([128, 512], F32) for i in range(2)]
    actb = [cv.get([128, GMAX, 512], BF16) for i in range(2)]
    ystage = [cv.get([128, 512], F32) for i in range(2)]

    S.op('pool', lambda e: e.memset(ones_bf, 1.0), w=['ones_bf'])
    S.op('pool', lambda e: e.memset(epsc, EPS), w=['epsc'])
    S.dma('sp', normw, normw_d, w=['normw'])
    xT_v = xT.rearrange("(c p) t -> p c t", p=128)
    for c in range(8):
        S.dma('sp', x[:, c, :], xT_v[:, c, :], w=[('x', c, tt) for tt in range(5)])

    TT = token_tiles()

    def rmsnorm(widx, out_fn, tiles=None):
        for tt, (t0, n) in enumerate(TT):
            if tiles is not None and tt not in tiles:
                continue
            for c in range(8):
                S.op('act', lambda e, c=c: e.activation(out=sq[:, c, :n], in_=x[:, c, t0:t0 + n], func=AF.Square),
                     r=[('x', c, tt)], w=[('sq', c)])
            for c in range(8):
                S.op('pe', lambda e, c=c: e.matmul(ps[0][:, :n], ones_bf, sq[:, c, :n], start=(c == 0), stop=(c == 7)),
                     r=[('sq', c), 'ones_bf'], w=[('ps', 0)], inc=(c == 7))
            rb = rstd[tt % 2]
            S.op('act', lambda e: e.activation(out=rb[:, :n], in_=ps[0][:, :n], func=AF.Sqrt, bias=epsc[:, 0:1], scale=1.0 / D),
                 r=[('ps', 0), 'epsc'], w=[('rstd', tt % 2)])
            S.op('dve', lambda e: e.reciprocal(out=rb[:, :n], in_=rb[:, :n]), r=[('rstd', tt % 2)], w=[('rstd', tt % 2)])
            for c in range(8):
                out_fn(tt, c, t0, n, rb)

    def norm_to_h(widx, tiles=None):
        def f(tt, c, t0, n, rb):
            S.op('dve', lambda e: e.scalar_tensor_tensor(out=h[:, c, t0:t0 + n], in0=x[:, c, t0:t0 + n],
                                                         scalar=normw[:, widx * 8 + c:widx * 8 + c + 1], in1=rb[:, :n],
                                                         op0=ALU.mult, op1=ALU.mult),
                 r=[('x', c, tt), ('rstd', tt % 2), 'normw'], w=[('h', c, tt)])
        rmsnorm(widx, f, tiles)

    def ffn(fi, after_tile=None):
        Wg = wg_d[fi].rearrange("(kc p) f -> p kc f", p=128)
        Wu = wu_d[fi].rearrange("(kc p) f -> p kc f", p=128)
        Wd = wd_d[fi].rearrange("(fc p) d -> p fc d", p=128)
        groups = [(0, 6), (6, 6), (12, 6), (18, 4)]

        def load(gi):
            c0, G = groups[gi]
            b = gi % 2
            for kc in range(0, 8, 4):
                S.dma('pool', wg_s[b][:, kc:kc + 4, :G * 128], Wg[:, kc:kc + 4, c0 * 128:(c0 + G) * 128], w=[('wg', b, kc)])
                S.dma('pool', wu_s[b][:, kc:kc + 4, :G * 128], Wu[:, kc:kc + 4, c0 * 128:(c0 + G) * 128], w=[('wu', b, kc)])
            S.dma('pool', wd_s[b][:, :G, :], Wd[:, c0:c0 + G, :], w=[('wd', b)])
        load(0)
        pcount = 0
        ocount = 0
        for gi, (c0, G) in enumerate(groups):
            b = gi % 2
            if gi + 1 < len(groups):
                load(gi + 1)
            for tt, (t0, n) in enumerate(TT):
                ab = (gi * len(TT) + tt) % 2
                for fc in range(G):
                    pgi = pcount % 2
                    pui = 2 + pcount % 2
                    pcount += 1
                    for kc in range(8):
                        S.op('pe', lambda e, kc=kc: e.matmul(ps[pgi][:, :n], wg_s[b][:, kc, fc * 128:(fc + 1) * 128], h[:, kc, t0:t0 + n],
                                                             start=(kc == 0), stop=(kc == 7)),
                             r=[('wg', b, kc // 4 * 4), ('h', kc, tt)], w=[('ps', pgi)], inc=(kc == 7))
                    for kc in range(8):
                        S.op('pe', lambda e, kc=kc: e.matmul(ps[pui][:, :n], wu_s[b][:, kc, fc * 128:(fc + 1) * 128], h[:, kc, t0:t0 + n],
                                                             start=(kc == 0), stop=(kc == 7)),
                             r=[('wu', b, kc // 4 * 4), ('h', kc, tt)], w=[('ps', pui)], inc=(kc == 7))
                    sgb = sg[pcount % 2]
                    S.op('act', lambda e: e.activation(out=sgb[:, :n], in_=ps[pgi][:, :n], func=AF.Silu),
                         r=[('ps', pgi)], w=[('sg', pcount % 2)])
                    S.op('dve', lambda e: e.tensor_tensor(out=actb[ab][:, fc, :n], in0=ps[pui][:, :n], in1=sgb[:, :n], op=ALU.mult),
                         r=[('ps', pui), ('sg', pcount % 2)], w=[('actb', ab, fc)])
                for dc in range(8):
                    poi = 4 + ocount % 4
                    ocount += 1
                    for fc in range(G):
                        S.op('pe', lambda e, fc=fc: e.matmul(ps[poi][:, :n], wd_s[b][:, fc, dc * 128:(dc + 1) * 128], actb[ab][:, fc, :n],
                                                             start=(fc == 0), stop=(fc == G - 1)),
                             r=[('wd', b), ('actb', ab, fc)], w=[('ps', poi)], inc=(fc == G - 1))
                    S.op('dve', lambda e: e.scalar_tensor_tensor(out=x[:, dc, t0:t0 + n], in0=ps[poi][:, :n], scalar=0.5,
                                                                 in1=x[:, dc, t0:t0 + n], op0=ALU.mult, op1=ALU.add),
                         r=[('ps', poi), ('x', dc, tt)], w=[('x', dc, tt)])
                if after_tile is not None and gi == len(groups) - 1:
                    after_tile(tt)

    def TTo(eng, out, a, b, op, r, w):
        S.op(eng, lambda e: e.tensor_tensor(out=out, in0=a, in1=b, op=op), r=r, w=w)

    def TSo(eng, out, a, s1, op0, r, w, s2=None, op1=None):
        if op1 is None:
            S.op(eng, lambda e: e.tensor_scalar(out=out, in0=a, scalar1=s1, scalar2=None, op0=op0), r=r, w=w)
        else:
            S.op(eng, lambda e: e.tensor_scalar(out=out, in0=a, scalar1=s1, scalar2=s2, op0=op0, op1=op1), r=r, w=w)

    def STTo(eng, out, in0, sc, in1, op0, op1, r, w):
        S.op(eng, lambda e: e.scalar_tensor_tensor(out=out, in0=in0, scalar=sc, in1=in1, op0=op0, op1=op1), r=r, w=w)

    def ACTo(out, in_, func, r, w, **kw):
        S.op('act', lambda e: e.activation(out=out, in_=in_, func=func, **kw), r=r, w=w)

    def MMo(out, lhsT, rhs, r, w, start=True, stop=True, inc=True):
        S.op('pe', lambda e: e.matmul(out, lhsT, rhs, start=start, stop=stop), r=r, w=w, inc=inc)

    def TRo(out, in_, ident, r, w, inc=True):
        S.op('pe', lambda e: e.transpose(out, in_, ident), r=r, w=w, inc=inc)

    def CPo(eng, out, in_, r, w):
        if eng == 'act':
            S.op(eng, lambda e: e.activation(out=out, in_=in_, func=AF.Copy), r=r, w=w)
        else:
            S.op(eng, lambda e: e.tensor_copy(out=out, in_=in_), r=r, w=w)

    big_rot = [0]

    def bigbank():
        b = big_rot[0] % 2
        big_rot[0] += 1
        return b

    def ssd_phase():
        S.barrier()
        cv = Carver()
        cm = cv.get([128, 2048], F32)
        S.dma('sp', cm, cmat_d, w=['cm'])
        TI = cm[:, 0:128]
        SU = cm[:, 128:256]
        TIb = cm[:, 256:384]
        SUb = cm[:, 384:512]
        identf = cm[:, 512:640]
        Emat = cm[:, 640:656]
        ident_bf = cv.get([128, 128], BF16)
        CPo('dve', ident_bf, identf, ['cm'], ['ident_bf'])
        ones_f = cv.get([128, 128], F32)
        S.op('pool', lambda e: e.memset(ones_f, 1.0), w=['ones_f'])
        onec = cv.get([128, 1], F32)
        S.op('pool', lambda e: e.memset(onec, 1.0), w=['onec'])
        hp = cv.get([128, 72], F32)
        S.dma('sp', hp, hp_d, w=['hp'])
        abc = cv.get([128, 24], F32)
        ACTo(abc, hp[:, 24:48], AF.Exp, ['hp'], ['abc'])
        TSo('dve', abc, abc, -1.0, ALU.mult, ['abc'], ['abc'])
        cwb = cv.get([128, 4, 5, 5], F32)
        S.dma('sp', cwb, cwb_d, w=['cwb'])

        wssd = cv.get([128, 8, 1030], BF16)
        wout = cv.get([128, 3, 1024], BF16)
        diag = cv.get([128, 5, 4, 128], BF16)
        nbc = cv.get([128, 384], F32)
        xraw = [cv.get([128, 5, 515], BF16) for _ in range(2)]
        xraw_s = cv.get([128, 5, 16, 7], BF16)
        conv0s = cv.get([128, 5, 48], F32)
        convo = cv.get([128, 5, 3], F32)
        convos = cv.get([128, 5, 16, 3], F32)
        fm = cv.get([128, 5, 512], BF16)
        ofm = cv.get([128, 3, 512], BF16)
        dt6 = cv.get([128, 6], F32)
        la = cv.get([128, 6], F32)
        rhsla = cv.get([128, 768], F32)
        rhscd = cv.get([128, 96], F32)
        decT = cv.get([128, 768], BF16)
        MT2 = [cv.get([128, 768], BF16) for _ in range(2)]
        CBm = cv.get([128, 128], BF16)
        ea2 = [cv.get([128, 12], F32) for _ in range(2)]
        cdx2 = [cv.get([128, 96], F32) for _ in range(2)]
        xd2 = [cv.get([128, 6, 64], BF16) for _ in range(2)]
        xdd2 = [cv.get([128, 6, 64], BF16) for _ in range(2)]
        Btm2 = [cv.get([128, 128], BF16) for _ in range(2)]
        Btmm = [cv.get([128, 128], BF16) for _ in range(4)]
        skipx2 = [cv.get([128, 6, 64], F32) for _ in range(2)]
        yv = cv.get([128, 6, 64], F32)
        sz4 = [cv.get([128, 384], F32) for _ in range(4)]
        t6s = [cv.get([128, 6], F32) for _ in range(4)]
        junk = cv.get([128, 384], BF16)
        ss = cv.get([128, 1], F32)
        otm = cv.get([128, 384], BF16)
        hT = cv.get([128, 6, 64], F32)
        htmp = cv.get([128, 6, 64], F32)
        hTb = cv.get([128, 384], BF16)
        hs = [cv.get([128, 6, 64], F32) for _ in range(4)]
        hsb = [cv.get([128, 384], BF16) for _ in range(4)]
        hso = [cv.get([128, 6, 64], F32) for _ in range(4)]
        Cmask = cv.get([128, 16, 64], BF16)
        print("ssd arena used", cv.off)

        def bc6(ap, T):
            return ap[:T, :, None].to_broadcast([T, 6, 64])

        def ssd_pre(g, T, tok0, ci):
            g6 = slice(g * 6, g * 6 + 6)
            bk = 2 if ci % 2 == 0 else 6
            kb = 'ps%d' % bk
            for kc in range(8):
                MMo(ps[bk][:T, 0:390], h[:, kc, tok0:tok0 + T], wssd[:, kc, 640:1030], r=[('h', kc), 'wssd'], w=[kb],
                    start=(kc == 0), stop=(kc == 7), inc=(kc == 7))
            TTo('dve', t6s[ci][:T], ps[bk][:T, 384:390], hp[:T, g6], ALU.add, [kb, 'hp'], [('t6', ci)])
            ACTo(sz4[ci][:T, :], ps[bk][:T, 0:384], AF.Silu, [kb], [('sz4', ci)])

        def ssd_late(g, T, tok0, col0, sample, last, p, tI, sU, MT, xd, xdd, Btm, skipx, sz, ea, cdx,
                     kMT, kxd, kxdd, kBtm, kskipx, ksz, kea, kcdx, yv2):
            for hh in range(6):
                MMo(ps[7][:T, hh * 64:(hh + 1) * 64], MT[:T, hh * T:(hh + 1) * T], xd[:T, hh, :], r=[kMT, kxd], w=['ps7'], inc=(hh == 5))
            if not sample:
                MMo(ps[3][:T, 0:384], fm[:, 4, col0:col0 + T], hTb, r=['fmC', 'hTb'], w=['ps3'])
                MMo(ps[4][:, 0:384], Btm[:T, :], xdd[:T].rearrange("p a b -> p (a b)"), r=[kBtm, kxdd], w=['ps4'])
            else:
                TTo('pool', Cmask, fm[:, 4, None, col0:col0 + T].to_broadcast([128, 16, T]),
                    cm[:, 1024:2048].rearrange("p (a b) -> p a b", a=16), ALU.mult, ['fmC', 'cm'], ['Cmask'])
                dh_banks = [(4, 'ps4'), (0, ('ps', 0)), (1, ('ps', 1)), (2, 'ps2')]
                for b4 in range(0, 16, 4):
                    bs_ = list(range(b4, b4 + 4))
                    for b in bs_:
                        rb_ = b % 4
                        S.dma('sp', hs[rb_].rearrange("p a b -> p (a b)"), ssd0_d[b, g], w=[('hs', rb_)])
                    for b in bs_:
                        rb_ = b % 4
                        CPo('act', hsb[rb_], hs[rb_].rearrange("p a b -> p (a b)"), [('hs', rb_)], [('hsb', rb_)])
                        TSo('dve', Btmm[rb_][:T, :], Btm[:T, :], Emat[:T, b:b + 1], ALU.mult, [kBtm, 'cm'], [('Btmm', rb_)])
                    for b in bs_:
                        rb_ = b % 4
                        MMo(ps[3][:T, 0:384], Cmask[:, b, :], hsb[rb_], r=['Cmask', ('hsb', rb_)], w=['ps3'], start=(b == 0), stop=(b == 15),
                            inc=(b == 15))
                    for b in bs_:
                        rb_ = b % 4
                        bk_, kb_ = dh_banks[rb_]
                        MMo(ps[bk_][:, 0:384], Btmm[rb_][:T, :], xdd[:T].rearrange("p a b -> p (a b)"), r=[('Btmm', rb_), kxdd], w=[kb_])
                    for b in bs_:
                        rb_ = b % 4
                        bk_, kb_ = dh_banks[rb_]
                        TTo('dve', hso[rb_], hs[rb_], cdx[:, b * 6:(b + 1) * 6][:, :, None].to_broadcast([128, 6, 64]), ALU.mult,
                            [('hs', rb_), kcdx], [('hso', rb_)])
                        TTo('dve', hso[rb_].rearrange("p a b -> p (a b)"), hso[rb_].rearrange("p a b -> p (a b)"), ps[bk_][:, 0:384], ALU.add,
                            [('hso', rb_), kb_], [('hso', rb_)])
                        S.dma('sp', ssds_d[b, g], hso[rb_].rearrange("p a b -> p (a b)"), r=[('hso', rb_)], w=[('ssds', b, g)])
            TTo('dve', yv[:T], ps[3][:T, 0:384].rearrange("p (a b) -> p a b", a=6), bc6(ea[:, 0:6], T), ALU.mult, ['ps3', kea], ['yv'])
            TTo('dve', yv2, yv2, ps[7][:T, 0:384], ALU.add, ['yv', 'ps7'], ['yv'])
            TTo('pool', yv[:T], yv[:T], skipx[:T], ALU.add, ['yv', kskipx], ['yv'])
            TTo('pool', yv2, yv2, sz[:T, :], ALU.mult, ['yv', ksz], ['yv'])
            ACTo(junk[:T, :], yv2, AF.Square, ['yv'], ['junk', 'ss'], accum_out=ss[:T, 0:1])
            ACTo(ss[:T, :], ss[:T, :], AF.Ln, ['ss', 'epsc'], ['ss'], bias=epsc[:T, 0:1], scale=1.0 / 384)
            ACTo(ss[:T, :], ss[:T, :], AF.Exp, ['ss'], ['ss'], scale=-0.5)
            STTo('dve', otm[:T, :], yv2, ss[:T, 0:1], nbc[:T, :], ALU.mult, ALU.mult, ['yv', 'ss', 'nbc'], ['otm'])
            if not sample:
                TTo('dve', htmp, hT, cdx[:, 0:6][:, :, None].to_broadcast([128, 6, 64]), ALU.mult, ['hT', kcdx], ['htmp'])
                TTo('dve', hT.rearrange("p a b -> p (a b)"), htmp.rearrange("p a b -> p (a b)"), ps[4][:, 0:384], ALU.add,
                    ['htmp', 'ps4'], ['hT'])
                if last:
                    S.dma('sp', ssdp_d[g], hT.rearrange("p a b -> p (a b)"), r=['hT'], w=[('ssdp', g)])
                else:
                    CPo('act', hTb, hT.rearrange("p a b -> p (a b)"), ['hT'], ['hTb'])
            po = ps[7].bitcast(BF16)[:, 0:512]
            for j in range(3):
                TRo(po[:, j * 128:j * 128 + T], otm[:T, j * 128:(j + 1) * 128], ident_bf[:T, :T], r=['otm', 'ident_bf'], w=['ps7'], inc=(j == 2))
            CPo('act', ofm[:, :, col0:col0 + T], po[:, 0:384].rearrange("p (a b) -> p a b", a=3)[:, :, :T], ['ps7'], ['ofm'])


        def ssd_chunk(g, T, tok0, col0, sample, last, p, part, ci=0):
            g6 = slice(g * 6, g * 6 + 6)
            tI = TIb if sample else TI
            sU = SUb if sample else SU
            MT, xd, xdd, Btm, skipx, sz, ea, cdx = MT2[p], xd2[p], xdd2[p], Btm2[p], skipx2[p], sz4[ci], ea2[p], cdx2[p]
            kMT, kxd, kxdd, kBtm, kskipx, ksz, kea, kcdx = [(nm, p) for nm in ('MT', 'xd', 'xdd', 'Btm', 'skipx', 'sz', 'ea', 'cdx')]
            ksz = ('sz4', ci)
            yv2 = yv[:T].rearrange("p a b -> p (a b)")
            if part == 'late':
                return ssd_late(g, T, tok0, col0, sample, last, p, tI, sU, MT, xd, xdd, Btm, skipx, sz, ea, cdx,
                                kMT, kxd, kxdd, kBtm, kskipx, ksz, kea, kcdx, yv2)
            t6 = t6s[ci]
            kt6 = ('t6', ci)
            ACTo(t6[:T], t6[:T], AF.Exp, [kt6], [kt6])
            ACTo(dt6[:T], t6[:T], AF.Ln, [kt6, 'onec'], ['dt6'], bias=onec[:T, 0:1], scale=1.0)
            TTo('dve', la[:T], dt6[:T], abc[:T, g6], ALU.mult, ['dt6', 'abc'], ['la'])
            rl3 = rhsla[:T, :6 * T].rearrange("p (a b) -> p a b", a=6)
            TTo('pool', rl3, la[:T, :, None].to_broadcast([T, 6, T]), tI[:T, None, :T].to_broadcast([T, 6, T]), ALU.mult,
                ['la', 'cm'], ['rhsla'])
            ncd = 6
            if sample:
                ncd = 96
                TTo('pool', rhscd[:T, :96].rearrange("p (a b) -> p a b", a=16), la[:T, None, :].to_broadcast([T, 16, 6]),
                    Emat[:T, :, None].to_broadcast([T, 16, 6]), ALU.mult, ['la', 'cm'], ['rhscd'])
            MMo(ps[0][:T, :3 * T], sU[:T, :T], rhsla[:T, 0:3 * T], r=['cm', 'rhsla'], w=[('ps', 0)])
            MMo(ps[1][:T, :3 * T], sU[:T, :T], rhsla[:T, 3 * T:6 * T], r=['cm', 'rhsla'], w=[('ps', 1)])
            MMo(ps[5][:T, 0:6], tI[:T, :T], la[:T, :], r=['cm', 'la'], w=['ps5'])
            MMo(ps[5][:T, 6:12], sU[:T, :T], la[:T, :], r=['cm', 'la'], w=['ps5'])
            if sample:
                MMo(ps[5][:, 16:16 + 96], ones_f[:T, :], rhscd[:T, :96], r=['ones_f', 'rhscd'], w=['ps5'])
            else:
                MMo(ps[5][:, 16:22], ones_f[:T, :], la[:T, :], r=['ones_f', 'la'], w=['ps5'])
            ACTo(decT[:T, 0:3 * T], ps[0][:T, :3 * T], AF.Exp, [('ps', 0)], ['decTa'])
            ACTo(decT[:T, 3 * T:6 * T], ps[1][:T, :3 * T], AF.Exp, [('ps', 1)], ['decTb'])
            ACTo(ea[:T, :], ps[5][:T, 0:12], AF.Exp, ['ps5'], [kea])
            ACTo(cdx[:, :ncd], ps[5][:, 16:16 + ncd], AF.Exp, ['ps5'], [kcdx])
            MMo(ps[5][:T, 128:128 + T], fm[:, 3, col0:col0 + T], fm[:, 4, col0:col0 + T], r=['fmB', 'fmC'], w=['ps5'])
            TTo('dve', CBm[:T, :T], ps[5][:T, 128:128 + T], tI[:T, :T], ALU.mult, ['ps5', 'cm'], ['CBm'])
            TTo('dve', MT[:T, :6 * T].rearrange("p (a b) -> p a b", a=6), decT[:T, :6 * T].rearrange("p (a b) -> p a b", a=6),
                CBm[:T, None, :T].to_broadcast([T, 6, T]), ALU.mult, ['decTa', 'decTb', 'CBm'], [kMT])
            pt = ps[6].bitcast(BF16)
            for j in range(4):
                TRo(pt[:T, j * 128:(j + 1) * 128], fm[:, j, col0:col0 + T], ident_bf, r=[('fmx', j) if j < 3 else 'fmB', 'ident_bf'],
                    w=['ps6'], inc=(j == 3))
            xs3 = pt[:T, 0:384].rearrange("p (a b) -> p a b", a=6)
            TTo('dve', xd[:T], xs3, bc6(dt6, T), ALU.mult, ['ps6', 'dt6'], [kxd])
            TTo('dve', xdd[:T], xd[:T], bc6(ea[:, 6:12], T), ALU.mult, [kxd, kea], [kxdd])
            CPo('act', Btm[:T, :], pt[:T, 384:512], ['ps6'], [kBtm])
            TTo('dve', skipx[:T], xs3, bc6(hp[:, 48 + g * 6:54 + g * 6], T), ALU.mult, ['ps6', 'hp'], [kskipx])
            return

        STOP = 0
        KT = os.environ.get('KTILES')
        KG = int(os.environ.get('KG', '4'))
        KCUT = int(os.environ.get('KCUT', '99'))
        for g in range(KG):
            def load_wssd(gg):
                S.dma('pool', wssd[:, 0:4, :], wssd_d[gg].rearrange("(kc p) f -> p kc f", p=128)[:, 0:4, :], w=['wssd'])
                S.dma('pool', wssd[:, 4:8, :], wssd_d[gg].rearrange("(kc p) f -> p kc f", p=128)[:, 4:8, :], w=['wssd'])
            if g == 0 or KT is not None:
                load_wssd(g)
            S.dma('pool', wout, woutssd_d[g].rearrange("(j p) d -> p j d", p=128), w=['wout'])
            S.dma('sp', nbc, nbc_d[:, g * 384:(g + 1) * 384], w=['nbc'])
            S.dma('sp', conv0s, conv0_d[:, g], w=['conv0s'])
            for cc in range(5):
                for k in range(4):
                    TSo('dve', diag[:, cc, k, :], ident_bf, cwb[:, g, cc, k:k + 1], ALU.mult, ['ident_bf', 'cwb'], ['diag'])
            S.op('pool', lambda e: e.memset(hT.rearrange("p a b -> p (a b)"), 0.0), w=['hT'])
            S.op('pool', lambda e: e.memset(hTb, 0.0), w=['hTb'])
            S.op('pool', lambda e: e.memset(xraw[1][:, :, 512:515], 0.0), w=[('xraw', 1)])
            CPo('dve', xraw_s[:, :, :, 0:3], conv0s.rearrange("p c (b k) -> p c b k", b=16), ['conv0s'], ['xraw_s'])
            for tt, (t0, n) in enumerate(TT):
                if KT is not None and str(tt) not in KT.split(','):
                    continue
                sample = (tt == 4)
                xr = xraw[tt % 2]
                for cc in range(5):
                    bk = bigbank()
                    for kc in range(8):
                        MMo(ps[bk][:, :n], wssd[:, kc, cc * 128:(cc + 1) * 128], h[:, kc, t0:t0 + n], r=['wssd', ('h', kc)], w=[('ps', bk)],
                            start=(kc == 0), stop=(kc == 7), inc=(kc == 7))
                    if not sample:
                        CPo('act', xr[:, cc, 3:3 + n], ps[bk][:, :n], [('ps', bk)], [('xraw', tt % 2)])
                        CPo('dve', xr[:, cc, 0:3], xraw[(tt + 1) % 2][:, cc, 512:515], [('xraw', (tt + 1) % 2)], [('xraw', tt % 2)])
                        if tt == 3:
                            CPo('dve', convo[:, cc, :], ps[bk][:, 509:512], [('ps', bk)], ['convo'])
                    else:
                        p3 = ps[bk][:, :64].rearrange("p (b l) -> p b l", b=16)
                        CPo('act', xraw_s[:, cc, :, 3:7], p3, [('ps', bk)], ['xraw_s'])
                        CPo('dve', convos[:, cc, :, :], p3[:, :, 1:4], [('ps', bk)], ['convos'])
                if tt == 3:
                    S.dma('sp', convp_d[:, g], convo, r=['convo'], w=[('convp', g)])
                if sample:
                    S.dma('sp', convs_d[:, g], convos.rearrange("p c b k -> p c (b k)"), r=['convos'], w=[('convs', g)])
                for cc in range(5):
                    bk = bigbank()
                    for k in range(4):
                        if not sample:
                            rhs = xr[:, cc, k:k + n]
                            rk = ('xraw', tt % 2)
                        else:
                            rhs = xraw_s[:, cc, :, k:k + 4]
                            rk = 'xraw_s'
                        MMo(ps[bk][:, :n], diag[:, cc, k, :], rhs, r=['diag', rk], w=[('ps', bk)], start=(k == 0), stop=(k == 3), inc=(k == 3))
                    wk = ('fmx', cc) if cc < 3 else ('fmB' if cc == 3 else 'fmC')
                    ACTo(fm[:, cc, :n], ps[bk][:, :n], AF.Silu, [('ps', bk), 'cwb'], [wk], bias=cwb[:, g, cc, 4:5], scale=1.0)
                if not sample:
                    def ch(ci, part):
                        return S.record(lambda: ssd_chunk(g, 128, t0 + ci * 128, ci * 128, False, (tt == 3 and ci == 3), ci % 2, part, ci))
                    for ci in range(4):
                        ssd_pre(g, 128, t0 + ci * 128, ci)
                    E = [ch(ci, 'early') for ci in range(4)]
                    Lt = [ch(ci, 'late') for ci in range(4)]
                    S.replay(E[0])
                    for ci in range(4):
                        S.replay(S.merge(E[ci + 1] if ci + 1 < 4 else [], Lt[ci]))
                else:
                    ssd_pre(g, 64, t0, 0)
                    if g + 1 < KG and KT is None:
                        load_wssd(g + 1)
                    ssd_chunk(g, 64, t0, 0, True, False, 0, 'early')
                    ssd_chunk(g, 64, t0, 0, True, False, 0, 'late')
                for dc in range(8):
                    bk = bigbank()
                    for j in range(3):
                        MMo(ps[bk][:, :n], wout[:, j, dc * 128:(dc + 1) * 128], ofm[:, j, :n], r=['wout', 'ofm'], w=[('ps', bk)],
                            start=(j == 0), stop=(j == 2), inc=(j == 2))
                    TTo('dve', x[:, dc, t0:t0 + n], x[:, dc, t0:t0 + n], ps[bk][:, :n], ALU.add, [('ps', bk), ('x', dc, tt)], [('x', dc, tt)])


    def s5_phase():
        S.barrier()
        cv = Carver()
        I32 = mybir.dt.int32
        PI = 3.14159265358979
        winu = cv.get([128, 8, 512], BF16)
        wouts5 = cv.get([128, 4, 1024], BF16)
        wglu = cv.get([128, 4, 512], BF16)
        lhsB = [cv.get([128, 4, 8, 64], BF16) for _ in range(2)]
        lhsC = [cv.get([128, 16, 128], BF16) for _ in range(2)]
        u_all = cv.get([128, 4, NT], BF16)
        S.dma('pool', winu, winu_d.rearrange("(kc p) f -> p kc f", p=128), w=['winu'])
        S.dma('pool', wouts5, wouts5_d.rearrange("(kc p) f -> p kc f", p=128), w=['wouts5'])
        S.dma('pool', wglu, wglu_d.rearrange("(kc p) f -> p kc f", p=128), w=['wglu'])
        for c in range(2):
            S.dma('pool', lhsC[c], cpad_d[c], w=[('lhsC', c)])
        TSo('dve', lhsC[1], lhsC[1], -1.0, ALU.mult, [('lhsC', 1)], [('lhsC', 1)])
        for tt, (t0, n) in enumerate(TT):
            for kt in range(4):
                bk = bigbank()
                for kc in range(8):
                    MMo(ps[bk][:, :n], winu[:, kc, kt * 128:(kt + 1) * 128], h[:, kc, t0:t0 + n], r=['winu'], w=[('ps', bk)],
                        start=(kc == 0), stop=(kc == 7), inc=(kc == 7))
                CPo('act', u_all[:, kt, t0:t0 + n], ps[bk][:, :n], [('ps', bk)], [('u', tt)])
        S.barrier()
        W = 272
        s5v = cv.get([128, 72], F32)
        S.dma('sp', s5v, s5v_d, w=['s5v'])
        emk = cv.get([128, 8], F32)
        S.dma('sp', emk, emask_d, w=['emk'])
        iot = cv.get([128, 256], F32)
        S.dma('sp', iot, cmat_d[:, 768:1024], w=['iot'])
        st0 = cv.get([128, 2, 16, 16], F32)
        S.dma('sp', st0, s5s0_d, w=['st0'])
        mag = cv.get([128, W], F32)
        carry = cv.get([128, 2, 16], F32)
        carry_s = cv.get([128, 2, 16, 16], F32)
        off_pre = cv.off
        lam = cv.get([128, 3, W], F32)
        S.dma('sp', lam, lam_d, w=['lam'])
        bT = cv.get([128, 2, 4, 64], F32)
        S.dma('sp', bT, bT_d, w=['bT'])
        pre = [cv.get([128, W], F32) for _ in range(11)]
        stp, xr, ang, sn, cs, ar, ai, t1, t2, cr, ci = pre
        ki = cv.get([128, 512], I32)
        kf = cv.get([128, 512], F32)
        rr = cv.get([128, 512], F32)

        def sin_of(out, a, n, shift, key):
            S.op('dve', lambda e: e.tensor_scalar(out=ki[:, :n], in0=a, scalar1=shift, scalar2=1.0 / (2 * PI), op0=ALU.add, op1=ALU.mult),
                 r=[key], w=['ki'])
            CPo('dve', kf[:, :n], ki[:, :n], ['ki'], ['kf'])
            STTo('dve', rr[:, :n], kf[:, :n], -2 * PI, a, ALU.mult, ALU.add, ['kf', key], ['rr'])
            S.op('dve', lambda e: e.tensor_scalar(out=rr[:, :n], in0=rr[:, :n], scalar1=shift, scalar2=-3.1415925, op0=ALU.add, op1=ALU.max),
                 r=['rr'], w=['rr'])
            TSo('dve', rr[:, :n], rr[:, :n], 3.1415925, ALU.min, ['rr'], ['rr'])
            ACTo(out, rr[:, :n], AF.Sin, ['rr'], [key + '_o'])

        ACTo(stp, lam[:, 2, :], AF.Exp, ['lam'], ['pre'])
        TTo('dve', xr, lam[:, 0, :], stp, ALU.mult, ['lam', 'pre'], ['pre'])
        TTo('dve', ang, lam[:, 1, :], stp, ALU.mult, ['lam', 'pre'], ['ang'])
        ACTo(mag, xr, AF.Exp, ['pre'], ['pre'])
        sin_of(sn, ang, W, 0.0, 'ang')
        sin_of(cs, ang, W, PI / 2, 'ang')
        P_ = ['pre', 'ang_o', 'lam']
        TTo('dve', ar, mag, cs, ALU.mult, P_, ['pre'])
        TTo('dve', ai, mag, sn, ALU.mult, P_, ['pre'])
        TTo('dve', t1, lam[:, 0, :], lam[:, 0, :], ALU.mult, P_, ['pre'])
        TTo('dve', t2, lam[:, 1, :], lam[:, 1, :], ALU.mult, P_, ['pre'])
        TTo('dve', t1, t1, t2, ALU.add, P_, ['pre'])
        S.op('dve', lambda e: e.reciprocal(out=t1, in_=t1), r=P_, w=['pre'])
        TSo('dve', t2, ar, -1.0, ALU.add, P_, ['pre'])
        TTo('dve', cr, t2, lam[:, 0, :], ALU.mult, P_, ['pre'])
        TTo('dve', ci, ai, lam[:, 1, :], ALU.mult, P_, ['pre'])
        TTo('dve', cr, cr, ci, ALU.add, P_, ['pre'])
        TTo('dve', cr, cr, t1, ALU.mult, P_, ['pre'])
        TTo('dve', ci, ai, lam[:, 0, :], ALU.mult, P_, ['pre'])
        TTo('dve', t2, t2, lam[:, 1, :], ALU.mult, P_, ['pre'])
        TTo('dve', ci, ci, t2, ALU.subtract, P_, ['pre'])
        TTo('dve', ci, ci, t1, ALU.mult, P_, ['pre'])
        bb = [cv.get([128, 4, 64], F32) for _ in range(2)]
        tb = cv.get([128, 4, 64], F32)
        cr3 = cr[:, 0:256].rearrange("p (a b) -> p a b", a=4)
        ci3 = ci[:, 0:256].rearrange("p (a b) -> p a b", a=4)
        TTo('dve', bb[0], cr3, bT[:, 0], ALU.mult, P_ + ['bT'], ['bb'])
        TTo('dve', tb, ci3, bT[:, 1], ALU.mult, P_ + ['bT'], ['tb'])
        TTo('dve', bb[0], bb[0], tb, ALU.subtract, ['bb', 'tb'], ['bb'])
        TTo('dve', bb[1], cr3, bT[:, 1], ALU.mult, P_ + ['bT'], ['bb'])
        TTo('dve', tb, ci3, bT[:, 0], ALU.mult, P_ + ['bT'], ['tb'])
        TTo('dve', bb[1], bb[1], tb, ALU.add, ['bb', 'tb'], ['bb'])
        for c in range(2):
            for kt in range(4):
                TTo('dve', lhsB[c][:, kt], bb[c][:, kt, None, :].to_broadcast([128, 8, 64]), emk[:, :, None].to_broadcast([128, 8, 64]),
                    ALU.mult, ['bb', 'emk'], [('lhsB', c)])
        tabf = h.rearrange("p a b -> p (a b)").bitcast(F32)
        Ec = tabf[:, 0:4096].rearrange("p (a b) -> p a b", a=16)
        Es = tabf[:, 4096:8192].rearrange("p (a b) -> p a b", a=16)
        angt = cv.get([128, 2, 256], F32)
        for i0 in range(0, 16, 2):
            TTo('dve', angt, ang[:, 256 + i0:258 + i0][:, :, None].to_broadcast([128, 2, 256]), iot[:, None, :].to_broadcast([128, 2, 256]),
                ALU.mult, ['ang', 'iot'], ['angt'])
            af = angt.rearrange("p a b -> p (a b)")
            sin_of(Es[:, i0:i0 + 2, :].rearrange("p a b -> p (a b)"), af, 512, 0.0, 'angt')
            sin_of(Ec[:, i0:i0 + 2, :].rearrange("p a b -> p (a b)"), af, 512, PI / 2, 'angt')
        TAB = ['angt_o']
        S.barrier()
        cv.off = off_pre
        S.op('pool', lambda e: e.memset(carry.rearrange("p a b -> p (a b)"), 0.0), w=['carry'])
        tmA = [cv.get([128, 2, 256], F32) for _ in range(4)]
        wreg = winu.rearrange("p a b -> p (a b)").bitcast(F32)
        tmB = [wreg[:, k * 512:(k + 1) * 512].rearrange("p (a b) -> p a b", a=2) for k in range(4)]
        tm2 = [tmA, tmB]
        wv2 = [[cv.get([128, 2, 256], F32) for _ in range(2)] for _ in range(2)]
        Wv2 = [[cv.get([128, 2, 256], F32) for _ in range(2)] for _ in range(2)]
        sv2 = [[cv.get([128, 2, 256], F32) for _ in range(2)] for _ in range(2)]
        rt_single = cv.get([128, 256], F32)
        rt2 = [rt_single, rt_single]
        hist2 = [cv.get([128, 2, 512], BF16) for _ in range(2)]
        y5 = cv.get([128, 512], F32)
        g1 = cv.get([128, 512], F32)
        v_bf = cv.get([128, 4, 512], BF16)
        gs = g1
        o5 = cv.get([128, 4, 512], BF16)
        print("s5 arena used", cv.off)
        magq = mag[:, 256:272]

        def bct(tab, i, nseg, L):
            return tab[:, i, None, 0:L].to_broadcast([128, nseg, L])

        for tt, (t0, n) in enumerate(TT):
            sample = (tt == 4)
            nseg, L = (16, 4) if sample else (2, 256)

            def v3(ap):
                return ap.rearrange("p a b -> p (a b)")[:, :nseg * L].rearrange("p (a b) -> p a b", a=nseg)
            for kt in range(4):
                cb = 4 if kt % 2 == 0 else 7
                kcb = 'ps%d' % cb
                for jp in (0, 2):
                    ctxs = []
                    for j in (jp, jp + 1):
                        i = 4 * kt + j
                        par = i % 2
                        pre_b, pim_b = (2, 3) if par == 0 else (5, 6)
                        ctxs.append(dict(i=i, j=j, par=par, tm=tm2[par], wv=wv2[par], Wv=Wv2[par], sv=sv2[par], hist=hist2[par],
                                         pre_b=pre_b, pim_b=pim_b, kre='ps%d' % pre_b, kim='ps%d' % pim_b))
                    for cx in ctxs:
                        i, j = cx['i'], cx['j']
                        lB = [lhsB[c].rearrange("p a b c -> p a (b c)")[:, kt, j * 128:(j + 1) * 128] for c in range(2)]
                        MMo(ps[cx['pre_b']][:, :n], lB[0], u_all[:, kt, t0:t0 + n], r=[('lhsB', 0), ('u', tt)], w=[cx['kre']])
                        MMo(ps[cx['pim_b']][:, :n], lB[1], u_all[:, kt, t0:t0 + n], r=[('lhsB', 1), ('u', tt)], w=[cx['kim']])
                    for cx in ctxs:
                        i, par, tm = cx['i'], cx['par'], cx['tm']
                        Pre = ps[cx['pre_b']][:, :n].rearrange("p (a b) -> p a b", a=nseg)
                        Pim = ps[cx['pim_b']][:, :n].rearrange("p (a b) -> p a b", a=nseg)
                        ec, es = bct(Ec, i, nseg, L), bct(Es, i, nseg, L)
                        TTo('dve', v3(tm[0]), Pre, ec, ALU.mult, [cx['kre']] + TAB, [('tm', par, 0)])
                        TTo('dve', v3(tm[1]), Pim, es, ALU.mult, [cx['kim']] + TAB, [('tm', par, 1)])
                        TTo('dve', v3(tm[2]), Pim, ec, ALU.mult, [cx['kim']] + TAB, [('tm', par, 2)])
                        TTo('dve', v3(tm[3]), Pre, es, ALU.mult, [cx['kre']] + TAB, [('tm', par, 3)])
                    for cx in ctxs:
                        par, tm, wv = cx['par'], cx['tm'], cx['wv']
                        TTo('pool', v3(wv[0]), v3(tm[0]), v3(tm[1]), ALU.add, [('tm', par, 0), ('tm', par, 1)], [('wv', par, 0)])
                        TTo('pool', v3(wv[1]), v3(tm[2]), v3(tm[3]), ALU.subtract, [('tm', par, 2), ('tm', par, 3)], [('wv', par, 1)])
                    groups = [list(range(16))] if sample else [[0], [1]]
                    for grp in groups:
                        g0, g1_ = grp[0], grp[-1] + 1
                        for cx in ctxs:
                            i, par, wv, Wv, sv = cx['i'], cx['par'], cx['wv'], cx['Wv'], cx['sv']
                            rbc = magq[:, i:i + 1].to_broadcast([128, L])
                            if sample:
                                rt64 = tm2[par][0].rearrange("p a b -> p (a b)")[:, 64:128]
                                TSo('dve', rt64, s5v[:, 8:72], magq[:, i:i + 1], ALU.mult, ['s5v', 'pre', ('tm', par, 0)], [('rt64', par)])
                                for c in range(2):
                                    w0 = v3(wv[c])[:, :, 0]
                                    STTo('dve', w0, st0[:, c, i, :], magq[:, i:i + 1], w0, ALU.mult, ALU.add, ['st0', 'pre', ('wv', par, c)], [('wv', par, c)])
                                    wf = wv[c].rearrange("p a b -> p (a b)")[:, 0:64]
                                    Wf = Wv[c].rearrange("p a b -> p (a b)")[:, 0:64]
                                    S.op('dve', lambda e: e.tensor_tensor_scan(out=Wf, data0=rt64, data1=wf, initial=0.0, op0=ALU.mult, op1=ALU.add),
                                         r=[('rt64', par), ('wv', par, c)], w=[('Wv', par, c)])
                                continue
                            for sg_ in grp:
                                for c in range(2):
                                    if sample:
                                        init = st0[:, c, i, sg_:sg_ + 1]
                                        ik = 'st0'
                                    elif sg_ == 0:
                                        init = carry[:, c, i:i + 1]
                                        ik = 'carry'
                                    else:
                                        init = sv[c][:, 0, 255:256]
                                        ik = ('sv', par, c)
                                    S.op('dve', lambda e: e.tensor_tensor_scan(out=v3(Wv[c])[:, sg_, :], data0=rbc, data1=v3(wv[c])[:, sg_, :],
                                                                               initial=init, op0=ALU.mult, op1=ALU.add),
                                         r=['pre', ('wv', par, c), ik], w=[('Wv', par, c)])
                        for cx in ctxs:
                            i, par, tm, Wv = cx['i'], cx['par'], cx['tm'], cx['Wv']
                            ecg = Ec[:, i, None, 0:L].to_broadcast([128, g1_ - g0, L])
                            esg = Es[:, i, None, 0:L].to_broadcast([128, g1_ - g0, L])
                            TTo('dve', v3(tm[0])[:, g0:g1_], v3(Wv[0])[:, g0:g1_], ecg, ALU.mult, [('Wv', par, 0)] + TAB, [('tm', par, 0)])
                            TTo('dve', v3(tm[1])[:, g0:g1_], v3(Wv[1])[:, g0:g1_], esg, ALU.mult, [('Wv', par, 1)] + TAB, [('tm', par, 1)])
                        for cx in ctxs:
                            i, par, tm, Wv, sv = cx['i'], cx['par'], cx['tm'], cx['Wv'], cx['sv']
                            ecg = Ec[:, i, None, 0:L].to_broadcast([128, g1_ - g0, L])
                            esg = Es[:, i, None, 0:L].to_broadcast([128, g1_ - g0, L])
                            TTo('pool', v3(tm[2])[:, g0:g1_], v3(Wv[1])[:, g0:g1_], ecg, ALU.mult, [('Wv', par, 1)] + TAB, [('tm', par, 2)])
                            TTo('pool', v3(tm[3])[:, g0:g1_], v3(Wv[0])[:, g0:g1_], esg, ALU.mult, [('Wv', par, 0)] + TAB, [('tm', par, 3)])
                            TTo('pool', v3(sv[0])[:, g0:g1_], v3(tm[0])[:, g0:g1_], v3(tm[1])[:, g0:g1_], ALU.subtract,
                                [('tm', par, 0), ('tm', par, 1)], [('sv', par, 0)])
                            TTo('pool', v3(sv[1])[:, g0:g1_], v3(tm[2])[:, g0:g1_], v3(tm[3])[:, g0:g1_], ALU.add,
                                [('tm', par, 2), ('tm', par, 3)], [('sv', par, 1)])
                    for cx in ctxs:
                        i, j, par, sv, hist = cx['i'], cx['j'], cx['par'], cx['sv'], cx['hist']
                        for c in range(2):
                            svf = sv[c].rearrange("p a b -> p (a b)")
                            CPo('act', hist[:, c, :n], svf[:, :n], [('sv', par, c)], [('hist', par, c)])
                            if sample:
                                CPo('act', carry_s[:, c, i, :], v3(sv[c])[:, :, 3], [('sv', par, c)], ['carry_s'])
                            else:
                                CPo('act', carry[:, c, i:i + 1], svf[:, 511:512], [('sv', par, c)], ['carry'])
                            MMo(ps[cb][:, :n], lhsC[c][:, i, :], hist[:, c, :n], r=[('lhsC', c), ('hist', par, c)], w=[kcb],
                                start=(j == 0 and c == 0), stop=(j == 3 and c == 1), inc=True)
                STTo('dve', y5[:, :n], u_all[:, kt, t0:t0 + n], s5v[:, kt:kt + 1], ps[cb][:, :n], ALU.mult, ALU.add, [('u', tt), 's5v', kcb], ['y5'])
                TTo('pool', g1[:, :n], y5[:, :n], y5[:, :n], ALU.mult, ['y5'], ['g1'])
                S.op('dve', lambda e: e.tensor_scalar(out=g1[:, :n], in0=g1[:, :n], scalar1=0.044715, scalar2=1.0, op0=ALU.mult, op1=ALU.add),
                     r=['g1'], w=['g1'])
                TTo('pool', g1[:, :n], g1[:, :n], y5[:, :n], ALU.mult, ['g1', 'y5'], ['g1'])
                ACTo(g1[:, :n], g1[:, :n], AF.Sigmoid, ['g1'], ['g1'], scale=1.5957691216057308)
                TTo('dve', v_bf[:, kt, :n], g1[:, :n], y5[:, :n], ALU.mult, ['g1', 'y5'], [('v', kt)])
            for mo in range(4):
                bk = bigbank()
                for kt in range(4):
                    MMo(ps[bk][:, :n], wglu[:, kt, mo * 128:(mo + 1) * 128], v_bf[:, kt, :n], r=['wglu', ('v', kt)], w=[('ps', bk)],
                        start=(kt == 0), stop=(kt == 3), inc=(kt == 3))
                ACTo(gs[:, :n], ps[bk][:, :n], AF.Sigmoid, [('ps', bk), 's5v'], ['g1'], bias=s5v[:, 4 + mo:5 + mo], scale=1.0)
                TTo('dve', o5[:, mo, :n], v_bf[:, mo, :n], gs[:, :n], ALU.mult, [('v', mo), 'g1'], [('o5', mo)])
            for dc in range(8):
                bk = bigbank()
                for mo in range(4):
                    MMo(ps[bk][:, :n], wouts5[:, mo, dc * 128:(dc + 1) * 128], o5[:, mo, :n], r=['wouts5', ('o5', mo)], w=[('ps', bk)],
                        start=(mo == 0), stop=(mo == 3), inc=(mo == 3))
                TTo('dve', x[:, dc, t0:t0 + n], x[:, dc, t0:t0 + n], ps[bk][:, :n], ALU.add, [('ps', bk), ('x', dc, tt)], [('x', dc, tt)])
        S.dma('sp', s5p_d, carry, r=['carry'], w=['s5p'])
        S.dma('sp', s5s_d, carry_s, r=['carry_s'], w=['s5s'])


    PH = os.environ.get('KPH', 'n1,f1,n2,ssd,s5,n3,f2').split(',')
    if 'n1' in PH:
        norm_to_h(0)
    if 'f1' in PH:
        ffn(0, after_tile=(lambda tt: norm_to_h(1, tiles=[tt])) if 'n2' in PH else None)
    elif 'n2' in PH:
        norm_to_h(1)
    if 'ssd' in PH:
        ssd_phase()
    if 's5' in PH:
        s5_phase()
    S.barrier()
    if 'n3' in PH:
        norm_to_h(2)
    yT_v = yT.rearrange("(c p) t -> p c t", p=128)
    ycount = [0]

    def final_out(tt, c, t0, n, rb):
        yb = ycount[0] % 2
        ycount[0] += 1
        S.op('dve', lambda e: e.scalar_tensor_tensor(out=ystage[yb][:, :n], in0=x[:, c, t0:t0 + n],
                                                     scalar=normw[:, 24 + c:24 + c + 1], in1=rb[:, :n],
                                                     op0=ALU.mult, op1=ALU.mult),
             r=[('x', c, tt), ('rstd', tt % 2), 'normw'], w=[('ystage', yb)])
        S.dma('sp', yT_v[:, c, t0:t0 + n], ystage[yb][:, :n], r=[('ystage', yb)], w=[('yT', c, tt)])
    if 'f2' in PH:
        ffn(1, after_tile=lambda tt: rmsnorm(3, final_out, tiles=[tt]))
    else:
        rmsnorm(3, final_out)
    S.finish()
    print("instructions", S.nins, "waits", S.nwait)
    return nc


_NC_CACHE = {}


def _consts():
    f32 = np.float32
    cm = np.zeros((128, 2048), f32)
    k = np.arange(128)
    cm[:, 0:128] = (k[:, None] <= k[None, :])
    cm[:, 128:256] = (k[:, None] > k[None, :])
    same = (k[:, None] // 4 == k[None, :] // 4) & (k[:, None] < 64) & (k[None, :] < 64)
    cm[:, 256:384] = (k[:, None] <= k[None, :]) & same
    cm[:, 384:512] = (k[:, None] > k[None, :]) & same
    cm[:, 512:640] = np.eye(128)
    cm[:64, 640:656] = (np.arange(64)[:, None] // 4 == np.arange(16)[None, :])
    cm[:, 768:1024] = np.arange(1, 257)[None, :]
    et = (np.arange(64)[None, :] // 4 == np.arange(16)[:, None]).astype(f32)
    cm[:, 1024:2048] = et.reshape(1, 1024)
    return cm


def _chan(g, cc):
    if cc < 3:
        return 384 * g + 128 * cc
    if cc == 3:
        return 1536 + 128 * g
    return 2048 + 128 * g


def _s5_host(inp):
    f32 = np.float32
    A = lambda k: np.asarray(inp[k], f32)
    w_in = A("w_in")[0]
    w_out = A("w_out")[0]
    s5v = np.empty((128, 72), f32)
    s5v[:, 8:72] = (np.arange(64) % 4 != 0).astype(f32)[None, :]
    s5v[:, 0:4] = A("s5_d")[0].reshape(4, 128).T
    s5v[:, 4:8] = A("s5_b_glu")[0].reshape(4, 128).T
    lre, lim, lst = A("s5_lambda_re")[0], A("s5_lambda_im")[0], A("s5_log_step")[0]
    lam = np.empty((128, 3, 272), f32)
    for arr_i, arr in enumerate((lre, lim)):
        rep = arr.reshape(4, 8, 64)
        rep = np.repeat(rep.transpose(1, 0, 2)[:, None], 16, axis=1)
        lam[:, arr_i, 0:256] = rep.reshape(128, 256)
        qq = arr.reshape(16, 2, 64).transpose(1, 2, 0).reshape(128, 16)
        lam[:, arr_i, 256:272] = qq
    rep = np.repeat(lst.reshape(4, 8).T[:, None, :, None], 16, axis=1)
    lam[:, 2, 0:256] = np.broadcast_to(rep, (8, 16, 4, 64)).reshape(128, 256)
    qq = np.broadcast_to(lst.reshape(16, 2).T[:, None, :], (2, 64, 16)).reshape(128, 16)
    lam[:, 2, 256:272] = qq
    bT = np.empty((128, 2, 4, 64), f32)
    for ci, k in enumerate(("s5_b_re", "s5_b_im")):
        b = A(k)[0].reshape(4, 8, 64, 16)
        bT[:, ci] = b.transpose(1, 3, 0, 2).reshape(128, 4, 64)
    emask = (np.arange(128)[:, None] // 16 == np.arange(8)[None, :]).astype(f32)
    cpad = np.zeros((2, 128, 16, 128), f32)
    for ci, k in enumerate(("s5_c_re", "s5_c_im")):
        cc = A(k)[0]
        for g in range(32):
            i, q0, gl = g // 2, (g % 2) * 64, g % 8
            cpad[ci, q0:q0 + 64, i, gl * 16:(gl + 1) * 16] = cc[g].T
    return {"w_in_u": np.ascontiguousarray(w_in[:, 0:512]), "w_out_s5": np.ascontiguousarray(w_out[0:512]),
            "w_glu": np.ascontiguousarray(A("s5_w_glu")[0]), "s5v": s5v, "lam": lam, "bT": bT, "emask": emask, "cpad": cpad}


def _s5_core(inp, c):
    f32 = np.float32
    out = np.empty((128, 2, 16, 16), f32)
    for ci, k in enumerate(("state_s5_re", "state_s5_im")):
        s = np.asarray(inp[k], f32)[0, 16 * c:16 * c + 16]
        out[:, ci] = s.reshape(16, 16, 2, 64).transpose(2, 3, 1, 0).reshape(128, 16, 16)
    return {"s5s0": out}


def kernel(**inp):
    f32 = np.float32
    A = lambda k: np.asarray(inp[k], f32)
    xp = A("x_prompt")
    xs = A("x_sample")
    normw = np.stack([A(k).reshape(D) for k in ("ffn1_norm", "mix_norm", "ffn2_norm", "final_norm")])
    normw_l = np.ascontiguousarray(normw.reshape(4, 8, 128).transpose(2, 0, 1).reshape(128, 32))
    w_in = A("w_in")[0]
    w_out = A("w_out")[0]
    w_ssd = np.empty((4, D, 1030), f32)
    w_out_ssd = np.empty((4, 384, D), f32)
    for g in range(4):
        w_ssd[g, :, 0:384] = w_in[:, 2048 + 384 * g:2048 + 384 * (g + 1)]
        w_ssd[g, :, 384:512] = w_in[:, 3584 + 128 * g:3584 + 128 * (g + 1)]
        w_ssd[g, :, 512:640] = w_in[:, 4096 + 128 * g:4096 + 128 * (g + 1)]
        w_ssd[g, :, 640:1024] = w_in[:, 512 + 384 * g:512 + 384 * (g + 1)]
        w_ssd[g, :, 1024:1030] = w_in[:, 4608 + 6 * g:4608 + 6 * (g + 1)]
        w_out_ssd[g] = w_out[512 + 384 * g:512 + 384 * (g + 1)]
    conv_w = A("ssd_conv_w")[0]
    conv_b = A("ssd_conv_b")[0]
    cwb = np.empty((128, 4, 5, 5), f32)
    for g in range(4):
        for cc in range(5):
            c0 = _chan(g, cc)
            cwb[:, g, cc, 0:4] = conv_w[:, c0:c0 + 128].T
            cwb[:, g, cc, 4] = conv_b[c0:c0 + 128]
    hp = np.concatenate([A("ssd_dt_bias")[0], A("ssd_a_log")[0], A("ssd_d")[0]])[None, :].repeat(128, 0)
    nbc = A("ssd_norm")[0][None, :].repeat(128, 0)
    sconv = A("state_conv")[0]
    sssd = A("state_ssd")[0]
    shared = {
        "normw": normw_l,
        "ffn1_wg": np.ascontiguousarray(A("ffn1_w_gate")[0]), "ffn1_wu": np.ascontiguousarray(A("ffn1_w_up")[0]),
        "ffn1_wd": np.ascontiguousarray(A("ffn1_w_down")[0]),
        "ffn2_wg": np.ascontiguousarray(A("ffn2_w_gate")[0]), "ffn2_wu": np.ascontiguousarray(A("ffn2_w_up")[0]),
        "ffn2_wd": np.ascontiguousarray(A("ffn2_w_down")[0]),
        "cmat": _consts(), "hp": np.ascontiguousarray(hp), "nbc": np.ascontiguousarray(nbc),
        "w_ssd": w_ssd, "w_out_ssd": w_out_ssd, "cwb": cwb,
    }
    shared.update(_s5_host(inp))
    in_maps = []
    for c in range(NCORES):
        bs = slice(16 * c, 16 * c + 16)
        xc = np.concatenate([xp[c], xs[bs].reshape(TS, D)], axis=0)
        m = dict(shared)
        m["xT"] = np.ascontiguousarray(xc.T)
        cv0 = np.empty((128, 4, 5, 16, 3), f32)
        for g in range(4):
            for cc in range(5):
                c0 = _chan(g, cc)
                cv0[:, g, cc] = sconv[bs, :, c0:c0 + 128].transpose(2, 0, 1)
        m["conv0"] = cv0.reshape(128, 4, 5, 48)
        st = sssd[bs].reshape(16, 4, 6, 64, 128).transpose(0, 1, 4, 2, 3).reshape(16, 4, 128, 384)
        m["ssd0"] = np.ascontiguousarray(st)
        m.update(_s5_core(inp, c))
        in_maps.append(m)
    if "nc" not in _NC_CACHE:
        _NC_CACHE["nc"] = build()
    nc = _NC_CACHE["nc"]
    res = run_bass_kernel_spmd(nc, in_maps, core_ids=list(range(NCORES)))
    R = res.results
    yp = np.empty((8, TP, D), f32)
    ys = np.empty((128, 4, D), f32)
    ssd_p = np.empty((1, 8, 24, 64, 128), f32)
    ssd_s = np.empty((1, 128, 24, 64, 128), f32)
    conv_p = np.empty((1, 8, 3, 2560), f32)
    conv_s = np.empty((1, 128, 3, 2560), f32)
    s5p = np.empty((2, 1, 8, 32, 64), f32)
    s5s = np.empty((2, 1, 128, 32, 64), f32)
    for c in range(NCORES):
        bs = slice(16 * c, 16 * c + 16)
        y = R[c]["yT"].T
        yp[c] = y[:TP]
        ys[bs] = y[TP:].reshape(16, 4, D)
        ssd_p[0, c] = R[c]["ssdp"].reshape(4, 128, 6, 64).transpose(0, 2, 3, 1).reshape(24, 64, 128)
        ssd_s[0, bs] = R[c]["ssds"].reshape(16, 4, 128, 6, 64).transpose(0, 1, 3, 4, 2).reshape(16, 24, 64, 128)
        cp = R[c]["convp"]
        cs = R[c]["convs"].reshape(128, 4, 5, 16, 3)
        for g in range(4):
            for cc in range(5):
                c0 = _chan(g, cc)
                conv_p[0, c, :, c0:c0 + 128] = cp[:, g, cc, :].T
                conv_s[0, bs, :, c0:c0 + 128] = cs[:, g, cc].transpose(1, 2, 0)
        a = R[c]["s5p"]
        s5p[:, 0, c] = a.reshape(2, 64, 2, 16).transpose(2, 3, 0, 1).reshape(2, 32, 64)
        b_ = R[c]["s5s"]
        s5s[:, 0, bs] = b_.reshape(2, 64, 2, 16, 16).transpose(2, 4, 3, 0, 1).reshape(2, 16, 32, 64)
    return (yp, ys, s5p[0], s5p[1], ssd_p, conv_p, s5s[0], s5s[1], ssd_s, conv_s)
```

```python
import os
import numpy as np
import concourse.bass as bass
import concourse.mybir as mybir
from concourse.bass_utils import run_bass_kernel_spmd

F32 = mybir.dt.float32
BF16 = mybir.dt.bfloat16
AF = mybir.ActivationFunctionType
ALU = mybir.AluOpType

NCORES = 8
D = 1024
DFF = 2816
TP = 2048
TS = 64
NT = TP + TS
EPS = 1e-6
NDMA = 24
NSW = 64


class Sched:
    def __init__(self, nc):
        self.nc = nc
        self.e = {'pe': nc.tensor, 'act': nc.scalar, 'dve': nc.vector, 'pool': nc.gpsimd, 'sp': nc.sync}
        self.sem = {k: nc.alloc_semaphore('s_' + k) for k in self.e}
        self.cnt = {k: 0 for k in self.e}
        self.waited = {}
        self.lastw = {}
        self.readers = {}
        self.dsem = [nc.alloc_semaphore('d%d' % i) for i in range(NDMA + NSW)]
        self.duse = [0] * (NDMA + NSW)
        self.dnext = 0
        self.swnext = NDMA
        self.nwait = 0
        self.nins = 0
        self._rec = None
        self.know = {k: {} for k in self.e}
        self.snaps = {}
        self.tokidx = {}
        self.tokctr = 0

    def _semobj(self, sk):
        return self.sem[sk] if isinstance(sk, str) else self.dsem[sk[1]]

    def _deps(self, r, w):
        deps = {}

        def add(tok):
            sk, v = tok
            if deps.get(sk, 0) < v:
                deps[sk] = v
        for k in r:
            if k in self.lastw:
                add(self.lastw[k])
        for k in w:
            if k in self.lastw:
                add(self.lastw[k])
            for sk, v in self.readers.get(k, {}).items():
                add((sk, v))
        return deps

    def _wait(self, eng, deps, attach=False):
        kn = self.know[eng]
        need = []
        items = sorted(deps.items(), key=lambda kv: -self.tokidx.get(kv, 0))
        for sk, v in items:
            if sk == eng:
                if eng == 'pe':
                    continue
                assert v <= self.cnt[eng], "same-engine dep on non-inc instruction"
            if kn.get(sk, 0) >= v:
                continue
            need.append((sk, v))
            kn[sk] = v
            snap = self.snaps.get((sk, v))
            if snap is not None and eng in ('pe', 'act', 'dve', 'pool'):
                for k2, v2 in snap.items():
                    if k2 != eng and kn.get(k2, 0) < v2:
                        kn[k2] = v2
        pend = None
        if attach and need and eng != 'pe':
            pend = need.pop()
        for sk, v in need:
            self.e[eng].wait_ge(self._semobj(sk), v)
            self.nwait += 1
        return pend

    def _newtok(self, tok, eng):
        self.tokctr += 1
        self.tokidx[tok] = self.tokctr
        if eng in ('pe', 'act', 'dve'):
            s = {k: v for k, v in self.know[eng].items() if k in ('pe', 'act', 'dve', 'pool')}
            self.snaps[tok] = s

    def _record(self, tok, r, w):
        sk, v = tok
        for k in r:
            d = self.readers.setdefault(k, {})
            if d.get(sk, 0) < v:
                d[sk] = v
        for k in w:
            self.lastw[k] = tok
            self.readers[k] = {}

    @staticmethod
    def _split(r, w):
        def isps(k):
            return (isinstance(k, tuple) and k[0] == 'ps') or (isinstance(k, str) and k.startswith('ps'))
        r2 = [k for k in r if not isps(k)]
        w2 = list(w) + [k for k in r if isps(k)]
        return r2, w2

    def record(self, f):
        self._rec = []
        f()
        out, self._rec = self._rec, None
        return out

    def replay(self, items):
        for it in items:
            if it[0] == 'op':
                self.op(*it[1:])
            else:
                self.dma(*it[1:-1], **it[-1])

    @staticmethod
    def merge(a, b):
        out = []
        ia = ib = 0
        na, nb = len(a), len(b)
        while ia < na or ib < nb:
            if ib >= nb or (ia < na and ia * nb <= ib * na):
                out.append(a[ia])
                ia += 1
            else:
                out.append(b[ib])
                ib += 1
        return out

    def op(self, eng, fn, r=(), w=(), inc=True):
        if self._rec is not None:
            self._rec.append(('op', eng, fn, tuple(r), tuple(w), inc))
            return
        r, w = self._split(r, w)
        pend = self._wait(eng, self._deps(r, w), attach=True)
        ins = fn(self.e[eng])
        if pend is not None:
            ins._wait_ge(self._semobj(pend[0]), pend[1])
        self.nins += 1
        if inc:
            self.cnt[eng] += 1
            ins.then_inc(self.sem[eng], 1)
            tok = (eng, self.cnt[eng])
            self._newtok(tok, eng)
        else:
            tok = (eng, self.cnt[eng] + 1)
        self._record(tok, r, w)

    def dma(self, q, out, in_, r=(), w=(), **kw):
        if self._rec is not None:
            self._rec.append(('dma', q, out, in_, tuple(r), tuple(w), kw))
            return
        if q == 'pool':
            i = self.swnext
            self.swnext += 1
            assert i < NDMA + NSW, "out of SW-DMA semaphores"
        else:
            i = self.dnext
            self.dnext = (self.dnext + 1) % NDMA
        deps = self._deps(r, w)
        if self.duse[i] > 0:
            sk = ('d', i)
            deps[sk] = max(deps.get(sk, 0), self.duse[i] * 16)
        pend = self._wait(q, deps, attach=(q == 'sp'))
        ins = self.e[q].dma_start(out=out, in_=in_, **kw)
        if pend is not None:
            ins._wait_ge(self._semobj(pend[0]), pend[1])
        self.duse[i] += 1
        ins.then_inc(self.dsem[i], 16)
        self.nins += 1
        self._record((('d', i), self.duse[i] * 16), r, w)

    def barrier(self):
        toks = {k: self.cnt[k] for k in ('pe', 'act', 'dve', 'pool') if self.cnt[k] > 0}
        dt = {('d', i): self.duse[i] * 16 for i in range(NDMA + NSW) if self.duse[i] > 0}
        for eng in ('pe', 'act', 'dve', 'pool', 'sp'):
            deps = {k: v for k, v in toks.items() if k != eng}
            deps.update(dt)
            self._wait(eng, deps)

    def finish(self):
        for i in range(NDMA + NSW):
            if self.duse[i] > 0:
                self.e['sp'].wait_ge(self.dsem[i], self.duse[i] * 16)
        for k in ('pe', 'act', 'dve', 'pool'):
            if self.cnt[k] > 0:
                self.e['sp'].wait_ge(self.sem[k], self.cnt[k])


def token_tiles():
    tiles = [(i * 512, 512) for i in range(TP // 512)]
    tiles.append((TP, TS))
    return tiles


def build(stage=1):
    nc = bass.Bass("TRN2", target_bir_lowering=False)
    S = Sched(nc)

    def din(name, shape, dt=F32):
        return nc.dram_tensor(name, list(shape), dt, kind="ExternalInput").ap()

    def dout(name, shape, dt=F32):
        return nc.dram_tensor(name, list(shape), dt, kind="ExternalOutput").ap()

    def sb(name, shape, dt):
        return nc.alloc_sbuf_tensor(name, list(shape), dt).ap()

    xT = din("xT", [D, NT])
    normw_d = din("normw", [128, 32])
    wg_d = [din("ffn%d_wg" % i, [D, DFF]) for i in (1, 2)]
    wu_d = [din("ffn%d_wu" % i, [D, DFF]) for i in (1, 2)]
    wd_d = [din("ffn%d_wd" % i, [DFF, D]) for i in (1, 2)]
    yT = dout("yT", [D, NT])
    cmat_d = din("cmat", [128, 2048])
    hp_d = din("hp", [128, 72])
    nbc_d = din("nbc", [128, 1536])
    wssd_d = din("w_ssd", [4, D, 1030])
    woutssd_d = din("w_out_ssd", [4, 384, D])
    cwb_d = din("cwb", [128, 4, 5, 5])
    conv0_d = din("conv0", [128, 4, 5, 48])
    ssd0_d = din("ssd0", [16, 4, 128, 384])
    convp_d = dout("convp", [128, 4, 5, 3])
    convs_d = dout("convs", [128, 4, 5, 48])
    ssdp_d = dout("ssdp", [4, 128, 384])
    ssds_d = dout("ssds", [16, 4, 128, 384])
    winu_d = din("w_in_u", [D, 512])
    wouts5_d = din("w_out_s5", [512, D])
    wglu_d = din("w_glu", [512, 512])
    s5v_d = din("s5v", [128, 72])
    lam_d = din("lam", [128, 3, 272])
    bT_d = din("bT", [128, 2, 4, 64])
    emask_d = din("emask", [128, 8])
    cpad_d = din("cpad", [2, 128, 16, 128])
    s5s0_d = din("s5s0", [128, 2, 16, 16])
    s5p_d = dout("s5p", [128, 2, 16])
    s5s_d = dout("s5s", [128, 2, 16, 16])

    x = sb("x", [128, 8, NT], F32)
    h = sb("h", [128, 8, NT], BF16)
    normw = sb("normw_sb", [128, 32], F32)
    ones_bf = sb("ones_bf", [128, 128], BF16)
    epsc = sb("epsc", [128, 1], F32)
    ps = [nc.alloc_psum_tensor("ps%d" % i, [128, 512], F32).ap() for i in range(8)]

    ARENA = 111040
    arena = sb("arena", [128, ARENA // 2], BF16)

    class Carver:
        def __init__(self):
            self.off = 0

        def get(self, shape, dt):
            n = 1
            for s_ in shape[1:]:
                n *= s_
            nb = n * (2 if dt == BF16 else 4)
            nb = (nb + 31) // 32 * 32
            assert self.off + nb <= ARENA, ("arena overflow", self.off + nb)
            ap = arena[:shape[0], self.off // 2:(self.off + nb) // 2]
            self.off += nb
            if dt != BF16:
                ap = ap.bitcast(dt)
            ap = ap[:, :n]
            if len(shape) == 3:
                ap = ap.rearrange("p (a b) -> p a b", a=shape[1])
            elif len(shape) == 4:
                ap = ap.rearrange("p (a b c) -> p a b c", a=shape[1], b=shape[2])
            return ap

    GMAX = 6
    cv = Carver()
    wg_s = [cv.get([128, 8, GMAX * 128], BF16) for i in range(2)]
    wu_s = [cv.get([128, 8, GMAX * 128], BF16) for i in range(2)]
    wd_s = [cv.get([128, GMAX, D], BF16) for i in range(2)]
    sq = cv.get([128, 8, 512], BF16)
    rstd = [cv.get([128, 512], F32) for i in range(2)]
    sg = [cv.get([128, 512], F32) for i in range(2)]
    actb = [cv.get([128, GMAX, 512], BF16) for i in range(2)]
    ystage = [cv.get([128, 512], F32) for i in range(2)]

    S.op('pool', lambda e: e.memset(ones_bf, 1.0), w=['ones_bf'])
    S.op('pool', lambda e: e.memset(epsc, EPS), w=['epsc'])
    S.dma('sp', normw, normw_d, w=['normw'])
    xT_v = xT.rearrange("(c p) t -> p c t", p=128)
    for c in range(8):
        S.dma('sp', x[:, c, :], xT_v[:, c, :], w=[('x', c, tt) for tt in range(5)])

    TT = token_tiles()

    def rmsnorm(widx, out_fn, tiles=None):
        for tt, (t0, n) in enumerate(TT):
            if tiles is not None and tt not in tiles:
                continue
            for c in range(8):
                S.op('act', lambda e, c=c: e.activation(out=sq[:, c, :n], in_=x[:, c, t0:t0 + n], func=AF.Square),
                     r=[('x', c, tt)], w=[('sq', c)])
            for c in range(8):
                S.op('pe', lambda e, c=c: e.matmul(ps[0][:, :n], ones_bf, sq[:, c, :n], start=(c == 0), stop=(c == 7)),
                     r=[('sq', c), 'ones_bf'], w=[('ps', 0)], inc=(c == 7))
            rb = rstd[tt % 2]
            S.op('act', lambda e: e.activation(out=rb[:, :n], in_=ps[0][:, :n], func=AF.Sqrt, bias=epsc[:, 0:1], scale=1.0 / D),
                 r=[('ps', 0), 'epsc'], w=[('rstd', tt % 2)])
            S.op('dve', lambda e: e.reciprocal(out=rb[:, :n], in_=rb[:, :n]), r=[('rstd', tt % 2)], w=[('rstd', tt % 2)])
            for c in range(8):
                out_fn(tt, c, t0, n, rb)

    def norm_to_h(widx, tiles=None):
        def f(tt, c, t0, n, rb):
            S.op('dve', lambda e: e.scalar_tensor_tensor(out=h[:, c, t0:t0 + n], in0=x[:, c, t0:t0 + n],
                                                         scalar=normw[:, widx * 8 + c:widx * 8 + c + 1], in1=rb[:, :n],
                                                         op0=ALU.mult, op1=ALU.mult),
                 r=[('x', c, tt), ('rstd', tt % 2), 'normw'], w=[('h', c, tt)])
        rmsnorm(widx, f, tiles)

    def ffn(fi, after_tile=None):
        Wg = wg_d[fi].rearrange("(kc p) f -> p kc f", p=128)
        Wu = wu_d[fi].rearrange("(kc p) f -> p kc f", p=128)
        Wd = wd_d[fi].rearrange("(fc p) d -> p fc d", p=128)
        groups = [(0, 6), (6, 6), (12, 6), (18, 4)]

        def load(gi):
            c0, G = groups[gi]
            b = gi % 2
            for kc in range(0, 8, 4):
                S.dma('pool', wg_s[b][:, kc:kc + 4, :G * 128], Wg[:, kc:kc + 4, c0 * 128:(c0 + G) * 128], w=[('wg', b, kc)])
                S.dma('pool', wu_s[b][:, kc:kc + 4, :G * 128], Wu[:, kc:kc + 4, c0 * 128:(c0 + G) * 128], w=[('wu', b, kc)])
            S.dma('pool', wd_s[b][:, :G, :], Wd[:, c0:c0 + G, :], w=[('wd', b)])
        load(0)
        pcount = 0
        ocount = 0
        for gi, (c0, G) in enumerate(groups):
            b = gi % 2
            if gi + 1 < len(groups):
                load(gi + 1)
            for tt, (t0, n) in enumerate(TT):
                ab = (gi * len(TT) + tt) % 2
                for fc in range(G):
                    pgi = pcount % 2
                    pui = 2 + pcount % 2
                    pcount += 1
                    for kc in range(8):
                        S.op('pe', lambda e, kc=kc: e.matmul(ps[pgi][:, :n], wg_s[b][:, kc, fc * 128:(fc + 1) * 128], h[:, kc, t0:t0 + n],
                                                             start=(kc == 0), stop=(kc == 7)),
                             r=[('wg', b, kc // 4 * 4), ('h', kc, tt)], w=[('ps', pgi)], inc=(kc == 7))
                    for kc in range(8):
                        S.op('pe', lambda e, kc=kc: e.matmul(ps[pui][:, :n], wu_s[b][:, kc, fc * 128:(fc + 1) * 128], h[:, kc, t0:t0 + n],
                                                             start=(kc == 0), stop=(kc == 7)),
                             r=[('wu', b, kc // 4 * 4), ('h', kc, tt)], w=[('ps', pui)], inc=(kc == 7))
                    sgb = sg[pcount % 2]
                    S.op('act', lambda e: e.activation(out=sgb[:, :n], in_=ps[pgi][:, :n], func=AF.Silu),
                         r=[('ps', pgi)], w=[('sg', pcount % 2)])
                    S.op('dve', lambda e: e.tensor_tensor(out=actb[ab][:, fc, :n], in0=ps[pui][:, :n], in1=sgb[:, :n], op=ALU.mult),
                         r=[('ps', pui), ('sg', pcount % 2)], w=[('actb', ab, fc)])
                for dc in range(8):
                    poi = 4 + ocount % 4
                    ocount += 1
                    for fc in range(G):
                        S.op('pe', lambda e, fc=fc: e.matmul(ps[poi][:, :n], wd_s[b][:, fc, dc * 128:(dc + 1) * 128], actb[ab][:, fc, :n],
                                                             start=(fc == 0), stop=(fc == G - 1)),
                             r=[('wd', b), ('actb', ab, fc)], w=[('ps', poi)], inc=(fc == G - 1))
                    S.op('dve', lambda e: e.scalar_tensor_tensor(out=x[:, dc, t0:t0 + n], in0=ps[poi][:, :n], scalar=0.5,
                                                                 in1=x[:, dc, t0:t0 + n], op0=ALU.mult, op1=ALU.add),
                         r=[('ps', poi), ('x', dc, tt)], w=[('x', dc, tt)])
                if after_tile is not None and gi == len(groups) - 1:
                    after_tile(tt)

    def TTo(eng, out, a, b, op, r, w):
        S.op(eng, lambda e: e.tensor_tensor(out=out, in0=a, in1=b, op=op), r=r, w=w)

    def TSo(eng, out, a, s1, op0, r, w, s2=None, op1=None):
        if op1 is None:
            S.op(eng, lambda e: e.tensor_scalar(out=out, in0=a, scalar1=s1, scalar2=None, op0=op0), r=r, w=w)
        else:
            S.op(eng, lambda e: e.tensor_scalar(out=out, in0=a, scalar1=s1, scalar2=s2, op0=op0, op1=op1), r=r, w=w)

    def STTo(eng, out, in0, sc, in1, op0, op1, r, w):
        S.op(eng, lambda e: e.scalar_tensor_tensor(out=out, in0=in0, scalar=sc, in1=in1, op0=op0, op1=op1), r=r, w=w)

    def ACTo(out, in_, func, r, w, **kw):
        S.op('act', lambda e: e.activation(out=out, in_=in_, func=func, **kw), r=r, w=w)

    def MMo(out, lhsT, rhs, r, w, start=True, stop=True, inc=True):
        S.op('pe', lambda e: e.matmul(out, lhsT, rhs, start=start, stop=stop), r=r, w=w, inc=inc)

    def TRo(out, in_, ident, r, w, inc=True):
        S.op('pe', lambda e: e.transpose(out, in_, ident), r=r, w=w, inc=inc)

    def CPo(eng, out, in_, r, w):
        if eng == 'act':
            S.op(eng, lambda e: e.activation(out=out, in_=in_, func=AF.Copy), r=r, w=w)
        else:
            S.op(eng, lambda e: e.tensor_copy(out=out, in_=in_), r=r, w=w)

    big_rot = [0]

    def bigbank():
        b = big_rot[0] % 2
        big_rot[0] += 1
        return b

    def ssd_phase():
        S.barrier()
        cv = Carver()
        cm = cv.get([128, 2048], F32)
        S.dma('sp', cm, cmat_d, w=['cm'])
        TI = cm[:, 0:128]
        SU = cm[:, 128:256]
        TIb = cm[:, 256:384]
        SUb = cm[:, 384:512]
        identf = cm[:, 512:640]
        Emat = cm[:, 640:656]
        ident_bf = cv.get([128, 128], BF16)
        CPo('dve', ident_bf, identf, ['cm'], ['ident_bf'])
        ones_f = cv.get([128, 128], F32)
        S.op('pool', lambda e: e.memset(ones_f, 1.0), w=['ones_f'])
        onec = cv.get([128, 1], F32)
        S.op('pool', lambda e: e.memset(onec, 1.0), w=['onec'])
        hp = cv.get([128, 72], F32)
        S.dma('sp', hp, hp_d, w=['hp'])
        abc = cv.get([128, 24], F32)
        ACTo(abc, hp[:, 24:48], AF.Exp, ['hp'], ['abc'])
        TSo('dve', abc, abc, -1.0, ALU.mult, ['abc'], ['abc'])
        cwb = cv.get([128, 4, 5, 5], F32)
        S.dma('sp', cwb, cwb_d, w=['cwb'])

        wssd = cv.get([128, 8, 1030], BF16)
        wout = cv.get([128, 3, 1024], BF16)
        diag = cv.get([128, 5, 4, 128], BF16)
        nbc = cv.get([128, 384], F32)
        xraw = [cv.get([128, 5, 515], BF16) for _ in range(2)]
        xraw_s = cv.get([128, 5, 16, 7], BF16)
        conv0s = cv.get([128, 5, 48], F32)
        convo = cv.get([128, 5, 3], F32)
        convos = cv.get([128, 5, 16, 3], F32)
        fm = cv.get([128, 5, 512], BF16)
        ofm = cv.get([128, 3, 512], BF16)
        dt6 = cv.get([128, 6], F32)
        la = cv.get([128, 6], F32)
        rhsla = cv.get([128, 768], F32)
        rhscd = cv.get([128, 96], F32)
        decT = cv.get([128, 768], BF16)
        MT2 = [cv.get([128, 768], BF16) for _ in range(2)]
        CBm = cv.get([128, 128], BF16)
        ea2 = [cv.get([128, 12], F32) for _ in range(2)]
        cdx2 = [cv.get([128, 96], F32) for _ in range(2)]
        xd2 = [cv.get([128, 6, 64], BF16) for _ in range(2)]
        xdd2 = [cv.get([128, 6, 64], BF16) for _ in range(2)]
        Btm2 = [cv.get([128, 128], BF16) for _ in range(2)]
        Btmm = [cv.get([128, 128], BF16) for _ in range(4)]
        skipx2 = [cv.get([128, 6, 64], F32) for _ in range(2)]
        yv = cv.get([128, 6, 64], F32)
        sz4 = [cv.get([128, 384], F32) for _ in range(4)]
        t6s = [cv.get([128, 6], F32) for _ in range(4)]
        junk = cv.get([128, 384], BF16)
        ss = cv.get([128, 1], F32)
        otm = cv.get([128, 384], BF16)
        hT = cv.get([128, 6, 64], F32)
        htmp = cv.get([128, 6, 64], F32)
        hTb = cv.get([128, 384], BF16)
        hs = [cv.get([128, 6, 64], F32) for _ in range(4)]
        hsb = [cv.get([128, 384], BF16) for _ in range(4)]
        hso = [cv.get([128, 6, 64], F32) for _ in range(4)]
        Cmask = cv.get([128, 16, 64], BF16)
        print("ssd arena used", cv.off)

        def bc6(ap, T):
            return ap[:T, :, None].to_broadcast([T, 6, 64])

        def ssd_pre(g, T, tok0, ci):
            g6 = slice(g * 6, g * 6 + 6)
            bk = 2 if ci % 2 == 0 else 6
            kb = 'ps%d' % bk
            for kc in range(8):
                MMo(ps[bk][:T, 0:390], h[:, kc, tok0:tok0 + T], wssd[:, kc, 640:1030], r=[('h', kc), 'wssd'], w=[kb],
                    start=(kc == 0), stop=(kc == 7), inc=(kc == 7))
            TTo('dve', t6s[ci][:T], ps[bk][:T, 384:390], hp[:T, g6], ALU.add, [kb, 'hp'], [('t6', ci)])
            ACTo(sz4[ci][:T, :], ps[bk][:T, 0:384], AF.Silu, [kb], [('sz4', ci)])

        def ssd_late(g, T, tok0, col0, sample, last, p, tI, sU, MT, xd, xdd, Btm, skipx, sz, ea, cdx,
                     kMT, kxd, kxdd, kBtm, kskipx, ksz, kea, kcdx, yv2):
            for hh in range(6):
                MMo(ps[7][:T, hh * 64:(hh + 1) * 64], MT[:T, hh * T:(hh + 1) * T], xd[:T, hh, :], r=[kMT, kxd], w=['ps7'], inc=(hh == 5))
            if not sample:
                MMo(ps[3][:T, 0:384], fm[:, 4, col0:col0 + T], hTb, r=['fmC', 'hTb'], w=['ps3'])
                MMo(ps[4][:, 0:384], Btm[:T, :], xdd[:T].rearrange("p a b -> p (a b)"), r=[kBtm, kxdd], w=['ps4'])
            else:
                TTo('pool', Cmask, fm[:, 4, None, col0:col0 + T].to_broadcast([128, 16, T]),
                    cm[:, 1024:2048].rearrange("p (a b) -> p a b", a=16), ALU.mult, ['fmC', 'cm'], ['Cmask'])
                dh_banks = [(4, 'ps4'), (0, ('ps', 0)), (1, ('ps', 1)), (2, 'ps2')]
                for b4 in range(0, 16, 4):
                    bs_ = list(range(b4, b4 + 4))
                    for b in bs_:
                        rb_ = b % 4
                        S.dma('sp', hs[rb_].rearrange("p a b -> p (a b)"), ssd0_d[b, g], w=[('hs', rb_)])
                    for b in bs_:
                        rb_ = b % 4
                        CPo('act', hsb[rb_], hs[rb_].rearrange("p a b -> p (a b)"), [('hs', rb_)], [('hsb', rb_)])
                        TSo('dve', Btmm[rb_][:T, :], Btm[:T, :], Emat[:T, b:b + 1], ALU.mult, [kBtm, 'cm'], [('Btmm', rb_)])
                    for b in bs_:
                        rb_ = b % 4
                        MMo(ps[3][:T, 0:384], Cmask[:, b, :], hsb[rb_], r=['Cmask', ('hsb', rb_)], w=['ps3'], start=(b == 0), stop=(b == 15),
                            inc=(b == 15))
                    for b in bs_:
                        rb_ = b % 4
                        bk_, kb_ = dh_banks[rb_]
                        MMo(ps[bk_][:, 0:384], Btmm[rb_][:T, :], xdd[:T].rearrange("p a b -> p (a b)"), r=[('Btmm', rb_), kxdd], w=[kb_])
                    for b in bs_:
                        rb_ = b % 4
                        bk_, kb_ = dh_banks[rb_]
                        TTo('dve', hso[rb_], hs[rb_], cdx[:, b * 6:(b + 1) * 6][:, :, None].to_broadcast([128, 6, 64]), ALU.mult,
                            [('hs', rb_), kcdx], [('hso', rb_)])
                        TTo('dve', hso[rb_].rearrange("p a b -> p (a b)"), hso[rb_].rearrange("p a b -> p (a b)"), ps[bk_][:, 0:384], ALU.add,
                            [('hso', rb_), kb_], [('hso', rb_)])
                        S.dma('sp', ssds_d[b, g], hso[rb_].rearrange("p a b -> p (a b)"), r=[('hso', rb_)], w=[('ssds', b, g)])
            TTo('dve', yv[:T], ps[3][:T, 0:384].rearrange("p (a b) -> p a b", a=6), bc6(ea[:, 0:6], T), ALU.mult, ['ps3', kea], ['yv'])
            TTo('dve', yv2, yv2, ps[7][:T, 0:384], ALU.add, ['yv', 'ps7'], ['yv'])
            TTo('pool', yv[:T], yv[:T], skipx[:T], ALU.add, ['yv', kskipx], ['yv'])
            TTo('pool', yv2, yv2, sz[:T, :], ALU.mult, ['yv', ksz], ['yv'])
            ACTo(junk[:T, :], yv2, AF.Square, ['yv'], ['junk', 'ss'], accum_out=ss[:T, 0:1])
            ACTo(ss[:T, :], ss[:T, :], AF.Ln, ['ss', 'epsc'], ['ss'], bias=epsc[:T, 0:1], scale=1.0 / 384)
            ACTo(ss[:T, :], ss[:T, :], AF.Exp, ['ss'], ['ss'], scale=-0.5)
            STTo('dve', otm[:T, :], yv2, ss[:T, 0:1], nbc[:T, :], ALU.mult, ALU.mult, ['yv', 'ss', 'nbc'], ['otm'])
            if not sample:
                TTo('dve', htmp, hT, cdx[:, 0:6][:, :, None].to_broadcast([128, 6, 64]), ALU.mult, ['hT', kcdx], ['htmp'])
                TTo('dve', hT.rearrange("p a b -> p (a b)"), htmp.rearrange("p a b -> p (a b)"), ps[4][:, 0:384], ALU.add,
                    ['htmp', 'ps4'], ['hT'])
                if last:
                    S.dma('sp', ssdp_d[g], hT.rearrange("p a b -> p (a b)"), r=['hT'], w=[('ssdp', g)])
                else:
                    CPo('act', hTb, hT.rearrange("p a b -> p (a b)"), ['hT'], ['hTb'])
            po = ps[7].bitcast(BF16)[:, 0:512]
            for j in range(3):
                TRo(po[:, j * 128:j * 128 + T], otm[:T, j * 128:(j + 1) * 128], ident_bf[:T, :T], r=['otm', 'ident_bf'], w=['ps7'], inc=(j == 2))
            CPo('act', ofm[:, :, col0:col0 + T], po[:, 0:384].rearrange("p (a b) -> p a b", a=3)[:, :, :T], ['ps7'], ['ofm'])


        def ssd_chunk(g, T, tok0, col0, sample, last, p, part, ci=0):
            g6 = slice(g * 6, g * 6 + 6)
            tI = TIb if sample else TI
            sU = SUb if sample else SU
            MT, xd, xdd, Btm, skipx, sz, ea, cdx = MT2[p], xd2[p], xdd2[p], Btm2[p], skipx2[p], sz4[ci], ea2[p], cdx2[p]
            kMT, kxd, kxdd, kBtm, kskipx, ksz, kea, kcdx = [(nm, p) for nm in ('MT', 'xd', 'xdd', 'Btm', 'skipx', 'sz', 'ea', 'cdx')]
            ksz = ('sz4', ci)
            yv2 = yv[:T].rearrange("p a b -> p (a b)")
            if part == 'late':
                return ssd_late(g, T, tok0, col0, sample, last, p, tI, sU, MT, xd, xdd, Btm, skipx, sz, ea, cdx,
                                kMT, kxd, kxdd, kBtm, kskipx, ksz, kea, kcdx, yv2)
            t6 = t6s[ci]
            kt6 = ('t6', ci)
            ACTo(t6[:T], t6[:T], AF.Exp, [kt6], [kt6])
            ACTo(dt6[:T], t6[:T], AF.Ln, [kt6, 'onec'], ['dt6'], bias=onec[:T, 0:1], scale=1.0)
            TTo('dve', la[:T], dt6[:T], abc[:T, g6], ALU.mult, ['dt6', 'abc'], ['la'])
            rl3 = rhsla[:T, :6 * T].rearrange("p (a b) -> p a b", a=6)
            TTo('pool', rl3, la[:T, :, None].to_broadcast([T, 6, T]), tI[:T, None, :T].to_broadcast([T, 6, T]), ALU.mult,
                ['la', 'cm'], ['rhsla'])
            ncd = 6
            if sample:
                ncd = 96
                TTo('pool', rhscd[:T, :96].rearrange("p (a b) -> p a b", a=16), la[:T, None, :].to_broadcast([T, 16, 6]),
                    Emat[:T, :, None].to_broadcast([T, 16, 6]), ALU.mult, ['la', 'cm'], ['rhscd'])
            MMo(ps[0][:T, :3 * T], sU[:T, :T], rhsla[:T, 0:3 * T], r=['cm', 'rhsla'], w=[('ps', 0)])
            MMo(ps[1][:T, :3 * T], sU[:T, :T], rhsla[:T, 3 * T:6 * T], r=['cm', 'rhsla'], w=[('ps', 1)])
            MMo(ps[5][:T, 0:6], tI[:T, :T], la[:T, :], r=['cm', 'la'], w=['ps5'])
            MMo(ps[5][:T, 6:12], sU[:T, :T], la[:T, :], r=['cm', 'la'], w=['ps5'])
            if sample:
                MMo(ps[5][:, 16:16 + 96], ones_f[:T, :], rhscd[:T, :96], r=['ones_f', 'rhscd'], w=['ps5'])
            else:
                MMo(ps[5][:, 16:22], ones_f[:T, :], la[:T, :], r=['ones_f', 'la'], w=['ps5'])
            ACTo(decT[:T, 0:3 * T], ps[0][:T, :3 * T], AF.Exp, [('ps', 0)], ['decTa'])
            ACTo(decT[:T, 3 * T:6 * T], ps[1][:T, :3 * T], AF.Exp, [('ps', 1)], ['decTb'])
            ACTo(ea[:T, :], ps[5][:T, 0:12], AF.Exp, ['ps5'], [kea])
            ACTo(cdx[:, :ncd], ps[5][:, 16:16 + ncd], AF.Exp, ['ps5'], [kcdx])
            MMo(ps[5][:T, 128:128 + T], fm[:, 3, col0:col0 + T], fm[:, 4, col0:col0 + T], r=['fmB', 'fmC'], w=['ps5'])
            TTo('dve', CBm[:T, :T], ps[5][:T, 128:128 + T], tI[:T, :T], ALU.mult, ['ps5', 'cm'], ['CBm'])
            TTo('dve', MT[:T, :6 * T].rearrange("p (a b) -> p a b", a=6), decT[:T, :6 * T].rearrange("p (a b) -> p a b", a=6),
                CBm[:T, None, :T].to_broadcast([T, 6, T]), ALU.mult, ['decTa', 'decTb', 'CBm'], [kMT])
            pt = ps[6].bitcast(BF16)
            for j in range(4):
                TRo(pt[:T, j * 128:(j + 1) * 128], fm[:, j, col0:col0 + T], ident_bf, r=[('fmx', j) if j < 3 else 'fmB', 'ident_bf'],
                    w=['ps6'], inc=(j == 3))
            xs3 = pt[:T, 0:384].rearrange("p (a b) -> p a b", a=6)
            TTo('dve', xd[:T], xs3, bc6(dt6, T), ALU.mult, ['ps6', 'dt6'], [kxd])
            TTo('dve', xdd[:T], xd[:T], bc6(ea[:, 6:12], T), ALU.mult, [kxd, kea], [kxdd])
            CPo('act', Btm[:T, :], pt[:T, 384:512], ['ps6'], [kBtm])
            TTo('dve', skipx[:T], xs3, bc6(hp[:, 48 + g * 6:54 + g * 6], T), ALU.mult, ['ps6', 'hp'], [kskipx])
            return

        STOP = 0
        KT = os.environ.get('KTILES')
        KG = int(os.environ.get('KG', '4'))
        KCUT = int(os.environ.get('KCUT', '99'))
        for g in range(KG):
            def load_wssd(gg):
                S.dma('pool', wssd[:, 0:4, :], wssd_d[gg].rearrange("(kc p) f -> p kc f", p=128)[:, 0:4, :], w=['wssd'])
                S.dma('pool', wssd[:, 4:8, :], wssd_d[gg].rearrange("(kc p) f -> p kc f", p=128)[:, 4:8, :], w=['wssd'])
            if g == 0 or KT is not None:
                load_wssd(g)
            S.dma('pool', wout, woutssd_d[g].rearrange("(j p) d -> p j d", p=128), w=['wout'])
            S.dma('sp', nbc, nbc_d[:, g * 384:(g + 1) * 384], w=['nbc'])
            S.dma('sp', conv0s, conv0_d[:, g], w=['conv0s'])
            for cc in range(5):
                for k in range(4):
                    TSo('dve', diag[:, cc, k, :], ident_bf, cwb[:, g, cc, k:k + 1], ALU.mult, ['ident_bf', 'cwb'], ['diag'])
            S.op('pool', lambda e: e.memset(hT.rearrange("p a b -> p (a b)"), 0.0), w=['hT'])
            S.op('pool', lambda e: e.memset(hTb, 0.0), w=['hTb'])
            S.op('pool', lambda e: e.memset(xraw[1][:, :, 512:515], 0.0), w=[('xraw', 1)])
            CPo('dve', xraw_s[:, :, :, 0:3], conv0s.rearrange("p c (b k) -> p c b k", b=16), ['conv0s'], ['xraw_s'])
            for tt, (t0, n) in enumerate(TT):
                if KT is not None and str(tt) not in KT.split(','):
                    continue
                sample = (tt == 4)
                xr = xraw[tt % 2]
                for cc in range(5):
                    bk = bigbank()
                    for kc in range(8):
                        MMo(ps[bk][:, :n], wssd[:, kc, cc * 128:(cc + 1) * 128], h[:, kc, t0:t0 + n], r=['wssd', ('h', kc)], w=[('ps', bk)],
                            start=(kc == 0), stop=(kc == 7), inc=(kc == 7))
                    if not sample:
                        CPo('act', xr[:, cc, 3:3 + n], ps[bk][:, :n], [('ps', bk)], [('xraw', tt % 2)])
                        CPo('dve', xr[:, cc, 0:3], xraw[(tt + 1) % 2][:, cc, 512:515], [('xraw', (tt + 1) % 2)], [('xraw', tt % 2)])
                        if tt == 3:
                            CPo('dve', convo[:, cc, :], ps[bk][:, 509:512], [('ps', bk)], ['convo'])
                    else:
                        p3 = ps[bk][:, :64].rearrange("p (b l) -> p b l", b=16)
                        CPo('act', xraw_s[:, cc, :, 3:7], p3, [('ps', bk)], ['xraw_s'])
                        CPo('dve', convos[:, cc, :, :], p3[:, :, 1:4], [('ps', bk)], ['convos'])
                if tt == 3:
                    S.dma('sp', convp_d[:, g], convo, r=['convo'], w=[('convp', g)])
                if sample:
                    S.dma('sp', convs_d[:, g], convos.rearrange("p c b k -> p c (b k)"), r=['convos'], w=[('convs', g)])
                for cc in range(5):
                    bk = bigbank()
                    for k in range(4):
                        if not sample:
                            rhs = xr[:, cc, k:k + n]
                            rk = ('xraw', tt % 2)
                        else:
                            rhs = xraw_s[:, cc, :, k:k + 4]
                            rk = 'xraw_s'
                        MMo(ps[bk][:, :n], diag[:, cc, k, :], rhs, r=['diag', rk], w=[('ps', bk)], start=(k == 0), stop=(k == 3), inc=(k == 3))
                    wk = ('fmx', cc) if cc < 3 else ('fmB' if cc == 3 else 'fmC')
                    ACTo(fm[:, cc, :n], ps[bk][:, :n], AF.Silu, [('ps', bk), 'cwb'], [wk], bias=cwb[:, g, cc, 4:5], scale=1.0)
                if not sample:
                    def ch(ci, part):
                        return S.record(lambda: ssd_chunk(g, 128, t0 + ci * 128, ci * 128, False, (tt == 3 and ci == 3), ci % 2, part, ci))
                    for ci in range(4):
                        ssd_pre(g, 128, t0 + ci * 128, ci)
                    E = [ch(ci, 'early') for ci in range(4)]
                    Lt = [ch(ci, 'late') for ci in range(4)]
                    S.replay(E[0])
                    for ci in range(4):
                        S.replay(S.merge(E[ci + 1] if ci + 1 < 4 else [], Lt[ci]))
                else:
                    ssd_pre(g, 64, t0, 0)
                    if g + 1 < KG and KT is None:
                        load_wssd(g + 1)
                    ssd_chunk(g, 64, t0, 0, True, False, 0, 'early')
                    ssd_chunk(g, 64, t0, 0, True, False, 0, 'late')
                for dc in range(8):
                    bk = bigbank()
                    for j in range(3):
                        MMo(ps[bk][:, :n], wout[:, j, dc * 128:(dc + 1) * 128], ofm[:, j, :n], r=['wout', 'ofm'], w=[('ps', bk)],
                            start=(j == 0), stop=(j == 2), inc=(j == 2))
                    TTo('dve', x[:, dc, t0:t0 + n], x[:, dc, t0:t0 + n], ps[bk][:, :n], ALU.add, [('ps', bk), ('x', dc, tt)], [('x', dc, tt)])


    def s5_phase():
        S.barrier()
        cv = Carver()
        I32 = mybir.dt.int32
        PI = 3.14159265358979
        winu = cv.get([128, 8, 512], BF16)
        wouts5 = cv.get([128, 4, 1024], BF16)
        wglu = cv.get([128, 4, 512], BF16)
        lhsB = [cv.get([128, 4, 8, 64], BF16) for _ in range(2)]
        lhsC = [cv.get([128, 16, 128], BF16) for _ in range(2)]
        u_all = cv.get([128, 4, NT], BF16)
        S.dma('pool', winu, winu_d.rearrange("(kc p) f -> p kc f", p=128), w=['winu'])
        S.dma('pool', wouts5, wouts5_d.rearrange("(kc p) f -> p kc f", p=128), w=['wouts5'])
        S.dma('pool', wglu, wglu_d.rearrange("(kc p) f -> p kc f", p=128), w=['wglu'])
        for c in range(2):
            S.dma('pool', lhsC[c], cpad_d[c], w=[('lhsC', c)])
        TSo('dve', lhsC[1], lhsC[1], -1.0, ALU.mult, [('lhsC', 1)], [('lhsC', 1)])
        for tt, (t0, n) in enumerate(TT):
            for kt in range(4):
                bk = bigbank()
                for kc in range(8):
                    MMo(ps[bk][:, :n], winu[:, kc, kt * 128:(kt + 1) * 128], h[:, kc, t0:t0 + n], r=['winu'], w=[('ps', bk)],
                        start=(kc == 0), stop=(kc == 7), inc=(kc == 7))
                CPo('act', u_all[:, kt, t0:t0 + n], ps[bk][:, :n], [('ps', bk)], [('u', tt)])
        S.barrier()
        W = 272
        s5v = cv.get([128, 72], F32)
        S.dma('sp', s5v, s5v_d, w=['s5v'])
        emk = cv.get([128, 8], F32)
        S.dma('sp', emk, emask_d, w=['emk'])
        iot = cv.get([128, 256], F32)
        S.dma('sp', iot, cmat_d[:, 768:1024], w=['iot'])
        st0 = cv.get([128, 2, 16, 16], F32)
        S.dma('sp', st0, s5s0_d, w=['st0'])
        mag = cv.get([128, W], F32)
        carry = cv.get([128, 2, 16], F32)
        carry_s = cv.get([128, 2, 16, 16], F32)
        off_pre = cv.off
        lam = cv.get([128, 3, W], F32)
        S.dma('sp', lam, lam_d, w=['lam'])
        bT = cv.get([128, 2, 4, 64], F32)
        S.dma('sp', bT, bT_d, w=['bT'])
        pre = [cv.get([128, W], F32) for _ in range(11)]
        stp, xr, ang, sn, cs, ar, ai, t1, t2, cr, ci = pre
        ki = cv.get([128, 512], I32)
        kf = cv.get([128, 512], F32)
        rr = cv.get([128, 512], F32)

        def sin_of(out, a, n, shift, key):
            S.op('dve', lambda e: e.tensor_scalar(out=ki[:, :n], in0=a, scalar1=shift, scalar2=1.0 / (2 * PI), op0=ALU.add, op1=ALU.mult),
                 r=[key], w=['ki'])
            CPo('dve', kf[:, :n], ki[:, :n], ['ki'], ['kf'])
            STTo('dve', rr[:, :n], kf[:, :n], -2 * PI, a, ALU.mult, ALU.add, ['kf', key], ['rr'])
            S.op('dve', lambda e: e.tensor_scalar(out=rr[:, :n], in0=rr[:, :n], scalar1=shift, scalar2=-3.1415925, op0=ALU.add, op1=ALU.max),
                 r=['rr'], w=['rr'])
            TSo('dve', rr[:, :n], rr[:, :n], 3.1415925, ALU.min, ['rr'], ['rr'])
            ACTo(out, rr[:, :n], AF.Sin, ['rr'], [key + '_o'])

        ACTo(stp, lam[:, 2, :], AF.Exp, ['lam'], ['pre'])
        TTo('dve', xr, lam[:, 0, :], stp, ALU.mult, ['lam', 'pre'], ['pre'])
        TTo('dve', ang, lam[:, 1, :], stp, ALU.mult, ['lam', 'pre'], ['ang'])
        ACTo(mag, xr, AF.Exp, ['pre'], ['pre'])
        sin_of(sn, ang, W, 0.0, 'ang')
        sin_of(cs, ang, W, PI / 2, 'ang')
        P_ = ['pre', 'ang_o', 'lam']
        TTo('dve', ar, mag, cs, ALU.mult, P_, ['pre'])
        TTo('dve', ai, mag, sn, ALU.mult, P_, ['pre'])
        TTo('dve', t1, lam[:, 0, :], lam[:, 0, :], ALU.mult, P_, ['pre'])
        TTo('dve', t2, lam[:, 1, :], lam[:, 1, :], ALU.mult, P_, ['pre'])
        TTo('dve', t1, t1, t2, ALU.add, P_, ['pre'])
        S.op('dve', lambda e: e.reciprocal(out=t1, in_=t1), r=P_, w=['pre'])
        TSo('dve', t2, ar, -1.0, ALU.add, P_, ['pre'])
        TTo('dve', cr, t2, lam[:, 0, :], ALU.mult, P_, ['pre'])
        TTo('dve', ci, ai, lam[:, 1, :], ALU.mult, P_, ['pre'])
        TTo('dve', cr, cr, ci, ALU.add, P_, ['pre'])
        TTo('dve', cr, cr, t1, ALU.mult, P_, ['pre'])
        TTo('dve', ci, ai, lam[:, 0, :], ALU.mult, P_, ['pre'])
        TTo('dve', t2, t2, lam[:, 1, :], ALU.mult, P_, ['pre'])
        TTo('dve', ci, ci, t2, ALU.subtract, P_, ['pre'])
        TTo('dve', ci, ci, t1, ALU.mult, P_, ['pre'])
        bb = [cv.get([128, 4, 64], F32) for _ in range(2)]
        tb = cv.get([128, 4, 64], F32)
        cr3 = cr[:, 0:256].rearrange("p (a b) -> p a b", a=4)
        ci3 = ci[:, 0:256].rearrange("p (a b) -> p a b", a=4)
        TTo('dve', bb[0], cr3, bT[:, 0], ALU.mult, P_ + ['bT'], ['bb'])
        TTo('dve', tb, ci3, bT[:, 1], ALU.mult, P_ + ['bT'], ['tb'])
        TTo('dve', bb[0], bb[0], tb, ALU.subtract, ['bb', 'tb'], ['bb'])
        TTo('dve', bb[1], cr3, bT[:, 1], ALU.mult, P_ + ['bT'], ['bb'])
        TTo('dve', tb, ci3, bT[:, 0], ALU.mult, P_ + ['bT'], ['tb'])
        TTo('dve', bb[1], bb[1], tb, ALU.add, ['bb', 'tb'], ['bb'])
        for c in range(2):
            for kt in range(4):
                TTo('dve', lhsB[c][:, kt], bb[c][:, kt, None, :].to_broadcast([128, 8, 64]), emk[:, :, None].to_broadcast([128, 8, 64]),
                    ALU.mult, ['bb', 'emk'], [('lhsB', c)])
        tabf = h.rearrange("p a b -> p (a b)").bitcast(F32)
        Ec = tabf[:, 0:4096].rearrange("p (a b) -> p a b", a=16)
        Es = tabf[:, 4096:8192].rearrange("p (a b) -> p a b", a=16)
        angt = cv.get([128, 2, 256], F32)
        for i0 in range(0, 16, 2):
            TTo('dve', angt, ang[:, 256 + i0:258 + i0][:, :, None].to_broadcast([128, 2, 256]), iot[:, None, :].to_broadcast([128, 2, 256]),
                ALU.mult, ['ang', 'iot'], ['angt'])
            af = angt.rearrange("p a b -> p (a b)")
            sin_of(Es[:, i0:i0 + 2, :].rearrange("p a b -> p (a b)"), af, 512, 0.0, 'angt')
            sin_of(Ec[:, i0:i0 + 2, :].rearrange("p a b -> p (a b)"), af, 512, PI / 2, 'angt')
        TAB = ['angt_o']
        S.barrier()
        cv.off = off_pre
        S.op('pool', lambda e: e.memset(carry.rearrange("p a b -> p (a b)"), 0.0), w=['carry'])
        tmA = [cv.get([128, 2, 256], F32) for _ in range(4)]
        wreg = winu.rearrange("p a b -> p (a b)").bitcast(F32)
        tmB = [wreg[:, k * 512:(k + 1) * 512].rearrange("p (a b) -> p a b", a=2) for k in range(4)]
        tm2 = [tmA, tmB]
        wv2 = [[cv.get([128, 2, 256], F32) for _ in range(2)] for _ in range(2)]
        Wv2 = [[cv.get([128, 2, 256], F32) for _ in range(2)] for _ in range(2)]
        sv2 = [[cv.get([128, 2, 256], F32) for _ in range(2)] for _ in range(2)]
        rt_single = cv.get([128, 256], F32)
        rt2 = [rt_single, rt_single]
        hist2 = [cv.get([128, 2, 512], BF16) for _ in range(2)]
        y5 = cv.get([128, 512], F32)
        g1 = cv.get([128, 512], F32)
        v_bf = cv.get([128, 4, 512], BF16)
        gs = g1
        o5 = cv.get([128, 4, 512], BF16)
        print("s5 arena used", cv.off)
        magq = mag[:, 256:272]

        def bct(tab, i, nseg, L):
            return tab[:, i, None, 0:L].to_broadcast([128, nseg, L])

        for tt, (t0, n) in enumerate(TT):
            sample = (tt == 4)
            nseg, L = (16, 4) if sample else (2, 256)

            def v3(ap):
                return ap.rearrange("p a b -> p (a b)")[:, :nseg * L].rearrange("p (a b) -> p a b", a=nseg)
            for kt in range(4):
                cb = 4 if kt % 2 == 0 else 7
                kcb = 'ps%d' % cb
                for jp in (0, 2):
                    ctxs = []
                    for j in (jp, jp + 1):
                        i = 4 * kt + j
                        par = i % 2
                        pre_b, pim_b = (2, 3) if par == 0 else (5, 6)
                        ctxs.append(dict(i=i, j=j, par=par, tm=tm2[par], wv=wv2[par], Wv=Wv2[par], sv=sv2[par], hist=hist2[par],
                                         pre_b=pre_b, pim_b=pim_b, kre='ps%d' % pre_b, kim='ps%d' % pim_b))
                    for cx in ctxs:
                        i, j = cx['i'], cx['j']
                        lB = [lhsB[c].rearrange("p a b c -> p a (b c)")[:, kt, j * 128:(j + 1) * 128] for c in range(2)]
                        MMo(ps[cx['pre_b']][:, :n], lB[0], u_all[:, kt, t0:t0 + n], r=[('lhsB', 0), ('u', tt)], w=[cx['kre']])
                        MMo(ps[cx['pim_b']][:, :n], lB[1], u_all[:, kt, t0:t0 + n], r=[('lhsB', 1), ('u', tt)], w=[cx['kim']])
                    for cx in ctxs:
                        i, par, tm = cx['i'], cx['par'], cx['tm']
                        Pre = ps[cx['pre_b']][:, :n].rearrange("p (a b) -> p a b", a=nseg)
                        Pim = ps[cx['pim_b']][:, :n].rearrange("p (a b) -> p a b", a=nseg)
                        ec, es = bct(Ec, i, nseg, L), bct(Es, i, nseg, L)
                        TTo('dve', v3(tm[0]), Pre, ec, ALU.mult, [cx['kre']] + TAB, [('tm', par, 0)])
                        TTo('dve', v3(tm[1]), Pim, es, ALU.mult, [cx['kim']] + TAB, [('tm', par, 1)])
                        TTo('dve', v3(tm[2]), Pim, ec, ALU.mult, [cx['kim']] + TAB, [('tm', par, 2)])
                        TTo('dve', v3(tm[3]), Pre, es, ALU.mult, [cx['kre']] + TAB, [('tm', par, 3)])
                    for cx in ctxs:
                        par, tm, wv = cx['par'], cx['tm'], cx['wv']
                        TTo('pool', v3(wv[0]), v3(tm[0]), v3(tm[1]), ALU.add, [('tm', par, 0), ('tm', par, 1)], [('wv', par, 0)])
                        TTo('pool', v3(wv[1]), v3(tm[2]), v3(tm[3]), ALU.subtract, [('tm', par, 2), ('tm', par, 3)], [('wv', par, 1)])
                    groups = [list(range(16))] if sample else [[0], [1]]
                    for grp in groups:
                        g0, g1_ = grp[0], grp[-1] + 1
                        for cx in ctxs:
                            i, par, wv, Wv, sv = cx['i'], cx['par'], cx['wv'], cx['Wv'], cx['sv']
                            rbc = magq[:, i:i + 1].to_broadcast([128, L])
                            if sample:
                                rt64 = tm2[par][0].rearrange("p a b -> p (a b)")[:, 64:128]
                                TSo('dve', rt64, s5v[:, 8:72], magq[:, i:i + 1], ALU.mult, ['s5v', 'pre', ('tm', par, 0)], [('rt64', par)])
                                for c in range(2):
                                    w0 = v3(wv[c])[:, :, 0]
                                    STTo('dve', w0, st0[:, c, i, :], magq[:, i:i + 1], w0, ALU.mult, ALU.add, ['st0', 'pre', ('wv', par, c)], [('wv', par, c)])
                                    wf = wv[c].rearrange("p a b -> p (a b)")[:, 0:64]
                                    Wf = Wv[c].rearrange("p a b -> p (a b)")[:, 0:64]
                                    S.op('dve', lambda e: e.tensor_tensor_scan(out=Wf, data0=rt64, data1=wf, initial=0.0, op0=ALU.mult, op1=ALU.add),
                                         r=[('rt64', par), ('wv', par, c)], w=[('Wv', par, c)])
                                continue
                            for sg_ in grp:
                                for c in range(2):
                                    if sample:
                                        init = st0[:, c, i, sg_:sg_ + 1]
                                        ik = 'st0'
                                    elif sg_ == 0:
                                        init = carry[:, c, i:i + 1]
                                        ik = 'carry'
                                    else:
                                        init = sv[c][:, 0, 255:256]
                                        ik = ('sv', par, c)
                                    S.op('dve', lambda e: e.tensor_tensor_scan(out=v3(Wv[c])[:, sg_, :], data0=rbc, data1=v3(wv[c])[:, sg_, :],
                                                                               initial=init, op0=ALU.mult, op1=ALU.add),
                                         r=['pre', ('wv', par, c), ik], w=[('Wv', par, c)])
                        for cx in ctxs:
                            i, par, tm, Wv = cx['i'], cx['par'], cx['tm'], cx['Wv']
                            ecg = Ec[:, i, None, 0:L].to_broadcast([128, g1_ - g0, L])
                            esg = Es[:, i, None, 0:L].to_broadcast([128, g1_ - g0, L])
                            TTo('dve', v3(tm[0])[:, g0:g1_], v3(Wv[0])[:, g0:g1_], ecg, ALU.mult, [('Wv', par, 0)] + TAB, [('tm', par, 0)])
                            TTo('dve', v3(tm[1])[:, g0:g1_], v3(Wv[1])[:, g0:g1_], esg, ALU.mult, [('Wv', par, 1)] + TAB, [('tm', par, 1)])
                        for cx in ctxs:
                            i, par, tm, Wv, sv = cx['i'], cx['par'], cx['tm'], cx['Wv'], cx['sv']
                            ecg = Ec[:, i, None, 0:L].to_broadcast([128, g1_ - g0, L])
                            esg = Es[:, i, None, 0:L].to_broadcast([128, g1_ - g0, L])
                            TTo('pool', v3(tm[2])[:, g0:g1_], v3(Wv[1])[:, g0:g1_], ecg, ALU.mult, [('Wv', par, 1)] + TAB, [('tm', par, 2)])
                            TTo('pool', v3(tm[3])[:, g0:g1_], v3(Wv[0])[:, g0:g1_], esg, ALU.mult, [('Wv', par, 0)] + TAB, [('tm', par, 3)])
                            TTo('pool', v3(sv[0])[:, g0:g1_], v3(tm[0])[:, g0:g1_], v3(tm[1])[:, g0:g1_], ALU.subtract,
                                [('tm', par, 0), ('tm', par, 1)], [('sv', par, 0)])
                            TTo('pool', v3(sv[1])[:, g0:g1_], v3(tm[2])[:, g0:g1_], v3(tm[3])[:, g0:g1_], ALU.add,
                                [('tm', par, 2), ('tm', par, 3)], [('sv', par, 1)])
                    for cx in ctxs:
                        i, j, par, sv, hist = cx['i'], cx['j'], cx['par'], cx['sv'], cx['hist']
                        for c in range(2):
                            svf = sv[c].rearrange("p a b -> p (a b)")
                            CPo('act', hist[:, c, :n], svf[:, :n], [('sv', par, c)], [('hist', par, c)])
                            if sample:
                                CPo('act', carry_s[:, c, i, :], v3(sv[c])[:, :, 3], [('sv', par, c)], ['carry_s'])
                            else:
                                CPo('act', carry[:, c, i:i + 1], svf[:, 511:512], [('sv', par, c)], ['carry'])
                            MMo(ps[cb][:, :n], lhsC[c][:, i, :], hist[:, c, :n], r=[('lhsC', c), ('hist', par, c)], w=[kcb],
                                start=(j == 0 and c == 0), stop=(j == 3 and c == 1), inc=True)
                STTo('dve', y5[:, :n], u_all[:, kt, t0:t0 + n], s5v[:, kt:kt + 1], ps[cb][:, :n], ALU.mult, ALU.add, [('u', tt), 's5v', kcb], ['y5'])
                TTo('pool', g1[:, :n], y5[:, :n], y5[:, :n], ALU.mult, ['y5'], ['g1'])
                S.op('dve', lambda e: e.tensor_scalar(out=g1[:, :n], in0=g1[:, :n], scalar1=0.044715, scalar2=1.0, op0=ALU.mult, op1=ALU.add),
                     r=['g1'], w=['g1'])
                TTo('pool', g1[:, :n], g1[:, :n], y5[:, :n], ALU.mult, ['g1', 'y5'], ['g1'])
                ACTo(g1[:, :n], g1[:, :n], AF.Sigmoid, ['g1'], ['g1'], scale=1.5957691216057308)
                TTo('dve', v_bf[:, kt, :n], g1[:, :n], y5[:, :n], ALU.mult, ['g1', 'y5'], [('v', kt)])
            for mo in range(4):
                bk = bigbank()
                for kt in range(4):
                    MMo(ps[bk][:, :n], wglu[:, kt, mo * 128:(mo + 1) * 128], v_bf[:, kt, :n], r=['wglu', ('v', kt)], w=[('ps', bk)],
                        start=(kt == 0), stop=(kt == 3), inc=(kt == 3))
                ACTo(gs[:, :n], ps[bk][:, :n], AF.Sigmoid, [('ps', bk), 's5v'], ['g1'], bias=s5v[:, 4 + mo:5 + mo], scale=1.0)
                TTo('dve', o5[:, mo, :n], v_bf[:, mo, :n], gs[:, :n], ALU.mult, [('v', mo), 'g1'], [('o5', mo)])
            for dc in range(8):
                bk = bigbank()
                for mo in range(4):
                    MMo(ps[bk][:, :n], wouts5[:, mo, dc * 128:(dc + 1) * 128], o5[:, mo, :n], r=['wouts5', ('o5', mo)], w=[('ps', bk)],
                        start=(mo == 0), stop=(mo == 3), inc=(mo == 3))
                TTo('dve', x[:, dc, t0:t0 + n], x[:, dc, t0:t0 + n], ps[bk][:, :n], ALU.add, [('ps', bk), ('x', dc, tt)], [('x', dc, tt)])
        S.dma('sp', s5p_d, carry, r=['carry'], w=['s5p'])
        S.dma('sp', s5s_d, carry_s, r=['carry_s'], w=['s5s'])


    PH = os.environ.get('KPH', 'n1,f1,n2,ssd,s5,n3,f2').split(',')
    if 'n1' in PH:
        norm_to_h(0)
    if 'f1' in PH:
        ffn(0, after_tile=(lambda tt: norm_to_h(1, tiles=[tt])) if 'n2' in PH else None)
    elif 'n2' in PH:
        norm_to_h(1)
    if 'ssd' in PH:
        ssd_phase()
    if 's5' in PH:
        s5_phase()
    S.barrier()
    if 'n3' in PH:
        norm_to_h(2)
    yT_v = yT.rearrange("(c p) t -> p c t", p=128)
    ycount = [0]

    def final_out(tt, c, t0, n, rb):
        yb = ycount[0] % 2
        ycount[0] += 1
        S.op('dve', lambda e: e.scalar_tensor_tensor(out=ystage[yb][:, :n], in0=x[:, c, t0:t0 + n],
                                                     scalar=normw[:, 24 + c:24 + c + 1], in1=rb[:, :n],
                                                     op0=ALU.mult, op1=ALU.mult),
             r=[('x', c, tt), ('rstd', tt % 2), 'normw'], w=[('ystage', yb)])
        S.dma('sp', yT_v[:, c, t0:t0 + n], ystage[yb][:, :n], r=[('ystage', yb)], w=[('yT', c, tt)])
    if 'f2' in PH:
        ffn(1, after_tile=lambda tt: rmsnorm(3, final_out, tiles=[tt]))
    else:
        rmsnorm(3, final_out)
    S.finish()
    print("instructions", S.nins, "waits", S.nwait)
    return nc


_NC_CACHE = {}


def _consts():
    f32 = np.float32
    cm = np.zeros((128, 2048), f32)
    k = np.arange(128)
    cm[:, 0:128] = (k[:, None] <= k[None, :])
    cm[:, 128:256] = (k[:, None] > k[None, :])
    same = (k[:, None] // 4 == k[None, :] // 4) & (k[:, None] < 64) & (k[None, :] < 64)
    cm[:, 256:384] = (k[:, None] <= k[None, :]) & same
    cm[:, 384:512] = (k[:, None] > k[None, :]) & same
    cm[:, 512:640] = np.eye(128)
    cm[:64, 640:656] = (np.arange(64)[:, None] // 4 == np.arange(16)[None, :])
    cm[:, 768:1024] = np.arange(1, 257)[None, :]
    et = (np.arange(64)[None, :] // 4 == np.arange(16)[:, None]).astype(f32)
    cm[:, 1024:2048] = et.reshape(1, 1024)
    return cm


def _chan(g, cc):
    if cc < 3:
        return 384 * g + 128 * cc
    if cc == 3:
        return 1536 + 128 * g
    return 2048 + 128 * g


def _s5_host(inp):
    f32 = np.float32
    A = lambda k: np.asarray(inp[k], f32)
    w_in = A("w_in")[0]
    w_out = A("w_out")[0]
    s5v = np.empty((128, 72), f32)
    s5v[:, 8:72] = (np.arange(64) % 4 != 0).astype(f32)[None, :]
    s5v[:, 0:4] = A("s5_d")[0].reshape(4, 128).T
    s5v[:, 4:8] = A("s5_b_glu")[0].reshape(4, 128).T
    lre, lim, lst = A("s5_lambda_re")[0], A("s5_lambda_im")[0], A("s5_log_step")[0]
    lam = np.empty((128, 3, 272), f32)
    for arr_i, arr in enumerate((lre, lim)):
        rep = arr.reshape(4, 8, 64)
        rep = np.repeat(rep.transpose(1, 0, 2)[:, None], 16, axis=1)
        lam[:, arr_i, 0:256] = rep.reshape(128, 256)
        qq = arr.reshape(16, 2, 64).transpose(1, 2, 0).reshape(128, 16)
        lam[:, arr_i, 256:272] = qq
    rep = np.repeat(lst.reshape(4, 8).T[:, None, :, None], 16, axis=1)
    lam[:, 2, 0:256] = np.broadcast_to(rep, (8, 16, 4, 64)).reshape(128, 256)
    qq = np.broadcast_to(lst.reshape(16, 2).T[:, None, :], (2, 64, 16)).reshape(128, 16)
    lam[:, 2, 256:272] = qq
    bT = np.empty((128, 2, 4, 64), f32)
    for ci, k in enumerate(("s5_b_re", "s5_b_im")):
        b = A(k)[0].reshape(4, 8, 64, 16)
        bT[:, ci] = b.transpose(1, 3, 0, 2).reshape(128, 4, 64)
    emask = (np.arange(128)[:, None] // 16 == np.arange(8)[None, :]).astype(f32)
    cpad = np.zeros((2, 128, 16, 128), f32)
    for ci, k in enumerate(("s5_c_re", "s5_c_im")):
        cc = A(k)[0]
        for g in range(32):
            i, q0, gl = g // 2, (g % 2) * 64, g % 8
            cpad[ci, q0:q0 + 64, i, gl * 16:(gl + 1) * 16] = cc[g].T
    return {"w_in_u": np.ascontiguousarray(w_in[:, 0:512]), "w_out_s5": np.ascontiguousarray(w_out[0:512]),
            "w_glu": np.ascontiguousarray(A("s5_w_glu")[0]), "s5v": s5v, "lam": lam, "bT": bT, "emask": emask, "cpad": cpad}


def _s5_core(inp, c):
    f32 = np.float32
    out = np.empty((128, 2, 16, 16), f32)
    for ci, k in enumerate(("state_s5_re", "state_s5_im")):
        s = np.asarray(inp[k], f32)[0, 16 * c:16 * c + 16]
        out[:, ci] = s.reshape(16, 16, 2, 64).transpose(2, 3, 1, 0).reshape(128, 16, 16)
    return {"s5s0": out}


def kernel(**inp):
    f32 = np.float32
    A = lambda k: np.asarray(inp[k], f32)
    xp = A("x_prompt")
    xs = A("x_sample")
    normw = np.stack([A(k).reshape(D) for k in ("ffn1_norm", "mix_norm", "ffn2_norm", "final_norm")])
    normw_l = np.ascontiguousarray(normw.reshape(4, 8, 128).transpose(2, 0, 1).reshape(128, 32))
    w_in = A("w_in")[0]
    w_out = A("w_out")[0]
    w_ssd = np.empty((4, D, 1030), f32)
    w_out_ssd = np.empty((4, 384, D), f32)
    for g in range(4):
        w_ssd[g, :, 0:384] = w_in[:, 2048 + 384 * g:2048 + 384 * (g + 1)]
        w_ssd[g, :, 384:512] = w_in[:, 3584 + 128 * g:3584 + 128 * (g + 1)]
        w_ssd[g, :, 512:640] = w_in[:, 4096 + 128 * g:4096 + 128 * (g + 1)]
        w_ssd[g, :, 640:1024] = w_in[:, 512 + 384 * g:512 + 384 * (g + 1)]
        w_ssd[g, :, 1024:1030] = w_in[:, 4608 + 6 * g:4608 + 6 * (g + 1)]
        w_out_ssd[g] = w_out[512 + 384 * g:512 + 384 * (g + 1)]
    conv_w = A("ssd_conv_w")[0]
    conv_b = A("ssd_conv_b")[0]
    cwb = np.empty((128, 4, 5, 5), f32)
    for g in range(4):
        for cc in range(5):
            c0 = _chan(g, cc)
            cwb[:, g, cc, 0:4] = conv_w[:, c0:c0 + 128].T
            cwb[:, g, cc, 4] = conv_b[c0:c0 + 128]
    hp = np.concatenate([A("ssd_dt_bias")[0], A("ssd_a_log")[0], A("ssd_d")[0]])[None, :].repeat(128, 0)
    nbc = A("ssd_norm")[0][None, :].repeat(128, 0)
    sconv = A("state_conv")[0]
    sssd = A("state_ssd")[0]
    shared = {
        "normw": normw_l,
        "ffn1_wg": np.ascontiguousarray(A("ffn1_w_gate")[0]), "ffn1_wu": np.ascontiguousarray(A("ffn1_w_up")[0]),
        "ffn1_wd": np.ascontiguousarray(A("ffn1_w_down")[0]),
        "ffn2_wg": np.ascontiguousarray(A("ffn2_w_gate")[0]), "ffn2_wu": np.ascontiguousarray(A("ffn2_w_up")[0]),
        "ffn2_wd": np.ascontiguousarray(A("ffn2_w_down")[0]),
        "cmat": _consts(), "hp": np.ascontiguousarray(hp), "nbc": np.ascontiguousarray(nbc),
        "w_ssd": w_ssd, "w_out_ssd": w_out_ssd, "cwb": cwb,
    }
    shared.update(_s5_host(inp))
    in_maps = []
    for c in range(NCORES):
        bs = slice(16 * c, 16 * c + 16)
        xc = np.concatenate([xp[c], xs[bs].reshape(TS, D)], axis=0)
        m = dict(shared)
        m["xT"] = np.ascontiguousarray(xc.T)
        cv0 = np.empty((128, 4, 5, 16, 3), f32)
        for g in range(4):
            for cc in range(5):
                c0 = _chan(g, cc)
                cv0[:, g, cc] = sconv[bs, :, c0:c0 + 128].transpose(2, 0, 1)
        m["conv0"] = cv0.reshape(128, 4, 5, 48)
        st = sssd[bs].reshape(16, 4, 6, 64, 128).transpose(0, 1, 4, 2, 3).reshape(16, 4, 128, 384)
        m["ssd0"] = np.ascontiguousarray(st)
        m.update(_s5_core(inp, c))
        in_maps.append(m)
    if "nc" not in _NC_CACHE:
        _NC_CACHE["nc"] = build()
    nc = _NC_CACHE["nc"]
    res = run_bass_kernel_spmd(nc, in_maps, core_ids=list(range(NCORES)))
    R = res.results
    yp = np.empty((8, TP, D), f32)
    ys = np.empty((128, 4, D), f32)
    ssd_p = np.empty((1, 8, 24, 64, 128), f32)
    ssd_s = np.empty((1, 128, 24, 64, 128), f32)
    conv_p = np.empty((1, 8, 3, 2560), f32)
    conv_s = np.empty((1, 128, 3, 2560), f32)
    s5p = np.empty((2, 1, 8, 32, 64), f32)
    s5s = np.empty((2, 1, 128, 32, 64), f32)
    for c in range(NCORES):
        bs = slice(16 * c, 16 * c + 16)
        y = R[c]["yT"].T
        yp[c] = y[:TP]
        ys[bs] = y[TP:].reshape(16, 4, D)
        ssd_p[0, c] = R[c]["ssdp"].reshape(4, 128, 6, 64).transpose(0, 2, 3, 1).reshape(24, 64, 128)
        ssd_s[0, bs] = R[c]["ssds"].reshape(16, 4, 128, 6, 64).transpose(0, 1, 3, 4, 2).reshape(16, 24, 64, 128)
        cp = R[c]["convp"]
        cs = R[c]["convs"].reshape(128, 4, 5, 16, 3)
        for g in range(4):
            for cc in range(5):
                c0 = _chan(g, cc)
                conv_p[0, c, :, c0:c0 + 128] = cp[:, g, cc, :].T
                conv_s[0, bs, :, c0:c0 + 128] = cs[:, g, cc].transpose(1, 2, 0)
        a = R[c]["s5p"]
        s5p[:, 0, c] = a.reshape(2, 64, 2, 16).transpose(2, 3, 0, 1).reshape(2, 32, 64)
        b_ = R[c]["s5s"]
        s5s[:, 0, bs] = b_.reshape(2, 64, 2, 16, 16).transpose(2, 4, 3, 0, 1).reshape(2, 16, 32, 64)
    return (yp, ys, s5p[0], s5p[1], ssd_p, conv_p, s5s[0], s5s[1], ssd_s, conv_s)
```
